# Optimizing a Trainium2 kernel written in Bass

```python
import math
import jax
import jax.numpy as jnp
from jax import lax
import numpy as np

D_MODEL = 2048
BATCH = 2
SEQ = 4096
DEPTH = 1

N_META = 16
D_MIX = D_MODEL
A_WIDTH = D_MIX // 2
A_V_DIM = 128
A_QK_DIM = A_V_DIM // 2
A_HEADS = A_WIDTH // A_V_DIM
A_QK_COLS = A_HEADS * 2 * A_QK_DIM
R_WIDTH = D_MIX - A_WIDTH
R_HEAD = 64
R_HEADS = R_WIDTH // R_HEAD
DECAY_LORA = 96
ICLR_LORA = 96
R_SHIFT_COLS = 3 * R_WIDTH + DECAY_LORA + ICLR_LORA
D_IN = 2 * A_QK_COLS + 2 * A_WIDTH + R_SHIFT_COLS + R_WIDTH
N_BUCKETS = 32
MAX_DISTANCE = 128
Q_BLOCK = 128
PAD_FRONT = Q_BLOCK - N_META
LN_EPS = 1e-5
SUBLN_EPS = 1e-5
GN_EPS = 64e-5
DEEPNORM_ALPHA = (2 * DEPTH) ** 0.25
DEEPNORM_BETA = (8 * DEPTH) ** -0.25

A_SPLITS = (A_QK_COLS, 2 * A_QK_COLS, 2 * A_QK_COLS + A_WIDTH, 2 * A_QK_COLS + 2 * A_WIDTH,
            2 * A_QK_COLS + 2 * A_WIDTH + R_SHIFT_COLS)
R_SPLITS = (R_WIDTH, 2 * R_WIDTH, 3 * R_WIDTH, 3 * R_WIDTH + DECAY_LORA)

kernel_name = 'hymba_diffattn_rwkv7_deepnorm'


def layer_norm(x, g, b):
    xf = x.astype(jnp.float32)
    mu = jnp.mean(xf, axis=-1, keepdims=True)
    var = jnp.mean(jnp.square(xf - mu), axis=-1, keepdims=True)
    y = (xf - mu) * lax.rsqrt(var + LN_EPS) * g.astype(jnp.float32) + b.astype(jnp.float32)
    return y.astype(x.dtype)


def t5_bucket(q_idx, k_idx):
    n = jnp.maximum(q_idx[:, None] - k_idx[None, :], 0)
    max_exact = N_BUCKETS // 2
    nf = jnp.maximum(n, 1).astype(jnp.float32)
    large = max_exact + (jnp.log(nf / max_exact) / math.log(MAX_DISTANCE / max_exact)
                         * (N_BUCKETS - max_exact)).astype(jnp.int32)
    large = jnp.minimum(large, N_BUCKETS - 1)
    return jnp.where(n < max_exact, n, large)


def diff_attention(q, k, v, rel_bias, lam):
    B, P = q.shape[0], q.shape[1]
    nb = P // Q_BLOCK
    k_idx = jnp.arange(P)
    q_blocks = jnp.moveaxis(q.reshape(B, nb, Q_BLOCK, A_HEADS, 2, A_QK_DIM), 1, 0)
    scale = A_QK_DIM ** -0.5
    neg = jnp.finfo(jnp.float32).min

    def one_block(args):
        blk, q_blk = args
        q_idx = blk * Q_BLOCK + jnp.arange(Q_BLOCK)
        bias = jnp.moveaxis(rel_bias[t5_bucket(q_idx, k_idx)].astype(jnp.float32), -1, 0)
        kk_, qq_ = k_idx[None, :], q_idx[:, None]
        visible = (kk_ <= qq_) & ((kk_ >= PAD_FRONT) | (kk_ == qq_))
        s = jnp.einsum('bqhmd,bkhmd->bhmqk', q_blk, k).astype(jnp.float32) * scale + bias[None, :, None]
        s = jnp.where(visible, s, neg)
        p = jax.nn.softmax(s, axis=-1)
        p = p[:, :, 0] - lam * p[:, :, 1]
        return jnp.einsum('bhqk,bkhd->bqhd', p.astype(v.dtype), v)

    out = lax.map(one_block, (jnp.arange(nb), q_blocks))
    return jnp.moveaxis(out, 0, 1).reshape(B, P, A_HEADS, A_V_DIM)


def rwkv7_step(S, inp):
    r_t, w_t, k_t, v_t, a_t, b_t = inp
    sa = jnp.einsum('bhvk,bhk->bhv', S, a_t)
    S = S * w_t[:, :, None, :] + sa[..., None] * b_t[:, :, None, :] + v_t[..., None] * k_t[:, :, None, :]
    y = jnp.einsum('bhvk,bhk->bhv', S, r_t)
    return S, y


def rwkv7_time_mix(zr, mu, w0, w_up, a0, a_up, k_k, k_a, r_k, gn_g, gn_b):
    B, L, _ = zr.shape
    f32 = jnp.float32
    z_prev = jnp.pad(zr, ((0, 0), (1, 0), (0, 0)))[:, :L]
    zs = zr + (z_prev - zr) * mu
    r, k, v, wd, ad = jnp.split(zs, R_SPLITS, axis=-1)
    w_log = -jax.nn.softplus(-(w0 + jnp.tanh(wd) @ w_up)) - 0.5
    decay = jnp.exp(-jnp.exp(w_log.astype(f32)))
    a = jax.nn.sigmoid(a0 + ad @ a_up)

    def heads(t):
        return t.astype(f32).reshape(B, L, R_HEADS, R_HEAD)

    def head_param(p):
        return p.astype(f32).reshape(R_HEADS, R_HEAD)

    r, k, v, a, decay = heads(r), heads(k), heads(v), heads(a), heads(decay)
    kk = k * head_param(k_k)
    kk = kk / jnp.maximum(jnp.sqrt(jnp.sum(kk * kk, axis=-1, keepdims=True)), 1e-12)
    k = k * (1.0 + (a - 1.0) * head_param(k_a))
    seq_first = lambda t: jnp.moveaxis(t, 1, 0)
    S0 = jnp.zeros((B, R_HEADS, R_HEAD, R_HEAD), f32)
    xs = (seq_first(r), seq_first(decay), seq_first(k), seq_first(v), seq_first(-kk), seq_first(kk * a))
    _, y = lax.scan(rwkv7_step, S0, xs)
    y = jnp.moveaxis(y, 0, 1)
    m = jnp.mean(y, axis=-1, keepdims=True)
    var = jnp.mean(jnp.square(y - m), axis=-1, keepdims=True)
    y = (y - m) * lax.rsqrt(var + GN_EPS) * head_param(gn_g) + head_param(gn_b)
    y = y + jnp.sum(r * k * r_k.astype(f32), axis=-1, keepdims=True) * v
    return y.reshape(B, L, R_WIDTH).astype(zr.dtype)


def setup_inputs(seed: int = 0) -> dict:
    key = jax.random.key(seed)
    ks = jax.random.split(key, 24)
    nrm = jax.random.normal
    f = jnp.float32
    return {
        'x': nrm(ks[0], (BATCH, SEQ, D_MODEL), f),
        'meta_tokens': nrm(ks[1], (N_META, D_MODEL), f),
        'ln_emb_g': 1.0 + 0.02 * nrm(ks[2], (D_MODEL,), f),
        'ln_emb_b': 0.02 * nrm(ks[3], (D_MODEL,), f),
        'rel_bias': 0.5 * nrm(ks[4], (N_BUCKETS, A_HEADS), f),
        'w_in': nrm(ks[5], (DEPTH, D_MODEL, D_IN), f) * D_MODEL ** -0.5,
        'w_out': nrm(ks[6], (DEPTH, D_MIX, D_MODEL), f) * (D_MIX ** -0.5 * DEEPNORM_BETA),
        'lambda_q1': 0.1 * nrm(ks[7], (DEPTH, A_QK_DIM), f),
        'lambda_k1': 0.1 * nrm(ks[8], (DEPTH, A_QK_DIM), f),
        'lambda_q2': 0.1 * nrm(ks[9], (DEPTH, A_QK_DIM), f),
        'lambda_k2': 0.1 * nrm(ks[10], (DEPTH, A_QK_DIM), f),
        'subln_g': 1.0 + 0.02 * nrm(ks[11], (DEPTH, A_V_DIM), f),
        'rw_mu': jax.random.uniform(ks[12], (DEPTH, R_SHIFT_COLS), f),
        'rw_w0': jax.random.uniform(ks[13], (DEPTH, R_WIDTH), f, minval=-6.0, maxval=-1.0),
        'rw_w_up': nrm(ks[14], (DEPTH, DECAY_LORA, R_WIDTH), f) * (0.5 * DECAY_LORA ** -0.5),
        'rw_a0': 0.5 * nrm(ks[15], (DEPTH, R_WIDTH), f),
        'rw_a_up': nrm(ks[16], (DEPTH, ICLR_LORA, R_WIDTH), f) * (0.5 * ICLR_LORA ** -0.5),
        'rw_k_k': 0.85 + 0.05 * nrm(ks[17], (DEPTH, R_WIDTH), f),
        'rw_k_a': 1.0 + 0.05 * nrm(ks[18], (DEPTH, R_WIDTH), f),
        'rw_r_k': 0.1 * nrm(ks[19], (DEPTH, R_HEADS, R_HEAD), f),
        'rw_gn_g': 1.0 + 0.02 * nrm(ks[20], (DEPTH, R_WIDTH), f),
        'rw_gn_b': 0.02 * nrm(ks[21], (DEPTH, R_WIDTH), f),
        'ln_post_g': 1.0 + 0.02 * nrm(ks[22], (DEPTH, D_MODEL), f),
        'ln_post_b': 0.02 * nrm(ks[23], (DEPTH, D_MODEL), f),
    }


def reference(x, meta_tokens, ln_emb_g, ln_emb_b, rel_bias, w_in, w_out, lambda_q1, lambda_k1,
              lambda_q2, lambda_k2, subln_g, rw_mu, rw_w0, rw_w_up, rw_a0, rw_a_up, rw_k_k, rw_k_a,
              rw_r_k, rw_gn_g, rw_gn_b, ln_post_g, ln_post_b):
    f32 = jnp.float32
    B = x.shape[0]
    meta = jnp.broadcast_to(meta_tokens.astype(x.dtype)[None], (B, N_META, D_MODEL))
    h = layer_norm(jnp.concatenate([meta, x], axis=1), ln_emb_g, ln_emb_b)
    L = h.shape[1]
    pad = ((0, 0), (PAD_FRONT, 0), (0, 0))
    for l in range(DEPTH):
        z = jnp.einsum('bld,de->ble', h, w_in[l])
        q, k, v, g_a, zr, g_r = jnp.split(z, A_SPLITS, axis=-1)
        q = jnp.pad(q, pad).reshape(B, PAD_FRONT + L, A_HEADS, 2, A_QK_DIM)
        k = jnp.pad(k, pad).reshape(B, PAD_FRONT + L, A_HEADS, 2, A_QK_DIM)
        v = jnp.pad(v, pad).reshape(B, PAD_FRONT + L, A_HEADS, A_V_DIM)
        lam_init = 0.8 - 0.6 * math.exp(-0.3 * l)
        lam = (jnp.exp(jnp.sum(lambda_q1[l].astype(f32) * lambda_k1[l].astype(f32)))
               - jnp.exp(jnp.sum(lambda_q2[l].astype(f32) * lambda_k2[l].astype(f32))) + lam_init)
        o = diff_attention(q, k, v, rel_bias, lam)[:, PAD_FRONT:].astype(f32)
        o = o * lax.rsqrt(jnp.mean(o * o, axis=-1, keepdims=True) + SUBLN_EPS) * subln_g[l].astype(f32)
        o_attn = (o * (1.0 - lam_init)).reshape(B, L, A_WIDTH).astype(x.dtype) * jax.nn.silu(g_a)
        o_rwkv = rwkv7_time_mix(zr, rw_mu[l], rw_w0[l], rw_w_up[l], rw_a0[l], rw_a_up[l], rw_k_k[l],
                                rw_k_a[l], rw_r_k[l], rw_gn_g[l], rw_gn_b[l]) * jax.nn.silu(g_r)
        y = jnp.einsum('ble,ed->bld', jnp.concatenate([o_attn, o_rwkv], axis=-1), w_out[l])
        h = layer_norm(DEEPNORM_ALPHA * h + y, ln_post_g[l], ln_post_b[l])
    return h[:, N_META:]
```

```python
import math
from contextlib import ExitStack

import numpy as np
import ml_dtypes

import concourse.bass as bass
import concourse.mybir as mybir
from concourse.bass_utils import run_bass_kernel_spmd

F32 = mybir.dt.float32
BF16 = mybir.dt.bfloat16
AF = mybir.ActivationFunctionType
ALU = mybir.AluOpType

D = 2048
SEQ = 4096
NB = 33
P_TOK = NB * 128
NCOL = 2240
C0 = math.exp(-0.5)
NEG = -1.0e4
TABW = 511
DEBUG = False
STOP = None


class _Stop(Exception):
    pass


_HITS = {}


def chk(tag):
    import os
    if STOP == tag:
        _HITS[tag] = _HITS.get(tag, 0) + 1
        if _HITS[tag] >= int(os.environ.get("NTH", "1")):
            raise _Stop()


class T:
    __slots__ = ("name", "w", "r")

    def __init__(self, name, init_r=None):
        self.name = name
        self.w = None
        self.r = dict(init_r) if init_r else {}


class _Rec:
    def __init__(self):
        self.call = None

    def __getattr__(self, name):
        def f(*a, **k):
            assert self.call is None
            self.call = (name, a, k)
            return self
        return f


def _freeze(fn):
    r = _Rec()
    fn(r)
    name, a, k = r.call
    return lambda e: getattr(e, name)(*a, **k)


class Sched:
    ENG = ("pe", "act", "dve", "pool", "sp")

    def __init__(self, nc, es):
        self.nc = nc
        self.es = es
        self.ops = {e: [] for e in self.ENG}
        self.seen = {e: {} for e in self.ENG}
        self.dsems = {}
        self.esem = {}
        self.bar = {}

    def tile(self, name):
        return T(name, self.bar)

    def barrier(self):
        b = {}
        for e in self.ENG:
            if e != "sp" and self.ops[e]:
                for i in range(len(self.ops[e]) - 1, -1, -1):
                    if self.ops[e][i]["dma"] is None:
                        b[('E', e)] = ('E', e, i)
                        break
        for k, v in self.dsems.items():
            if v[1] > 0:
                b[('D', k)] = ('D', k, v[1])
        self.bar = b

    @staticmethod
    def _dep(waits, ev):
        key = ev[:2]
        if waits.get(key, -1) < ev[2]:
            waits[key] = ev[2]

    def _collect(self, eng, reads, writes):
        waits = {}
        for t in reads:
            if t.w is not None:
                self._dep(waits, t.w)
        for t in writes:
            if t.w is not None:
                self._dep(waits, t.w)
            for ev in t.r.values():
                self._dep(waits, ev)
        wl = []
        for key, val in waits.items():
            if key[0] == 'E' and key[1] == eng and eng == 'pe':
                continue
            if self.seen[eng].get(key, -1) >= val:
                continue
            self.seen[eng][key] = val
            wl.append((key, val))
        return wl

    def op(self, eng, fn, reads=(), writes=()):
        fn = _freeze(fn)
        wl = self._collect(eng, reads, writes)
        idx = len(self.ops[eng])
        ev = ('E', eng, idx)
        self.ops[eng].append(dict(waits=wl, fn=fn, sig=False, dma=None))
        for t in reads:
            t.r[('E', eng)] = ev
        for t in writes:
            t.w = ev
            t.r = {}
        return ev

    def dsem(self, key):
        if key not in self.dsems:
            h = self.es.enter_context(self.nc.semaphore("d_" + key))
            self.dsems[key] = [h, 0]
        return self.dsems[key]

    def dma(self, fn, reads=(), writes=(), key=None, queue="sp", inc=16):
        fn = _freeze(fn)
        wl = self._collect(queue, reads, writes)
        ds = self.dsem(key)
        ds[1] += inc
        ev = ('D', key, ds[1])
        self.ops[queue].append(dict(waits=wl, fn=fn, sig=False, dma=(key, inc)))
        for t in reads:
            t.r[('D', key)] = ev
        for t in writes:
            t.w = ev
            t.r = {}
        return ev

    def emit(self):
        nc = self.nc
        for e in self.ENG:
            for o in self.ops[e]:
                for key, val in o["waits"]:
                    if key[0] == 'E':
                        self.ops[key[1]][val]["sig"] = True
        cnt = {}
        for e in self.ENG:
            c = 0
            lst = []
            for o in self.ops[e]:
                if o["sig"]:
                    c += 1
                lst.append(c)
            cnt[e] = lst
            self.esem[e] = self.es.enter_context(nc.semaphore("e_" + e))
        finals = [(('D', k), v[1]) for k, v in self.dsems.items() if v[1] > 0]

        def resolve(key, val):
            if key[0] == 'E':
                return self.esem[key[1]], cnt[key[1]][val]
            return self.dsems[key[1]][0], val

        def body(e):
            def run(eng):
                for o in self.ops[e]:
                    for key, val in o["waits"]:
                        s, v = resolve(key, val)
                        eng.wait_ge(s, v)
                    ins = o["fn"](eng)
                    if o["dma"] is not None:
                        ins.then_inc(self.dsems[o["dma"][0]][0], o["dma"][1])
                    elif o["sig"]:
                        ins.then_inc(self.esem[e], 1)
                if e == "sp":
                    for key, val in finals:
                        s, v = resolve(key, val)
                        eng.wait_ge(s, v)
            return run

        with nc.Block() as block:
            block.tensor(body("pe"))
            block.scalar(body("act"))
            block.vector(body("dve"))
            block.gpsimd(body("pool"))
            block.sync(body("sp"))


def _bucket(n):
    n = np.maximum(n, 0)
    nf = np.maximum(n, 1).astype(np.float32)
    large = 16 + (np.log(nf / np.float32(16)) / np.float32(math.log(128 / 16)) * np.float32(16)).astype(np.int32)
    large = np.minimum(large, 31)
    return np.where(n < 16, n, large)


PV_MU = 0
PV_MUWD = 6
PV_MUAD = 7
PV_W0 = 8
PV_A0 = 10
PV_KK = 12
PV_KA = 14
PV_RK = 16
PV_GNG = 18
PV_GNB = 20
PV_SUBLN = 22
PV_LNG = 23
PV_LNB = 39
PV_N = 55


def _consts():
    c = {}
    c["ident"] = np.eye(128, dtype=np.float32).astype(ml_dtypes.bfloat16)
    c["ones_bf"] = np.ones((128, 128), dtype=ml_dtypes.bfloat16)
    c["ones_f"] = np.ones((128, 128), dtype=np.float32)
    bo = np.zeros((128, 128), np.float32)
    bo[:64, :64] = 1.0
    bo[64:, 64:] = 1.0
    c["blockones"] = bo
    s = np.arange(128)[:, None]
    t = np.arange(128)[None, :]
    strict = (s < t).astype(np.float32)
    incl = (s <= t).astype(np.float32)
    m = np.zeros((128, 2, 2, 128), np.float32)
    m[:, :, 0, :] = strict[:, None, :]
    m[:, :, 1, :] = incl[:, None, :]
    c["maskT"] = m.reshape(128, 512)
    c["masklow"] = (s > t).astype(np.float32)
    d = np.arange(TABW) - 127
    oh = np.zeros((33, TABW), np.float32)
    b = _bucket(d)
    for i in range(TABW):
        if d[i] >= 0:
            oh[b[i], i] = 1.0
        else:
            oh[32, i] = NEG
    c["oh"] = oh
    return c


def _prep_inputs(inp):
    f = np.float32
    w_in = np.asarray(inp["w_in"][0], f)
    w_out = np.asarray(inp["w_out"][0], f)
    x = np.asarray(inp["x"], f)
    consts = _consts()
    lam4 = np.stack([inp["lambda_q1"][0], inp["lambda_k1"][0], inp["lambda_q2"][0], inp["lambda_k2"][0]]).astype(f)
    rows = []
    for r in range(4):
        rows += list(range((2 * r) * 128, (2 * r + 2) * 128))
        rows += list(range(1024 + (4 * r) * 64, 1024 + (4 * r + 4) * 64))
    w_out_p = np.ascontiguousarray(w_out[rows])
    gb = np.stack([inp["ln_emb_g"], inp["ln_emb_b"], inp["ln_post_g"][0], inp["ln_post_b"][0]]).astype(f)
    maps = []
    for c in range(8):
        b, hg = c // 4, c % 4
        h0 = 2 * hg
        cols = []
        cols += list(range(h0 * 128, (h0 + 2) * 128))
        cols += list(range(1024 + h0 * 128, 1024 + (h0 + 2) * 128))
        cols += list(range(3072 + h0 * 128, 3072 + (h0 + 2) * 128))
        rb = 4096 + hg * 256
        cols += list(range(rb, rb + 256))
        cols += list(range(rb + 1024, rb + 1024 + 256))
        cols += list(range(rb + 2048, rb + 2048 + 256))
        cols += list(range(7360 + hg * 256, 7360 + hg * 256 + 256))
        cols += list(range(4096 + 3072, 4096 + 3072 + 192))
        cols += list(range(2048 + h0 * 128, 2048 + (h0 + 2) * 128))
        assert len(cols) == NCOL
        wi = np.ascontiguousarray(w_in[:, cols])
        pv = np.zeros((128, PV_N), f)
        mu = inp["rw_mu"][0]
        rs = slice(hg * 256, hg * 256 + 256)
        for p in range(2):
            ps_ = slice(hg * 256 + p * 128, hg * 256 + (p + 1) * 128)
            pv[:, PV_MU + 0 + p] = mu[0:1024][ps_]
            pv[:, PV_MU + 2 + p] = mu[1024:2048][ps_]
            pv[:, PV_MU + 4 + p] = mu[2048:3072][ps_]
            pv[:, PV_W0 + p] = inp["rw_w0"][0][ps_]
            pv[:, PV_A0 + p] = inp["rw_a0"][0][ps_]
            pv[:, PV_KK + p] = inp["rw_k_k"][0][ps_]
            pv[:, PV_KA + p] = inp["rw_k_a"][0][ps_]
            pv[:, PV_RK + p] = inp["rw_r_k"][0].reshape(-1)[ps_]
            pv[:, PV_GNG + p] = inp["rw_gn_g"][0][ps_]
            pv[:, PV_GNB + p] = inp["rw_gn_b"][0][ps_]
        pv[:96, PV_MUWD] = mu[3072:3168]
        pv[:96, PV_MUAD] = mu[3168:3264]
        pv[:, PV_SUBLN] = inp["subln_g"][0]
        pv[:, PV_LNG:PV_LNG + 16] = np.asarray(inp["ln_emb_g"], f).reshape(16, 128).T
        pv[:, PV_LNB:PV_LNB + 16] = np.asarray(inp["ln_emb_b"], f).reshape(16, 128).T
        relb = np.ones((33, 2, 128), f)
        for j in range(2):
            relb[:32, j, :] = np.asarray(inp["rel_bias"], f)[:, h0 + j][:, None]
        lora = np.zeros((96, 2, 256), f)
        lora[:, 0, :] = inp["rw_w_up"][0][:, rs]
        lora[:, 1, :] = inp["rw_a_up"][0][:, rs]
        m = {
            "x": np.ascontiguousarray(x[b]),
            "meta": np.asarray(inp["meta_tokens"], f),
            "w_in": wi,
            "w_out": w_out_p,
            "pvec": pv,
            "relb": relb,
            "lora": lora,
            "lam4": lam4,
            "gb": gb,
        }
        m.update(consts)
        maps.append(m)
    return maps


def build_nc():
    nc = bass.Bass("TRN2", target_bir_lowering=False)

    def din(name, shape, dt=F32):
        return nc.dram_tensor(name, list(shape), dt, kind="ExternalInput").ap()

    x_d = din("x", [SEQ, D])
    meta_d = din("meta", [16, D])
    win_d = din("w_in", [D, NCOL])
    wout_d = din("w_out", [D, D])
    pvec_d = din("pvec", [128, PV_N])
    relb_d = din("relb", [33, 2, 128])
    lora_d = din("lora", [96, 2, 256])
    lam4_d = din("lam4", [4, 64])
    gb_d = din("gb", [4, D])
    ident_d = din("ident", [128, 128], BF16)
    onesbf_d = din("ones_bf", [128, 128], BF16)
    onesf_d = din("ones_f", [128, 128])
    bo_d = din("blockones", [128, 128])
    maskT_d = din("maskT", [128, 512])
    masklow_d = din("masklow", [128, 128])
    oh_d = din("oh", [33, TABW])
    xq_d = din("xq", [1024, D])
    idx_d = din("idx", [128, 4], mybir.dt.int32)
    out_d = nc.dram_tensor("out", [1024, D], F32, kind="ExternalOutput").ap()
    if DEBUG:
        dbg_d = nc.dram_tensor("dbg", [512, SEQ], BF16, kind="ExternalOutput").ap()
    agin = [nc.dram_tensor("agin%d" % c, [128, SEQ], BF16) for c in range(4)]
    agout = [nc.dram_tensor("agout%d" % c, [512, SEQ], BF16) for c in range(4)]
    tab_d = [nc.dram_tensor("tab%d" % j, [128, TABW], F32) for j in range(2)]

    with ExitStack() as es:
        S = Sched(nc, es)

        def sbt(name, shape, dt=F32):
            return es.enter_context(nc.sbuf_tensor("s_" + name, list(shape), dt))

        W = sbt("W", [128, 16, NCOL], BF16)
        tW = S.tile("W")
        xb = [sbt("xb%d" % i, [128, NCOL]) for i in range(2)]
        txb = [S.tile("xb%d" % i) for i in range(2)]
        pvec = sbt("pvec", [128, PV_N])
        pder = sbt("pder", [128, 8])
        ident = sbt("ident", [128, 128], BF16)
        ones_bf = sbt("ones_bf", [128, 128], BF16)
        ones_f = sbt("ones_f", [128, 128])
        blockones = sbt("blockones", [128, 128])
        maskT = sbt("maskT", [128, 512])
        masklow = sbt("masklow", [128, 128])
        lora32 = sbt("lora32", [96, 2, 256])
        lora = sbt("lora", [96, 2, 256], BF16)
        bext = [sbt("bext%d" % j, [128, 384]) for j in range(2)]
        bcol = sbt("bcol", [128, 4])
        tC = S.tile("consts")
        arena_w = 28100
        arena = sbt("arena", [128, arena_w])
        apos = [0]

        def carve(nbytes, dt, shape_str=None, **kw):
            n32 = (nbytes + 3) // 4
            a = apos[0]
            apos[0] += n32
            assert apos[0] <= arena_w, apos[0]
            ap = arena[:, a:a + n32]
            if dt != F32:
                ap = ap.bitcast(dt)
            if shape_str:
                ap = ap.rearrange(shape_str, **kw)
            return ap

        ps = [es.enter_context(nc.psum_tensor("ps%d" % i, [128, 512], F32)) for i in range(7)]
        tps = [S.tile("ps%d" % i) for i in range(7)]
        psb = es.enter_context(nc.psum_tensor("psb", [128, 1024], BF16))
        tpsb = S.tile("psb")

        KT = carve(2 * P_TOK * 2, BF16, "p (h n) -> p h n", h=2)
        tKT = S.tile("KT")
        Vt = carve(NB * 256 * 2, BF16, "p (b n) -> p b n", b=NB)
        tV = S.tile("V")
        xn = carve(D * 2, BF16)
        txn = S.tile("xn")
        hT = carve(16 * 256 * 2, BF16, "p (c n) -> p c n", c=16)
        thT = S.tile("hT")
        QT = carve(2 * 256 * 2, BF16, "p (h n) -> p h n", h=2)
        tQT = S.tile("QT")
        gate_a = carve(2 * 256 * 2, BF16, "p (h n) -> p h n", h=2)
        tga = S.tile("gate_a")
        gate_r = carve(2 * 256 * 2, BF16, "p (h n) -> p h n", h=2)
        tgr = S.tile("gate_r")
        zb = [carve(257 * 4, F32) for _ in range(8)]
        tzb = [S.tile("z%d" % i) for i in range(8)]
        NWK = 12
        wk = [carve(256 * 4, F32) for _ in range(NWK)]
        twk = [S.tile("wk%d" % i) for i in range(NWK)]
        bonus = [carve(256 * 4, F32) for _ in range(2)]
        tbonus = [S.tile("bonus%d" % i) for i in range(2)]
        yT = [wk[1], wk[5]]
        tyT = [twk[1], twk[5]]
        twb = carve(256 * 2, BF16)
        adb = carve(256 * 2, BF16)
        ttwb = S.tile("twb")
        tadb = S.tile("adb")
        arT = [carve(2 * 256 * 2, BF16, "p (a n) -> p a n", a=2) for _ in range(2)]
        tarT = [S.tile("arT%d" % i) for i in range(2)]
        ktl = [carve(256 * 2, BF16) for _ in range(2)]
        btl = [carve(256 * 2, BF16) for _ in range(2)]
        tktl = [S.tile("ktl%d" % i) for i in range(2)]
        tbtl = [S.tile("btl%d" % i) for i in range(2)]
        fm3 = [carve(256 * 2, BF16) for _ in range(3)]
        tfm3 = [S.tile("fm3_%d" % i) for i in range(3)]
        tok3 = [carve(2 * 256 * 2, BF16, "p (c n) -> p c n", c=2) for _ in range(3)]
        ttok3 = [S.tile("tok3_%d" % i) for i in range(3)]
        WC = carve(4 * 4, F32)
        tWC = S.tile("WC")
        nbias = carve(4 * 4, F32)
        tnbias = S.tile("nbias")
        AT = [carve(4 * 128 * 2, BF16, "p (a n) -> p a n", a=4) for _ in range(4)]
        tAT = [S.tile("AT%d" % i) for i in range(4)]
        Mc = carve(4 * 128 * 2, BF16, "p (h n) -> p h n", h=4)
        Mtc = carve(4 * 128 * 2, BF16, "p (h n) -> p h n", h=4)
        Ptc = carve(4 * 128 * 2, BF16, "p (h n) -> p h n", h=4)
        tMc = S.tile("Mc")
        tMtc = S.tile("Mtc")
        tPtc = S.tile("Ptc")
        btlm = [[carve(256 * 2, BF16) for _ in range(2)] for _ in range(2)]
        ktlm = [[carve(256 * 2, BF16) for _ in range(2)] for _ in range(2)]
        tbtlm = [[S.tile("btlm%d%d" % (i, j)) for j in range(2)] for i in range(2)]
        tktlm = [[S.tile("ktlm%d%d" % (i, j)) for j in range(2)] for i in range(2)]
        Vz = carve(2 * 4 * 128 * 2, BF16, "p (c h n) -> p c h n", c=2, h=4)
        tVz = S.tile("Vz")
        Uz = carve(4 * 128 * 2, BF16, "p (h n) -> p h n", h=4)
        tUz = S.tile("Uz")
        Xsb = carve(256 * 2, BF16, "p (h n) -> p h n", h=4)
        Usb = carve(256 * 2, BF16, "p (h n) -> p h n", h=4)
        tXsb = S.tile("Xsb")
        tUsb = S.tile("Usb")
        S32 = carve(2 * 128 * 4, F32, "p (a n) -> p a n", a=2)
        Sbf = carve(2 * 128 * 2, BF16, "p (a n) -> p a n", a=2)
        tS32 = S.tile("S32")
        tSbf = S.tile("Sbf")
        Ssb = [wk[0], wk[2]]
        tSsb = [twk[0], twk[2]]
        Eb = [carve(2 * 256 * 2, BF16, "p (m n) -> p m n", m=2) for _ in range(2)]
        tEb = [S.tile("Eb%d" % i) for i in range(2)]
        oT = [carve(4 * 256 * 2, BF16, "p (c n) -> p c n", c=4) for _ in range(2)]
        toT = [S.tile("oT%d" % i) for i in range(2)]
        stats = carve(4 * 6 * 4, F32, "p (a n) -> p a n", a=4)
        mv = carve(8 * 4, F32)
        tst = S.tile("stats")
        lamw = carve(4 * 64 * 4, F32, "p (a n) -> p a n", a=4)
        tabsb = carve(TABW * 4, F32)
        ttab = S.tile("tabsb")
        main_end = apos[0]
        print("arena main words", main_end, "of", arena_w)

        tagin = [S.tile("agin%d" % c) for c in range(4)]
        tagout = [S.tile("agout%d" % c) for c in range(4)]
        ttabd = [S.tile("tabd%d" % j) for j in range(2)]

        try:
            def cload(dst, src):
                S.dma(lambda e: e.dma_start(out=dst, in_=src), writes=[tC], key="consts")

            cload(pvec[:], pvec_d)
            cload(ident[:], ident_d)
            cload(ones_bf[:], onesbf_d)
            cload(ones_f[:], onesf_d)
            cload(blockones[:], bo_d)
            cload(maskT[:], maskT_d)
            cload(masklow[:], masklow_d)
            cload(lora32[:], lora_d)
            relb = carve(2 * 128 * 4, F32, "p (a n) -> p a n", a=2)
            ohsb = carve(TABW * 4, F32)
            cload(relb[0:33], relb_d)
            cload(ohsb[0:33], oh_d)
            lam_src = bass.AP(tensor=lam4_d.tensor, offset=0, ap=[[0, 128], [64, 4], [1, 64]])
            cload(lamw, lam_src)

            S.op("dve", lambda e: e.tensor_copy(out=lora[:], in_=lora32[:]), reads=[tC], writes=[tC])
            S.op("dve", lambda e: e.tensor_scalar(out=pder[:, 0:2], in0=pvec[:, PV_KA:PV_KA + 2], scalar1=-1.0, scalar2=1.0,
                                                  op0=ALU.mult, op1=ALU.add), reads=[tC], writes=[tC])
            S.op("dve", lambda e: e.tensor_scalar(out=pder[:, 2:3], in0=pvec[:, PV_SUBLN:PV_SUBLN + 1], scalar1=0.8, scalar2=None,
                                                  op0=ALU.mult), reads=[tC], writes=[tC])
            S.op("dve", lambda e: e.tensor_tensor(out=lamw[:, 0, :], in0=lamw[:, 0, :], in1=lamw[:, 1, :], op=ALU.mult),
                 reads=[tC], writes=[tC])
            S.op("dve", lambda e: e.tensor_tensor(out=lamw[:, 2, :], in0=lamw[:, 2, :], in1=lamw[:, 3, :], op=ALU.mult),
                 reads=[tC], writes=[tC])
            S.op("dve", lambda e: e.reduce_sum(out=pder[:, 4:5], in_=lamw[:, 0, :], axis=mybir.AxisListType.X),
                 reads=[tC], writes=[tC])
            S.op("dve", lambda e: e.reduce_sum(out=pder[:, 5:6], in_=lamw[:, 2, :], axis=mybir.AxisListType.X),
                 reads=[tC], writes=[tC])
            S.op("act", lambda e: e.activation(out=pder[:, 4:6], in_=pder[:, 4:6], func=AF.Exp), reads=[tC], writes=[tC])
            S.op("dve", lambda e: e.tensor_tensor(out=pder[:, 3:4], in0=pder[:, 5:6], in1=pder[:, 4:5], op=ALU.subtract),
                 reads=[tC], writes=[tC])
            S.op("dve", lambda e: e.tensor_scalar(out=pder[:, 3:4], in0=pder[:, 3:4], scalar1=-0.2, scalar2=None, op0=ALU.add),
                 reads=[tC], writes=[tC])
            S.op("pool", lambda e: e.memset(S32.rearrange("p a n -> p (a n)"), 0.0), writes=[tS32])
            S.op("pool", lambda e: e.memset(Sbf.rearrange("p a n -> p (a n)"), 0.0), writes=[tSbf])
            S.op("pool", lambda e: e.memset(Vz.rearrange("p c h n -> p (c h n)"), 0.0), writes=[tVz])
            S.op("pool", lambda e: e.memset(Uz.rearrange("p h n -> p (h n)"), 0.0), writes=[tUz])
            for i in range(8):
                S.op("pool", lambda e, i=i: e.memset(zb[i], 0.0), writes=[tzb[i]])

            for j in range(2):
                S.op("pe", lambda e, j=j: e.matmul(ps[0][:, 0:256], lhsT=relb[0:33, j, :], rhs=ohsb[0:33, 0:256], start=True, stop=True),
                     reads=[tC], writes=[tps[0]])
                S.op("pe", lambda e, j=j: e.matmul(ps[1][:, 0:255], lhsT=relb[0:33, j, :], rhs=ohsb[0:33, 256:511], start=True, stop=True),
                     reads=[tC], writes=[tps[1]])
                S.op("dve", lambda e: e.tensor_copy(out=tabsb[:, 0:256], in_=ps[0][:, 0:256]), reads=[tps[0]], writes=[ttab])
                S.op("dve", lambda e: e.tensor_copy(out=tabsb[:, 256:511], in_=ps[1][:, 0:255]), reads=[tps[1]], writes=[ttab])
                S.op("dve", lambda e, j=j: e.tensor_copy(out=bcol[:, j:j + 1], in_=tabsb[:, 510:511]), reads=[ttab], writes=[tC])
                S.op("dve", lambda e, j=j: e.tensor_copy(out=bcol[:, 2 + j:3 + j], in_=tabsb[:, 510:511]), reads=[ttab], writes=[tC])
                S.op("dve", lambda e, j=j: e.memset(bcol[0:112, 2 + j:3 + j], NEG), reads=[], writes=[tC])
                S.dma(lambda e, j=j: e.dma_start(out=tab_d[j].ap(), in_=tabsb), reads=[ttab], writes=[ttabd[j]], key="tabst")
                src = bass.AP(tensor=tab_d[j].ap().tensor, offset=127, ap=[[TABW - 1, 128], [1, 384]])
                S.dma(lambda e, j=j, src=src: e.dma_start(out=bext[j][:], in_=src), reads=[ttabd[j]], writes=[tC], key="consts")

            for kc in range(16):
                b = kc % 2
                S.dma(lambda e, kc=kc, b=b: e.dma_start(out=xb[b][:], in_=win_d[kc * 128:(kc + 1) * 128, :]),
                      writes=[txb[b]], key="xb%d" % b)
                eng = ("dve", "pool", "act")[kc % 3]
                if eng == "act":
                    S.op("act", lambda e, kc=kc, b=b: e.activation(out=W[:, kc, :], in_=xb[b][:], func=AF.Copy),
                         reads=[txb[b]], writes=[tW])
                else:
                    S.op(eng, lambda e, kc=kc, b=b: e.tensor_copy(out=W[:, kc, :], in_=xb[b][:]), reads=[txb[b]], writes=[tW])

            chk('setup')
            rr = {"ps": 0, "ev": 0, "xb": 0, "oT": 0}

            def next_ps(n=4):
                i = rr["ps"] % n
                rr["ps"] += 1
                return i

            def evac_copy(out, in_, reads, writes, bf=False):
                k = 1
                if k == 0:
                    S.op("dve", lambda e: e.tensor_scalar(out=out, in0=in_, scalar1=1.0, scalar2=None, op0=ALU.mult), reads=reads, writes=writes)
                else:
                    S.op("act", lambda e: e.activation(out=out, in_=in_, func=AF.Copy), reads=reads, writes=writes)

            def tt(eng, out, a, b, op, reads, writes):
                S.op(eng, lambda e: e.tensor_tensor(out=out, in0=a, in1=b, op=op), reads=reads, writes=writes)

            def ts(eng, out, a, s1, s2, op0, op1, reads, writes):
                if s2 is None:
                    S.op(eng, lambda e: e.tensor_scalar(out=out, in0=a, scalar1=s1, scalar2=None, op0=op0), reads=reads, writes=writes)
                else:
                    S.op(eng, lambda e: e.tensor_scalar(out=out, in0=a, scalar1=s1, scalar2=s2, op0=op0, op1=op1),
                         reads=reads, writes=writes)

            def stt(out, a, sc, b, op0, op1, reads, writes):
                S.op("dve", lambda e: e.scalar_tensor_tensor(out=out, in0=a, scalar=sc, in1=b, op0=op0, op1=op1),
                     reads=reads, writes=writes)

            def act(out, in_, func, reads, writes, bias=0.0, scale=1.0):
                S.op("act", lambda e: e.activation(out=out, in_=in_, func=func, bias=bias, scale=scale), reads=reads, writes=writes)

            def rsqrt_inplace(t, tt_, n, scale, eps):
                act(t[:, 0:n], t[:, 0:n], AF.Ln, [tt_], [tt_], bias=eps_ap(eps), scale=scale)
                act(t[:, 0:n], t[:, 0:n], AF.Exp, [tt_], [tt_], scale=-0.5)

            epsc = {}

            def eps_ap(v):
                if v == 0.0:
                    return 0.0
                if v not in epsc:
                    col = 6 + len(epsc)
                    S.op("pool", lambda e, col=col, v=v: e.memset(pder[:, col:col + 1], float(v)), writes=[tC])
                    epsc[v] = pder[:, col:col + 1]
                return epsc[v]

            eps_ap(1e-5)
            eps_ap(64e-5)

            def layer_norm_block(src_ap, is_meta, blk_slot):
                b = rr["xb"] % 2
                rr["xb"] += 1
                xt = xb[b]
                if is_meta:
                    S.op("pool", lambda e: e.memset(xt[:, 0:D], 0.0), writes=[txb[b]])
                    S.dma(lambda e: e.dma_start(out=xt[112:128, 0:D], in_=meta_d), writes=[txb[b]], key="xb%d" % b)
                else:
                    S.dma(lambda e: e.dma_start(out=xt[:, 0:D], in_=src_ap), writes=[txb[b]], key="xb%d" % b)
                for q in range(4):
                    S.op("dve", lambda e, q=q: e.bn_stats(out=stats[:, q, :], in_=xt[:, q * 512:(q + 1) * 512]),
                         reads=[txb[b]], writes=[tst])
                S.op("dve", lambda e: e.bn_aggr(out=mv[:, 0:2], in_=stats.rearrange("p a n -> p (a n)")), reads=[tst], writes=[tst])
                act(mv[:, 2:3], mv[:, 1:2], AF.Ln, [tst], [tst], bias=eps_ap(1e-5))
                act(mv[:, 2:3], mv[:, 2:3], AF.Exp, [tst], [tst], scale=-0.5)
                stt(mv[:, 3:4], mv[:, 0:1], -1.0, mv[:, 2:3], ALU.mult, ALU.mult, [tst], [tst])
                S.op("act", lambda e: e.activation(out=xn, in_=xt[:, 0:D], func=AF.Identity, bias=mv[:, 3:4], scale=mv[:, 2:3]),
                     reads=[txb[b], tst], writes=[txn])
                for half in range(2):
                    for dc in range(8):
                        c = half * 8 + dc
                        S.op("pe", lambda e, c=c, dc=dc: e.transpose(psb[:, dc * 128:(dc + 1) * 128], xn[:, c * 128:(c + 1) * 128], ident[:]),
                             reads=[txn, tC], writes=[tpsb])
                    for dc in range(8):
                        c = half * 8 + dc
                        dst = hT[:, c, blk_slot * 128:(blk_slot + 1) * 128]
                        src = psb[:, dc * 128:(dc + 1) * 128]
                        if dc % 2 == 0:
                            ts("dve", dst, src, pvec[:, PV_LNG + c:PV_LNG + c + 1], pvec[:, PV_LNB + c:PV_LNB + c + 1],
                               ALU.mult, ALU.add, [tpsb, tC], [thT])
                        else:
                            S.op("act", lambda e, dst=dst, src=src, c=c: e.activation(
                                out=dst, in_=src, func=AF.Identity, bias=pvec[:, PV_LNB + c:PV_LNB + c + 1],
                                scale=pvec[:, PV_LNG + c:PV_LNG + c + 1]), reads=[tpsb, tC], writes=[thT])
                if is_meta:
                    S.op("pool", lambda e: e.memset(hT[:, :, 0:112], 0.0), writes=[thT])

            def inproj_fm(ct_off, m, nt, sink):
                i = next_ps()
                for kc in range(16):
                    S.op("pe", lambda e, kc=kc, i=i: e.matmul(ps[i][0:m, 0:nt], lhsT=W[:, kc, ct_off:ct_off + m], rhs=hT[:, kc, 0:nt],
                                                            start=(kc == 0), stop=(kc == 15)), reads=[tW, thT], writes=[tps[i]])
                sink(ps[i][0:m, 0:nt], tps[i])

            tiles = [(-1, 128)] + [(i, 256) for i in range(16)]
            for (ti, nt) in tiles:
                nblk = nt // 128
                blk0 = 0 if ti < 0 else 1 + 2 * ti
                pos0 = blk0 * 128
                for j in range(nblk):
                    if ti < 0:
                        layer_norm_block(None, True, 0)
                    else:
                        r0 = ti * 256 + j * 128
                        layer_norm_block(x_d[r0:r0 + 128, :], False, j)
                if ti >= 0:
                    for h in range(2):
                        inproj_fm(h * 128, 128, nt, lambda p, tp, h=h: evac_copy(QT[:, h, 0:nt], p, [tp], [tQT]))
                for h in range(2):
                    inproj_fm(256 + h * 128, 128, nt, lambda p, tp, h=h: evac_copy(KT[:, h, pos0:pos0 + nt], p, [tp], [tKT]))
                if ti >= 0:
                    for h in range(2):
                        inproj_fm(512 + h * 128, 128, nt, lambda p, tp, h=h: act(gate_a[:, h, 0:nt], p, AF.Silu, [tp], [tga]))
                    for p_ in range(2):
                        inproj_fm(1536 + p_ * 128, 128, nt, lambda p, tp, p_=p_: act(gate_r[:, p_, 0:nt], p, AF.Silu, [tp], [tgr]))
                for zi in range(6):
                    inproj_fm(768 + zi * 128, 128, nt, lambda p, tp, zi=zi: evac_copy(zb[zi][:, 1:1 + nt], p, [tp], [tzb[zi]]))
                for zi in range(2):
                    inproj_fm(1792 + zi * 96, 96, nt, lambda p, tp, zi=zi: evac_copy(zb[6 + zi][0:96, 1:1 + nt], p, [tp], [tzb[6 + zi]]))
                for j in range(nblk):
                    i = next_ps()
                    for kc in range(16):
                        S.op("pe", lambda e, kc=kc, i=i, j=j: e.matmul(ps[i][:, 0:256], lhsT=hT[:, kc, j * 128:(j + 1) * 128],
                                                                     rhs=W[:, kc, 1984:2240], start=(kc == 0), stop=(kc == 15)),
                             reads=[tW, thT], writes=[tps[i]])
                    evac_copy(Vt[:, blk0 + j, :], ps[i][:, 0:256], [tps[i]], [tV])

                chk('inproj%d' % ti)
                r32, k32, v32, t1, t2, sg, aic, kkn, k2, bvec, cs, t3 = wk
                tr32, tk32, tv32, tt1, tt2, tsg, taic, tkkn, tk2, tbvec, tcs, tt3 = twk
                n = nt

                def shift(zi, mucol, out, tout, rows=128):
                    z = zb[zi]
                    tz = tzb[zi]
                    tt("pool", t3[0:rows, 0:n], z[0:rows, 0:n], z[0:rows, 1:1 + n], ALU.subtract, [tz], [tt3])
                    stt(out[0:rows, 0:n], t3[0:rows, 0:n], pvec[0:rows, mucol:mucol + 1], z[0:rows, 1:1 + n], ALU.mult, ALU.add,
                        [tt3, tz, tC], [tout])
                    S.op("pool", lambda e: e.tensor_copy(out=z[0:rows, 0:1], in_=z[0:rows, n:n + 1]), reads=[tz], writes=[tz])

                shift(6, PV_MUWD, t1, tt1, rows=96)
                act(twb[0:96, 0:n], t1[0:96, 0:n], AF.Tanh, [tt1], [ttwb])
                shift(7, PV_MUAD, t2, tt2, rows=96)
                S.op("dve", lambda e: e.tensor_scalar(out=adb[0:96, 0:n], in0=t2[0:96, 0:n], scalar1=1.0, scalar2=None, op0=ALU.mult), reads=[tt2], writes=[tadb])

                chk('pa%d' % ti)
                import os as _os
                for p in [int(v) for v in _os.environ.get('PAIRS', '0,1').split(',')]:
                    shift(0 + p, PV_MU + 0 + p, r32, tr32)
                    shift(2 + p, PV_MU + 2 + p, k32, tk32)
                    shift(4 + p, PV_MU + 4 + p, v32, tv32)
                    S.op("pe", lambda e, p=p: e.matmul(ps[4][:, 0:n], lhsT=lora[0:96, 0, p * 128:(p + 1) * 128], rhs=twb[0:96, 0:n],
                                                       start=True, stop=True), reads=[tC, ttwb], writes=[tps[4]])
                    act(sg[:, 0:n], ps[4][:, 0:n], AF.Sigmoid, [tps[4], tC], [tsg], bias=pvec[:, PV_W0 + p:PV_W0 + p + 1])
                    S.op("pe", lambda e, p=p: e.matmul(ps[5][:, 0:n], lhsT=lora[0:96, 1, p * 128:(p + 1) * 128], rhs=adb[0:96, 0:n],
                                                       start=True, stop=True), reads=[tC, tadb], writes=[tps[5]])
                    act(aic[:, 0:n], ps[5][:, 0:n], AF.Sigmoid, [tps[5], tC], [taic], bias=pvec[:, PV_A0 + p:PV_A0 + p + 1])
                    chk('pb%d' % ti)
                    ts("dve", t1[:, 0:n], k32[:, 0:n], pvec[:, PV_KK + p:PV_KK + p + 1], None, ALU.mult, None, [tk32, tC], [tt1])
                    tt("pool", t2[:, 0:n], t1[:, 0:n], t1[:, 0:n], ALU.mult, [tt1], [tt2])
                    S.op("pe", lambda e: e.matmul(ps[6][:, 0:n], lhsT=blockones[:], rhs=t2[:, 0:n], start=True, stop=True),
                         reads=[tC, tt2], writes=[tps[6]])
                    ts("dve", t2[:, 0:n], ps[6][:, 0:n], 1e-24, None, ALU.max, None, [tps[6]], [tt2])
                    rsqrt_inplace(t2, tt2, n, 1.0, 0.0)
                    tt("dve", kkn[:, 0:n], t1[:, 0:n], t2[:, 0:n], ALU.mult, [tt1, tt2], [tkkn])
                    chk('pc%d' % ti)
                    ts("dve", t1[:, 0:n], aic[:, 0:n], pvec[:, PV_KA + p:PV_KA + p + 1], pder[:, p:p + 1], ALU.mult, ALU.add,
                       [taic, tC], [tt1])
                    tt("pool", k2[:, 0:n], k32[:, 0:n], t1[:, 0:n], ALU.mult, [tk32, tt1], [tk2])
                    tt("pool", bvec[:, 0:n], kkn[:, 0:n], aic[:, 0:n], ALU.mult, [tkkn, taic], [tbvec])
                    chk('pd%d' % ti)
                    stt(t1[:, 0:n], r32[:, 0:n], pvec[:, PV_RK + p:PV_RK + p + 1], k2[:, 0:n], ALU.mult, ALU.mult, [tr32, tk2, tC], [tt1])
                    S.op("pe", lambda e: e.matmul(ps[4][:, 0:n], lhsT=blockones[:], rhs=t1[:, 0:n], start=True, stop=True),
                         reads=[tC, tt1], writes=[tps[4]])
                    tt("dve", bonus[p][:, 0:n], ps[4][:, 0:n], v32[:, 0:n], ALU.mult, [tps[4], tv32], [tbonus[p]])
                    chk('pe%d' % ti)
                    for c in range(nblk):
                        S.op("dve", lambda e, c=c: e.tensor_tensor_scan(out=cs[:, c * 128:(c + 1) * 128], data0=ones_f[:, 0:128],
                                                                     data1=sg[:, c * 128:(c + 1) * 128], initial=0.0,
                                                                     op0=ALU.mult, op1=ALU.add), reads=[tsg, tC], writes=[tcs])
                    chk('pf%d' % ti)
                    act(t1[:, 0:n], cs[:, 0:n], AF.Exp, [tcs], [tt1], scale=-C0)
                    tt("dve", arT[p][:, 1, 0:n], r32[:, 0:n], t1[:, 0:n], ALU.mult, [tr32, tt1], [tarT[p]])
                    for c in range(nblk):
                        S.op("pool", lambda e, c=c, p=p: e.tensor_copy(out=WC[:, p * 2 + c:p * 2 + c + 1], in_=t1[:, c * 128 + 127:c * 128 + 128]),
                             reads=[tt1], writes=[tWC])
                        ts("dve", nbias[:, c:c + 1], cs[:, c * 128 + 127:c * 128 + 128], -C0, None, ALU.mult, None, [tcs], [tnbias])
                    tt("pool", t2[:, 0:n], cs[:, 0:n], sg[:, 0:n], ALU.subtract, [tcs, tsg], [tt2])
                    act(t2[:, 0:n], t2[:, 0:n], AF.Exp, [tt2], [tt2], scale=-C0)
                    stt(arT[p][:, 0, 0:n], kkn[:, 0:n], -1.0, t2[:, 0:n], ALU.mult, ALU.mult, [tkkn, tt2], [tarT[p]])
                    act(t1[:, 0:n], cs[:, 0:n], AF.Exp, [tcs], [tt1], scale=C0)
                    tt("dve", ktl[p][:, 0:n], k2[:, 0:n], t1[:, 0:n], ALU.mult, [tk2, tt1], [tktl[p]])
                    tt("pool", btl[p][:, 0:n], bvec[:, 0:n], t1[:, 0:n], ALU.mult, [tbvec, tt1], [tbtl[p]])
                    for hh in ([] if _os.environ.get('NOMASK') else range(2)):
                        hm = blockones[:, hh * 64:hh * 64 + 1]
                        ts("dve" if hh == 0 else "pool", btlm[p][hh][:, 0:n], btl[p][:, 0:n], hm, None, ALU.mult, None, [tbtl[p], tC], [tbtlm[p][hh]])
                        ts("pool" if hh == 0 else "dve", ktlm[p][hh][:, 0:n], ktl[p][:, 0:n], hm, None, ALU.mult, None, [tktl[p], tC], [tktlm[p][hh]])
                    for c in range(nblk):
                        S.op("act", lambda e, c=c: e.activation(out=t2[:, c * 128:(c + 1) * 128], in_=cs[:, c * 128:(c + 1) * 128],
                                                                 func=AF.Exp, bias=nbias[:, c:c + 1], scale=C0),
                             reads=[tcs, tnbias], writes=[tt2])
                    chk('pg%d' % ti)
                    tt("dve", fm3[0][:, 0:n], k2[:, 0:n], t2[:, 0:n], ALU.mult, [tk2, tt2], [tfm3[0]])
                    tt("pool", fm3[1][:, 0:n], bvec[:, 0:n], t2[:, 0:n], ALU.mult, [tbvec, tt2], [tfm3[1]])
                    S.op("act", lambda e: e.activation(out=fm3[2][:, 0:n], in_=v32[:, 0:n], func=AF.Copy), reads=[tv32], writes=[tfm3[2]])
                    chk('ph%d' % ti)
                    for c in range(nblk):
                        for q in range(3):
                            S.op("pe", lambda e, c=c, q=q: e.matmul(ps[4 + q][:, 0:128], lhsT=fm3[q][:, c * 128:(c + 1) * 128], rhs=ident[:],
                                                                    start=True, stop=True), reads=[tfm3[q], tC], writes=[tps[4 + q]])
                        chk('pt%d' % ti)
                        for q in range(3):
                            evac_copy(tok3[q][:, c, p * 128:(p + 1) * 128], ps[4 + q][:, 0:128], [tps[4 + q]], [ttok3[q]])
                        for hh in ([] if _os.environ.get('NOVZ') else range(2)):
                            S.op('act', lambda e: e.activation(out=Vz[:, c, 2 * p + hh, hh * 64:(hh + 1) * 64], in_=ps[6][:, hh * 64:(hh + 1) * 64], func=AF.Copy), reads=[tps[6]], writes=[tVz])

                    chk('pi%d' % ti)
                chk('rwpre%d' % ti)
                Khat, Bhat, Vtok = tok3
                tKhat, tBhat, tVtok = ttok3
                for c in range(nblk):
                    cs_ = slice(c * 128, (c + 1) * 128)
                    for h in range(4):
                        p, hh = h // 2, h % 2
                        i = h % 2
                        S.op("pe", lambda e: e.matmul(ps[i][:, 0:256], lhsT=btlm[p][hh][:, cs_], rhs=arT[p][:, :, cs_], start=True, stop=True),
                             reads=[tbtlm[p][hh], tarT[p]], writes=[tps[i]])
                        S.op("pe", lambda e: e.matmul(ps[i][:, 256:512], lhsT=ktlm[p][hh][:, cs_], rhs=arT[p][:, :, cs_], start=True, stop=True),
                             reads=[tktlm[p][hh], tarT[p]], writes=[tps[i]])
                        chk('c0%d' % ti)
                        tt("dve", AT[h].rearrange("p a n -> p (a n)"), ps[i][:, :], maskT[:], ALU.mult, [tps[i], tC], [tAT[h]])
                        chk('c1%d' % ti)
                        S.op("pe", lambda e: e.matmul(ps[2][:, h * 128:(h + 1) * 128], lhsT=arT[p][:, 0, cs_], rhs=btlm[p][hh][:, cs_], start=True, stop=True),
                             reads=[tbtlm[p][hh], tarT[p]], writes=[tps[2]])
                    chk('ca%d' % ti)
                    for h in range(4):
                        tt("dve", Mc[:, h, :], ps[2][:, h * 128:(h + 1) * 128], masklow[:], ALU.mult, [tps[2], tC], [tMc])
                        S.op("pool", lambda e: e.tensor_copy(out=Mtc[:, h, :], in_=AT[h][:, 0, :]), reads=[tAT[h]], writes=[tMtc])
                        tt("pool", Ptc[:, h, :], AT[h][:, 0, :], ident[:], ALU.add, [tAT[h], tC], [tPtc])
                    chk('cb%d' % ti)
                    for lvl in range(7):
                        if lvl >= 1:
                            for h in range(4):
                                S.op("pe", lambda e: e.matmul(ps[3][:, h * 128:(h + 1) * 128], lhsT=Mc[:, h, :], rhs=Ptc[:, h, :], start=True, stop=True),
                                     reads=[tMc, tPtc], writes=[tps[3]])
                        if lvl < 6:
                            for h in range(4):
                                S.op("pe", lambda e: e.matmul(ps[4][:, h * 128:(h + 1) * 128], lhsT=Mtc[:, h, :], rhs=Mc[:, h, :], start=True, stop=True),
                                     reads=[tMc, tMtc], writes=[tps[4]])
                            if lvl < 5:
                                for h in range(4):
                                    S.op("pe", lambda e: e.matmul(ps[5][:, h * 128:(h + 1) * 128], lhsT=Mc[:, h, :], rhs=Mtc[:, h, :], start=True, stop=True),
                                         reads=[tMc, tMtc], writes=[tps[5]])
                        if lvl >= 1:
                            tt("dve", Ptc.rearrange("p h n -> p (h n)"), ps[3][:, :], Ptc.rearrange("p h n -> p (h n)"), ALU.add, [tps[3], tPtc], [tPtc])
                        if lvl < 6:
                            S.op("act", lambda e: e.activation(out=Mc.rearrange("p h n -> p (h n)"), in_=ps[4][:, :], func=AF.Copy), reads=[tps[4]], writes=[tMc])
                            if lvl < 5:
                                S.op("act", lambda e: e.activation(out=Mtc.rearrange("p h n -> p (h n)"), in_=ps[5][:, :], func=AF.Copy), reads=[tps[5]], writes=[tMtc])
                    chk('cc%d' % ti)
                    for p in range(2):
                        S.op("pe", lambda e: e.matmul(ps[0][:, p * 128:(p + 1) * 128], lhsT=arT[p][:, 0, cs_], rhs=Sbf[:, p, :], start=True, stop=False),
                             reads=[tarT[p], tSbf], writes=[tps[0]])
                        for hh in range(2):
                            h = 2 * p + hh
                            S.op("pe", lambda e: e.matmul(ps[0][:, h * 64:(h + 1) * 64], lhsT=AT[h][:, 2, :], rhs=Vtok[:, c, h * 64:(h + 1) * 64],
                                                          start=False, stop=(hh == 1)), reads=[tAT[h], tVtok], writes=[tps[0]])
                    S.op("act", lambda e: e.activation(out=Xsb.rearrange("p h n -> p (h n)"), in_=ps[0][:, 0:256], func=AF.Copy), reads=[tps[0]], writes=[tXsb])
                    for h in range(4):
                        S.op("pe", lambda e: e.matmul(ps[1][:, h * 64:(h + 1) * 64], lhsT=Ptc[:, h, :], rhs=Xsb[:, h, :], start=True, stop=True),
                             reads=[tPtc, tXsb], writes=[tps[1]])
                    S.op("act", lambda e: e.activation(out=Usb.rearrange("p h n -> p (h n)"), in_=ps[1][:, 0:256], func=AF.Copy), reads=[tps[1]], writes=[tUsb])
                    for hh in range(2):
                        src = ps[1][:, 0:256].rearrange("p (a b n) -> p a b n", a=2, b=2)[:, :, hh, :]
                        dst = Uz.rearrange("p (a b) n -> p a b n", b=2)[:, :, hh, hh * 64:(hh + 1) * 64]
                        S.op("act", lambda e: e.activation(out=dst, in_=src, func=AF.Copy), reads=[tps[1]], writes=[tUz])
                    chk('cd%d' % ti)
                    for p in range(2):
                        S.op("pe", lambda e: e.matmul(ps[2 + p][:, 0:128], lhsT=Sbf[:, p, :], rhs=arT[p][:, 1, cs_], start=True, stop=False),
                             reads=[tSbf, tarT[p]], writes=[tps[2 + p]])
                        for hh in range(2):
                            h = 2 * p + hh
                            S.op("pe", lambda e: e.matmul(ps[2 + p][:, 0:128], lhsT=Uz[:, h, :], rhs=AT[h][:, 1, :], start=False, stop=False),
                                 reads=[tUz, tAT[h]], writes=[tps[2 + p]])
                            S.op("pe", lambda e: e.matmul(ps[2 + p][:, 0:128], lhsT=Vz[:, c, h, :], rhs=AT[h][:, 3, :], start=False, stop=(hh == 1)),
                                 reads=[tVz, tAT[h]], writes=[tps[2 + p]])
                        if ti >= 0:
                            evac_copy(yT[p][:, cs_], ps[2 + p][:, 0:128], [tps[2 + p]], [tyT[p]])
                    for p in range(2):
                        S.op("pe", lambda e: e.matmul(ps[4 + p][:, 0:128], lhsT=Bhat[:, c, p * 128:(p + 1) * 128],
                                                      rhs=Usb.rearrange("p h n -> p (h n)")[:, p * 128:(p + 1) * 128], start=True, stop=False),
                             reads=[tBhat, tUsb], writes=[tps[4 + p]])
                        S.op("pe", lambda e: e.matmul(ps[4 + p][:, 0:128], lhsT=Khat[:, c, p * 128:(p + 1) * 128], rhs=Vtok[:, c, p * 128:(p + 1) * 128],
                                                      start=False, stop=True), reads=[tKhat, tVtok], writes=[tps[4 + p]])
                        tt("dve", t3[:, 0:128], ps[4 + p][:, 0:128], blockones[:], ALU.mult, [tps[4 + p], tC], [tt3])
                        stt(S32[:, p, :], S32[:, p, :], WC[:, p * 2 + c:p * 2 + c + 1], t3[:, 0:128], ALU.mult, ALU.add, [tS32, tWC, tt3], [tS32])
                    S.op("act", lambda e: e.activation(out=Sbf.rearrange("p a n -> p (a n)"), in_=S32.rearrange("p a n -> p (a n)"), func=AF.Copy),
                         reads=[tS32], writes=[tSbf])

                chk('chain%d' % ti)
                if ti < 0:
                    continue

                ob = rr["oT"] % 2
                rr["oT"] += 1
                oTt = oT[ob]
                toTt = toT[ob]

                for p in range(2):
                    y = yT[p]
                    S.op("pe", lambda e, y=y: e.matmul(ps[0][:, 0:n], lhsT=blockones[:], rhs=y[:, 0:n], start=True, stop=True),
                         reads=[tC, tyT[p]], writes=[tps[0]])
                    stt(t1[:, 0:n], ps[0][:, 0:n], -1.0 / 64.0, y[:, 0:n], ALU.mult, ALU.add, [tps[0], tyT[p]], [tt1])
                    tt("pool", t2[:, 0:n], t1[:, 0:n], t1[:, 0:n], ALU.mult, [tt1], [tt2])
                    S.op("pe", lambda e: e.matmul(ps[1][:, 0:n], lhsT=blockones[:], rhs=t2[:, 0:n], start=True, stop=True),
                         reads=[tC, tt2], writes=[tps[1]])
                    act(t2[:, 0:n], ps[1][:, 0:n], AF.Ln, [tps[1], tC], [tt2], bias=eps_ap(64e-5), scale=1.0 / 64.0)
                    act(t2[:, 0:n], t2[:, 0:n], AF.Exp, [tt2], [tt2], scale=-0.5)
                    tt("dve", t1[:, 0:n], t1[:, 0:n], t2[:, 0:n], ALU.mult, [tt1, tt2], [tt1])
                    ts("dve", t1[:, 0:n], t1[:, 0:n], pvec[:, PV_GNG + p:PV_GNG + p + 1], pvec[:, PV_GNB + p:PV_GNB + p + 1],
                       ALU.mult, ALU.add, [tt1, tC], [tt1])
                    tt("pool", t1[:, 0:n], t1[:, 0:n], bonus[p][:, 0:n], ALU.add, [tt1, tbonus[p]], [tt1])
                    tt("dve", oTt[:, 2 + p, 0:n], t1[:, 0:n], gate_r[:, p, 0:n], ALU.mult, [tt1, tgr], [toTt])

                chk('rwpost%d' % ti)
                qb0 = blk0
                OO = ps[4][:, :].rearrange("p (m n) -> p m n", m=2)
                SS = ps[5][:, :].rearrange("p (m n) -> p m n", m=2)
                for h in range(2):
                    kbs = list(range(0, qb0 + 2))
                    for kb in kbs:
                        delta = kb - qb0
                        q_lo = 128 if delta == 1 else 0
                        nq = n - q_lo
                        near = delta >= -1
                        par = kb % 2
                        E = Eb[par]
                        tE = tEb[par]
                        for m in range(2):
                            rs = slice(m * 64, m * 64 + 64)
                            pi = 2 * par + m
                            S.op("pe", lambda e: e.matmul(ps[pi][:, 0:nq], lhsT=KT[rs, h, kb * 128:(kb + 1) * 128], rhs=QT[rs, h, q_lo:n],
                                                          start=True, stop=True), reads=[tKT, tQT], writes=[tps[pi]])
                        for m in range(2):
                            pi = 2 * par + m
                            if near:
                                boff = 0 if delta >= 0 else 128
                                stt(Ssb[m][:, 0:nq], ps[pi][:, 0:nq], 0.125, bext[h][:, boff:boff + nq], ALU.mult, ALU.add,
                                    [tps[pi], tC], [tSsb[m]])
                                if kb == 0:
                                    S.op("pool", lambda e: e.memset(Ssb[m][0:112, 0:nq], NEG), reads=[], writes=[tSsb[m]])
                                act(E[:, m, 0:nq], Ssb[m][:, 0:nq], AF.Exp, [tSsb[m]], [tE])
                            else:
                                bc = bcol[:, 2 + h:3 + h] if kb == 0 else bcol[:, h:h + 1]
                                act(E[:, m, 0:nq], ps[pi][:, 0:nq], AF.Exp, [tps[pi], tC], [tE], bias=bc, scale=0.125)
                        first = (kb == kbs[0])
                        last = (kb == kbs[-1])
                        if q_lo == 0:
                            Ef = E.rearrange("p m n -> p (m n)")
                            S.op("pe", lambda e: e.matmul(ps[4][:, :], lhsT=Vt[:, kb, h * 128:(h + 1) * 128], rhs=Ef, start=first, stop=last),
                                 reads=[tV, tE], writes=[tps[4]])
                            S.op("pe", lambda e: e.matmul(ps[5][:, :], lhsT=ones_bf[:], rhs=Ef, start=first, stop=last),
                                 reads=[tC, tE], writes=[tps[5]])
                        else:
                            for m in range(2):
                                S.op("pe", lambda e: e.matmul(OO[:, m, q_lo:n], lhsT=Vt[:, kb, h * 128:(h + 1) * 128], rhs=E[:, m, 0:nq],
                                                              start=first, stop=(last and m == 1)), reads=[tV, tE], writes=[tps[4]])
                                S.op("pe", lambda e: e.matmul(SS[:, m, q_lo:n], lhsT=ones_bf[:], rhs=E[:, m, 0:nq],
                                                              start=first, stop=(last and m == 1)), reads=[tC, tE], writes=[tps[5]])
                    S.op("dve", lambda e: e.reciprocal(out=t1[:, 0:n], in_=SS[:, 0, 0:n]), reads=[tps[5]], writes=[tt1])
                    S.op("dve", lambda e: e.reciprocal(out=t2[:, 0:n], in_=SS[:, 1, 0:n]), reads=[tps[5]], writes=[tt2])
                    tt("dve", t1[:, 0:n], OO[:, 0, 0:n], t1[:, 0:n], ALU.mult, [tps[4], tt1], [tt1])
                    tt("dve", t2[:, 0:n], OO[:, 1, 0:n], t2[:, 0:n], ALU.mult, [tps[4], tt2], [tt2])
                    stt(t1[:, 0:n], t2[:, 0:n], pder[:, 3:4], t1[:, 0:n], ALU.mult, ALU.add, [tt1, tt2, tC], [tt1])
                    tt("pool", t2[:, 0:n], t1[:, 0:n], t1[:, 0:n], ALU.mult, [tt1], [tt2])
                    S.op("pe", lambda e: e.matmul(ps[6][:, 0:n], lhsT=ones_f[:], rhs=t2[:, 0:n], start=True, stop=True),
                         reads=[tC, tt2], writes=[tps[6]])
                    act(t2[:, 0:n], ps[6][:, 0:n], AF.Ln, [tps[6], tC], [tt2], bias=eps_ap(1e-5), scale=1.0 / 128.0)
                    act(t2[:, 0:n], t2[:, 0:n], AF.Exp, [tt2], [tt2], scale=-0.5)
                    tt("dve", t1[:, 0:n], t1[:, 0:n], t2[:, 0:n], ALU.mult, [tt1, tt2], [tt1])
                    stt(oTt[:, h, 0:n], t1[:, 0:n], pder[:, 2:3], gate_a[:, h, 0:n], ALU.mult, ALU.mult, [tt1, tga, tC], [toTt])

                chk('attn%d' % ti)
                for cc in range(4):
                    S.dma(lambda e: e.dma_start(out=agin[cc].ap()[:, ti * 256:ti * 256 + 256], in_=oTt[:, cc, 0:256]), reads=[toTt], writes=[tagin[cc]],
                          key="oT%d" % ob)
                if DEBUG:
                    dd = dbg_d.rearrange("(c p) n -> p c n", p=128)[:, :, ti * 256:ti * 256 + 256]
                    S.dma(lambda e, dd=dd, oTt=oTt: e.dma_start(out=dd, in_=oTt[:, :, 0:256]), reads=[toTt], key="oT%d" % ob)

            chk('main')
            for cc in range(4):
                S.dma(lambda e: e.collective_compute("AllGather", ALU.bypass, replica_groups=[[0, 1, 2, 3], [4, 5, 6, 7]],
                                                     ins=[agin[cc].ap()], outs=[agout[cc].ap()]),
                      reads=[tagin[cc]], writes=[tagout[cc]], key="cc%d" % cc, queue="pool", inc=1)
            chk('ag')
            S.barrier()
            apos[0] = 0
            oq = carve(16 * 1024 * 2, BF16, "p (c n) -> p c n", c=16)
            toq = S.tile("oq")
            gbt = carve(4 * D * 4, F32, "p (a n) -> p a n", a=4)
            tgb = S.tile("gbt")
            rsd = [carve(D * 4, F32) for _ in range(2)]
            trsd = [S.tile("rsd%d" % i) for i in range(2)]
            fst = carve(4 * 6 * 4, F32, "p (a n) -> p a n", a=4)
            fmv = carve(8 * 4, F32)
            tfst = S.tile("fst")
            assert apos[0] <= arena_w

            for kc in range(16):
                b = kc % 2
                S.dma(lambda e, kc=kc, b=b: e.dma_start(out=xb[b][:, 0:D], in_=wout_d[kc * 128:(kc + 1) * 128, :]),
                      writes=[txb[b]], key="xb%d" % b)
                eng = ("dve", "pool", "act")[kc % 3]
                if eng == "act":
                    S.op("act", lambda e, kc=kc, b=b: e.activation(out=W[:, kc, 0:D], in_=xb[b][:, 0:D], func=AF.Copy),
                         reads=[txb[b]], writes=[tW])
                else:
                    S.op(eng, lambda e, kc=kc, b=b: e.tensor_copy(out=W[:, kc, 0:D], in_=xb[b][:, 0:D]), reads=[txb[b]], writes=[tW])
            gsrc = bass.AP(tensor=gb_d.tensor, offset=0, ap=[[0, 128], [D, 4], [1, D]])
            S.dma(lambda e: e.dma_start(out=gbt, in_=gsrc), writes=[tgb], key="gbt")

            idxt = sbt("idxt", [128, 4], mybir.dt.int32)
            tidx = S.tile("idxt")
            S.dma(lambda e: e.dma_start(out=idxt[:], in_=idx_d), writes=[tidx], key="idxt")
            for kc in range(16):
                r_, c_ = kc // 4, kc % 4
                agv = agout[c_].ap().rearrange("e (q n) -> (e q) n", q=4)
                S.dma(lambda e: e.indirect_dma_start(out=oq[:, kc, :], out_offset=None, in_=agv,
                                                     in_offset=bass.IndirectOffsetOnAxis(ap=idxt[:, r_:r_ + 1], axis=0)),
                      reads=[tagout[c_], tidx], writes=[toq], key="oq", queue="pool")
            alpha = 2.0 ** 0.25
            for blk in range(8):
                b = blk % 2
                xt = xb[b]
                r = rsd[b]
                tr = trsd[b]
                S.dma(lambda e, blk=blk, xt=xt: e.dma_start(out=xt[:, 0:D], in_=xq_d[blk * 128:(blk + 1) * 128, :]), writes=[txb[b]], key="xb%d" % b)
                for q in range(4):
                    S.op("dve", lambda e, q=q, xt=xt: e.bn_stats(out=fst[:, q, :], in_=xt[:, q * 512:(q + 1) * 512]), reads=[txb[b]], writes=[tfst])
                S.op("dve", lambda e: e.bn_aggr(out=fmv[:, 0:2], in_=fst.rearrange("p a n -> p (a n)")), reads=[tfst], writes=[tfst])
                act(fmv[:, 2:3], fmv[:, 1:2], AF.Ln, [tfst], [tfst], bias=eps_ap(1e-5))
                act(fmv[:, 2:3], fmv[:, 2:3], AF.Exp, [tfst], [tfst], scale=-0.5)
                stt(fmv[:, 3:4], fmv[:, 0:1], -1.0, fmv[:, 2:3], ALU.mult, ALU.mult, [tfst], [tfst])
                S.op("act", lambda e, xt=xt, r=r: e.activation(out=r, in_=xt[:, 0:D], func=AF.Identity, bias=fmv[:, 3:4], scale=fmv[:, 2:3]),
                     reads=[txb[b], tfst], writes=[tr])
                tt("pool", r, r, gbt[:, 0, :], ALU.mult, [tr, tgb], [tr])
                tt("pool", r, r, gbt[:, 1, :], ALU.add, [tr, tgb], [tr])
                for ct in range(4):
                    for kc in range(16):
                        S.op("pe", lambda e, kc=kc, ct=ct, blk=blk: e.matmul(ps[ct][:, :], lhsT=oq[:, kc, blk * 128:(blk + 1) * 128],
                                                                           rhs=W[:, kc, ct * 512:(ct + 1) * 512], start=(kc == 0), stop=(kc == 15)),
                             reads=[toq, tW], writes=[tps[ct]])
                    stt(r[:, ct * 512:(ct + 1) * 512], r[:, ct * 512:(ct + 1) * 512], alpha, ps[ct][:, :], ALU.mult, ALU.add,
                        [tr, tps[ct]], [tr])
                for q in range(4):
                    S.op("dve", lambda e, q=q, r=r: e.bn_stats(out=fst[:, q, :], in_=r[:, q * 512:(q + 1) * 512]), reads=[tr], writes=[tfst])
                S.op("dve", lambda e: e.bn_aggr(out=fmv[:, 4:6], in_=fst.rearrange("p a n -> p (a n)")), reads=[tfst], writes=[tfst])
                act(fmv[:, 6:7], fmv[:, 5:6], AF.Ln, [tfst], [tfst], bias=eps_ap(1e-5))
                act(fmv[:, 6:7], fmv[:, 6:7], AF.Exp, [tfst], [tfst], scale=-0.5)
                stt(fmv[:, 7:8], fmv[:, 4:5], -1.0, fmv[:, 6:7], ALU.mult, ALU.mult, [tfst], [tfst])
                S.op("act", lambda e, r=r: e.activation(out=r, in_=r, func=AF.Identity, bias=fmv[:, 7:8], scale=fmv[:, 6:7]),
                     reads=[tr, tfst], writes=[tr])
                tt("dve", r, r, gbt[:, 2, :], ALU.mult, [tr, tgb], [tr])
                tt("pool", r, r, gbt[:, 3, :], ALU.add, [tr, tgb], [tr])
                S.dma(lambda e, blk=blk, r=r: e.dma_start(out=out_d[blk * 128:(blk + 1) * 128, :], in_=r), reads=[tr], key="rsd%d" % b)

        except _Stop:
            pass
        S.emit()
    return nc


_NC_CACHE = {}


def kernel(**inp):
    maps = _prep_inputs(inp)
    x = np.asarray(inp["x"], np.float32)
    for c in range(8):
        b, r = c // 4, c % 4
        maps[c]["xq"] = np.ascontiguousarray(x[b, r * 1024:(r + 1) * 1024])
        p = np.arange(128)[:, None]
        kc = np.arange(4)[None, :]
        maps[c]["idx"] = ((kc * 128 + p) * 4 + r).astype(np.int32)
    if "nc" not in _NC_CACHE:
        _NC_CACHE["nc"] = build_nc()
    nc = _NC_CACHE["nc"]
    res = run_bass_kernel_spmd(nc, maps, core_ids=list(range(8)))
    out = np.zeros((2, SEQ, D), np.float32)
    for c in range(8):
        b, r = c // 4, c % 4
        out[b, r * 1024:(r + 1) * 1024] = res.results[c]["out"]
    if DEBUG:
        kernel.dbg = [res.results[c]["dbg"] for c in range(8)]
    return out
```

```python
import math
from contextlib import ExitStack

import numpy as np
import ml_dtypes

import concourse.bass as bass
import concourse.mybir as mybir
from concourse.bass_utils import run_bass_kernel_spmd

F32 = mybir.dt.float32
BF16 = mybir.dt.bfloat16
AF = mybir.ActivationFunctionType
ALU = mybir.AluOpType

D = 2048
SEQ = 4096
NB = 33
P_TOK = NB * 128
NCOL = 2240
C0 = math.exp(-0.5)
NEG = -1.0e4
TABW = 511
DEBUG = False
STOP = None


class _Stop(Exception):
    pass


_HITS = {}


def chk(tag):
    import os
    if STOP == tag:
        _HITS[tag] = _HITS.get(tag, 0) + 1
        if _HITS[tag] >= int(os.environ.get("NTH", "1")):
            raise _Stop()


class T:
    __slots__ = ("name", "w", "r")

    def __init__(self, name, init_r=None):
        self.name = name
        self.w = None
        self.r = dict(init_r) if init_r else {}


class _Rec:
    def __init__(self):
        self.call = None

    def __getattr__(self, name):
        def f(*a, **k):
            assert self.call is None
            self.call = (name, a, k)
            return self
        return f


def _freeze(fn):
    r = _Rec()
    fn(r)
    name, a, k = r.call
    return lambda e: getattr(e, name)(*a, **k)


class Sched:
    ENG = ("pe", "act", "dve", "pool", "sp")

    def __init__(self, nc, es):
        self.nc = nc
        self.es = es
        self.ops = {e: [] for e in self.ENG}
        self.seen = {e: {} for e in self.ENG}
        self.dsems = {}
        self.esem = {}
        self.bar = {}

    def tile(self, name):
        return T(name, self.bar)

    def barrier(self):
        b = {}
        for e in self.ENG:
            if e != "sp" and self.ops[e]:
                for i in range(len(self.ops[e]) - 1, -1, -1):
                    if self.ops[e][i]["dma"] is None:
                        b[('E', e)] = ('E', e, i)
                        break
        for k, v in self.dsems.items():
            if v[1] > 0:
                b[('D', k)] = ('D', k, v[1])
        self.bar = b

    @staticmethod
    def _dep(waits, ev):
        key = ev[:2]
        if waits.get(key, -1) < ev[2]:
            waits[key] = ev[2]

    def _collect(self, eng, reads, writes):
        waits = {}
        for t in reads:
            if t.w is not None:
                self._dep(waits, t.w)
        for t in writes:
            if t.w is not None:
                self._dep(waits, t.w)
            for ev in t.r.values():
                self._dep(waits, ev)
        wl = []
        for key, val in waits.items():
            if key[0] == 'E' and key[1] == eng and eng == 'pe':
                continue
            if self.seen[eng].get(key, -1) >= val:
                continue
            self.seen[eng][key] = val
            wl.append((key, val))
        return wl

    def op(self, eng, fn, reads=(), writes=()):
        fn = _freeze(fn)
        wl = self._collect(eng, reads, writes)
        idx = len(self.ops[eng])
        ev = ('E', eng, idx)
        self.ops[eng].append(dict(waits=wl, fn=fn, sig=False, dma=None))
        for t in reads:
            t.r[('E', eng)] = ev
        for t in writes:
            t.w = ev
            t.r = {}
        return ev

    def dsem(self, key):
        if key not in self.dsems:
            h = self.es.enter_context(self.nc.semaphore("d_" + key))
            self.dsems[key] = [h, 0]
        return self.dsems[key]

    def dma(self, fn, reads=(), writes=(), key=None, queue="sp", inc=16):
        fn = _freeze(fn)
        wl = self._collect(queue, reads, writes)
        ds = self.dsem(key)
        ds[1] += inc
        ev = ('D', key, ds[1])
        self.ops[queue].append(dict(waits=wl, fn=fn, sig=False, dma=(key, inc)))
        for t in reads:
            t.r[('D', key)] = ev
        for t in writes:
            t.w = ev
            t.r = {}
        return ev

    def emit(self):
        nc = self.nc
        for e in self.ENG:
            for o in self.ops[e]:
                for key, val in o["waits"]:
                    if key[0] == 'E':
                        self.ops[key[1]][val]["sig"] = True
        cnt = {}
        for e in self.ENG:
            c = 0
            lst = []
            for o in self.ops[e]:
                if o["sig"]:
                    c += 1
                lst.append(c)
            cnt[e] = lst
            self.esem[e] = self.es.enter_context(nc.semaphore("e_" + e))
        finals = [(('D', k), v[1]) for k, v in self.dsems.items() if v[1] > 0]

        def resolve(key, val):
            if key[0] == 'E':
                return self.esem[key[1]], cnt[key[1]][val]
            return self.dsems[key[1]][0], val

        def body(e):
            def run(eng):
                for o in self.ops[e]:
                    for key, val in o["waits"]:
                        s, v = resolve(key, val)
                        eng.wait_ge(s, v)
                    ins = o["fn"](eng)
                    if o["dma"] is not None:
                        ins.then_inc(self.dsems[o["dma"][0]][0], o["dma"][1])
                    elif o["sig"]:
                        ins.then_inc(self.esem[e], 1)
                if e == "sp":
                    for key, val in finals:
                        s, v = resolve(key, val)
                        eng.wait_ge(s, v)
            return run

        with nc.Block() as block:
            block.tensor(body("pe"))
            block.scalar(body("act"))
            block.vector(body("dve"))
            block.gpsimd(body("pool"))
            block.sync(body("sp"))


def _bucket(n):
    n = np.maximum(n, 0)
    nf = np.maximum(n, 1).astype(np.float32)
    large = 16 + (np.log(nf / np.float32(16)) / np.float32(math.log(128 / 16)) * np.float32(16)).astype(np.int32)
    large = np.minimum(large, 31)
    return np.where(n < 16, n, large)


PV_MU = 0
PV_MUWD = 6
PV_MUAD = 7
PV_W0 = 8
PV_A0 = 10
PV_KK = 12
PV_KA = 14
PV_RK = 16
PV_GNG = 18
PV_GNB = 20
PV_SUBLN = 22
PV_LNG = 23
PV_LNB = 39
PV_N = 55


def _consts():
    c = {}
    c["ident"] = np.eye(128, dtype=np.float32).astype(ml_dtypes.bfloat16)
    c["ones_bf"] = np.ones((128, 128), dtype=ml_dtypes.bfloat16)
    c["ones_f"] = np.ones((128, 128), dtype=np.float32)
    bo = np.zeros((128, 128), np.float32)
    bo[:64, :64] = 1.0
    bo[64:, 64:] = 1.0
    c["blockones"] = bo
    s = np.arange(128)[:, None]
    t = np.arange(128)[None, :]
    strict = (s < t).astype(np.float32)
    incl = (s <= t).astype(np.float32)
    m = np.zeros((128, 2, 2, 128), np.float32)
    m[:, :, 0, :] = strict[:, None, :]
    m[:, :, 1, :] = incl[:, None, :]
    c["maskT"] = m.reshape(128, 512)
    c["masklow"] = (s > t).astype(np.float32)
    d = np.arange(TABW) - 127
    oh = np.zeros((33, TABW), np.float32)
    b = _bucket(d)
    for i in range(TABW):
        if d[i] >= 0:
            oh[b[i], i] = 1.0
        else:
            oh[32, i] = NEG
    c["oh"] = oh
    return c


def _prep_inputs(inp):
    f = np.float32
    w_in = np.asarray(inp["w_in"][0], f)
    w_out = np.asarray(inp["w_out"][0], f)
    x = np.asarray(inp["x"], f)
    consts = _consts()
    lam4 = np.stack([inp["lambda_q1"][0], inp["lambda_k1"][0], inp["lambda_q2"][0], inp["lambda_k2"][0]]).astype(f)
    rows = []
    for r in range(4):
        rows += list(range((2 * r) * 128, (2 * r + 2) * 128))
        rows += list(range(1024 + (4 * r) * 64, 1024 + (4 * r + 4) * 64))
    w_out_p = np.ascontiguousarray(w_out[rows])
    gb = np.stack([inp["ln_emb_g"], inp["ln_emb_b"], inp["ln_post_g"][0], inp["ln_post_b"][0]]).astype(f)
    maps = []
    for c in range(8):
        b, hg = c // 4, c % 4
        h0 = 2 * hg
        cols = []
        cols += list(range(h0 * 128, (h0 + 2) * 128))
        cols += list(range(1024 + h0 * 128, 1024 + (h0 + 2) * 128))
        cols += list(range(3072 + h0 * 128, 3072 + (h0 + 2) * 128))
        rb = 4096 + hg * 256
        cols += list(range(rb, rb + 256))
        cols += list(range(rb + 1024, rb + 1024 + 256))
        cols += list(range(rb + 2048, rb + 2048 + 256))
        cols += list(range(7360 + hg * 256, 7360 + hg * 256 + 256))
        cols += list(range(4096 + 3072, 4096 + 3072 + 192))
        cols += list(range(2048 + h0 * 128, 2048 + (h0 + 2) * 128))
        assert len(cols) == NCOL
        wi = np.ascontiguousarray(w_in[:, cols])
        pv = np.zeros((128, PV_N), f)
        mu = inp["rw_mu"][0]
        rs = slice(hg * 256, hg * 256 + 256)
        for p in range(2):
            ps_ = slice(hg * 256 + p * 128, hg * 256 + (p + 1) * 128)
            pv[:, PV_MU + 0 + p] = mu[0:1024][ps_]
            pv[:, PV_MU + 2 + p] = mu[1024:2048][ps_]
            pv[:, PV_MU + 4 + p] = mu[2048:3072][ps_]
            pv[:, PV_W0 + p] = inp["rw_w0"][0][ps_]
            pv[:, PV_A0 + p] = inp["rw_a0"][0][ps_]
            pv[:, PV_KK + p] = inp["rw_k_k"][0][ps_]
            pv[:, PV_KA + p] = inp["rw_k_a"][0][ps_]
            pv[:, PV_RK + p] = inp["rw_r_k"][0].reshape(-1)[ps_]
            pv[:, PV_GNG + p] = inp["rw_gn_g"][0][ps_]
            pv[:, PV_GNB + p] = inp["rw_gn_b"][0][ps_]
        pv[:96, PV_MUWD] = mu[3072:3168]
        pv[:96, PV_MUAD] = mu[3168:3264]
        pv[:, PV_SUBLN] = inp["subln_g"][0]
        pv[:, PV_LNG:PV_LNG + 16] = np.asarray(inp["ln_emb_g"], f).reshape(16, 128).T
        pv[:, PV_LNB:PV_LNB + 16] = np.asarray(inp["ln_emb_b"], f).reshape(16, 128).T
        relb = np.ones((33, 2, 128), f)
        for j in range(2):
            relb[:32, j, :] = np.asarray(inp["rel_bias"], f)[:, h0 + j][:, None]
        lora = np.zeros((96, 2, 256), f)
        lora[:, 0, :] = inp["rw_w_up"][0][:, rs]
        lora[:, 1, :] = inp["rw_a_up"][0][:, rs]
        m = {
            "x": np.ascontiguousarray(x[b]),
            "meta": np.asarray(inp["meta_tokens"], f),
            "w_in": wi,
            "w_out": w_out_p,
            "pvec": pv,
            "relb": relb,
            "lora": lora,
            "lam4": lam4,
            "gb": gb,
        }
        m.update(consts)
        maps.append(m)
    return maps


def build_nc():
    nc = bass.Bass("TRN2", target_bir_lowering=False)

    def din(name, shape, dt=F32):
        return nc.dram_tensor(name, list(shape), dt, kind="ExternalInput").ap()

    x_d = din("x", [SEQ, D])
    meta_d = din("meta", [16, D])
    win_d = din("w_in", [D, NCOL])
    wout_d = din("w_out", [D, D])
    pvec_d = din("pvec", [128, PV_N])
    relb_d = din("relb", [33, 2, 128])
    lora_d = din("lora", [96, 2, 256])
    lam4_d = din("lam4", [4, 64])
    gb_d = din("gb", [4, D])
    ident_d = din("ident", [128, 128], BF16)
    onesbf_d = din("ones_bf", [128, 128], BF16)
    onesf_d = din("ones_f", [128, 128])
    bo_d = din("blockones", [128, 128])
    maskT_d = din("maskT", [128, 512])
    masklow_d = din("masklow", [128, 128])
    oh_d = din("oh", [33, TABW])
    xq_d = din("xq", [1024, D])
    idx_d = din("idx", [128, 4], mybir.dt.int32)
    out_d = nc.dram_tensor("out", [1024, D], F32, kind="ExternalOutput").ap()
    if DEBUG:
        dbg_d = nc.dram_tensor("dbg", [512, SEQ], BF16, kind="ExternalOutput").ap()
    agin = [nc.dram_tensor("agin%d" % c, [128, SEQ], BF16) for c in range(4)]
    agout = [nc.dram_tensor("agout%d" % c, [512, SEQ], BF16) for c in range(4)]
    tab_d = [nc.dram_tensor("tab%d" % j, [128, TABW], F32) for j in range(2)]

    with ExitStack() as es:
        S = Sched(nc, es)

        def sbt(name, shape, dt=F32):
            return es.enter_context(nc.sbuf_tensor("s_" + name, list(shape), dt))

        W = sbt("W", [128, 16, NCOL], BF16)
        tW = S.tile("W")
        xb = [sbt("xb%d" % i, [128, NCOL]) for i in range(2)]
        txb = [S.tile("xb%d" % i) for i in range(2)]
        pvec = sbt("pvec", [128, PV_N])
        pder = sbt("pder", [128, 8])
        ident = sbt("ident", [128, 128], BF16)
        ones_bf = sbt("ones_bf", [128, 128], BF16)
        ones_f = sbt("ones_f", [128, 128])
        blockones = sbt("blockones", [128, 128])
        maskT = sbt("maskT", [128, 512])
        masklow = sbt("masklow", [128, 128])
        lora32 = sbt("lora32", [96, 2, 256])
        lora = sbt("lora", [96, 2, 256], BF16)
        bext = [sbt("bext%d" % j, [128, 384]) for j in range(2)]
        bcol = sbt("bcol", [128, 4])
        tC = S.tile("consts")
        arena_w = 28100
        arena = sbt("arena", [128, arena_w])
        apos = [0]

        def carve(nbytes, dt, shape_str=None, **kw):
            n32 = (nbytes + 3) // 4
            a = apos[0]
            apos[0] += n32
            assert apos[0] <= arena_w, apos[0]
            ap = arena[:, a:a + n32]
            if dt != F32:
                ap = ap.bitcast(dt)
            if shape_str:
                ap = ap.rearrange(shape_str, **kw)
            return ap

        ps = [es.enter_context(nc.psum_tensor("ps%d" % i, [128, 512], F32)) for i in range(7)]
        tps = [S.tile("ps%d" % i) for i in range(7)]
        psb = es.enter_context(nc.psum_tensor("psb", [128, 1024], BF16))
        tpsb = S.tile("psb")

        KT = carve(2 * P_TOK * 2, BF16, "p (h n) -> p h n", h=2)
        tKT = S.tile("KT")
        Vt = carve(NB * 256 * 2, BF16, "p (b n) -> p b n", b=NB)
        tV = S.tile("V")
        xn = carve(D * 2, BF16)
        txn = S.tile("xn")
        hT = carve(16 * 256 * 2, BF16, "p (c n) -> p c n", c=16)
        thT = S.tile("hT")
        QT = carve(2 * 256 * 2, BF16, "p (h n) -> p h n", h=2)
        tQT = S.tile("QT")
        gate_a = carve(2 * 256 * 2, BF16, "p (h n) -> p h n", h=2)
        tga = S.tile("gate_a")
        gate_r = carve(2 * 256 * 2, BF16, "p (h n) -> p h n", h=2)
        tgr = S.tile("gate_r")
        zb = [carve(257 * 4, F32) for _ in range(8)]
        tzb = [S.tile("z%d" % i) for i in range(8)]
        NWK = 12
        wk = [carve(256 * 4, F32) for _ in range(NWK)]
        twk = [S.tile("wk%d" % i) for i in range(NWK)]
        bonus = [carve(256 * 4, F32) for _ in range(2)]
        tbonus = [S.tile("bonus%d" % i) for i in range(2)]
        yT = [wk[1], wk[5]]
        tyT = [twk[1], twk[5]]
        twb = carve(256 * 2, BF16)
        adb = carve(256 * 2, BF16)
        ttwb = S.tile("twb")
        tadb = S.tile("adb")
        arT = [carve(2 * 256 * 2, BF16, "p (a n) -> p a n", a=2) for _ in range(2)]
        tarT = [S.tile("arT%d" % i) for i in range(2)]
        ktl = [carve(256 * 2, BF16) for _ in range(2)]
        btl = [carve(256 * 2, BF16) for _ in range(2)]
        tktl = [S.tile("ktl%d" % i) for i in range(2)]
        tbtl = [S.tile("btl%d" % i) for i in range(2)]
        fm3 = [carve(256 * 2, BF16) for _ in range(3)]
        tfm3 = [S.tile("fm3_%d" % i) for i in range(3)]
        tok3 = [carve(2 * 256 * 2, BF16, "p (c n) -> p c n", c=2) for _ in range(3)]
        ttok3 = [S.tile("tok3_%d" % i) for i in range(3)]
        WC = carve(4 * 4, F32)
        tWC = S.tile("WC")
        nbias = carve(4 * 4, F32)
        tnbias = S.tile("nbias")
        AT = [carve(4 * 128 * 2, BF16, "p (a n) -> p a n", a=4) for _ in range(4)]
        tAT = [S.tile("AT%d" % i) for i in range(4)]
        Mc = carve(4 * 128 * 2, BF16, "p (h n) -> p h n", h=4)
        Mtc = carve(4 * 128 * 2, BF16, "p (h n) -> p h n", h=4)
        Ptc = carve(4 * 128 * 2, BF16, "p (h n) -> p h n", h=4)
        tMc = S.tile("Mc")
        tMtc = S.tile("Mtc")
        tPtc = S.tile("Ptc")
        btlm = [[carve(256 * 2, BF16) for _ in range(2)] for _ in range(2)]
        ktlm = [[carve(256 * 2, BF16) for _ in range(2)] for _ in range(2)]
        tbtlm = [[S.tile("btlm%d%d" % (i, j)) for j in range(2)] for i in range(2)]
        tktlm = [[S.tile("ktlm%d%d" % (i, j)) for j in range(2)] for i in range(2)]
        Vz = carve(2 * 4 * 128 * 2, BF16, "p (c h n) -> p c h n", c=2, h=4)
        tVz = S.tile("Vz")
        Uz = carve(4 * 128 * 2, BF16, "p (h n) -> p h n", h=4)
        tUz = S.tile("Uz")
        Xsb = carve(256 * 2, BF16, "p (h n) -> p h n", h=4)
        Usb = carve(256 * 2, BF16, "p (h n) -> p h n", h=4)
        tXsb = S.tile("Xsb")
        tUsb = S.tile("Usb")
        S32 = carve(2 * 128 * 4, F32, "p (a n) -> p a n", a=2)
        Sbf = carve(2 * 128 * 2, BF16, "p (a n) -> p a n", a=2)
        tS32 = S.tile("S32")
        tSbf = S.tile("Sbf")
        Ssb = [wk[0], wk[2]]
        tSsb = [twk[0], twk[2]]
        Eb = [carve(2 * 256 * 2, BF16, "p (m n) -> p m n", m=2) for _ in range(2)]
        tEb = [S.tile("Eb%d" % i) for i in range(2)]
        oT = [carve(4 * 256 * 2, BF16, "p (c n) -> p c n", c=4) for _ in range(2)]
        toT = [S.tile("oT%d" % i) for i in range(2)]
        stats = carve(4 * 6 * 4, F32, "p (a n) -> p a n", a=4)
        mv = carve(8 * 4, F32)
        tst = S.tile("stats")
        lamw = carve(4 * 64 * 4, F32, "p (a n) -> p a n", a=4)
        tabsb = carve(TABW * 4, F32)
        ttab = S.tile("tabsb")
        main_end = apos[0]
        print("arena main words", main_end, "of", arena_w)

        tagin = [S.tile("agin%d" % c) for c in range(4)]
        tagout = [S.tile("agout%d" % c) for c in range(4)]
        ttabd = [S.tile("tabd%d" % j) for j in range(2)]

        try:
            def cload(dst, src):
                S.dma(lambda e: e.dma_start(out=dst, in_=src), writes=[tC], key="consts")

            cload(pvec[:], pvec_d)
            cload(ident[:], ident_d)
            cload(ones_bf[:], onesbf_d)
            cload(ones_f[:], onesf_d)
            cload(blockones[:], bo_d)
            cload(maskT[:], maskT_d)
            cload(masklow[:], masklow_d)
            cload(lora32[:], lora_d)
            relb = carve(2 * 128 * 4, F32, "p (a n) -> p a n", a=2)
            ohsb = carve(TABW * 4, F32)
            cload(relb[0:33], relb_d)
            cload(ohsb[0:33], oh_d)
            lam_src = bass.AP(tensor=lam4_d.tensor, offset=0, ap=[[0, 128], [64, 4], [1, 64]])
            cload(lamw, lam_src)

            S.op("dve", lambda e: e.tensor_copy(out=lora[:], in_=lora32[:]), reads=[tC], writes=[tC])
            S.op("dve", lambda e: e.tensor_scalar(out=pder[:, 0:2], in0=pvec[:, PV_KA:PV_KA + 2], scalar1=-1.0, scalar2=1.0,
                                                  op0=ALU.mult, op1=ALU.add), reads=[tC], writes=[tC])
            S.op("dve", lambda e: e.tensor_scalar(out=pder[:, 2:3], in0=pvec[:, PV_SUBLN:PV_SUBLN + 1], scalar1=0.8, scalar2=None,
                                                  op0=ALU.mult), reads=[tC], writes=[tC])
            S.op("dve", lambda e: e.tensor_tensor(out=lamw[:, 0, :], in0=lamw[:, 0, :], in1=lamw[:, 1, :], op=ALU.mult),
                 reads=[tC], writes=[tC])
            S.op("dve", lambda e: e.tensor_tensor(out=lamw[:, 2, :], in0=lamw[:, 2, :], in1=lamw[:, 3, :], op=ALU.mult),
                 reads=[tC], writes=[tC])
            S.op("dve", lambda e: e.reduce_sum(out=pder[:, 4:5], in_=lamw[:, 0, :], axis=mybir.AxisListType.X),
                 reads=[tC], writes=[tC])
            S.op("dve", lambda e: e.reduce_sum(out=pder[:, 5:6], in_=lamw[:, 2, :], axis=mybir.AxisListType.X),
                 reads=[tC], writes=[tC])
            S.op("act", lambda e: e.activation(out=pder[:, 4:6], in_=pder[:, 4:6], func=AF.Exp), reads=[tC], writes=[tC])
            S.op("dve", lambda e: e.tensor_tensor(out=pder[:, 3:4], in0=pder[:, 5:6], in1=pder[:, 4:5], op=ALU.subtract),
                 reads=[tC], writes=[tC])
            S.op("dve", lambda e: e.tensor_scalar(out=pder[:, 3:4], in0=pder[:, 3:4], scalar1=-0.2, scalar2=None, op0=ALU.add),
                 reads=[tC], writes=[tC])
            S.op("pool", lambda e: e.memset(S32.rearrange("p a n -> p (a n)"), 0.0), writes=[tS32])
            S.op("pool", lambda e: e.memset(Sbf.rearrange("p a n -> p (a n)"), 0.0), writes=[tSbf])
            S.op("pool", lambda e: e.memset(Vz.rearrange("p c h n -> p (c h n)"), 0.0), writes=[tVz])
            S.op("pool", lambda e: e.memset(Uz.rearrange("p h n -> p (h n)"), 0.0), writes=[tUz])
            for i in range(8):
                S.op("pool", lambda e, i=i: e.memset(zb[i], 0.0), writes=[tzb[i]])

            for j in range(2):
                S.op("pe", lambda e, j=j: e.matmul(ps[0][:, 0:256], lhsT=relb[0:33, j, :], rhs=ohsb[0:33, 0:256], start=True, stop=True),
                     reads=[tC], writes=[tps[0]])
                S.op("pe", lambda e, j=j: e.matmul(ps[1][:, 0:255], lhsT=relb[0:33, j, :], rhs=ohsb[0:33, 256:511], start=True, stop=True),
                     reads=[tC], writes=[tps[1]])
                S.op("dve", lambda e: e.tensor_copy(out=tabsb[:, 0:256], in_=ps[0][:, 0:256]), reads=[tps[0]], writes=[ttab])
                S.op("dve", lambda e: e.tensor_copy(out=tabsb[:, 256:511], in_=ps[1][:, 0:255]), reads=[tps[1]], writes=[ttab])
                S.op("dve", lambda e, j=j: e.tensor_copy(out=bcol[:, j:j + 1], in_=tabsb[:, 510:511]), reads=[ttab], writes=[tC])
                S.op("dve", lambda e, j=j: e.tensor_copy(out=bcol[:, 2 + j:3 + j], in_=tabsb[:, 510:511]), reads=[ttab], writes=[tC])
                S.op("dve", lambda e, j=j: e.memset(bcol[0:112, 2 + j:3 + j], NEG), reads=[], writes=[tC])
                S.dma(lambda e, j=j: e.dma_start(out=tab_d[j].ap(), in_=tabsb), reads=[ttab], writes=[ttabd[j]], key="tabst")
                src = bass.AP(tensor=tab_d[j].ap().tensor, offset=127, ap=[[TABW - 1, 128], [1, 384]])
                S.dma(lambda e, j=j, src=src: e.dma_start(out=bext[j][:], in_=src), reads=[ttabd[j]], writes=[tC], key="consts")

            for kc in range(16):
                b = kc % 2
                S.dma(lambda e, kc=kc, b=b: e.dma_start(out=xb[b][:], in_=win_d[kc * 128:(kc + 1) * 128, :]),
                      writes=[txb[b]], key="xb%d" % b)
                eng = ("dve", "pool", "act")[kc % 3]
                if eng == "act":
                    S.op("act", lambda e, kc=kc, b=b: e.activation(out=W[:, kc, :], in_=xb[b][:], func=AF.Copy),
                         reads=[txb[b]], writes=[tW])
                else:
                    S.op(eng, lambda e, kc=kc, b=b: e.tensor_copy(out=W[:, kc, :], in_=xb[b][:]), reads=[txb[b]], writes=[tW])

            chk('setup')
            rr = {"ps": 0, "ev": 0, "xb": 0, "oT": 0}

            def next_ps(n=4):
                i = rr["ps"] % n
                rr["ps"] += 1
                return i

            def evac_copy(out, in_, reads, writes, bf=False):
                k = 1
                if k == 0:
                    S.op("dve", lambda e: e.tensor_scalar(out=out, in0=in_, scalar1=1.0, scalar2=None, op0=ALU.mult), reads=reads, writes=writes)
                else:
                    S.op("act", lambda e: e.activation(out=out, in_=in_, func=AF.Copy), reads=reads, writes=writes)

            def tt(eng, out, a, b, op, reads, writes):
                S.op(eng, lambda e: e.tensor_tensor(out=out, in0=a, in1=b, op=op), reads=reads, writes=writes)

            def ts(eng, out, a, s1, s2, op0, op1, reads, writes):
                if s2 is None:
                    S.op(eng, lambda e: e.tensor_scalar(out=out, in0=a, scalar1=s1, scalar2=None, op0=op0), reads=reads, writes=writes)
                else:
                    S.op(eng, lambda e: e.tensor_scalar(out=out, in0=a, scalar1=s1, scalar2=s2, op0=op0, op1=op1),
                         reads=reads, writes=writes)

            def stt(out, a, sc, b, op0, op1, reads, writes):
                S.op("dve", lambda e: e.scalar_tensor_tensor(out=out, in0=a, scalar=sc, in1=b, op0=op0, op1=op1),
                     reads=reads, writes=writes)

            def act(out, in_, func, reads, writes, bias=0.0, scale=1.0):
                S.op("act", lambda e: e.activation(out=out, in_=in_, func=func, bias=bias, scale=scale), reads=reads, writes=writes)

            def rsqrt_inplace(t, tt_, n, scale, eps):
                act(t[:, 0:n], t[:, 0:n], AF.Ln, [tt_], [tt_], bias=eps_ap(eps), scale=scale)
                act(t[:, 0:n], t[:, 0:n], AF.Exp, [tt_], [tt_], scale=-0.5)

            epsc = {}

            def eps_ap(v):
                if v == 0.0:
                    return 0.0
                if v not in epsc:
                    col = 6 + len(epsc)
                    S.op("pool", lambda e, col=col, v=v: e.memset(pder[:, col:col + 1], float(v)), writes=[tC])
                    epsc[v] = pder[:, col:col + 1]
                return epsc[v]

            eps_ap(1e-5)
            eps_ap(64e-5)

            def layer_norm_block(src_ap, is_meta, blk_slot):
                b = rr["xb"] % 2
                rr["xb"] += 1
                xt = xb[b]
                if is_meta:
                    S.op("pool", lambda e: e.memset(xt[:, 0:D], 0.0), writes=[txb[b]])
                    S.dma(lambda e: e.dma_start(out=xt[112:128, 0:D], in_=meta_d), writes=[txb[b]], key="xb%d" % b)
                else:
                    S.dma(lambda e: e.dma_start(out=xt[:, 0:D], in_=src_ap), writes=[txb[b]], key="xb%d" % b)
                for q in range(4):
                    S.op("dve", lambda e, q=q: e.bn_stats(out=stats[:, q, :], in_=xt[:, q * 512:(q + 1) * 512]),
                         reads=[txb[b]], writes=[tst])
                S.op("dve", lambda e: e.bn_aggr(out=mv[:, 0:2], in_=stats.rearrange("p a n -> p (a n)")), reads=[tst], writes=[tst])
                act(mv[:, 2:3], mv[:, 1:2], AF.Ln, [tst], [tst], bias=eps_ap(1e-5))
                act(mv[:, 2:3], mv[:, 2:3], AF.Exp, [tst], [tst], scale=-0.5)
                stt(mv[:, 3:4], mv[:, 0:1], -1.0, mv[:, 2:3], ALU.mult, ALU.mult, [tst], [tst])
                S.op("act", lambda e: e.activation(out=xn, in_=xt[:, 0:D], func=AF.Identity, bias=mv[:, 3:4], scale=mv[:, 2:3]),
                     reads=[txb[b], tst], writes=[txn])
                for half in range(2):
                    for dc in range(8):
                        c = half * 8 + dc
                        S.op("pe", lambda e, c=c, dc=dc: e.transpose(psb[:, dc * 128:(dc + 1) * 128], xn[:, c * 128:(c + 1) * 128], ident[:]),
                             reads=[txn, tC], writes=[tpsb])
                    for dc in range(8):
                        c = half * 8 + dc
                        dst = hT[:, c, blk_slot * 128:(blk_slot + 1) * 128]
                        src = psb[:, dc * 128:(dc + 1) * 128]
                        if dc % 2 == 0:
                            ts("dve", dst, src, pvec[:, PV_LNG + c:PV_LNG + c + 1], pvec[:, PV_LNB + c:PV_LNB + c + 1],
                               ALU.mult, ALU.add, [tpsb, tC], [thT])
                        else:
                            S.op("act", lambda e, dst=dst, src=src, c=c: e.activation(
                                out=dst, in_=src, func=AF.Identity, bias=pvec[:, PV_LNB + c:PV_LNB + c + 1],
                                scale=pvec[:, PV_LNG + c:PV_LNG + c + 1]), reads=[tpsb, tC], writes=[thT])
                if is_meta:
                    S.op("pool", lambda e: e.memset(hT[:, :, 0:112], 0.0), writes=[thT])

            def inproj_fm(ct_off, m, nt, sink):
                i = next_ps()
                for kc in range(16):
                    S.op("pe", lambda e, kc=kc, i=i: e.matmul(ps[i][0:m, 0:nt], lhsT=W[:, kc, ct_off:ct_off + m], rhs=hT[:, kc, 0:nt],
                                                            start=(kc == 0), stop=(kc == 15)), reads=[tW, thT], writes=[tps[i]])
                sink(ps[i][0:m, 0:nt], tps[i])

            tiles = [(-1, 128)] + [(i, 256) for i in range(16)]
            for (ti, nt) in tiles:
                nblk = nt // 128
                blk0 = 0 if ti < 0 else 1 + 2 * ti
                pos0 = blk0 * 128
                for j in range(nblk):
                    if ti < 0:
                        layer_norm_block(None, True, 0)
                    else:
                        r0 = ti * 256 + j * 128
                        layer_norm_block(x_d[r0:r0 + 128, :], False, j)
                if ti >= 0:
                    for h in range(2):
                        inproj_fm(h * 128, 128, nt, lambda p, tp, h=h: evac_copy(QT[:, h, 0:nt], p, [tp], [tQT]))
                for h in range(2):
                    inproj_fm(256 + h * 128, 128, nt, lambda p, tp, h=h: evac_copy(KT[:, h, pos0:pos0 + nt], p, [tp], [tKT]))
                if ti >= 0:
                    for h in range(2):
                        inproj_fm(512 + h * 128, 128, nt, lambda p, tp, h=h: act(gate_a[:, h, 0:nt], p, AF.Silu, [tp], [tga]))
                    for p_ in range(2):
                        inproj_fm(1536 + p_ * 128, 128, nt, lambda p, tp, p_=p_: act(gate_r[:, p_, 0:nt], p, AF.Silu, [tp], [tgr]))
                for zi in range(6):
                    inproj_fm(768 + zi * 128, 128, nt, lambda p, tp, zi=zi: evac_copy(zb[zi][:, 1:1 + nt], p, [tp], [tzb[zi]]))
                for zi in range(2):
                    inproj_fm(1792 + zi * 96, 96, nt, lambda p, tp, zi=zi: evac_copy(zb[6 + zi][0:96, 1:1 + nt], p, [tp], [tzb[6 + zi]]))
                for j in range(nblk):
                    i = next_ps()
                    for kc in range(16):
                        S.op("pe", lambda e, kc=kc, i=i, j=j: e.matmul(ps[i][:, 0:256], lhsT=hT[:, kc, j * 128:(j + 1) * 128],
                                                                     rhs=W[:, kc, 1984:2240], start=(kc == 0), stop=(kc == 15)),
                             reads=[tW, thT], writes=[tps[i]])
                    evac_copy(Vt[:, blk0 + j, :], ps[i][:, 0:256], [tps[i]], [tV])

                chk('inproj%d' % ti)
                r32, k32, v32, t1, t2, sg, aic, kkn, k2, bvec, cs, t3 = wk
                tr32, tk32, tv32, tt1, tt2, tsg, taic, tkkn, tk2, tbvec, tcs, tt3 = twk
                n = nt

                def shift(zi, mucol, out, tout, rows=128):
                    z = zb[zi]
                    tz = tzb[zi]
                    tt("pool", t3[0:rows, 0:n], z[0:rows, 0:n], z[0:rows, 1:1 + n], ALU.subtract, [tz], [tt3])
                    stt(out[0:rows, 0:n], t3[0:rows, 0:n], pvec[0:rows, mucol:mucol + 1], z[0:rows, 1:1 + n], ALU.mult, ALU.add,
                        [tt3, tz, tC], [tout])
                    S.op("pool", lambda e: e.tensor_copy(out=z[0:rows, 0:1], in_=z[0:rows, n:n + 1]), reads=[tz], writes=[tz])

                shift(6, PV_MUWD, t1, tt1, rows=96)
                act(twb[0:96, 0:n], t1[0:96, 0:n], AF.Tanh, [tt1], [ttwb])
                shift(7, PV_MUAD, t2, tt2, rows=96)
                S.op("dve", lambda e: e.tensor_scalar(out=adb[0:96, 0:n], in0=t2[0:96, 0:n], scalar1=1.0, scalar2=None, op0=ALU.mult), reads=[tt2], writes=[tadb])

                chk('pa%d' % ti)
                import os as _os
                for p in [int(v) for v in _os.environ.get('PAIRS', '0,1').split(',')]:
                    shift(0 + p, PV_MU + 0 + p, r32, tr32)
                    shift(2 + p, PV_MU + 2 + p, k32, tk32)
                    shift(4 + p, PV_MU + 4 + p, v32, tv32)
                    S.op("pe", lambda e, p=p: e.matmul(ps[4][:, 0:n], lhsT=lora[0:96, 0, p * 128:(p + 1) * 128], rhs=twb[0:96, 0:n],
                                                       start=True, stop=True), reads=[tC, ttwb], writes=[tps[4]])
                    act(sg[:, 0:n], ps[4][:, 0:n], AF.Sigmoid, [tps[4], tC], [tsg], bias=pvec[:, PV_W0 + p:PV_W0 + p + 1])
                    S.op("pe", lambda e, p=p: e.matmul(ps[5][:, 0:n], lhsT=lora[0:96, 1, p * 128:(p + 1) * 128], rhs=adb[0:96, 0:n],
                                                       start=True, stop=True), reads=[tC, tadb], writes=[tps[5]])
                    act(aic[:, 0:n], ps[5][:, 0:n], AF.Sigmoid, [tps[5], tC], [taic], bias=pvec[:, PV_A0 + p:PV_A0 + p + 1])
                    chk('pb%d' % ti)
                    ts("dve", t1[:, 0:n], k32[:, 0:n], pvec[:, PV_KK + p:PV_KK + p + 1], None, ALU.mult, None, [tk32, tC], [tt1])
                    tt("pool", t2[:, 0:n], t1[:, 0:n], t1[:, 0:n], ALU.mult, [tt1], [tt2])
                    S.op("pe", lambda e: e.matmul(ps[6][:, 0:n], lhsT=blockones[:], rhs=t2[:, 0:n], start=True, stop=True),
                         reads=[tC, tt2], writes=[tps[6]])
                    ts("dve", t2[:, 0:n], ps[6][:, 0:n], 1e-24, None, ALU.max, None, [tps[6]], [tt2])
                    rsqrt_inplace(t2, tt2, n, 1.0, 0.0)
                    tt("dve", kkn[:, 0:n], t1[:, 0:n], t2[:, 0:n], ALU.mult, [tt1, tt2], [tkkn])
                    chk('pc%d' % ti)
                    ts("dve", t1[:, 0:n], aic[:, 0:n], pvec[:, PV_KA + p:PV_KA + p + 1], pder[:, p:p + 1], ALU.mult, ALU.add,
                       [taic, tC], [tt1])
                    tt("pool", k2[:, 0:n], k32[:, 0:n], t1[:, 0:n], ALU.mult, [tk32, tt1], [tk2])
                    tt("pool", bvec[:, 0:n], kkn[:, 0:n], aic[:, 0:n], ALU.mult, [tkkn, taic], [tbvec])
                    chk('pd%d' % ti)
                    stt(t1[:, 0:n], r32[:, 0:n], pvec[:, PV_RK + p:PV_RK + p + 1], k2[:, 0:n], ALU.mult, ALU.mult, [tr32, tk2, tC], [tt1])
                    S.op("pe", lambda e: e.matmul(ps[4][:, 0:n], lhsT=blockones[:], rhs=t1[:, 0:n], start=True, stop=True),
                         reads=[tC, tt1], writes=[tps[4]])
                    tt("dve", bonus[p][:, 0:n], ps[4][:, 0:n], v32[:, 0:n], ALU.mult, [tps[4], tv32], [tbonus[p]])
                    chk('pe%d' % ti)
                    for c in range(nblk):
                        S.op("dve", lambda e, c=c: e.tensor_tensor_scan(out=cs[:, c * 128:(c + 1) * 128], data0=ones_f[:, 0:128],
                                                                     data1=sg[:, c * 128:(c + 1) * 128], initial=0.0,
                                                                     op0=ALU.mult, op1=ALU.add), reads=[tsg, tC], writes=[tcs])
                    chk('pf%d' % ti)
                    act(t1[:, 0:n], cs[:, 0:n], AF.Exp, [tcs], [tt1], scale=-C0)
                    tt("dve", arT[p][:, 1, 0:n], r32[:, 0:n], t1[:, 0:n], ALU.mult, [tr32, tt1], [tarT[p]])
                    for c in range(nblk):
                        S.op("pool", lambda e, c=c, p=p: e.tensor_copy(out=WC[:, p * 2 + c:p * 2 + c + 1], in_=t1[:, c * 128 + 127:c * 128 + 128]),
                             reads=[tt1], writes=[tWC])
                        ts("dve", nbias[:, c:c + 1], cs[:, c * 128 + 127:c * 128 + 128], -C0, None, ALU.mult, None, [tcs], [tnbias])
                    tt("pool", t2[:, 0:n], cs[:, 0:n], sg[:, 0:n], ALU.subtract, [tcs, tsg], [tt2])
                    act(t2[:, 0:n], t2[:, 0:n], AF.Exp, [tt2], [tt2], scale=-C0)
                    stt(arT[p][:, 0, 0:n], kkn[:, 0:n], -1.0, t2[:, 0:n], ALU.mult, ALU.mult, [tkkn, tt2], [tarT[p]])
                    act(t1[:, 0:n], cs[:, 0:n], AF.Exp, [tcs], [tt1], scale=C0)
                    tt("dve", ktl[p][:, 0:n], k2[:, 0:n], t1[:, 0:n], ALU.mult, [tk2, tt1], [tktl[p]])
                    tt("pool", btl[p][:, 0:n], bvec[:, 0:n], t1[:, 0:n], ALU.mult, [tbvec, tt1], [tbtl[p]])
                    for hh in ([] if _os.environ.get('NOMASK') else range(2)):
                        hm = blockones[:, hh * 64:hh * 64 + 1]
                        ts("dve" if hh == 0 else "pool", btlm[p][hh][:, 0:n], btl[p][:, 0:n], hm, None, ALU.mult, None, [tbtl[p], tC], [tbtlm[p][hh]])
                        ts("pool" if hh == 0 else "dve", ktlm[p][hh][:, 0:n], ktl[p][:, 0:n], hm, None, ALU.mult, None, [tktl[p], tC], [tktlm[p][hh]])
                    for c in range(nblk):
                        S.op("act", lambda e, c=c: e.activation(out=t2[:, c * 128:(c + 1) * 128], in_=cs[:, c * 128:(c + 1) * 128],
                                                                 func=AF.Exp, bias=nbias[:, c:c + 1], scale=C0),
                             reads=[tcs, tnbias], writes=[tt2])
                    chk('pg%d' % ti)
                    tt("dve", fm3[0][:, 0:n], k2[:, 0:n], t2[:, 0:n], ALU.mult, [tk2, tt2], [tfm3[0]])
                    tt("pool", fm3[1][:, 0:n], bvec[:, 0:n], t2[:, 0:n], ALU.mult, [tbvec, tt2], [tfm3[1]])
                    S.op("act", lambda e: e.activation(out=fm3[2][:, 0:n], in_=v32[:, 0:n], func=AF.Copy), reads=[tv32], writes=[tfm3[2]])
                    chk('ph%d' % ti)
                    for c in range(nblk):
                        for q in range(3):
                            S.op("pe", lambda e, c=c, q=q: e.matmul(ps[4 + q][:, 0:128], lhsT=fm3[q][:, c * 128:(c + 1) * 128], rhs=ident[:],
                                                                    start=True, stop=True), reads=[tfm3[q], tC], writes=[tps[4 + q]])
                        chk('pt%d' % ti)
                        for q in range(3):
                            evac_copy(tok3[q][:, c, p * 128:(p + 1) * 128], ps[4 + q][:, 0:128], [tps[4 + q]], [ttok3[q]])
                        for hh in ([] if _os.environ.get('NOVZ') else range(2)):
                            S.op('act', lambda e: e.activation(out=Vz[:, c, 2 * p + hh, hh * 64:(hh + 1) * 64], in_=ps[6][:, hh * 64:(hh + 1) * 64], func=AF.Copy), reads=[tps[6]], writes=[tVz])

                    chk('pi%d' % ti)
                chk('rwpre%d' % ti)
                Khat, Bhat, Vtok = tok3
                tKhat, tBhat, tVtok = ttok3
                for c in range(nblk):
                    cs_ = slice(c * 128, (c + 1) * 128)
                    for h in range(4):
                        p, hh = h // 2, h % 2
                        i = h % 2
                        S.op("pe", lambda e: e.matmul(ps[i][:, 0:256], lhsT=btlm[p][hh][:, cs_], rhs=arT[p][:, :, cs_], start=True, stop=True),
                             reads=[tbtlm[p][hh], tarT[p]], writes=[tps[i]])
                        S.op("pe", lambda e: e.matmul(ps[i][:, 256:512], lhsT=ktlm[p][hh][:, cs_], rhs=arT[p][:, :, cs_], start=True, stop=True),
                             reads=[tktlm[p][hh], tarT[p]], writes=[tps[i]])
                        chk('c0%d' % ti)
                        tt("dve", AT[h].rearrange("p a n -> p (a n)"), ps[i][:, :], maskT[:], ALU.mult, [tps[i], tC], [tAT[h]])
                        chk('c1%d' % ti)
                        S.op("pe", lambda e: e.matmul(ps[2][:, h * 128:(h + 1) * 128], lhsT=arT[p][:, 0, cs_], rhs=btlm[p][hh][:, cs_], start=True, stop=True),
                             reads=[tbtlm[p][hh], tarT[p]], writes=[tps[2]])
                    chk('ca%d' % ti)
                    for h in range(4):
                        tt("dve", Mc[:, h, :], ps[2][:, h * 128:(h + 1) * 128], masklow[:], ALU.mult, [tps[2], tC], [tMc])
                        S.op("pool", lambda e: e.tensor_copy(out=Mtc[:, h, :], in_=AT[h][:, 0, :]), reads=[tAT[h]], writes=[tMtc])
                        tt("pool", Ptc[:, h, :], AT[h][:, 0, :], ident[:], ALU.add, [tAT[h], tC], [tPtc])
                    chk('cb%d' % ti)
                    for lvl in range(7):
                        if lvl >= 1:
                            for h in range(4):
                                S.op("pe", lambda e: e.matmul(ps[3][:, h * 128:(h + 1) * 128], lhsT=Mc[:, h, :], rhs=Ptc[:, h, :], start=True, stop=True),
                                     reads=[tMc, tPtc], writes=[tps[3]])
                        if lvl < 6:
                            for h in range(4):
                                S.op("pe", lambda e: e.matmul(ps[4][:, h * 128:(h + 1) * 128], lhsT=Mtc[:, h, :], rhs=Mc[:, h, :], start=True, stop=True),
                                     reads=[tMc, tMtc], writes=[tps[4]])
                            if lvl < 5:
                                for h in range(4):
                                    S.op("pe", lambda e: e.matmul(ps[5][:, h * 128:(h + 1) * 128], lhsT=Mc[:, h, :], rhs=Mtc[:, h, :], start=True, stop=True),
                                         reads=[tMc, tMtc], writes=[tps[5]])
                        if lvl >= 1:
                            tt("dve", Ptc.rearrange("p h n -> p (h n)"), ps[3][:, :], Ptc.rearrange("p h n -> p (h n)"), ALU.add, [tps[3], tPtc], [tPtc])
                        if lvl < 6:
                            S.op("act", lambda e: e.activation(out=Mc.rearrange("p h n -> p (h n)"), in_=ps[4][:, :], func=AF.Copy), reads=[tps[4]], writes=[tMc])
                            if lvl < 5:
                                S.op("act", lambda e: e.activation(out=Mtc.rearrange("p h n -> p (h n)"), in_=ps[5][:, :], func=AF.Copy), reads=[tps[5]], writes=[tMtc])
                    chk('cc%d' % ti)
                    for p in range(2):
                        S.op("pe", lambda e: e.matmul(ps[0][:, p * 128:(p + 1) * 128], lhsT=arT[p][:, 0, cs_], rhs=Sbf[:, p, :], start=True, stop=False),
                             reads=[tarT[p], tSbf], writes=[tps[0]])
                        for hh in range(2):
                            h = 2 * p + hh
                            S.op("pe", lambda e: e.matmul(ps[0][:, h * 64:(h + 1) * 64], lhsT=AT[h][:, 2, :], rhs=Vtok[:, c, h * 64:(h + 1) * 64],
                                                          start=False, stop=(hh == 1)), reads=[tAT[h], tVtok], writes=[tps[0]])
                    S.op("act", lambda e: e.activation(out=Xsb.rearrange("p h n -> p (h n)"), in_=ps[0][:, 0:256], func=AF.Copy), reads=[tps[0]], writes=[tXsb])
                    for h in range(4):
                        S.op("pe", lambda e: e.matmul(ps[1][:, h * 64:(h + 1) * 64], lhsT=Ptc[:, h, :], rhs=Xsb[:, h, :], start=True, stop=True),
                             reads=[tPtc, tXsb], writes=[tps[1]])
                    S.op("act", lambda e: e.activation(out=Usb.rearrange("p h n -> p (h n)"), in_=ps[1][:, 0:256], func=AF.Copy), reads=[tps[1]], writes=[tUsb])
                    for hh in range(2):
                        src = ps[1][:, 0:256].rearrange("p (a b n) -> p a b n", a=2, b=2)[:, :, hh, :]
                        dst = Uz.rearrange("p (a b) n -> p a b n", b=2)[:, :, hh, hh * 64:(hh + 1) * 64]
                        S.op("act", lambda e: e.activation(out=dst, in_=src, func=AF.Copy), reads=[tps[1]], writes=[tUz])
                    chk('cd%d' % ti)
                    for p in range(2):
                        S.op("pe", lambda e: e.matmul(ps[2 + p][:, 0:128], lhsT=Sbf[:, p, :], rhs=arT[p][:, 1, cs_], start=True, stop=False),
                             reads=[tSbf, tarT[p]], writes=[tps[2 + p]])
                        for hh in range(2):
                            h = 2 * p + hh
                            S.op("pe", lambda e: e.matmul(ps[2 + p][:, 0:128], lhsT=Uz[:, h, :], rhs=AT[h][:, 1, :], start=False, stop=False),
                                 reads=[tUz, tAT[h]], writes=[tps[2 + p]])
                            S.op("pe", lambda e: e.matmul(ps[2 + p][:, 0:128], lhsT=Vz[:, c, h, :], rhs=AT[h][:, 3, :], start=False, stop=(hh == 1)),
                                 reads=[tVz, tAT[h]], writes=[tps[2 + p]])
                        if ti >= 0:
                            evac_copy(yT[p][:, cs_], ps[2 + p][:, 0:128], [tps[2 + p]], [tyT[p]])
                    for p in range(2):
                        S.op("pe", lambda e: e.matmul(ps[4 + p][:, 0:128], lhsT=Bhat[:, c, p * 128:(p + 1) * 128],
                                                      rhs=Usb.rearrange("p h n -> p (h n)")[:, p * 128:(p + 1) * 128], start=True, stop=False),
                             reads=[tBhat, tUsb], writes=[tps[4 + p]])
                        S.op("pe", lambda e: e.matmul(ps[4 + p][:, 0:128], lhsT=Khat[:, c, p * 128:(p + 1) * 128], rhs=Vtok[:, c, p * 128:(p + 1) * 128],
                                                      start=False, stop=True), reads=[tKhat, tVtok], writes=[tps[4 + p]])
                        tt("dve", t3[:, 0:128], ps[4 + p][:, 0:128], blockones[:], ALU.mult, [tps[4 + p], tC], [tt3])
                        stt(S32[:, p, :], S32[:, p, :], WC[:, p * 2 + c:p * 2 + c + 1], t3[:, 0:128], ALU.mult, ALU.add, [tS32, tWC, tt3], [tS32])
                    S.op("act", lambda e: e.activation(out=Sbf.rearrange("p a n -> p (a n)"), in_=S32.rearrange("p a n -> p (a n)"), func=AF.Copy),
                         reads=[tS32], writes=[tSbf])

                chk('chain%d' % ti)
                if ti < 0:
                    continue

                ob = rr["oT"] % 2
                rr["oT"] += 1
                oTt = oT[ob]
                toTt = toT[ob]

                for p in range(2):
                    y = yT[p]
                    S.op("pe", lambda e, y=y: e.matmul(ps[0][:, 0:n], lhsT=blockones[:], rhs=y[:, 0:n], start=True, stop=True),
                         reads=[tC, tyT[p]], writes=[tps[0]])
                    stt(t1[:, 0:n], ps[0][:, 0:n], -1.0 / 64.0, y[:, 0:n], ALU.mult, ALU.add, [tps[0], tyT[p]], [tt1])
                    tt("pool", t2[:, 0:n], t1[:, 0:n], t1[:, 0:n], ALU.mult, [tt1], [tt2])
                    S.op("pe", lambda e: e.matmul(ps[1][:, 0:n], lhsT=blockones[:], rhs=t2[:, 0:n], start=True, stop=True),
                         reads=[tC, tt2], writes=[tps[1]])
                    act(t2[:, 0:n], ps[1][:, 0:n], AF.Ln, [tps[1], tC], [tt2], bias=eps_ap(64e-5), scale=1.0 / 64.0)
                    act(t2[:, 0:n], t2[:, 0:n], AF.Exp, [tt2], [tt2], scale=-0.5)
                    tt("dve", t1[:, 0:n], t1[:, 0:n], t2[:, 0:n], ALU.mult, [tt1, tt2], [tt1])
                    ts("dve", t1[:, 0:n], t1[:, 0:n], pvec[:, PV_GNG + p:PV_GNG + p + 1], pvec[:, PV_GNB + p:PV_GNB + p + 1],
                       ALU.mult, ALU.add, [tt1, tC], [tt1])
                    tt("pool", t1[:, 0:n], t1[:, 0:n], bonus[p][:, 0:n], ALU.add, [tt1, tbonus[p]], [tt1])
                    tt("dve", oTt[:, 2 + p, 0:n], t1[:, 0:n], gate_r[:, p, 0:n], ALU.mult, [tt1, tgr], [toTt])

                chk('rwpost%d' % ti)
                qb0 = blk0
                OO = ps[4][:, :].rearrange("p (m n) -> p m n", m=2)
                SS = ps[5][:, :].rearrange("p (m n) -> p m n", m=2)
                for h in range(2):
                    kbs = list(range(0, qb0 + 2))

                    def geom(kb):
                        delta = kb - qb0
                        q_lo = 128 if delta == 1 else 0
                        return delta, q_lo, n - q_lo, kb % 2

                    def stage_qk(kb):
                        delta, q_lo, nq, par = geom(kb)
                        for m in range(2):
                            rs = slice(m * 64, m * 64 + 64)
                            pi = 2 * par + m
                            S.op("pe", lambda e: e.matmul(ps[pi][:, 0:nq], lhsT=KT[rs, h, kb * 128:(kb + 1) * 128], rhs=QT[rs, h, q_lo:n],
                                                          start=True, stop=True), reads=[tKT, tQT], writes=[tps[pi]])

                    def stage_rest(kb):
                        delta, q_lo, nq, par = geom(kb)
                        near = delta >= -1
                        E = Eb[par]
                        tE = tEb[par]
                        for m in range(2):
                            pi = 2 * par + m
                            if near:
                                boff = 0 if delta >= 0 else 128
                                stt(Ssb[m][:, 0:nq], ps[pi][:, 0:nq], 0.125, bext[h][:, boff:boff + nq], ALU.mult, ALU.add,
                                    [tps[pi], tC], [tSsb[m]])
                                if kb == 0:
                                    S.op("pool", lambda e: e.memset(Ssb[m][0:112, 0:nq], NEG), reads=[], writes=[tSsb[m]])
                                act(E[:, m, 0:nq], Ssb[m][:, 0:nq], AF.Exp, [tSsb[m]], [tE])
                            else:
                                bc = bcol[:, 2 + h:3 + h] if kb == 0 else bcol[:, h:h + 1]
                                act(E[:, m, 0:nq], ps[pi][:, 0:nq], AF.Exp, [tps[pi], tC], [tE], bias=bc, scale=0.125)
                        first = (kb == kbs[0])
                        last = (kb == kbs[-1])
                        if q_lo == 0:
                            Ef = E.rearrange("p m n -> p (m n)")
                            S.op("pe", lambda e: e.matmul(ps[4][:, :], lhsT=Vt[:, kb, h * 128:(h + 1) * 128], rhs=Ef, start=first, stop=last),
                                 reads=[tV, tE], writes=[tps[4]])
                            S.op("pe", lambda e: e.matmul(ps[5][:, :], lhsT=ones_bf[:], rhs=Ef, start=first, stop=last),
                                 reads=[tC, tE], writes=[tps[5]])
                        else:
                            for m in range(2):
                                S.op("pe", lambda e: e.matmul(OO[:, m, q_lo:n], lhsT=Vt[:, kb, h * 128:(h + 1) * 128], rhs=E[:, m, 0:nq],
                                                              start=first, stop=(last and m == 1)), reads=[tV, tE], writes=[tps[4]])
                                S.op("pe", lambda e: e.matmul(SS[:, m, q_lo:n], lhsT=ones_bf[:], rhs=E[:, m, 0:nq],
                                                              start=first, stop=(last and m == 1)), reads=[tC, tE], writes=[tps[5]])

                    stage_qk(kbs[0])
                    for i_, kb in enumerate(kbs):
                        if i_ + 1 < len(kbs):
                            stage_qk(kbs[i_ + 1])
                        stage_rest(kb)
                    S.op("dve", lambda e: e.reciprocal(out=t1[:, 0:n], in_=SS[:, 0, 0:n]), reads=[tps[5]], writes=[tt1])
                    S.op("dve", lambda e: e.reciprocal(out=t2[:, 0:n], in_=SS[:, 1, 0:n]), reads=[tps[5]], writes=[tt2])
                    tt("dve", t1[:, 0:n], OO[:, 0, 0:n], t1[:, 0:n], ALU.mult, [tps[4], tt1], [tt1])
                    tt("dve", t2[:, 0:n], OO[:, 1, 0:n], t2[:, 0:n], ALU.mult, [tps[4], tt2], [tt2])
                    stt(t1[:, 0:n], t2[:, 0:n], pder[:, 3:4], t1[:, 0:n], ALU.mult, ALU.add, [tt1, tt2, tC], [tt1])
                    tt("pool", t2[:, 0:n], t1[:, 0:n], t1[:, 0:n], ALU.mult, [tt1], [tt2])
                    S.op("pe", lambda e: e.matmul(ps[6][:, 0:n], lhsT=ones_f[:], rhs=t2[:, 0:n], start=True, stop=True),
                         reads=[tC, tt2], writes=[tps[6]])
                    act(t2[:, 0:n], ps[6][:, 0:n], AF.Ln, [tps[6], tC], [tt2], bias=eps_ap(1e-5), scale=1.0 / 128.0)
                    act(t2[:, 0:n], t2[:, 0:n], AF.Exp, [tt2], [tt2], scale=-0.5)
                    tt("dve", t1[:, 0:n], t1[:, 0:n], t2[:, 0:n], ALU.mult, [tt1, tt2], [tt1])
                    stt(oTt[:, h, 0:n], t1[:, 0:n], pder[:, 2:3], gate_a[:, h, 0:n], ALU.mult, ALU.mult, [tt1, tga, tC], [toTt])

                chk('attn%d' % ti)
                for cc in range(4):
                    S.dma(lambda e: e.dma_start(out=agin[cc].ap()[:, ti * 256:ti * 256 + 256], in_=oTt[:, cc, 0:256]), reads=[toTt], writes=[tagin[cc]],
                          key="oT%d" % ob)
                if DEBUG:
                    dd = dbg_d.rearrange("(c p) n -> p c n", p=128)[:, :, ti * 256:ti * 256 + 256]
                    S.dma(lambda e, dd=dd, oTt=oTt: e.dma_start(out=dd, in_=oTt[:, :, 0:256]), reads=[toTt], key="oT%d" % ob)

            chk('main')
            for cc in range(4):
                S.dma(lambda e: e.collective_compute("AllGather", ALU.bypass, replica_groups=[[0, 1, 2, 3], [4, 5, 6, 7]],
                                                     ins=[agin[cc].ap()], outs=[agout[cc].ap()]),
                      reads=[tagin[cc]], writes=[tagout[cc]], key="cc%d" % cc, queue="pool", inc=1)
            chk('ag')
            S.barrier()
            apos[0] = 0
            oq = carve(16 * 1024 * 2, BF16, "p (c n) -> p c n", c=16)
            toq = S.tile("oq")
            gbt = carve(4 * D * 4, F32, "p (a n) -> p a n", a=4)
            tgb = S.tile("gbt")
            rsd = [carve(D * 4, F32) for _ in range(2)]
            trsd = [S.tile("rsd%d" % i) for i in range(2)]
            fst = carve(4 * 6 * 4, F32, "p (a n) -> p a n", a=4)
            fmv = carve(8 * 4, F32)
            tfst = S.tile("fst")
            assert apos[0] <= arena_w

            for kc in range(16):
                b = kc % 2
                S.dma(lambda e, kc=kc, b=b: e.dma_start(out=xb[b][:, 0:D], in_=wout_d[kc * 128:(kc + 1) * 128, :]),
                      writes=[txb[b]], key="xb%d" % b)
                eng = ("dve", "pool", "act")[kc % 3]
                if eng == "act":
                    S.op("act", lambda e, kc=kc, b=b: e.activation(out=W[:, kc, 0:D], in_=xb[b][:, 0:D], func=AF.Copy),
                         reads=[txb[b]], writes=[tW])
                else:
                    S.op(eng, lambda e, kc=kc, b=b: e.tensor_copy(out=W[:, kc, 0:D], in_=xb[b][:, 0:D]), reads=[txb[b]], writes=[tW])
            gsrc = bass.AP(tensor=gb_d.tensor, offset=0, ap=[[0, 128], [D, 4], [1, D]])
            S.dma(lambda e: e.dma_start(out=gbt, in_=gsrc), writes=[tgb], key="gbt")

            idxt = sbt("idxt", [128, 4], mybir.dt.int32)
            tidx = S.tile("idxt")
            S.dma(lambda e: e.dma_start(out=idxt[:], in_=idx_d), writes=[tidx], key="idxt")
            for kc in range(16):
                r_, c_ = kc // 4, kc % 4
                agv = agout[c_].ap().rearrange("e (q n) -> (e q) n", q=4)
                S.dma(lambda e: e.indirect_dma_start(out=oq[:, kc, :], out_offset=None, in_=agv,
                                                     in_offset=bass.IndirectOffsetOnAxis(ap=idxt[:, r_:r_ + 1], axis=0)),
                      reads=[tagout[c_], tidx], writes=[toq], key="oq", queue="pool")
            alpha = 2.0 ** 0.25
            for blk in range(8):
                b = blk % 2
                xt = xb[b]
                r = rsd[b]
                tr = trsd[b]
                S.dma(lambda e, blk=blk, xt=xt: e.dma_start(out=xt[:, 0:D], in_=xq_d[blk * 128:(blk + 1) * 128, :]), writes=[txb[b]], key="xb%d" % b)
                for q in range(4):
                    S.op("dve", lambda e, q=q, xt=xt: e.bn_stats(out=fst[:, q, :], in_=xt[:, q * 512:(q + 1) * 512]), reads=[txb[b]], writes=[tfst])
                S.op("dve", lambda e: e.bn_aggr(out=fmv[:, 0:2], in_=fst.rearrange("p a n -> p (a n)")), reads=[tfst], writes=[tfst])
                act(fmv[:, 2:3], fmv[:, 1:2], AF.Ln, [tfst], [tfst], bias=eps_ap(1e-5))
                act(fmv[:, 2:3], fmv[:, 2:3], AF.Exp, [tfst], [tfst], scale=-0.5)
                stt(fmv[:, 3:4], fmv[:, 0:1], -1.0, fmv[:, 2:3], ALU.mult, ALU.mult, [tfst], [tfst])
                S.op("act", lambda e, xt=xt, r=r: e.activation(out=r, in_=xt[:, 0:D], func=AF.Identity, bias=fmv[:, 3:4], scale=fmv[:, 2:3]),
                     reads=[txb[b], tfst], writes=[tr])
                tt("pool", r, r, gbt[:, 0, :], ALU.mult, [tr, tgb], [tr])
                tt("pool", r, r, gbt[:, 1, :], ALU.add, [tr, tgb], [tr])
                for ct in range(4):
                    for kc in range(16):
                        S.op("pe", lambda e, kc=kc, ct=ct, blk=blk: e.matmul(ps[ct][:, :], lhsT=oq[:, kc, blk * 128:(blk + 1) * 128],
                                                                           rhs=W[:, kc, ct * 512:(ct + 1) * 512], start=(kc == 0), stop=(kc == 15)),
                             reads=[toq, tW], writes=[tps[ct]])
                    stt(r[:, ct * 512:(ct + 1) * 512], r[:, ct * 512:(ct + 1) * 512], alpha, ps[ct][:, :], ALU.mult, ALU.add,
                        [tr, tps[ct]], [tr])
                for q in range(4):
                    S.op("dve", lambda e, q=q, r=r: e.bn_stats(out=fst[:, q, :], in_=r[:, q * 512:(q + 1) * 512]), reads=[tr], writes=[tfst])
                S.op("dve", lambda e: e.bn_aggr(out=fmv[:, 4:6], in_=fst.rearrange("p a n -> p (a n)")), reads=[tfst], writes=[tfst])
                act(fmv[:, 6:7], fmv[:, 5:6], AF.Ln, [tfst], [tfst], bias=eps_ap(1e-5))
                act(fmv[:, 6:7], fmv[:, 6:7], AF.Exp, [tfst], [tfst], scale=-0.5)
                stt(fmv[:, 7:8], fmv[:, 4:5], -1.0, fmv[:, 6:7], ALU.mult, ALU.mult, [tfst], [tfst])
                S.op("act", lambda e, r=r: e.activation(out=r, in_=r, func=AF.Identity, bias=fmv[:, 7:8], scale=fmv[:, 6:7]),
                     reads=[tr, tfst], writes=[tr])
                tt("dve", r, r, gbt[:, 2, :], ALU.mult, [tr, tgb], [tr])
                tt("pool", r, r, gbt[:, 3, :], ALU.add, [tr, tgb], [tr])
                S.dma(lambda e, blk=blk, r=r: e.dma_start(out=out_d[blk * 128:(blk + 1) * 128, :], in_=r), reads=[tr], key="rsd%d" % b)

        except _Stop:
            pass
        S.emit()
    return nc


_NC_CACHE = {}


def kernel(**inp):
    maps = _prep_inputs(inp)
    x = np.asarray(inp["x"], np.float32)
    for c in range(8):
        b, r = c // 4, c % 4
        maps[c]["xq"] = np.ascontiguousarray(x[b, r * 1024:(r + 1) * 1024])
        p = np.arange(128)[:, None]
        kc = np.arange(4)[None, :]
        maps[c]["idx"] = ((kc * 128 + p) * 4 + r).astype(np.int32)
    if "nc" not in _NC_CACHE:
        _NC_CACHE["nc"] = build_nc()
    nc = _NC_CACHE["nc"]
    res = run_bass_kernel_spmd(nc, maps, core_ids=list(range(8)))
    out = np.zeros((2, SEQ, D), np.float32)
    for c in range(8):
        b, r = c // 4, c % 4
        out[b, r * 1024:(r + 1) * 1024] = res.results[c]["out"]
    if DEBUG:
        kernel.dbg = [res.results[c]["dbg"] for c in range(8)]
    return out
```

```python
import math
from contextlib import ExitStack

import numpy as np
import ml_dtypes

import concourse.bass as bass
import concourse.mybir as mybir
from concourse.bass_utils import run_bass_kernel_spmd

F32 = mybir.dt.float32
BF16 = mybir.dt.bfloat16
AF = mybir.ActivationFunctionType
ALU = mybir.AluOpType

D = 2048
SEQ = 4096
NB = 33
P_TOK = NB * 128
NCOL = 2240
C0 = math.exp(-0.5)
NEG = -1.0e4
TABW = 511
DEBUG = False
STOP = None


class _Stop(Exception):
    pass


_HITS = {}


def chk(tag):
    import os
    if STOP == tag:
        _HITS[tag] = _HITS.get(tag, 0) + 1
        if _HITS[tag] >= int(os.environ.get("NTH", "1")):
            raise _Stop()


class T:
    __slots__ = ("name", "w", "r")

    def __init__(self, name, init_r=None):
        self.name = name
        self.w = None
        self.r = dict(init_r) if init_r else {}


class _Rec:
    def __init__(self):
        self.call = None

    def __getattr__(self, name):
        def f(*a, **k):
            assert self.call is None
            self.call = (name, a, k)
            return self
        return f


def _freeze(fn):
    r = _Rec()
    fn(r)
    name, a, k = r.call
    return lambda e: getattr(e, name)(*a, **k)


class Sched:
    ENG = ("pe", "act", "dve", "pool", "sp")

    def __init__(self, nc, es):
        self.nc = nc
        self.es = es
        self.ops = {e: [] for e in self.ENG}
        self.seen = {e: {} for e in self.ENG}
        self.dsems = {}
        self.esem = {}
        self.bar = {}

    def tile(self, name):
        return T(name, self.bar)

    def barrier(self):
        b = {}
        for e in self.ENG:
            if e != "sp" and self.ops[e]:
                for i in range(len(self.ops[e]) - 1, -1, -1):
                    if self.ops[e][i]["dma"] is None:
                        b[('E', e)] = ('E', e, i)
                        break
        for k, v in self.dsems.items():
            if v[1] > 0:
                b[('D', k)] = ('D', k, v[1])
        self.bar = b

    @staticmethod
    def _dep(waits, ev):
        key = ev[:2]
        if waits.get(key, -1) < ev[2]:
            waits[key] = ev[2]

    def _collect(self, eng, reads, writes):
        waits = {}
        for t in reads:
            if t.w is not None:
                self._dep(waits, t.w)
        for t in writes:
            if t.w is not None:
                self._dep(waits, t.w)
            for ev in t.r.values():
                self._dep(waits, ev)
        wl = []
        for key, val in waits.items():
            if key[0] == 'E' and key[1] == eng and eng == 'pe':
                continue
            if self.seen[eng].get(key, -1) >= val:
                continue
            self.seen[eng][key] = val
            wl.append((key, val))
        return wl

    def op(self, eng, fn, reads=(), writes=()):
        fn = _freeze(fn)
        wl = self._collect(eng, reads, writes)
        idx = len(self.ops[eng])
        ev = ('E', eng, idx)
        self.ops[eng].append(dict(waits=wl, fn=fn, sig=False, dma=None))
        for t in reads:
            t.r[('E', eng)] = ev
        for t in writes:
            t.w = ev
            t.r = {}
        return ev

    def dsem(self, key):
        if key not in self.dsems:
            h = self.es.enter_context(self.nc.semaphore("d_" + key))
            self.dsems[key] = [h, 0]
        return self.dsems[key]

    def dma(self, fn, reads=(), writes=(), key=None, queue="sp", inc=16):
        fn = _freeze(fn)
        wl = self._collect(queue, reads, writes)
        ds = self.dsem(key)
        ds[1] += inc
        ev = ('D', key, ds[1])
        self.ops[queue].append(dict(waits=wl, fn=fn, sig=False, dma=(key, inc)))
        for t in reads:
            t.r[('D', key)] = ev
        for t in writes:
            t.w = ev
            t.r = {}
        return ev

    def emit(self):
        nc = self.nc
        for e in self.ENG:
            for o in self.ops[e]:
                for key, val in o["waits"]:
                    if key[0] == 'E':
                        self.ops[key[1]][val]["sig"] = True
        cnt = {}
        for e in self.ENG:
            c = 0
            lst = []
            for o in self.ops[e]:
                if o["sig"]:
                    c += 1
                lst.append(c)
            cnt[e] = lst
            self.esem[e] = self.es.enter_context(nc.semaphore("e_" + e))
        finals = [(('D', k), v[1]) for k, v in self.dsems.items() if v[1] > 0]

        def resolve(key, val):
            if key[0] == 'E':
                return self.esem[key[1]], cnt[key[1]][val]
            return self.dsems[key[1]][0], val

        def body(e):
            def run(eng):
                for o in self.ops[e]:
                    for key, val in o["waits"]:
                        s, v = resolve(key, val)
                        eng.wait_ge(s, v)
                    ins = o["fn"](eng)
                    if o["dma"] is not None:
                        ins.then_inc(self.dsems[o["dma"][0]][0], o["dma"][1])
                    elif o["sig"]:
                        ins.then_inc(self.esem[e], 1)
                if e == "sp":
                    for key, val in finals:
                        s, v = resolve(key, val)
                        eng.wait_ge(s, v)
            return run

        with nc.Block() as block:
            block.tensor(body("pe"))
            block.scalar(body("act"))
            block.vector(body("dve"))
            block.gpsimd(body("pool"))
            block.sync(body("sp"))


def _bucket(n):
    n = np.maximum(n, 0)
    nf = np.maximum(n, 1).astype(np.float32)
    large = 16 + (np.log(nf / np.float32(16)) / np.float32(math.log(128 / 16)) * np.float32(16)).astype(np.int32)
    large = np.minimum(large, 31)
    return np.where(n < 16, n, large)


PV_MU = 0
PV_MUWD = 6
PV_MUAD = 7
PV_W0 = 8
PV_A0 = 10
PV_KK = 12
PV_KA = 14
PV_RK = 16
PV_GNG = 18
PV_GNB = 20
PV_SUBLN = 22
PV_LNG = 23
PV_LNB = 39
PV_N = 55


def _consts():
    c = {}
    c["ident"] = np.eye(128, dtype=np.float32).astype(ml_dtypes.bfloat16)
    c["ones_bf"] = np.ones((128, 128), dtype=ml_dtypes.bfloat16)
    c["ones_f"] = np.ones((128, 128), dtype=np.float32)
    bo = np.zeros((128, 128), np.float32)
    bo[:64, :64] = 1.0
    bo[64:, 64:] = 1.0
    c["blockones"] = bo
    s = np.arange(128)[:, None]
    t = np.arange(128)[None, :]
    strict = (s < t).astype(np.float32)
    incl = (s <= t).astype(np.float32)
    m = np.zeros((128, 2, 2, 128), np.float32)
    m[:, :, 0, :] = strict[:, None, :]
    m[:, :, 1, :] = incl[:, None, :]
    c["maskT"] = m.reshape(128, 512)
    c["masklow"] = (s > t).astype(np.float32)
    d = np.arange(TABW) - 127
    oh = np.zeros((33, TABW), np.float32)
    b = _bucket(d)
    for i in range(TABW):
        if d[i] >= 0:
            oh[b[i], i] = 1.0
        else:
            oh[32, i] = NEG
    c["oh"] = oh
    return c


def _prep_inputs(inp):
    f = np.float32
    w_in = np.asarray(inp["w_in"][0], f)
    w_out = np.asarray(inp["w_out"][0], f)
    x = np.asarray(inp["x"], f)
    consts = _consts()
    lam4 = np.stack([inp["lambda_q1"][0], inp["lambda_k1"][0], inp["lambda_q2"][0], inp["lambda_k2"][0]]).astype(f)
    rows = []
    for r in range(4):
        rows += list(range((2 * r) * 128, (2 * r + 2) * 128))
        rows += list(range(1024 + (4 * r) * 64, 1024 + (4 * r + 4) * 64))
    w_out_p = np.ascontiguousarray(w_out[rows])
    gb = np.stack([inp["ln_emb_g"], inp["ln_emb_b"], inp["ln_post_g"][0], inp["ln_post_b"][0]]).astype(f)
    maps = []
    for c in range(8):
        b, hg = c // 4, c % 4
        h0 = 2 * hg
        cols = []
        cols += list(range(h0 * 128, (h0 + 2) * 128))
        cols += list(range(1024 + h0 * 128, 1024 + (h0 + 2) * 128))
        cols += list(range(3072 + h0 * 128, 3072 + (h0 + 2) * 128))
        rb = 4096 + hg * 256
        cols += list(range(rb, rb + 256))
        cols += list(range(rb + 1024, rb + 1024 + 256))
        cols += list(range(rb + 2048, rb + 2048 + 256))
        cols += list(range(7360 + hg * 256, 7360 + hg * 256 + 256))
        cols += list(range(4096 + 3072, 4096 + 3072 + 192))
        cols += list(range(2048 + h0 * 128, 2048 + (h0 + 2) * 128))
        assert len(cols) == NCOL
        wi = np.ascontiguousarray(w_in[:, cols])
        pv = np.zeros((128, PV_N), f)
        mu = inp["rw_mu"][0]
        rs = slice(hg * 256, hg * 256 + 256)
        for p in range(2):
            ps_ = slice(hg * 256 + p * 128, hg * 256 + (p + 1) * 128)
            pv[:, PV_MU + 0 + p] = mu[0:1024][ps_]
            pv[:, PV_MU + 2 + p] = mu[1024:2048][ps_]
            pv[:, PV_MU + 4 + p] = mu[2048:3072][ps_]
            pv[:, PV_W0 + p] = inp["rw_w0"][0][ps_]
            pv[:, PV_A0 + p] = inp["rw_a0"][0][ps_]
            pv[:, PV_KK + p] = inp["rw_k_k"][0][ps_]
            pv[:, PV_KA + p] = inp["rw_k_a"][0][ps_]
            pv[:, PV_RK + p] = inp["rw_r_k"][0].reshape(-1)[ps_]
            pv[:, PV_GNG + p] = inp["rw_gn_g"][0][ps_]
            pv[:, PV_GNB + p] = inp["rw_gn_b"][0][ps_]
        pv[:96, PV_MUWD] = mu[3072:3168]
        pv[:96, PV_MUAD] = mu[3168:3264]
        pv[:, PV_SUBLN] = inp["subln_g"][0]
        pv[:, PV_LNG:PV_LNG + 16] = np.asarray(inp["ln_emb_g"], f).reshape(16, 128).T
        pv[:, PV_LNB:PV_LNB + 16] = np.asarray(inp["ln_emb_b"], f).reshape(16, 128).T
        relb = np.ones((33, 2, 128), f)
        for j in range(2):
            relb[:32, j, :] = np.asarray(inp["rel_bias"], f)[:, h0 + j][:, None]
        lora = np.zeros((96, 2, 256), f)
        lora[:, 0, :] = inp["rw_w_up"][0][:, rs]
        lora[:, 1, :] = inp["rw_a_up"][0][:, rs]
        m = {
            "x": np.ascontiguousarray(x[b]),
            "meta": np.asarray(inp["meta_tokens"], f),
            "w_in": wi,
            "w_out": w_out_p,
            "pvec": pv,
            "relb": relb,
            "lora": lora,
            "lam4": lam4,
            "gb": gb,
        }
        m.update(consts)
        maps.append(m)
    return maps


def build_nc():
    nc = bass.Bass("TRN2", target_bir_lowering=False)

    def din(name, shape, dt=F32):
        return nc.dram_tensor(name, list(shape), dt, kind="ExternalInput").ap()

    x_d = din("x", [SEQ, D])
    meta_d = din("meta", [16, D])
    win_d = din("w_in", [D, NCOL])
    wout_d = din("w_out", [D, D])
    pvec_d = din("pvec", [128, PV_N])
    relb_d = din("relb", [33, 2, 128])
    lora_d = din("lora", [96, 2, 256])
    lam4_d = din("lam4", [4, 64])
    gb_d = din("gb", [4, D])
    ident_d = din("ident", [128, 128], BF16)
    onesbf_d = din("ones_bf", [128, 128], BF16)
    onesf_d = din("ones_f", [128, 128])
    bo_d = din("blockones", [128, 128])
    maskT_d = din("maskT", [128, 512])
    masklow_d = din("masklow", [128, 128])
    oh_d = din("oh", [33, TABW])
    xq_d = din("xq", [1024, D])
    idx_d = din("idx", [128, 4], mybir.dt.int32)
    out_d = nc.dram_tensor("out", [1024, D], F32, kind="ExternalOutput").ap()
    if DEBUG:
        dbg_d = nc.dram_tensor("dbg", [512, SEQ], BF16, kind="ExternalOutput").ap()
    agin = [nc.dram_tensor("agin%d" % c, [128, SEQ], BF16) for c in range(4)]
    agout = [nc.dram_tensor("agout%d" % c, [512, SEQ], BF16) for c in range(4)]
    tab_d = [nc.dram_tensor("tab%d" % j, [128, TABW], F32) for j in range(2)]

    with ExitStack() as es:
        S = Sched(nc, es)

        def sbt(name, shape, dt=F32):
            return es.enter_context(nc.sbuf_tensor("s_" + name, list(shape), dt))

        W = sbt("W", [128, 16, NCOL], BF16)
        tW = S.tile("W")
        xb = [sbt("xb%d" % i, [128, NCOL]) for i in range(2)]
        txb = [S.tile("xb%d" % i) for i in range(2)]
        pvec = sbt("pvec", [128, PV_N])
        pder = sbt("pder", [128, 8])
        ident = sbt("ident", [128, 128], BF16)
        ones_bf = sbt("ones_bf", [128, 128], BF16)
        ones_f = sbt("ones_f", [128, 128])
        blockones = sbt("blockones", [128, 128])
        maskT = sbt("maskT", [128, 512])
        masklow = sbt("masklow", [128, 128])
        lora32 = sbt("lora32", [96, 2, 256])
        lora = sbt("lora", [96, 2, 256], BF16)
        bext = [sbt("bext%d" % j, [128, 384]) for j in range(2)]
        bcol = sbt("bcol", [128, 4])
        tC = S.tile("consts")
        arena_w = 28100
        arena = sbt("arena", [128, arena_w])
        apos = [0]

        def carve(nbytes, dt, shape_str=None, **kw):
            n32 = (nbytes + 3) // 4
            a = apos[0]
            apos[0] += n32
            assert apos[0] <= arena_w, apos[0]
            ap = arena[:, a:a + n32]
            if dt != F32:
                ap = ap.bitcast(dt)
            if shape_str:
                ap = ap.rearrange(shape_str, **kw)
            return ap

        ps = [es.enter_context(nc.psum_tensor("ps%d" % i, [128, 512], F32)) for i in range(7)]
        tps = [S.tile("ps%d" % i) for i in range(7)]
        psb = es.enter_context(nc.psum_tensor("psb", [128, 1024], BF16))
        tpsb = S.tile("psb")

        KT = carve(2 * P_TOK * 2, BF16, "p (h n) -> p h n", h=2)
        tKT = S.tile("KT")
        Vt = carve(NB * 256 * 2, BF16, "p (b n) -> p b n", b=NB)
        tV = S.tile("V")
        xn = carve(D * 2, BF16)
        txn = S.tile("xn")
        hT = carve(16 * 256 * 2, BF16, "p (c n) -> p c n", c=16)
        thT = S.tile("hT")
        QT = carve(2 * 256 * 2, BF16, "p (h n) -> p h n", h=2)
        tQT = S.tile("QT")
        gate_a = carve(2 * 256 * 2, BF16, "p (h n) -> p h n", h=2)
        tga = S.tile("gate_a")
        gate_r = carve(2 * 256 * 2, BF16, "p (h n) -> p h n", h=2)
        tgr = S.tile("gate_r")
        zb = [carve(257 * 4, F32) for _ in range(8)]
        tzb = [S.tile("z%d" % i) for i in range(8)]
        NWK = 12
        wk = [carve(256 * 4, F32) for _ in range(NWK)]
        twk = [S.tile("wk%d" % i) for i in range(NWK)]
        bonus = [carve(256 * 4, F32) for _ in range(2)]
        tbonus = [S.tile("bonus%d" % i) for i in range(2)]
        yT = [wk[1], wk[5]]
        tyT = [twk[1], twk[5]]
        twb = carve(256 * 2, BF16)
        adb = carve(256 * 2, BF16)
        ttwb = S.tile("twb")
        tadb = S.tile("adb")
        arT = [carve(2 * 256 * 2, BF16, "p (a n) -> p a n", a=2) for _ in range(2)]
        tarT = [S.tile("arT%d" % i) for i in range(2)]
        ktl = [carve(256 * 2, BF16) for _ in range(2)]
        btl = [carve(256 * 2, BF16) for _ in range(2)]
        tktl = [S.tile("ktl%d" % i) for i in range(2)]
        tbtl = [S.tile("btl%d" % i) for i in range(2)]
        fm3 = [carve(256 * 2, BF16) for _ in range(3)]
        tfm3 = [S.tile("fm3_%d" % i) for i in range(3)]
        tok3 = [carve(2 * 256 * 2, BF16, "p (c n) -> p c n", c=2) for _ in range(3)]
        ttok3 = [S.tile("tok3_%d" % i) for i in range(3)]
        WC = carve(4 * 4, F32)
        tWC = S.tile("WC")
        nbias = carve(4 * 4, F32)
        tnbias = S.tile("nbias")
        AT = [carve(4 * 128 * 2, BF16, "p (a n) -> p a n", a=4) for _ in range(4)]
        tAT = [S.tile("AT%d" % i) for i in range(4)]
        Mc = carve(4 * 128 * 2, BF16, "p (h n) -> p h n", h=4)
        Mtc = carve(4 * 128 * 2, BF16, "p (h n) -> p h n", h=4)
        Ptc = carve(4 * 128 * 2, BF16, "p (h n) -> p h n", h=4)
        tMc = S.tile("Mc")
        tMtc = S.tile("Mtc")
        tPtc = S.tile("Ptc")
        btlm = [[carve(256 * 2, BF16) for _ in range(2)] for _ in range(2)]
        ktlm = [[carve(256 * 2, BF16) for _ in range(2)] for _ in range(2)]
        tbtlm = [[S.tile("btlm%d%d" % (i, j)) for j in range(2)] for i in range(2)]
        tktlm = [[S.tile("ktlm%d%d" % (i, j)) for j in range(2)] for i in range(2)]
        Vz = carve(2 * 4 * 128 * 2, BF16, "p (c h n) -> p c h n", c=2, h=4)
        tVz = S.tile("Vz")
        Uz = carve(4 * 128 * 2, BF16, "p (h n) -> p h n", h=4)
        tUz = S.tile("Uz")
        Xsb = carve(256 * 2, BF16, "p (h n) -> p h n", h=4)
        Usb = carve(256 * 2, BF16, "p (h n) -> p h n", h=4)
        tXsb = S.tile("Xsb")
        tUsb = S.tile("Usb")
        S32 = carve(2 * 128 * 4, F32, "p (a n) -> p a n", a=2)
        Sbf = carve(2 * 128 * 2, BF16, "p (a n) -> p a n", a=2)
        tS32 = S.tile("S32")
        tSbf = S.tile("Sbf")
        Ssb = [wk[0], wk[2]]
        tSsb = [twk[0], twk[2]]
        Eb = [carve(2 * 256 * 2, BF16, "p (m n) -> p m n", m=2) for _ in range(2)]
        tEb = [S.tile("Eb%d" % i) for i in range(2)]
        oT = [carve(4 * 256 * 2, BF16, "p (c n) -> p c n", c=4) for _ in range(2)]
        toT = [S.tile("oT%d" % i) for i in range(2)]
        stats = carve(4 * 6 * 4, F32, "p (a n) -> p a n", a=4)
        mv = carve(8 * 4, F32)
        tst = S.tile("stats")
        lamw = carve(4 * 64 * 4, F32, "p (a n) -> p a n", a=4)
        tabsb = carve(TABW * 4, F32)
        ttab = S.tile("tabsb")
        main_end = apos[0]
        print("arena main words", main_end, "of", arena_w)

        tagin = [S.tile("agin%d" % c) for c in range(4)]
        tagout = [S.tile("agout%d" % c) for c in range(4)]
        ttabd = [S.tile("tabd%d" % j) for j in range(2)]

        try:
            def cload(dst, src):
                S.dma(lambda e: e.dma_start(out=dst, in_=src), writes=[tC], key="consts")

            cload(pvec[:], pvec_d)
            cload(ident[:], ident_d)
            cload(ones_bf[:], onesbf_d)
            cload(ones_f[:], onesf_d)
            cload(blockones[:], bo_d)
            cload(maskT[:], maskT_d)
            cload(masklow[:], masklow_d)
            cload(lora32[:], lora_d)
            relb = carve(2 * 128 * 4, F32, "p (a n) -> p a n", a=2)
            ohsb = carve(TABW * 4, F32)
            cload(relb[0:33], relb_d)
            cload(ohsb[0:33], oh_d)
            lam_src = bass.AP(tensor=lam4_d.tensor, offset=0, ap=[[0, 128], [64, 4], [1, 64]])
            cload(lamw, lam_src)

            S.op("dve", lambda e: e.tensor_copy(out=lora[:], in_=lora32[:]), reads=[tC], writes=[tC])
            S.op("dve", lambda e: e.tensor_scalar(out=pder[:, 0:2], in0=pvec[:, PV_KA:PV_KA + 2], scalar1=-1.0, scalar2=1.0,
                                                  op0=ALU.mult, op1=ALU.add), reads=[tC], writes=[tC])
            S.op("dve", lambda e: e.tensor_scalar(out=pder[:, 2:3], in0=pvec[:, PV_SUBLN:PV_SUBLN + 1], scalar1=0.8, scalar2=None,
                                                  op0=ALU.mult), reads=[tC], writes=[tC])
            S.op("dve", lambda e: e.tensor_tensor(out=lamw[:, 0, :], in0=lamw[:, 0, :], in1=lamw[:, 1, :], op=ALU.mult),
                 reads=[tC], writes=[tC])
            S.op("dve", lambda e: e.tensor_tensor(out=lamw[:, 2, :], in0=lamw[:, 2, :], in1=lamw[:, 3, :], op=ALU.mult),
                 reads=[tC], writes=[tC])
            S.op("dve", lambda e: e.reduce_sum(out=pder[:, 4:5], in_=lamw[:, 0, :], axis=mybir.AxisListType.X),
                 reads=[tC], writes=[tC])
            S.op("dve", lambda e: e.reduce_sum(out=pder[:, 5:6], in_=lamw[:, 2, :], axis=mybir.AxisListType.X),
                 reads=[tC], writes=[tC])
            S.op("act", lambda e: e.activation(out=pder[:, 4:6], in_=pder[:, 4:6], func=AF.Exp), reads=[tC], writes=[tC])
            S.op("dve", lambda e: e.tensor_tensor(out=pder[:, 3:4], in0=pder[:, 5:6], in1=pder[:, 4:5], op=ALU.subtract),
                 reads=[tC], writes=[tC])
            S.op("dve", lambda e: e.tensor_scalar(out=pder[:, 3:4], in0=pder[:, 3:4], scalar1=-0.2, scalar2=None, op0=ALU.add),
                 reads=[tC], writes=[tC])
            S.op("pool", lambda e: e.memset(S32.rearrange("p a n -> p (a n)"), 0.0), writes=[tS32])
            S.op("pool", lambda e: e.memset(Sbf.rearrange("p a n -> p (a n)"), 0.0), writes=[tSbf])
            S.op("pool", lambda e: e.memset(Vz.rearrange("p c h n -> p (c h n)"), 0.0), writes=[tVz])
            S.op("pool", lambda e: e.memset(Uz.rearrange("p h n -> p (h n)"), 0.0), writes=[tUz])
            for i in range(8):
                S.op("pool", lambda e, i=i: e.memset(zb[i], 0.0), writes=[tzb[i]])

            for j in range(2):
                S.op("pe", lambda e, j=j: e.matmul(ps[0][:, 0:256], lhsT=relb[0:33, j, :], rhs=ohsb[0:33, 0:256], start=True, stop=True),
                     reads=[tC], writes=[tps[0]])
                S.op("pe", lambda e, j=j: e.matmul(ps[1][:, 0:255], lhsT=relb[0:33, j, :], rhs=ohsb[0:33, 256:511], start=True, stop=True),
                     reads=[tC], writes=[tps[1]])
                S.op("dve", lambda e: e.tensor_copy(out=tabsb[:, 0:256], in_=ps[0][:, 0:256]), reads=[tps[0]], writes=[ttab])
                S.op("dve", lambda e: e.tensor_copy(out=tabsb[:, 256:511], in_=ps[1][:, 0:255]), reads=[tps[1]], writes=[ttab])
                S.op("dve", lambda e, j=j: e.tensor_copy(out=bcol[:, j:j + 1], in_=tabsb[:, 510:511]), reads=[ttab], writes=[tC])
                S.op("dve", lambda e, j=j: e.tensor_copy(out=bcol[:, 2 + j:3 + j], in_=tabsb[:, 510:511]), reads=[ttab], writes=[tC])
                S.op("dve", lambda e, j=j: e.memset(bcol[0:112, 2 + j:3 + j], NEG), reads=[], writes=[tC])
                S.dma(lambda e, j=j: e.dma_start(out=tab_d[j].ap(), in_=tabsb), reads=[ttab], writes=[ttabd[j]], key="tabst")
                src = bass.AP(tensor=tab_d[j].ap().tensor, offset=127, ap=[[TABW - 1, 128], [1, 384]])
                S.dma(lambda e, j=j, src=src: e.dma_start(out=bext[j][:], in_=src), reads=[ttabd[j]], writes=[tC], key="consts")

            for kc in range(16):
                b = kc % 2
                S.dma(lambda e, kc=kc, b=b: e.dma_start(out=xb[b][:], in_=win_d[kc * 128:(kc + 1) * 128, :]),
                      writes=[txb[b]], key="xb%d" % b)
                eng = ("dve", "pool", "act")[kc % 3]
                if eng == "act":
                    S.op("act", lambda e, kc=kc, b=b: e.activation(out=W[:, kc, :], in_=xb[b][:], func=AF.Copy),
                         reads=[txb[b]], writes=[tW])
                else:
                    S.op(eng, lambda e, kc=kc, b=b: e.tensor_copy(out=W[:, kc, :], in_=xb[b][:]), reads=[txb[b]], writes=[tW])

            chk('setup')
            rr = {"ps": 0, "ev": 0, "xb": 0, "oT": 0}

            def next_ps(n=4):
                i = rr["ps"] % n
                rr["ps"] += 1
                return i

            def evac_copy(out, in_, reads, writes, bf=False):
                k = 1
                if k == 0:
                    S.op("dve", lambda e: e.tensor_scalar(out=out, in0=in_, scalar1=1.0, scalar2=None, op0=ALU.mult), reads=reads, writes=writes)
                else:
                    S.op("act", lambda e: e.activation(out=out, in_=in_, func=AF.Copy), reads=reads, writes=writes)

            def tt(eng, out, a, b, op, reads, writes):
                S.op(eng, lambda e: e.tensor_tensor(out=out, in0=a, in1=b, op=op), reads=reads, writes=writes)

            def ts(eng, out, a, s1, s2, op0, op1, reads, writes):
                if s2 is None:
                    S.op(eng, lambda e: e.tensor_scalar(out=out, in0=a, scalar1=s1, scalar2=None, op0=op0), reads=reads, writes=writes)
                else:
                    S.op(eng, lambda e: e.tensor_scalar(out=out, in0=a, scalar1=s1, scalar2=s2, op0=op0, op1=op1),
                         reads=reads, writes=writes)

            def stt(out, a, sc, b, op0, op1, reads, writes):
                S.op("dve", lambda e: e.scalar_tensor_tensor(out=out, in0=a, scalar=sc, in1=b, op0=op0, op1=op1),
                     reads=reads, writes=writes)

            def act(out, in_, func, reads, writes, bias=0.0, scale=1.0):
                S.op("act", lambda e: e.activation(out=out, in_=in_, func=func, bias=bias, scale=scale), reads=reads, writes=writes)

            def rsqrt_inplace(t, tt_, n, scale, eps):
                act(t[:, 0:n], t[:, 0:n], AF.Ln, [tt_], [tt_], bias=eps_ap(eps), scale=scale)
                act(t[:, 0:n], t[:, 0:n], AF.Exp, [tt_], [tt_], scale=-0.5)

            epsc = {}

            def eps_ap(v):
                if v == 0.0:
                    return 0.0
                if v not in epsc:
                    col = 6 + len(epsc)
                    S.op("pool", lambda e, col=col, v=v: e.memset(pder[:, col:col + 1], float(v)), writes=[tC])
                    epsc[v] = pder[:, col:col + 1]
                return epsc[v]

            eps_ap(1e-5)
            eps_ap(64e-5)

            def layer_norm_block(src_ap, is_meta, blk_slot):
                b = rr["xb"] % 2
                rr["xb"] += 1
                xt = xb[b]
                if is_meta:
                    S.op("pool", lambda e: e.memset(xt[:, 0:D], 0.0), writes=[txb[b]])
                    S.dma(lambda e: e.dma_start(out=xt[112:128, 0:D], in_=meta_d), writes=[txb[b]], key="xb%d" % b)
                else:
                    S.dma(lambda e: e.dma_start(out=xt[:, 0:D], in_=src_ap), writes=[txb[b]], key="xb%d" % b)
                for q in range(4):
                    S.op("dve", lambda e, q=q: e.bn_stats(out=stats[:, q, :], in_=xt[:, q * 512:(q + 1) * 512]),
                         reads=[txb[b]], writes=[tst])
                S.op("dve", lambda e: e.bn_aggr(out=mv[:, 0:2], in_=stats.rearrange("p a n -> p (a n)")), reads=[tst], writes=[tst])
                act(mv[:, 2:3], mv[:, 1:2], AF.Ln, [tst], [tst], bias=eps_ap(1e-5))
                act(mv[:, 2:3], mv[:, 2:3], AF.Exp, [tst], [tst], scale=-0.5)
                stt(mv[:, 3:4], mv[:, 0:1], -1.0, mv[:, 2:3], ALU.mult, ALU.mult, [tst], [tst])
                S.op("act", lambda e: e.activation(out=xn, in_=xt[:, 0:D], func=AF.Identity, bias=mv[:, 3:4], scale=mv[:, 2:3]),
                     reads=[txb[b], tst], writes=[txn])
                for half in range(2):
                    for dc in range(8):
                        c = half * 8 + dc
                        S.op("pe", lambda e, c=c, dc=dc: e.transpose(psb[:, dc * 128:(dc + 1) * 128], xn[:, c * 128:(c + 1) * 128], ident[:]),
                             reads=[txn, tC], writes=[tpsb])
                    for dc in range(8):
                        c = half * 8 + dc
                        dst = hT[:, c, blk_slot * 128:(blk_slot + 1) * 128]
                        src = psb[:, dc * 128:(dc + 1) * 128]
                        if dc % 2 == 0:
                            ts("dve", dst, src, pvec[:, PV_LNG + c:PV_LNG + c + 1], pvec[:, PV_LNB + c:PV_LNB + c + 1],
                               ALU.mult, ALU.add, [tpsb, tC], [thT])
                        else:
                            S.op("act", lambda e, dst=dst, src=src, c=c: e.activation(
                                out=dst, in_=src, func=AF.Identity, bias=pvec[:, PV_LNB + c:PV_LNB + c + 1],
                                scale=pvec[:, PV_LNG + c:PV_LNG + c + 1]), reads=[tpsb, tC], writes=[thT])
                if is_meta:
                    S.op("pool", lambda e: e.memset(hT[:, :, 0:112], 0.0), writes=[thT])

            def inproj_fm(ct_off, m, nt, sink):
                i = next_ps()
                for kc in range(16):
                    S.op("pe", lambda e, kc=kc, i=i: e.matmul(ps[i][0:m, 0:nt], lhsT=W[:, kc, ct_off:ct_off + m], rhs=hT[:, kc, 0:nt],
                                                            start=(kc == 0), stop=(kc == 15)), reads=[tW, thT], writes=[tps[i]])
                sink(ps[i][0:m, 0:nt], tps[i])

            tiles = [(-1, 128)] + [(i, 256) for i in range(16)]
            for (ti, nt) in tiles:
                nblk = nt // 128
                blk0 = 0 if ti < 0 else 1 + 2 * ti
                pos0 = blk0 * 128
                for j in range(nblk):
                    if ti < 0:
                        layer_norm_block(None, True, 0)
                    else:
                        r0 = ti * 256 + j * 128
                        layer_norm_block(x_d[r0:r0 + 128, :], False, j)
                if ti >= 0:
                    for h in range(2):
                        inproj_fm(h * 128, 128, nt, lambda p, tp, h=h: evac_copy(QT[:, h, 0:nt], p, [tp], [tQT]))
                for h in range(2):
                    inproj_fm(256 + h * 128, 128, nt, lambda p, tp, h=h: evac_copy(KT[:, h, pos0:pos0 + nt], p, [tp], [tKT]))
                if ti >= 0:
                    for h in range(2):
                        inproj_fm(512 + h * 128, 128, nt, lambda p, tp, h=h: act(gate_a[:, h, 0:nt], p, AF.Silu, [tp], [tga]))
                    for p_ in range(2):
                        inproj_fm(1536 + p_ * 128, 128, nt, lambda p, tp, p_=p_: act(gate_r[:, p_, 0:nt], p, AF.Silu, [tp], [tgr]))
                for zi in range(6):
                    inproj_fm(768 + zi * 128, 128, nt, lambda p, tp, zi=zi: evac_copy(zb[zi][:, 1:1 + nt], p, [tp], [tzb[zi]]))
                for zi in range(2):
                    inproj_fm(1792 + zi * 96, 96, nt, lambda p, tp, zi=zi: evac_copy(zb[6 + zi][0:96, 1:1 + nt], p, [tp], [tzb[6 + zi]]))
                for j in range(nblk):
                    i = next_ps()
                    for kc in range(16):
                        S.op("pe", lambda e, kc=kc, i=i, j=j: e.matmul(ps[i][:, 0:256], lhsT=hT[:, kc, j * 128:(j + 1) * 128],
                                                                     rhs=W[:, kc, 1984:2240], start=(kc == 0), stop=(kc == 15)),
                             reads=[tW, thT], writes=[tps[i]])
                    evac_copy(Vt[:, blk0 + j, :], ps[i][:, 0:256], [tps[i]], [tV])

                chk('inproj%d' % ti)
                r32, k32, v32, t1, t2, sg, aic, kkn, k2, bvec, cs, t3 = wk
                tr32, tk32, tv32, tt1, tt2, tsg, taic, tkkn, tk2, tbvec, tcs, tt3 = twk
                n = nt

                def shift(zi, mucol, out, tout, rows=128):
                    z = zb[zi]
                    tz = tzb[zi]
                    tt("pool", t3[0:rows, 0:n], z[0:rows, 0:n], z[0:rows, 1:1 + n], ALU.subtract, [tz], [tt3])
                    stt(out[0:rows, 0:n], t3[0:rows, 0:n], pvec[0:rows, mucol:mucol + 1], z[0:rows, 1:1 + n], ALU.mult, ALU.add,
                        [tt3, tz, tC], [tout])
                    S.op("pool", lambda e: e.tensor_copy(out=z[0:rows, 0:1], in_=z[0:rows, n:n + 1]), reads=[tz], writes=[tz])

                shift(6, PV_MUWD, t1, tt1, rows=96)
                act(twb[0:96, 0:n], t1[0:96, 0:n], AF.Tanh, [tt1], [ttwb])
                shift(7, PV_MUAD, t2, tt2, rows=96)
                S.op("dve", lambda e: e.tensor_scalar(out=adb[0:96, 0:n], in0=t2[0:96, 0:n], scalar1=1.0, scalar2=None, op0=ALU.mult), reads=[tt2], writes=[tadb])

                chk('pa%d' % ti)
                import os as _os
                for p in [int(v) for v in _os.environ.get('PAIRS', '0,1').split(',')]:
                    shift(0 + p, PV_MU + 0 + p, r32, tr32)
                    shift(2 + p, PV_MU + 2 + p, k32, tk32)
                    shift(4 + p, PV_MU + 4 + p, v32, tv32)
                    S.op("pe", lambda e, p=p: e.matmul(ps[4][:, 0:n], lhsT=lora[0:96, 0, p * 128:(p + 1) * 128], rhs=twb[0:96, 0:n],
                                                       start=True, stop=True), reads=[tC, ttwb], writes=[tps[4]])
                    act(sg[:, 0:n], ps[4][:, 0:n], AF.Sigmoid, [tps[4], tC], [tsg], bias=pvec[:, PV_W0 + p:PV_W0 + p + 1])
                    S.op("pe", lambda e, p=p: e.matmul(ps[5][:, 0:n], lhsT=lora[0:96, 1, p * 128:(p + 1) * 128], rhs=adb[0:96, 0:n],
                                                       start=True, stop=True), reads=[tC, tadb], writes=[tps[5]])
                    act(aic[:, 0:n], ps[5][:, 0:n], AF.Sigmoid, [tps[5], tC], [taic], bias=pvec[:, PV_A0 + p:PV_A0 + p + 1])
                    chk('pb%d' % ti)
                    ts("dve", t1[:, 0:n], k32[:, 0:n], pvec[:, PV_KK + p:PV_KK + p + 1], None, ALU.mult, None, [tk32, tC], [tt1])
                    tt("pool", t2[:, 0:n], t1[:, 0:n], t1[:, 0:n], ALU.mult, [tt1], [tt2])
                    S.op("pe", lambda e: e.matmul(ps[6][:, 0:n], lhsT=blockones[:], rhs=t2[:, 0:n], start=True, stop=True),
                         reads=[tC, tt2], writes=[tps[6]])
                    ts("dve", t2[:, 0:n], ps[6][:, 0:n], 1e-24, None, ALU.max, None, [tps[6]], [tt2])
                    rsqrt_inplace(t2, tt2, n, 1.0, 0.0)
                    tt("dve", kkn[:, 0:n], t1[:, 0:n], t2[:, 0:n], ALU.mult, [tt1, tt2], [tkkn])
                    chk('pc%d' % ti)
                    ts("dve", t1[:, 0:n], aic[:, 0:n], pvec[:, PV_KA + p:PV_KA + p + 1], pder[:, p:p + 1], ALU.mult, ALU.add,
                       [taic, tC], [tt1])
                    tt("pool", k2[:, 0:n], k32[:, 0:n], t1[:, 0:n], ALU.mult, [tk32, tt1], [tk2])
                    tt("pool", bvec[:, 0:n], kkn[:, 0:n], aic[:, 0:n], ALU.mult, [tkkn, taic], [tbvec])
                    chk('pd%d' % ti)
                    stt(t1[:, 0:n], r32[:, 0:n], pvec[:, PV_RK + p:PV_RK + p + 1], k2[:, 0:n], ALU.mult, ALU.mult, [tr32, tk2, tC], [tt1])
                    S.op("pe", lambda e: e.matmul(ps[4][:, 0:n], lhsT=blockones[:], rhs=t1[:, 0:n], start=True, stop=True),
                         reads=[tC, tt1], writes=[tps[4]])
                    tt("dve", bonus[p][:, 0:n], ps[4][:, 0:n], v32[:, 0:n], ALU.mult, [tps[4], tv32], [tbonus[p]])
                    chk('pe%d' % ti)
                    for c in range(nblk):
                        S.op("dve", lambda e, c=c: e.tensor_tensor_scan(out=cs[:, c * 128:(c + 1) * 128], data0=ones_f[:, 0:128],
                                                                     data1=sg[:, c * 128:(c + 1) * 128], initial=0.0,
                                                                     op0=ALU.mult, op1=ALU.add), reads=[tsg, tC], writes=[tcs])
                    chk('pf%d' % ti)
                    act(t1[:, 0:n], cs[:, 0:n], AF.Exp, [tcs], [tt1], scale=-C0)
                    tt("dve", arT[p][:, 1, 0:n], r32[:, 0:n], t1[:, 0:n], ALU.mult, [tr32, tt1], [tarT[p]])
                    for c in range(nblk):
                        S.op("pool", lambda e, c=c, p=p: e.tensor_copy(out=WC[:, p * 2 + c:p * 2 + c + 1], in_=t1[:, c * 128 + 127:c * 128 + 128]),
                             reads=[tt1], writes=[tWC])
                        ts("dve", nbias[:, c:c + 1], cs[:, c * 128 + 127:c * 128 + 128], -C0, None, ALU.mult, None, [tcs], [tnbias])
                    tt("pool", t2[:, 0:n], cs[:, 0:n], sg[:, 0:n], ALU.subtract, [tcs, tsg], [tt2])
                    act(t2[:, 0:n], t2[:, 0:n], AF.Exp, [tt2], [tt2], scale=-C0)
                    stt(arT[p][:, 0, 0:n], kkn[:, 0:n], -1.0, t2[:, 0:n], ALU.mult, ALU.mult, [tkkn, tt2], [tarT[p]])
                    act(t1[:, 0:n], cs[:, 0:n], AF.Exp, [tcs], [tt1], scale=C0)
                    tt("dve", ktl[p][:, 0:n], k2[:, 0:n], t1[:, 0:n], ALU.mult, [tk2, tt1], [tktl[p]])
                    tt("pool", btl[p][:, 0:n], bvec[:, 0:n], t1[:, 0:n], ALU.mult, [tbvec, tt1], [tbtl[p]])
                    for hh in ([] if _os.environ.get('NOMASK') else range(2)):
                        hm = blockones[:, hh * 64:hh * 64 + 1]
                        ts("dve" if hh == 0 else "pool", btlm[p][hh][:, 0:n], btl[p][:, 0:n], hm, None, ALU.mult, None, [tbtl[p], tC], [tbtlm[p][hh]])
                        ts("pool" if hh == 0 else "dve", ktlm[p][hh][:, 0:n], ktl[p][:, 0:n], hm, None, ALU.mult, None, [tktl[p], tC], [tktlm[p][hh]])
                    for c in range(nblk):
                        S.op("act", lambda e, c=c: e.activation(out=t2[:, c * 128:(c + 1) * 128], in_=cs[:, c * 128:(c + 1) * 128],
                                                                 func=AF.Exp, bias=nbias[:, c:c + 1], scale=C0),
                             reads=[tcs, tnbias], writes=[tt2])
                    chk('pg%d' % ti)
                    tt("dve", fm3[0][:, 0:n], k2[:, 0:n], t2[:, 0:n], ALU.mult, [tk2, tt2], [tfm3[0]])
                    tt("pool", fm3[1][:, 0:n], bvec[:, 0:n], t2[:, 0:n], ALU.mult, [tbvec, tt2], [tfm3[1]])
                    S.op("act", lambda e: e.activation(out=fm3[2][:, 0:n], in_=v32[:, 0:n], func=AF.Copy), reads=[tv32], writes=[tfm3[2]])
                    chk('ph%d' % ti)
                    for c in range(nblk):
                        for q in range(3):
                            S.op("pe", lambda e, c=c, q=q: e.matmul(ps[4 + q][:, 0:128], lhsT=fm3[q][:, c * 128:(c + 1) * 128], rhs=ident[:],
                                                                    start=True, stop=True), reads=[tfm3[q], tC], writes=[tps[4 + q]])
                        chk('pt%d' % ti)
                        for q in range(3):
                            evac_copy(tok3[q][:, c, p * 128:(p + 1) * 128], ps[4 + q][:, 0:128], [tps[4 + q]], [ttok3[q]])
                        for hh in ([] if _os.environ.get('NOVZ') else range(2)):
                            S.op('act', lambda e: e.activation(out=Vz[:, c, 2 * p + hh, hh * 64:(hh + 1) * 64], in_=ps[6][:, hh * 64:(hh + 1) * 64], func=AF.Copy), reads=[tps[6]], writes=[tVz])

                    chk('pi%d' % ti)
                chk('rwpre%d' % ti)
                Khat, Bhat, Vtok = tok3
                tKhat, tBhat, tVtok = ttok3
                for c in range(nblk):
                    cs_ = slice(c * 128, (c + 1) * 128)
                    for h in range(4):
                        p, hh = h // 2, h % 2
                        i = h % 2
                        S.op("pe", lambda e: e.matmul(ps[i][:, 0:256], lhsT=btlm[p][hh][:, cs_], rhs=arT[p][:, :, cs_], start=True, stop=True),
                             reads=[tbtlm[p][hh], tarT[p]], writes=[tps[i]])
                        S.op("pe", lambda e: e.matmul(ps[i][:, 256:512], lhsT=ktlm[p][hh][:, cs_], rhs=arT[p][:, :, cs_], start=True, stop=True),
                             reads=[tktlm[p][hh], tarT[p]], writes=[tps[i]])
                        chk('c0%d' % ti)
                        tt("dve", AT[h].rearrange("p a n -> p (a n)"), ps[i][:, :], maskT[:], ALU.mult, [tps[i], tC], [tAT[h]])
                        chk('c1%d' % ti)
                        S.op("pe", lambda e: e.matmul(ps[2][:, h * 128:(h + 1) * 128], lhsT=arT[p][:, 0, cs_], rhs=btlm[p][hh][:, cs_], start=True, stop=True),
                             reads=[tbtlm[p][hh], tarT[p]], writes=[tps[2]])
                    chk('ca%d' % ti)
                    for h in range(4):
                        tt("dve", Mc[:, h, :], ps[2][:, h * 128:(h + 1) * 128], masklow[:], ALU.mult, [tps[2], tC], [tMc])
                        S.op("pool", lambda e: e.tensor_copy(out=Mtc[:, h, :], in_=AT[h][:, 0, :]), reads=[tAT[h]], writes=[tMtc])
                        tt("pool", Ptc[:, h, :], AT[h][:, 0, :], ident[:], ALU.add, [tAT[h], tC], [tPtc])
                    chk('cb%d' % ti)
                    for lvl in range(7):
                        if lvl >= 1:
                            for h in range(4):
                                S.op("pe", lambda e: e.matmul(ps[3][:, h * 128:(h + 1) * 128], lhsT=Mc[:, h, :], rhs=Ptc[:, h, :], start=True, stop=True),
                                     reads=[tMc, tPtc], writes=[tps[3]])
                        if lvl < 6:
                            for h in range(4):
                                S.op("pe", lambda e: e.matmul(ps[4][:, h * 128:(h + 1) * 128], lhsT=Mtc[:, h, :], rhs=Mc[:, h, :], start=True, stop=True),
                                     reads=[tMc, tMtc], writes=[tps[4]])
                            if lvl < 5:
                                for h in range(4):
                                    S.op("pe", lambda e: e.matmul(ps[5][:, h * 128:(h + 1) * 128], lhsT=Mc[:, h, :], rhs=Mtc[:, h, :], start=True, stop=True),
                                         reads=[tMc, tMtc], writes=[tps[5]])
                        if lvl >= 1:
                            tt("dve", Ptc.rearrange("p h n -> p (h n)"), ps[3][:, :], Ptc.rearrange("p h n -> p (h n)"), ALU.add, [tps[3], tPtc], [tPtc])
                        if lvl < 6:
                            S.op("act", lambda e: e.activation(out=Mc.rearrange("p h n -> p (h n)"), in_=ps[4][:, :], func=AF.Copy), reads=[tps[4]], writes=[tMc])
                            if lvl < 5:
                                S.op("act", lambda e: e.activation(out=Mtc.rearrange("p h n -> p (h n)"), in_=ps[5][:, :], func=AF.Copy), reads=[tps[5]], writes=[tMtc])
                    chk('cc%d' % ti)
                    for p in range(2):
                        S.op("pe", lambda e: e.matmul(ps[0][:, p * 128:(p + 1) * 128], lhsT=arT[p][:, 0, cs_], rhs=Sbf[:, p, :], start=True, stop=False),
                             reads=[tarT[p], tSbf], writes=[tps[0]])
                        for hh in range(2):
                            h = 2 * p + hh
                            S.op("pe", lambda e: e.matmul(ps[0][:, h * 64:(h + 1) * 64], lhsT=AT[h][:, 2, :], rhs=Vtok[:, c, h * 64:(h + 1) * 64],
                                                          start=False, stop=(hh == 1)), reads=[tAT[h], tVtok], writes=[tps[0]])
                    S.op("act", lambda e: e.activation(out=Xsb.rearrange("p h n -> p (h n)"), in_=ps[0][:, 0:256], func=AF.Copy), reads=[tps[0]], writes=[tXsb])
                    for h in range(4):
                        S.op("pe", lambda e: e.matmul(ps[1][:, h * 64:(h + 1) * 64], lhsT=Ptc[:, h, :], rhs=Xsb[:, h, :], start=True, stop=True),
                             reads=[tPtc, tXsb], writes=[tps[1]])
                    S.op("act", lambda e: e.activation(out=Usb.rearrange("p h n -> p (h n)"), in_=ps[1][:, 0:256], func=AF.Copy), reads=[tps[1]], writes=[tUsb])
                    for hh in range(2):
                        src = ps[1][:, 0:256].rearrange("p (a b n) -> p a b n", a=2, b=2)[:, :, hh, :]
                        dst = Uz.rearrange("p (a b) n -> p a b n", b=2)[:, :, hh, hh * 64:(hh + 1) * 64]
                        S.op("act", lambda e: e.activation(out=dst, in_=src, func=AF.Copy), reads=[tps[1]], writes=[tUz])
                    chk('cd%d' % ti)
                    for p in range(2):
                        S.op("pe", lambda e: e.matmul(ps[2 + p][:, 0:128], lhsT=Sbf[:, p, :], rhs=arT[p][:, 1, cs_], start=True, stop=False),
                             reads=[tSbf, tarT[p]], writes=[tps[2 + p]])
                        for hh in range(2):
                            h = 2 * p + hh
                            S.op("pe", lambda e: e.matmul(ps[2 + p][:, 0:128], lhsT=Uz[:, h, :], rhs=AT[h][:, 1, :], start=False, stop=False),
                                 reads=[tUz, tAT[h]], writes=[tps[2 + p]])
                            S.op("pe", lambda e: e.matmul(ps[2 + p][:, 0:128], lhsT=Vz[:, c, h, :], rhs=AT[h][:, 3, :], start=False, stop=(hh == 1)),
                                 reads=[tVz, tAT[h]], writes=[tps[2 + p]])
                        if ti >= 0:
                            evac_copy(yT[p][:, cs_], ps[2 + p][:, 0:128], [tps[2 + p]], [tyT[p]])
                    for p in range(2):
                        S.op("pe", lambda e: e.matmul(ps[4 + p][:, 0:128], lhsT=Bhat[:, c, p * 128:(p + 1) * 128],
                                                      rhs=Usb.rearrange("p h n -> p (h n)")[:, p * 128:(p + 1) * 128], start=True, stop=False),
                             reads=[tBhat, tUsb], writes=[tps[4 + p]])
                        S.op("pe", lambda e: e.matmul(ps[4 + p][:, 0:128], lhsT=Khat[:, c, p * 128:(p + 1) * 128], rhs=Vtok[:, c, p * 128:(p + 1) * 128],
                                                      start=False, stop=True), reads=[tKhat, tVtok], writes=[tps[4 + p]])
                        tt("dve", t3[:, 0:128], ps[4 + p][:, 0:128], blockones[:], ALU.mult, [tps[4 + p], tC], [tt3])
                        stt(S32[:, p, :], S32[:, p, :], WC[:, p * 2 + c:p * 2 + c + 1], t3[:, 0:128], ALU.mult, ALU.add, [tS32, tWC, tt3], [tS32])
                    S.op("act", lambda e: e.activation(out=Sbf.rearrange("p a n -> p (a n)"), in_=S32.rearrange("p a n -> p (a n)"), func=AF.Copy),
                         reads=[tS32], writes=[tSbf])

                chk('chain%d' % ti)
                if ti < 0:
                    continue

                ob = rr["oT"] % 2
                rr["oT"] += 1
                oTt = oT[ob]
                toTt = toT[ob]

                for p in range(2):
                    y = yT[p]
                    S.op("pe", lambda e, y=y: e.matmul(ps[0][:, 0:n], lhsT=blockones[:], rhs=y[:, 0:n], start=True, stop=True),
                         reads=[tC, tyT[p]], writes=[tps[0]])
                    stt(t1[:, 0:n], ps[0][:, 0:n], -1.0 / 64.0, y[:, 0:n], ALU.mult, ALU.add, [tps[0], tyT[p]], [tt1])
                    tt("pool", t2[:, 0:n], t1[:, 0:n], t1[:, 0:n], ALU.mult, [tt1], [tt2])
                    S.op("pe", lambda e: e.matmul(ps[1][:, 0:n], lhsT=blockones[:], rhs=t2[:, 0:n], start=True, stop=True),
                         reads=[tC, tt2], writes=[tps[1]])
                    act(t2[:, 0:n], ps[1][:, 0:n], AF.Ln, [tps[1], tC], [tt2], bias=eps_ap(64e-5), scale=1.0 / 64.0)
                    act(t2[:, 0:n], t2[:, 0:n], AF.Exp, [tt2], [tt2], scale=-0.5)
                    tt("dve", t1[:, 0:n], t1[:, 0:n], t2[:, 0:n], ALU.mult, [tt1, tt2], [tt1])
                    ts("dve", t1[:, 0:n], t1[:, 0:n], pvec[:, PV_GNG + p:PV_GNG + p + 1], pvec[:, PV_GNB + p:PV_GNB + p + 1],
                       ALU.mult, ALU.add, [tt1, tC], [tt1])
                    tt("pool", t1[:, 0:n], t1[:, 0:n], bonus[p][:, 0:n], ALU.add, [tt1, tbonus[p]], [tt1])
                    tt("dve", oTt[:, 2 + p, 0:n], t1[:, 0:n], gate_r[:, p, 0:n], ALU.mult, [tt1, tgr], [toTt])

                chk('rwpost%d' % ti)
                qb0 = blk0
                OO = ps[4][:, :].rearrange("p (m n) -> p m n", m=2)
                SS = ps[5][:, :].rearrange("p (m n) -> p m n", m=2)
                for h in range(2):
                    kbs = list(range(0, qb0 + 2))

                    def geom(kb):
                        delta = kb - qb0
                        q_lo = 128 if delta == 1 else 0
                        return delta, q_lo, n - q_lo, kb % 2

                    def stage_qk(kb):
                        delta, q_lo, nq, par = geom(kb)
                        for m in range(2):
                            rs = slice(m * 64, m * 64 + 64)
                            pi = 2 * par + m
                            S.op("pe", lambda e: e.matmul(ps[pi][:, 0:nq], lhsT=KT[rs, h, kb * 128:(kb + 1) * 128], rhs=QT[rs, h, q_lo:n],
                                                          start=True, stop=True), reads=[tKT, tQT], writes=[tps[pi]])

                    def stage_rest(kb):
                        delta, q_lo, nq, par = geom(kb)
                        near = delta >= -1
                        E = Eb[par]
                        tE = tEb[par]
                        for m in range(2):
                            pi = 2 * par + m
                            if near:
                                boff = 0 if delta >= 0 else 128
                                stt(Ssb[m][:, 0:nq], ps[pi][:, 0:nq], 0.125, bext[h][:, boff:boff + nq], ALU.mult, ALU.add,
                                    [tps[pi], tC], [tSsb[m]])
                                if kb == 0:
                                    S.op("pool", lambda e: e.memset(Ssb[m][0:112, 0:nq], NEG), reads=[], writes=[tSsb[m]])
                                act(E[:, m, 0:nq], Ssb[m][:, 0:nq], AF.Exp, [tSsb[m]], [tE])
                            else:
                                bc = bcol[:, 2 + h:3 + h] if kb == 0 else bcol[:, h:h + 1]
                                act(E[:, m, 0:nq], ps[pi][:, 0:nq], AF.Exp, [tps[pi], tC], [tE], bias=bc, scale=0.125)
                        first = (kb == kbs[0])
                        last = (kb == kbs[-1])
                        if q_lo == 0:
                            Ef = E.rearrange("p m n -> p (m n)")
                            S.op("pe", lambda e: e.matmul(ps[4][:, :], lhsT=Vt[:, kb, h * 128:(h + 1) * 128], rhs=Ef, start=first, stop=last),
                                 reads=[tV, tE], writes=[tps[4]])
                            S.op("pe", lambda e: e.matmul(ps[5][:, :], lhsT=ones_bf[:], rhs=Ef, start=first, stop=last),
                                 reads=[tC, tE], writes=[tps[5]])
                        else:
                            for m in range(2):
                                S.op("pe", lambda e: e.matmul(OO[:, m, q_lo:n], lhsT=Vt[:, kb, h * 128:(h + 1) * 128], rhs=E[:, m, 0:nq],
                                                              start=first, stop=(last and m == 1)), reads=[tV, tE], writes=[tps[4]])
                                S.op("pe", lambda e: e.matmul(SS[:, m, q_lo:n], lhsT=ones_bf[:], rhs=E[:, m, 0:nq],
                                                              start=first, stop=(last and m == 1)), reads=[tC, tE], writes=[tps[5]])

                    def post_a():
                        S.op("dve", lambda e: e.reciprocal(out=t1[:, 0:n], in_=SS[:, 0, 0:n]), reads=[tps[5]], writes=[tt1])
                        S.op("dve", lambda e: e.reciprocal(out=t2[:, 0:n], in_=SS[:, 1, 0:n]), reads=[tps[5]], writes=[tt2])
                        tt("dve", t1[:, 0:n], OO[:, 0, 0:n], t1[:, 0:n], ALU.mult, [tps[4], tt1], [tt1])
                        tt("dve", t2[:, 0:n], OO[:, 1, 0:n], t2[:, 0:n], ALU.mult, [tps[4], tt2], [tt2])

                    def post_b(hp):
                        stt(t1[:, 0:n], t2[:, 0:n], pder[:, 3:4], t1[:, 0:n], ALU.mult, ALU.add, [tt1, tt2, tC], [tt1])
                        tt("pool", t2[:, 0:n], t1[:, 0:n], t1[:, 0:n], ALU.mult, [tt1], [tt2])
                        S.op("pe", lambda e: e.matmul(ps[6][:, 0:n], lhsT=ones_f[:], rhs=t2[:, 0:n], start=True, stop=True),
                             reads=[tC, tt2], writes=[tps[6]])
                        act(t2[:, 0:n], ps[6][:, 0:n], AF.Ln, [tps[6], tC], [tt2], bias=eps_ap(1e-5), scale=1.0 / 128.0)
                        act(t2[:, 0:n], t2[:, 0:n], AF.Exp, [tt2], [tt2], scale=-0.5)
                        tt("dve", t1[:, 0:n], t1[:, 0:n], t2[:, 0:n], ALU.mult, [tt1, tt2], [tt1])
                        stt(oTt[:, hp, 0:n], t1[:, 0:n], pder[:, 2:3], gate_a[:, hp, 0:n], ALU.mult, ALU.mult, [tt1, tga, tC], [toTt])

                    stage_qk(kbs[0])
                    for i_, kb in enumerate(kbs):
                        if i_ + 1 < len(kbs):
                            stage_qk(kbs[i_ + 1])
                        stage_rest(kb)
                        if h == 1 and i_ == min(1, len(kbs) - 1):
                            post_b(0)
                    post_a()
                    if h == 1:
                        post_b(1)

                chk('attn%d' % ti)
                for cc in range(4):
                    S.dma(lambda e: e.dma_start(out=agin[cc].ap()[:, ti * 256:ti * 256 + 256], in_=oTt[:, cc, 0:256]), reads=[toTt], writes=[tagin[cc]],
                          key="oT%d" % ob)
                if DEBUG:
                    dd = dbg_d.rearrange("(c p) n -> p c n", p=128)[:, :, ti * 256:ti * 256 + 256]
                    S.dma(lambda e, dd=dd, oTt=oTt: e.dma_start(out=dd, in_=oTt[:, :, 0:256]), reads=[toTt], key="oT%d" % ob)

            chk('main')
            for cc in range(4):
                S.dma(lambda e: e.collective_compute("AllGather", ALU.bypass, replica_groups=[[0, 1, 2, 3], [4, 5, 6, 7]],
                                                     ins=[agin[cc].ap()], outs=[agout[cc].ap()]),
                      reads=[tagin[cc]], writes=[tagout[cc]], key="cc%d" % cc, queue="pool", inc=1)
            chk('ag')
            S.barrier()
            apos[0] = 0
            oq = carve(16 * 1024 * 2, BF16, "p (c n) -> p c n", c=16)
            toq = S.tile("oq")
            gbt = carve(4 * D * 4, F32, "p (a n) -> p a n", a=4)
            tgb = S.tile("gbt")
            rsd = [carve(D * 4, F32) for _ in range(2)]
            trsd = [S.tile("rsd%d" % i) for i in range(2)]
            fst = carve(4 * 6 * 4, F32, "p (a n) -> p a n", a=4)
            fmv = carve(8 * 4, F32)
            tfst = S.tile("fst")
            assert apos[0] <= arena_w

            for kc in range(16):
                b = kc % 2
                S.dma(lambda e, kc=kc, b=b: e.dma_start(out=xb[b][:, 0:D], in_=wout_d[kc * 128:(kc + 1) * 128, :]),
                      writes=[txb[b]], key="xb%d" % b)
                eng = ("dve", "pool", "act")[kc % 3]
                if eng == "act":
                    S.op("act", lambda e, kc=kc, b=b: e.activation(out=W[:, kc, 0:D], in_=xb[b][:, 0:D], func=AF.Copy),
                         reads=[txb[b]], writes=[tW])
                else:
                    S.op(eng, lambda e, kc=kc, b=b: e.tensor_copy(out=W[:, kc, 0:D], in_=xb[b][:, 0:D]), reads=[txb[b]], writes=[tW])
            gsrc = bass.AP(tensor=gb_d.tensor, offset=0, ap=[[0, 128], [D, 4], [1, D]])
            S.dma(lambda e: e.dma_start(out=gbt, in_=gsrc), writes=[tgb], key="gbt")

            idxt = sbt("idxt", [128, 4], mybir.dt.int32)
            tidx = S.tile("idxt")
            S.dma(lambda e: e.dma_start(out=idxt[:], in_=idx_d), writes=[tidx], key="idxt")
            for kc in range(16):
                r_, c_ = kc // 4, kc % 4
                agv = agout[c_].ap().rearrange("e (q n) -> (e q) n", q=4)
                S.dma(lambda e: e.indirect_dma_start(out=oq[:, kc, :], out_offset=None, in_=agv,
                                                     in_offset=bass.IndirectOffsetOnAxis(ap=idxt[:, r_:r_ + 1], axis=0)),
                      reads=[tagout[c_], tidx], writes=[toq], key="oq", queue="pool")
            alpha = 2.0 ** 0.25
            for blk in range(8):
                b = blk % 2
                xt = xb[b]
                r = rsd[b]
                tr = trsd[b]
                S.dma(lambda e, blk=blk, xt=xt: e.dma_start(out=xt[:, 0:D], in_=xq_d[blk * 128:(blk + 1) * 128, :]), writes=[txb[b]], key="xb%d" % b)
                for q in range(4):
                    S.op("dve", lambda e, q=q, xt=xt: e.bn_stats(out=fst[:, q, :], in_=xt[:, q * 512:(q + 1) * 512]), reads=[txb[b]], writes=[tfst])
                S.op("dve", lambda e: e.bn_aggr(out=fmv[:, 0:2], in_=fst.rearrange("p a n -> p (a n)")), reads=[tfst], writes=[tfst])
                act(fmv[:, 2:3], fmv[:, 1:2], AF.Ln, [tfst], [tfst], bias=eps_ap(1e-5))
                act(fmv[:, 2:3], fmv[:, 2:3], AF.Exp, [tfst], [tfst], scale=-0.5)
                stt(fmv[:, 3:4], fmv[:, 0:1], -1.0, fmv[:, 2:3], ALU.mult, ALU.mult, [tfst], [tfst])
                S.op("act", lambda e, xt=xt, r=r: e.activation(out=r, in_=xt[:, 0:D], func=AF.Identity, bias=fmv[:, 3:4], scale=fmv[:, 2:3]),
                     reads=[txb[b], tfst], writes=[tr])
                tt("pool", r, r, gbt[:, 0, :], ALU.mult, [tr, tgb], [tr])
                tt("pool", r, r, gbt[:, 1, :], ALU.add, [tr, tgb], [tr])
                for ct in range(4):
                    for kc in range(16):
                        S.op("pe", lambda e, kc=kc, ct=ct, blk=blk: e.matmul(ps[ct][:, :], lhsT=oq[:, kc, blk * 128:(blk + 1) * 128],
                                                                           rhs=W[:, kc, ct * 512:(ct + 1) * 512], start=(kc == 0), stop=(kc == 15)),
                             reads=[toq, tW], writes=[tps[ct]])
                    stt(r[:, ct * 512:(ct + 1) * 512], r[:, ct * 512:(ct + 1) * 512], alpha, ps[ct][:, :], ALU.mult, ALU.add,
                        [tr, tps[ct]], [tr])
                for q in range(4):
                    S.op("dve", lambda e, q=q, r=r: e.bn_stats(out=fst[:, q, :], in_=r[:, q * 512:(q + 1) * 512]), reads=[tr], writes=[tfst])
                S.op("dve", lambda e: e.bn_aggr(out=fmv[:, 4:6], in_=fst.rearrange("p a n -> p (a n)")), reads=[tfst], writes=[tfst])
                act(fmv[:, 6:7], fmv[:, 5:6], AF.Ln, [tfst], [tfst], bias=eps_ap(1e-5))
                act(fmv[:, 6:7], fmv[:, 6:7], AF.Exp, [tfst], [tfst], scale=-0.5)
                stt(fmv[:, 7:8], fmv[:, 4:5], -1.0, fmv[:, 6:7], ALU.mult, ALU.mult, [tfst], [tfst])
                S.op("act", lambda e, r=r: e.activation(out=r, in_=r, func=AF.Identity, bias=fmv[:, 7:8], scale=fmv[:, 6:7]),
                     reads=[tr, tfst], writes=[tr])
                tt("dve", r, r, gbt[:, 2, :], ALU.mult, [tr, tgb], [tr])
                tt("pool", r, r, gbt[:, 3, :], ALU.add, [tr, tgb], [tr])
                S.dma(lambda e, blk=blk, r=r: e.dma_start(out=out_d[blk * 128:(blk + 1) * 128, :], in_=r), reads=[tr], key="rsd%d" % b)

        except _Stop:
            pass
        S.emit()
    return nc


_NC_CACHE = {}


def kernel(**inp):
    maps = _prep_inputs(inp)
    x = np.asarray(inp["x"], np.float32)
    for c in range(8):
        b, r = c // 4, c % 4
        maps[c]["xq"] = np.ascontiguousarray(x[b, r * 1024:(r + 1) * 1024])
        p = np.arange(128)[:, None]
        kc = np.arange(4)[None, :]
        maps[c]["idx"] = ((kc * 128 + p) * 4 + r).astype(np.int32)
    if "nc" not in _NC_CACHE:
        _NC_CACHE["nc"] = build_nc()
    nc = _NC_CACHE["nc"]
    res = run_bass_kernel_spmd(nc, maps, core_ids=list(range(8)))
    out = np.zeros((2, SEQ, D), np.float32)
    for c in range(8):
        b, r = c // 4, c % 4
        out[b, r * 1024:(r + 1) * 1024] = res.results[c]["out"]
    if DEBUG:
        kernel.dbg = [res.results[c]["dbg"] for c in range(8)]
    return out
```

```python
import math
from contextlib import ExitStack

import numpy as np
import ml_dtypes

import concourse.bass as bass
import concourse.mybir as mybir
from concourse.bass_utils import run_bass_kernel_spmd

F32 = mybir.dt.float32
BF16 = mybir.dt.bfloat16
AF = mybir.ActivationFunctionType
ALU = mybir.AluOpType

D = 2048
SEQ = 4096
NB = 33
P_TOK = NB * 128
NCOL = 2240
C0 = math.exp(-0.5)
NEG = -1.0e4
TABW = 511
DEBUG = False
STOP = None


class _Stop(Exception):
    pass


_HITS = {}


def chk(tag):
    import os
    if STOP == tag:
        _HITS[tag] = _HITS.get(tag, 0) + 1
        if _HITS[tag] >= int(os.environ.get("NTH", "1")):
            raise _Stop()


class T:
    __slots__ = ("name", "w", "r")

    def __init__(self, name, init_r=None):
        self.name = name
        self.w = None
        self.r = dict(init_r) if init_r else {}


class _Rec:
    def __init__(self):
        self.call = None

    def __getattr__(self, name):
        def f(*a, **k):
            assert self.call is None
            self.call = (name, a, k)
            return self
        return f


def _freeze(fn):
    r = _Rec()
    fn(r)
    name, a, k = r.call
    return lambda e: getattr(e, name)(*a, **k)


class Sched:
    ENG = ("pe", "act", "dve", "pool", "sp")

    def __init__(self, nc, es):
        self.nc = nc
        self.es = es
        self.ops = {e: [] for e in self.ENG}
        self.seen = {e: {} for e in self.ENG}
        self.dsems = {}
        self.esem = {}
        self.bar = {}

    def tile(self, name):
        return T(name, self.bar)

    def barrier(self):
        b = {}
        for e in self.ENG:
            if e != "sp" and self.ops[e]:
                for i in range(len(self.ops[e]) - 1, -1, -1):
                    if self.ops[e][i]["dma"] is None:
                        b[('E', e)] = ('E', e, i)
                        break
        for k, v in self.dsems.items():
            if v[1] > 0:
                b[('D', k)] = ('D', k, v[1])
        self.bar = b

    @staticmethod
    def _dep(waits, ev):
        key = ev[:2]
        if waits.get(key, -1) < ev[2]:
            waits[key] = ev[2]

    def _collect(self, eng, reads, writes):
        waits = {}
        for t in reads:
            if t.w is not None:
                self._dep(waits, t.w)
        for t in writes:
            if t.w is not None:
                self._dep(waits, t.w)
            for ev in t.r.values():
                self._dep(waits, ev)
        wl = []
        for key, val in waits.items():
            if key[0] == 'E' and key[1] == eng and eng == 'pe':
                continue
            if self.seen[eng].get(key, -1) >= val:
                continue
            self.seen[eng][key] = val
            wl.append((key, val))
        return wl

    def op(self, eng, fn, reads=(), writes=()):
        fn = _freeze(fn)
        wl = self._collect(eng, reads, writes)
        idx = len(self.ops[eng])
        ev = ('E', eng, idx)
        self.ops[eng].append(dict(waits=wl, fn=fn, sig=False, dma=None))
        for t in reads:
            t.r[('E', eng)] = ev
        for t in writes:
            t.w = ev
            t.r = {}
        return ev

    def dsem(self, key):
        if key not in self.dsems:
            h = self.es.enter_context(self.nc.semaphore("d_" + key))
            self.dsems[key] = [h, 0]
        return self.dsems[key]

    def dma(self, fn, reads=(), writes=(), key=None, queue="sp", inc=16):
        fn = _freeze(fn)
        wl = self._collect(queue, reads, writes)
        ds = self.dsem(key)
        ds[1] += inc
        ev = ('D', key, ds[1])
        self.ops[queue].append(dict(waits=wl, fn=fn, sig=False, dma=(key, inc)))
        for t in reads:
            t.r[('D', key)] = ev
        for t in writes:
            t.w = ev
            t.r = {}
        return ev

    def emit(self):
        nc = self.nc
        for e in self.ENG:
            for o in self.ops[e]:
                for key, val in o["waits"]:
                    if key[0] == 'E':
                        self.ops[key[1]][val]["sig"] = True
        cnt = {}
        for e in self.ENG:
            c = 0
            lst = []
            for o in self.ops[e]:
                if o["sig"]:
                    c += 1
                lst.append(c)
            cnt[e] = lst
            self.esem[e] = self.es.enter_context(nc.semaphore("e_" + e))
        finals = [(('D', k), v[1]) for k, v in self.dsems.items() if v[1] > 0]

        def resolve(key, val):
            if key[0] == 'E':
                return self.esem[key[1]], cnt[key[1]][val]
            return self.dsems[key[1]][0], val

        def body(e):
            def run(eng):
                for o in self.ops[e]:
                    for key, val in o["waits"]:
                        s, v = resolve(key, val)
                        eng.wait_ge(s, v)
                    ins = o["fn"](eng)
                    if o["dma"] is not None:
                        ins.then_inc(self.dsems[o["dma"][0]][0], o["dma"][1])
                    elif o["sig"]:
                        ins.then_inc(self.esem[e], 1)
                if e == "sp":
                    for key, val in finals:
                        s, v = resolve(key, val)
                        eng.wait_ge(s, v)
            return run

        with nc.Block() as block:
            block.tensor(body("pe"))
            block.scalar(body("act"))
            block.vector(body("dve"))
            block.gpsimd(body("pool"))
            block.sync(body("sp"))


def _bucket(n):
    n = np.maximum(n, 0)
    nf = np.maximum(n, 1).astype(np.float32)
    large = 16 + (np.log(nf / np.float32(16)) / np.float32(math.log(128 / 16)) * np.float32(16)).astype(np.int32)
    large = np.minimum(large, 31)
    return np.where(n < 16, n, large)


PV_MU = 0
PV_MUWD = 6
PV_MUAD = 7
PV_W0 = 8
PV_A0 = 10
PV_KK = 12
PV_KA = 14
PV_RK = 16
PV_GNG = 18
PV_GNB = 20
PV_SUBLN = 22
PV_LNG = 23
PV_LNB = 39
PV_N = 55


def _consts():
    c = {}
    c["ident"] = np.eye(128, dtype=np.float32).astype(ml_dtypes.bfloat16)
    c["ones_bf"] = np.ones((128, 128), dtype=ml_dtypes.bfloat16)
    c["ones_f"] = np.ones((128, 128), dtype=np.float32)
    bo = np.zeros((128, 128), np.float32)
    bo[:64, :64] = 1.0
    bo[64:, 64:] = 1.0
    c["blockones"] = bo
    s = np.arange(128)[:, None]
    t = np.arange(128)[None, :]
    strict = (s < t).astype(np.float32)
    incl = (s <= t).astype(np.float32)
    m = np.zeros((128, 2, 2, 128), np.float32)
    m[:, :, 0, :] = strict[:, None, :]
    m[:, :, 1, :] = incl[:, None, :]
    c["maskT"] = m.reshape(128, 512)
    c["masklow"] = (s > t).astype(np.float32)
    d = np.arange(TABW) - 127
    oh = np.zeros((33, TABW), np.float32)
    b = _bucket(d)
    for i in range(TABW):
        if d[i] >= 0:
            oh[b[i], i] = 1.0
        else:
            oh[32, i] = NEG
    c["oh"] = oh
    return c


def _prep_inputs(inp):
    f = np.float32
    w_in = np.asarray(inp["w_in"][0], f)
    w_out = np.asarray(inp["w_out"][0], f)
    x = np.asarray(inp["x"], f)
    consts = _consts()
    lam4 = np.stack([inp["lambda_q1"][0], inp["lambda_k1"][0], inp["lambda_q2"][0], inp["lambda_k2"][0]]).astype(f)
    rows = []
    for r in range(4):
        rows += list(range((2 * r) * 128, (2 * r + 2) * 128))
        rows += list(range(1024 + (4 * r) * 64, 1024 + (4 * r + 4) * 64))
    w_out_p = np.ascontiguousarray(w_out[rows])
    gb = np.stack([inp["ln_emb_g"], inp["ln_emb_b"], inp["ln_post_g"][0], inp["ln_post_b"][0]]).astype(f)
    maps = []
    for c in range(8):
        b, hg = c // 4, c % 4
        h0 = 2 * hg
        cols = []
        cols += list(range(h0 * 128, (h0 + 2) * 128))
        cols += list(range(1024 + h0 * 128, 1024 + (h0 + 2) * 128))
        cols += list(range(3072 + h0 * 128, 3072 + (h0 + 2) * 128))
        rb = 4096 + hg * 256
        cols += list(range(rb, rb + 256))
        cols += list(range(rb + 1024, rb + 1024 + 256))
        cols += list(range(rb + 2048, rb + 2048 + 256))
        cols += list(range(7360 + hg * 256, 7360 + hg * 256 + 256))
        cols += list(range(4096 + 3072, 4096 + 3072 + 192))
        cols += list(range(2048 + h0 * 128, 2048 + (h0 + 2) * 128))
        assert len(cols) == NCOL
        wi = np.ascontiguousarray(w_in[:, cols])
        pv = np.zeros((128, PV_N), f)
        mu = inp["rw_mu"][0]
        rs = slice(hg * 256, hg * 256 + 256)
        for p in range(2):
            ps_ = slice(hg * 256 + p * 128, hg * 256 + (p + 1) * 128)
            pv[:, PV_MU + 0 + p] = mu[0:1024][ps_]
            pv[:, PV_MU + 2 + p] = mu[1024:2048][ps_]
            pv[:, PV_MU + 4 + p] = mu[2048:3072][ps_]
            pv[:, PV_W0 + p] = inp["rw_w0"][0][ps_]
            pv[:, PV_A0 + p] = inp["rw_a0"][0][ps_]
            pv[:, PV_KK + p] = inp["rw_k_k"][0][ps_]
            pv[:, PV_KA + p] = inp["rw_k_a"][0][ps_]
            pv[:, PV_RK + p] = inp["rw_r_k"][0].reshape(-1)[ps_]
            pv[:, PV_GNG + p] = inp["rw_gn_g"][0][ps_]
            pv[:, PV_GNB + p] = inp["rw_gn_b"][0][ps_]
        pv[:96, PV_MUWD] = mu[3072:3168]
        pv[:96, PV_MUAD] = mu[3168:3264]
        pv[:, PV_SUBLN] = inp["subln_g"][0]
        pv[:, PV_LNG:PV_LNG + 16] = np.asarray(inp["ln_emb_g"], f).reshape(16, 128).T
        pv[:, PV_LNB:PV_LNB + 16] = np.asarray(inp["ln_emb_b"], f).reshape(16, 128).T
        relb = np.ones((33, 2, 128), f)
        for j in range(2):
            relb[:32, j, :] = np.asarray(inp["rel_bias"], f)[:, h0 + j][:, None]
        lora = np.zeros((96, 2, 256), f)
        lora[:, 0, :] = inp["rw_w_up"][0][:, rs]
        lora[:, 1, :] = inp["rw_a_up"][0][:, rs]
        m = {
            "x": np.ascontiguousarray(x[b]),
            "meta": np.asarray(inp["meta_tokens"], f),
            "w_in": wi,
            "w_out": w_out_p,
            "pvec": pv,
            "relb": relb,
            "lora": lora,
            "lam4": lam4,
            "gb": gb,
        }
        m.update(consts)
        maps.append(m)
    return maps


def build_nc():
    nc = bass.Bass("TRN2", target_bir_lowering=False)

    def din(name, shape, dt=F32):
        return nc.dram_tensor(name, list(shape), dt, kind="ExternalInput").ap()

    x_d = din("x", [SEQ, D])
    meta_d = din("meta", [16, D])
    win_d = din("w_in", [D, NCOL])
    wout_d = din("w_out", [D, D])
    pvec_d = din("pvec", [128, PV_N])
    relb_d = din("relb", [33, 2, 128])
    lora_d = din("lora", [96, 2, 256])
    lam4_d = din("lam4", [4, 64])
    gb_d = din("gb", [4, D])
    ident_d = din("ident", [128, 128], BF16)
    onesbf_d = din("ones_bf", [128, 128], BF16)
    onesf_d = din("ones_f", [128, 128])
    bo_d = din("blockones", [128, 128])
    maskT_d = din("maskT", [128, 512])
    masklow_d = din("masklow", [128, 128])
    oh_d = din("oh", [33, TABW])
    xq_d = din("xq", [1024, D])
    idx_d = din("idx", [128, 4], mybir.dt.int32)
    out_d = nc.dram_tensor("out", [1024, D], F32, kind="ExternalOutput").ap()
    if DEBUG:
        dbg_d = nc.dram_tensor("dbg", [512, SEQ], BF16, kind="ExternalOutput").ap()
    agin = [nc.dram_tensor("agin%d" % c, [128, SEQ], BF16) for c in range(4)]
    agout = [nc.dram_tensor("agout%d" % c, [512, SEQ], BF16) for c in range(4)]
    tab_d = [nc.dram_tensor("tab%d" % j, [128, TABW], F32) for j in range(2)]

    with ExitStack() as es:
        S = Sched(nc, es)

        def sbt(name, shape, dt=F32):
            return es.enter_context(nc.sbuf_tensor("s_" + name, list(shape), dt))

        W = sbt("W", [128, 16, NCOL], BF16)
        tW = S.tile("W")
        xb = [sbt("xb%d" % i, [128, NCOL]) for i in range(2)]
        txb = [S.tile("xb%d" % i) for i in range(2)]
        pvec = sbt("pvec", [128, PV_N])
        pder = sbt("pder", [128, 8])
        ident = sbt("ident", [128, 128], BF16)
        ones_bf = sbt("ones_bf", [128, 128], BF16)
        ones_f = sbt("ones_f", [128, 128])
        blockones = sbt("blockones", [128, 128])
        maskT = sbt("maskT", [128, 512])
        masklow = sbt("masklow", [128, 128])
        lora32 = sbt("lora32", [96, 2, 256])
        lora = sbt("lora", [96, 2, 256], BF16)
        bext = [sbt("bext%d" % j, [128, 384]) for j in range(2)]
        bcol = sbt("bcol", [128, 4])
        tC = S.tile("consts")
        arena_w = 28100
        arena = sbt("arena", [128, arena_w])
        apos = [0]

        def carve(nbytes, dt, shape_str=None, **kw):
            n32 = (nbytes + 3) // 4
            a = apos[0]
            apos[0] += n32
            assert apos[0] <= arena_w, apos[0]
            ap = arena[:, a:a + n32]
            if dt != F32:
                ap = ap.bitcast(dt)
            if shape_str:
                ap = ap.rearrange(shape_str, **kw)
            return ap

        ps = [es.enter_context(nc.psum_tensor("ps%d" % i, [128, 512], F32)) for i in range(7)]
        tps = [S.tile("ps%d" % i) for i in range(7)]
        psb = es.enter_context(nc.psum_tensor("psb", [128, 1024], BF16))
        tpsb = S.tile("psb")

        KT = carve(2 * P_TOK * 2, BF16, "p (h n) -> p h n", h=2)
        tKT = S.tile("KT")
        Vt = carve(NB * 256 * 2, BF16, "p (b n) -> p b n", b=NB)
        tV = S.tile("V")
        xn = carve(D * 2, BF16)
        txn = S.tile("xn")
        hT = carve(16 * 256 * 2, BF16, "p (c n) -> p c n", c=16)
        thT = S.tile("hT")
        QT = carve(2 * 256 * 2, BF16, "p (h n) -> p h n", h=2)
        tQT = S.tile("QT")
        gate_a = carve(2 * 256 * 2, BF16, "p (h n) -> p h n", h=2)
        tga = S.tile("gate_a")
        gate_r = carve(2 * 256 * 2, BF16, "p (h n) -> p h n", h=2)
        tgr = S.tile("gate_r")
        zb = [carve(257 * 4, F32) for _ in range(8)]
        tzb = [S.tile("z%d" % i) for i in range(8)]
        NWK = 12
        wk = [carve(256 * 4, F32) for _ in range(NWK)]
        twk = [S.tile("wk%d" % i) for i in range(NWK)]
        bonus = [carve(256 * 4, F32) for _ in range(2)]
        tbonus = [S.tile("bonus%d" % i) for i in range(2)]
        yT = [wk[1], wk[5]]
        tyT = [twk[1], twk[5]]
        twb = carve(256 * 2, BF16)
        adb = carve(256 * 2, BF16)
        ttwb = S.tile("twb")
        tadb = S.tile("adb")
        arT = [carve(2 * 256 * 2, BF16, "p (a n) -> p a n", a=2) for _ in range(2)]
        tarT = [S.tile("arT%d" % i) for i in range(2)]
        ktl = [carve(256 * 2, BF16) for _ in range(2)]
        btl = [carve(256 * 2, BF16) for _ in range(2)]
        tktl = [S.tile("ktl%d" % i) for i in range(2)]
        tbtl = [S.tile("btl%d" % i) for i in range(2)]
        fm3 = [carve(256 * 2, BF16) for _ in range(3)]
        tfm3 = [S.tile("fm3_%d" % i) for i in range(3)]
        tok3 = [carve(2 * 256 * 2, BF16, "p (c n) -> p c n", c=2) for _ in range(3)]
        ttok3 = [S.tile("tok3_%d" % i) for i in range(3)]
        WC = carve(4 * 4, F32)
        tWC = S.tile("WC")
        nbias = carve(4 * 4, F32)
        tnbias = S.tile("nbias")
        AT = [carve(4 * 128 * 2, BF16, "p (a n) -> p a n", a=4) for _ in range(4)]
        tAT = [S.tile("AT%d" % i) for i in range(4)]
        Mc = carve(4 * 128 * 2, BF16, "p (h n) -> p h n", h=4)
        Mtc = carve(4 * 128 * 2, BF16, "p (h n) -> p h n", h=4)
        Ptc = carve(4 * 128 * 2, BF16, "p (h n) -> p h n", h=4)
        tMc = S.tile("Mc")
        tMtc = S.tile("Mtc")
        tPtc = S.tile("Ptc")
        btlm = [[carve(256 * 2, BF16) for _ in range(2)] for _ in range(2)]
        ktlm = [[carve(256 * 2, BF16) for _ in range(2)] for _ in range(2)]
        tbtlm = [[S.tile("btlm%d%d" % (i, j)) for j in range(2)] for i in range(2)]
        tktlm = [[S.tile("ktlm%d%d" % (i, j)) for j in range(2)] for i in range(2)]
        Vz = carve(2 * 4 * 128 * 2, BF16, "p (c h n) -> p c h n", c=2, h=4)
        tVz = S.tile("Vz")
        Uz = carve(4 * 128 * 2, BF16, "p (h n) -> p h n", h=4)
        tUz = S.tile("Uz")
        Xsb = carve(256 * 2, BF16, "p (h n) -> p h n", h=4)
        Usb = carve(256 * 2, BF16, "p (h n) -> p h n", h=4)
        tXsb = S.tile("Xsb")
        tUsb = S.tile("Usb")
        S32 = carve(2 * 128 * 4, F32, "p (a n) -> p a n", a=2)
        Sbf = carve(2 * 128 * 2, BF16, "p (a n) -> p a n", a=2)
        tS32 = S.tile("S32")
        tSbf = S.tile("Sbf")
        Ssb = [wk[0], wk[2]]
        tSsb = [twk[0], twk[2]]
        Eb = [carve(2 * 256 * 2, BF16, "p (m n) -> p m n", m=2) for _ in range(2)]
        tEb = [S.tile("Eb%d" % i) for i in range(2)]
        oT = [carve(4 * 256 * 2, BF16, "p (c n) -> p c n", c=4) for _ in range(2)]
        toT = [S.tile("oT%d" % i) for i in range(2)]
        stats = carve(4 * 6 * 4, F32, "p (a n) -> p a n", a=4)
        mv = carve(8 * 4, F32)
        tst = S.tile("stats")
        lamw = carve(4 * 64 * 4, F32, "p (a n) -> p a n", a=4)
        tabsb = carve(TABW * 4, F32)
        ttab = S.tile("tabsb")
        main_end = apos[0]
        print("arena main words", main_end, "of", arena_w)

        tagin = [S.tile("agin%d" % c) for c in range(4)]
        tagout = [S.tile("agout%d" % c) for c in range(4)]
        ttabd = [S.tile("tabd%d" % j) for j in range(2)]

        try:
            def cload(dst, src):
                S.dma(lambda e: e.dma_start(out=dst, in_=src), writes=[tC], key="consts")

            cload(pvec[:], pvec_d)
            cload(ident[:], ident_d)
            cload(ones_bf[:], onesbf_d)
            cload(ones_f[:], onesf_d)
            cload(blockones[:], bo_d)
            cload(maskT[:], maskT_d)
            cload(masklow[:], masklow_d)
            cload(lora32[:], lora_d)
            relb = carve(2 * 128 * 4, F32, "p (a n) -> p a n", a=2)
            ohsb = carve(TABW * 4, F32)
            cload(relb[0:33], relb_d)
            cload(ohsb[0:33], oh_d)
            lam_src = bass.AP(tensor=lam4_d.tensor, offset=0, ap=[[0, 128], [64, 4], [1, 64]])
            cload(lamw, lam_src)

            S.op("dve", lambda e: e.tensor_copy(out=lora[:], in_=lora32[:]), reads=[tC], writes=[tC])
            S.op("dve", lambda e: e.tensor_scalar(out=pder[:, 0:2], in0=pvec[:, PV_KA:PV_KA + 2], scalar1=-1.0, scalar2=1.0,
                                                  op0=ALU.mult, op1=ALU.add), reads=[tC], writes=[tC])
            S.op("dve", lambda e: e.tensor_scalar(out=pder[:, 2:3], in0=pvec[:, PV_SUBLN:PV_SUBLN + 1], scalar1=0.8, scalar2=None,
                                                  op0=ALU.mult), reads=[tC], writes=[tC])
            S.op("dve", lambda e: e.tensor_tensor(out=lamw[:, 0, :], in0=lamw[:, 0, :], in1=lamw[:, 1, :], op=ALU.mult),
                 reads=[tC], writes=[tC])
            S.op("dve", lambda e: e.tensor_tensor(out=lamw[:, 2, :], in0=lamw[:, 2, :], in1=lamw[:, 3, :], op=ALU.mult),
                 reads=[tC], writes=[tC])
            S.op("dve", lambda e: e.reduce_sum(out=pder[:, 4:5], in_=lamw[:, 0, :], axis=mybir.AxisListType.X),
                 reads=[tC], writes=[tC])
            S.op("dve", lambda e: e.reduce_sum(out=pder[:, 5:6], in_=lamw[:, 2, :], axis=mybir.AxisListType.X),
                 reads=[tC], writes=[tC])
            S.op("act", lambda e: e.activation(out=pder[:, 4:6], in_=pder[:, 4:6], func=AF.Exp), reads=[tC], writes=[tC])
            S.op("dve", lambda e: e.tensor_tensor(out=pder[:, 3:4], in0=pder[:, 5:6], in1=pder[:, 4:5], op=ALU.subtract),
                 reads=[tC], writes=[tC])
            S.op("dve", lambda e: e.tensor_scalar(out=pder[:, 3:4], in0=pder[:, 3:4], scalar1=-0.2, scalar2=None, op0=ALU.add),
                 reads=[tC], writes=[tC])
            S.op("pool", lambda e: e.memset(S32.rearrange("p a n -> p (a n)"), 0.0), writes=[tS32])
            S.op("pool", lambda e: e.memset(Sbf.rearrange("p a n -> p (a n)"), 0.0), writes=[tSbf])
            S.op("pool", lambda e: e.memset(Vz.rearrange("p c h n -> p (c h n)"), 0.0), writes=[tVz])
            S.op("pool", lambda e: e.memset(Uz.rearrange("p h n -> p (h n)"), 0.0), writes=[tUz])
            for i in range(8):
                S.op("pool", lambda e, i=i: e.memset(zb[i], 0.0), writes=[tzb[i]])

            for j in range(2):
                S.op("pe", lambda e, j=j: e.matmul(ps[0][:, 0:256], lhsT=relb[0:33, j, :], rhs=ohsb[0:33, 0:256], start=True, stop=True),
                     reads=[tC], writes=[tps[0]])
                S.op("pe", lambda e, j=j: e.matmul(ps[1][:, 0:255], lhsT=relb[0:33, j, :], rhs=ohsb[0:33, 256:511], start=True, stop=True),
                     reads=[tC], writes=[tps[1]])
                S.op("dve", lambda e: e.tensor_copy(out=tabsb[:, 0:256], in_=ps[0][:, 0:256]), reads=[tps[0]], writes=[ttab])
                S.op("dve", lambda e: e.tensor_copy(out=tabsb[:, 256:511], in_=ps[1][:, 0:255]), reads=[tps[1]], writes=[ttab])
                S.op("dve", lambda e, j=j: e.tensor_copy(out=bcol[:, j:j + 1], in_=tabsb[:, 510:511]), reads=[ttab], writes=[tC])
                S.op("dve", lambda e, j=j: e.tensor_copy(out=bcol[:, 2 + j:3 + j], in_=tabsb[:, 510:511]), reads=[ttab], writes=[tC])
                S.op("dve", lambda e, j=j: e.memset(bcol[0:112, 2 + j:3 + j], NEG), reads=[], writes=[tC])
                S.dma(lambda e, j=j: e.dma_start(out=tab_d[j].ap(), in_=tabsb), reads=[ttab], writes=[ttabd[j]], key="tabst")
                src = bass.AP(tensor=tab_d[j].ap().tensor, offset=127, ap=[[TABW - 1, 128], [1, 384]])
                S.dma(lambda e, j=j, src=src: e.dma_start(out=bext[j][:], in_=src), reads=[ttabd[j]], writes=[tC], key="consts")

            for kc in range(16):
                b = kc % 2
                S.dma(lambda e, kc=kc, b=b: e.dma_start(out=xb[b][:], in_=win_d[kc * 128:(kc + 1) * 128, :]),
                      writes=[txb[b]], key="xb%d" % b)
                eng = ("dve", "pool", "act")[kc % 3]
                if eng == "act":
                    S.op("act", lambda e, kc=kc, b=b: e.activation(out=W[:, kc, :], in_=xb[b][:], func=AF.Copy),
                         reads=[txb[b]], writes=[tW])
                else:
                    S.op(eng, lambda e, kc=kc, b=b: e.tensor_copy(out=W[:, kc, :], in_=xb[b][:]), reads=[txb[b]], writes=[tW])

            chk('setup')
            rr = {"ps": 0, "ev": 0, "xb": 0, "oT": 0}

            def next_ps(n=4):
                i = rr["ps"] % n
                rr["ps"] += 1
                return i

            def evac_copy(out, in_, reads, writes, bf=False):
                k = 1
                if k == 0:
                    S.op("dve", lambda e: e.tensor_scalar(out=out, in0=in_, scalar1=1.0, scalar2=None, op0=ALU.mult), reads=reads, writes=writes)
                else:
                    S.op("act", lambda e: e.activation(out=out, in_=in_, func=AF.Copy), reads=reads, writes=writes)

            def tt(eng, out, a, b, op, reads, writes):
                S.op(eng, lambda e: e.tensor_tensor(out=out, in0=a, in1=b, op=op), reads=reads, writes=writes)

            def ts(eng, out, a, s1, s2, op0, op1, reads, writes):
                if s2 is None:
                    S.op(eng, lambda e: e.tensor_scalar(out=out, in0=a, scalar1=s1, scalar2=None, op0=op0), reads=reads, writes=writes)
                else:
                    S.op(eng, lambda e: e.tensor_scalar(out=out, in0=a, scalar1=s1, scalar2=s2, op0=op0, op1=op1),
                         reads=reads, writes=writes)

            def stt(out, a, sc, b, op0, op1, reads, writes):
                S.op("dve", lambda e: e.scalar_tensor_tensor(out=out, in0=a, scalar=sc, in1=b, op0=op0, op1=op1),
                     reads=reads, writes=writes)

            def act(out, in_, func, reads, writes, bias=0.0, scale=1.0):
                S.op("act", lambda e: e.activation(out=out, in_=in_, func=func, bias=bias, scale=scale), reads=reads, writes=writes)

            def rsqrt_inplace(t, tt_, n, scale, eps):
                act(t[:, 0:n], t[:, 0:n], AF.Ln, [tt_], [tt_], bias=eps_ap(eps), scale=scale)
                act(t[:, 0:n], t[:, 0:n], AF.Exp, [tt_], [tt_], scale=-0.5)

            epsc = {}

            def eps_ap(v):
                if v == 0.0:
                    return 0.0
                if v not in epsc:
                    col = 6 + len(epsc)
                    S.op("pool", lambda e, col=col, v=v: e.memset(pder[:, col:col + 1], float(v)), writes=[tC])
                    epsc[v] = pder[:, col:col + 1]
                return epsc[v]

            eps_ap(1e-5)
            eps_ap(64e-5)

            def layer_norm_block(src_ap, is_meta, blk_slot):
                b = rr["xb"] % 2
                rr["xb"] += 1
                xt = xb[b]
                if is_meta:
                    S.op("pool", lambda e: e.memset(xt[:, 0:D], 0.0), writes=[txb[b]])
                    S.dma(lambda e: e.dma_start(out=xt[112:128, 0:D], in_=meta_d), writes=[txb[b]], key="xb%d" % b)
                else:
                    S.dma(lambda e: e.dma_start(out=xt[:, 0:D], in_=src_ap), writes=[txb[b]], key="xb%d" % b)
                for q in range(4):
                    S.op("dve", lambda e, q=q: e.bn_stats(out=stats[:, q, :], in_=xt[:, q * 512:(q + 1) * 512]),
                         reads=[txb[b]], writes=[tst])
                S.op("dve", lambda e: e.bn_aggr(out=mv[:, 0:2], in_=stats.rearrange("p a n -> p (a n)")), reads=[tst], writes=[tst])
                act(mv[:, 2:3], mv[:, 1:2], AF.Ln, [tst], [tst], bias=eps_ap(1e-5))
                act(mv[:, 2:3], mv[:, 2:3], AF.Exp, [tst], [tst], scale=-0.5)
                stt(mv[:, 3:4], mv[:, 0:1], -1.0, mv[:, 2:3], ALU.mult, ALU.mult, [tst], [tst])
                S.op("act", lambda e: e.activation(out=xn, in_=xt[:, 0:D], func=AF.Identity, bias=mv[:, 3:4], scale=mv[:, 2:3]),
                     reads=[txb[b], tst], writes=[txn])
                for half in range(2):
                    for dc in range(8):
                        c = half * 8 + dc
                        S.op("pe", lambda e, c=c, dc=dc: e.transpose(psb[:, dc * 128:(dc + 1) * 128], xn[:, c * 128:(c + 1) * 128], ident[:]),
                             reads=[txn, tC], writes=[tpsb])
                    for dc in range(8):
                        c = half * 8 + dc
                        dst = hT[:, c, blk_slot * 128:(blk_slot + 1) * 128]
                        src = psb[:, dc * 128:(dc + 1) * 128]
                        if dc % 2 == 0:
                            ts("dve", dst, src, pvec[:, PV_LNG + c:PV_LNG + c + 1], pvec[:, PV_LNB + c:PV_LNB + c + 1],
                               ALU.mult, ALU.add, [tpsb, tC], [thT])
                        else:
                            S.op("act", lambda e, dst=dst, src=src, c=c: e.activation(
                                out=dst, in_=src, func=AF.Identity, bias=pvec[:, PV_LNB + c:PV_LNB + c + 1],
                                scale=pvec[:, PV_LNG + c:PV_LNG + c + 1]), reads=[tpsb, tC], writes=[thT])
                if is_meta:
                    S.op("pool", lambda e: e.memset(hT[:, :, 0:112], 0.0), writes=[thT])

            def inproj_fm(ct_off, m, nt, sink):
                i = next_ps()
                for kc in range(16):
                    S.op("pe", lambda e, kc=kc, i=i: e.matmul(ps[i][0:m, 0:nt], lhsT=W[:, kc, ct_off:ct_off + m], rhs=hT[:, kc, 0:nt],
                                                            start=(kc == 0), stop=(kc == 15)), reads=[tW, thT], writes=[tps[i]])
                sink(ps[i][0:m, 0:nt], tps[i])

            tiles = [(-1, 128)] + [(i, 256) for i in range(16)]
            for (ti, nt) in tiles:
                nblk = nt // 128
                blk0 = 0 if ti < 0 else 1 + 2 * ti
                pos0 = blk0 * 128
                for j in range(nblk):
                    if ti < 0:
                        layer_norm_block(None, True, 0)
                    else:
                        r0 = ti * 256 + j * 128
                        layer_norm_block(x_d[r0:r0 + 128, :], False, j)
                if ti >= 0:
                    for h in range(2):
                        inproj_fm(h * 128, 128, nt, lambda p, tp, h=h: evac_copy(QT[:, h, 0:nt], p, [tp], [tQT]))
                for h in range(2):
                    inproj_fm(256 + h * 128, 128, nt, lambda p, tp, h=h: evac_copy(KT[:, h, pos0:pos0 + nt], p, [tp], [tKT]))
                if ti >= 0:
                    for h in range(2):
                        inproj_fm(512 + h * 128, 128, nt, lambda p, tp, h=h: act(gate_a[:, h, 0:nt], p, AF.Silu, [tp], [tga]))
                    for p_ in range(2):
                        inproj_fm(1536 + p_ * 128, 128, nt, lambda p, tp, p_=p_: act(gate_r[:, p_, 0:nt], p, AF.Silu, [tp], [tgr]))
                for zi in range(6):
                    inproj_fm(768 + zi * 128, 128, nt, lambda p, tp, zi=zi: evac_copy(zb[zi][:, 1:1 + nt], p, [tp], [tzb[zi]]))
                for zi in range(2):
                    inproj_fm(1792 + zi * 96, 96, nt, lambda p, tp, zi=zi: evac_copy(zb[6 + zi][0:96, 1:1 + nt], p, [tp], [tzb[6 + zi]]))
                for j in range(nblk):
                    i = next_ps()
                    for kc in range(16):
                        S.op("pe", lambda e, kc=kc, i=i, j=j: e.matmul(ps[i][:, 0:256], lhsT=hT[:, kc, j * 128:(j + 1) * 128],
                                                                     rhs=W[:, kc, 1984:2240], start=(kc == 0), stop=(kc == 15)),
                             reads=[tW, thT], writes=[tps[i]])
                    evac_copy(Vt[:, blk0 + j, :], ps[i][:, 0:256], [tps[i]], [tV])

                chk('inproj%d' % ti)
                r32, k32, v32, t1, t2, sg, aic, kkn, k2, bvec, cs, t3 = wk
                tr32, tk32, tv32, tt1, tt2, tsg, taic, tkkn, tk2, tbvec, tcs, tt3 = twk
                n = nt

                def shift(zi, mucol, out, tout, rows=128):
                    z = zb[zi]
                    tz = tzb[zi]
                    tt("pool", t3[0:rows, 0:n], z[0:rows, 0:n], z[0:rows, 1:1 + n], ALU.subtract, [tz], [tt3])
                    stt(out[0:rows, 0:n], t3[0:rows, 0:n], pvec[0:rows, mucol:mucol + 1], z[0:rows, 1:1 + n], ALU.mult, ALU.add,
                        [tt3, tz, tC], [tout])
                    S.op("pool", lambda e: e.tensor_copy(out=z[0:rows, 0:1], in_=z[0:rows, n:n + 1]), reads=[tz], writes=[tz])

                shift(6, PV_MUWD, t1, tt1, rows=96)
                act(twb[0:96, 0:n], t1[0:96, 0:n], AF.Tanh, [tt1], [ttwb])
                shift(7, PV_MUAD, t2, tt2, rows=96)
                S.op("dve", lambda e: e.tensor_scalar(out=adb[0:96, 0:n], in0=t2[0:96, 0:n], scalar1=1.0, scalar2=None, op0=ALU.mult), reads=[tt2], writes=[tadb])

                chk('pa%d' % ti)
                import os as _os
                for p in [int(v) for v in _os.environ.get('PAIRS', '0,1').split(',')]:
                    shift(0 + p, PV_MU + 0 + p, r32, tr32)
                    shift(2 + p, PV_MU + 2 + p, k32, tk32)
                    shift(4 + p, PV_MU + 4 + p, v32, tv32)
                    S.op("pe", lambda e, p=p: e.matmul(ps[4][:, 0:n], lhsT=lora[0:96, 0, p * 128:(p + 1) * 128], rhs=twb[0:96, 0:n],
                                                       start=True, stop=True), reads=[tC, ttwb], writes=[tps[4]])
                    act(sg[:, 0:n], ps[4][:, 0:n], AF.Sigmoid, [tps[4], tC], [tsg], bias=pvec[:, PV_W0 + p:PV_W0 + p + 1])
                    S.op("pe", lambda e, p=p: e.matmul(ps[5][:, 0:n], lhsT=lora[0:96, 1, p * 128:(p + 1) * 128], rhs=adb[0:96, 0:n],
                                                       start=True, stop=True), reads=[tC, tadb], writes=[tps[5]])
                    act(aic[:, 0:n], ps[5][:, 0:n], AF.Sigmoid, [tps[5], tC], [taic], bias=pvec[:, PV_A0 + p:PV_A0 + p + 1])
                    chk('pb%d' % ti)
                    ts("dve", t1[:, 0:n], k32[:, 0:n], pvec[:, PV_KK + p:PV_KK + p + 1], None, ALU.mult, None, [tk32, tC], [tt1])
                    tt("pool", t2[:, 0:n], t1[:, 0:n], t1[:, 0:n], ALU.mult, [tt1], [tt2])
                    S.op("pe", lambda e: e.matmul(ps[6][:, 0:n], lhsT=blockones[:], rhs=t2[:, 0:n], start=True, stop=True),
                         reads=[tC, tt2], writes=[tps[6]])
                    ts("dve", t2[:, 0:n], ps[6][:, 0:n], 1e-24, None, ALU.max, None, [tps[6]], [tt2])
                    rsqrt_inplace(t2, tt2, n, 1.0, 0.0)
                    tt("dve", kkn[:, 0:n], t1[:, 0:n], t2[:, 0:n], ALU.mult, [tt1, tt2], [tkkn])
                    chk('pc%d' % ti)
                    ts("dve", t1[:, 0:n], aic[:, 0:n], pvec[:, PV_KA + p:PV_KA + p + 1], pder[:, p:p + 1], ALU.mult, ALU.add,
                       [taic, tC], [tt1])
                    tt("pool", k2[:, 0:n], k32[:, 0:n], t1[:, 0:n], ALU.mult, [tk32, tt1], [tk2])
                    tt("pool", bvec[:, 0:n], kkn[:, 0:n], aic[:, 0:n], ALU.mult, [tkkn, taic], [tbvec])
                    chk('pd%d' % ti)
                    stt(t1[:, 0:n], r32[:, 0:n], pvec[:, PV_RK + p:PV_RK + p + 1], k2[:, 0:n], ALU.mult, ALU.mult, [tr32, tk2, tC], [tt1])
                    S.op("pe", lambda e: e.matmul(ps[4][:, 0:n], lhsT=blockones[:], rhs=t1[:, 0:n], start=True, stop=True),
                         reads=[tC, tt1], writes=[tps[4]])
                    tt("dve", bonus[p][:, 0:n], ps[4][:, 0:n], v32[:, 0:n], ALU.mult, [tps[4], tv32], [tbonus[p]])
                    chk('pe%d' % ti)
                    for c in range(nblk):
                        S.op("dve", lambda e, c=c: e.tensor_tensor_scan(out=cs[:, c * 128:(c + 1) * 128], data0=ones_f[:, 0:128],
                                                                     data1=sg[:, c * 128:(c + 1) * 128], initial=0.0,
                                                                     op0=ALU.mult, op1=ALU.add), reads=[tsg, tC], writes=[tcs])
                    chk('pf%d' % ti)
                    act(t1[:, 0:n], cs[:, 0:n], AF.Exp, [tcs], [tt1], scale=-C0)
                    tt("dve", arT[p][:, 1, 0:n], r32[:, 0:n], t1[:, 0:n], ALU.mult, [tr32, tt1], [tarT[p]])
                    for c in range(nblk):
                        S.op("pool", lambda e, c=c, p=p: e.tensor_copy(out=WC[:, p * 2 + c:p * 2 + c + 1], in_=t1[:, c * 128 + 127:c * 128 + 128]),
                             reads=[tt1], writes=[tWC])
                        ts("dve", nbias[:, c:c + 1], cs[:, c * 128 + 127:c * 128 + 128], -C0, None, ALU.mult, None, [tcs], [tnbias])
                    tt("pool", t2[:, 0:n], cs[:, 0:n], sg[:, 0:n], ALU.subtract, [tcs, tsg], [tt2])
                    act(t2[:, 0:n], t2[:, 0:n], AF.Exp, [tt2], [tt2], scale=-C0)
                    stt(arT[p][:, 0, 0:n], kkn[:, 0:n], -1.0, t2[:, 0:n], ALU.mult, ALU.mult, [tkkn, tt2], [tarT[p]])
                    act(t1[:, 0:n], cs[:, 0:n], AF.Exp, [tcs], [tt1], scale=C0)
                    tt("dve", ktl[p][:, 0:n], k2[:, 0:n], t1[:, 0:n], ALU.mult, [tk2, tt1], [tktl[p]])
                    tt("pool", btl[p][:, 0:n], bvec[:, 0:n], t1[:, 0:n], ALU.mult, [tbvec, tt1], [tbtl[p]])
                    for hh in ([] if _os.environ.get('NOMASK') else range(2)):
                        hm = blockones[:, hh * 64:hh * 64 + 1]
                        ts("dve" if hh == 0 else "pool", btlm[p][hh][:, 0:n], btl[p][:, 0:n], hm, None, ALU.mult, None, [tbtl[p], tC], [tbtlm[p][hh]])
                        ts("pool" if hh == 0 else "dve", ktlm[p][hh][:, 0:n], ktl[p][:, 0:n], hm, None, ALU.mult, None, [tktl[p], tC], [tktlm[p][hh]])
                    for c in range(nblk):
                        S.op("act", lambda e, c=c: e.activation(out=t2[:, c * 128:(c + 1) * 128], in_=cs[:, c * 128:(c + 1) * 128],
                                                                 func=AF.Exp, bias=nbias[:, c:c + 1], scale=C0),
                             reads=[tcs, tnbias], writes=[tt2])
                    chk('pg%d' % ti)
                    tt("dve", fm3[0][:, 0:n], k2[:, 0:n], t2[:, 0:n], ALU.mult, [tk2, tt2], [tfm3[0]])
                    tt("pool", fm3[1][:, 0:n], bvec[:, 0:n], t2[:, 0:n], ALU.mult, [tbvec, tt2], [tfm3[1]])
                    S.op("act", lambda e: e.activation(out=fm3[2][:, 0:n], in_=v32[:, 0:n], func=AF.Copy), reads=[tv32], writes=[tfm3[2]])
                    chk('ph%d' % ti)
                    for c in range(nblk):
                        for q in range(3):
                            S.op("pe", lambda e, c=c, q=q: e.matmul(ps[4 + q][:, 0:128], lhsT=fm3[q][:, c * 128:(c + 1) * 128], rhs=ident[:],
                                                                    start=True, stop=True), reads=[tfm3[q], tC], writes=[tps[4 + q]])
                        chk('pt%d' % ti)
                        for q in range(3):
                            evac_copy(tok3[q][:, c, p * 128:(p + 1) * 128], ps[4 + q][:, 0:128], [tps[4 + q]], [ttok3[q]])
                        for hh in ([] if _os.environ.get('NOVZ') else range(2)):
                            S.op('act', lambda e: e.activation(out=Vz[:, c, 2 * p + hh, hh * 64:(hh + 1) * 64], in_=ps[6][:, hh * 64:(hh + 1) * 64], func=AF.Copy), reads=[tps[6]], writes=[tVz])

                    chk('pi%d' % ti)
                chk('rwpre%d' % ti)
                Khat, Bhat, Vtok = tok3
                tKhat, tBhat, tVtok = ttok3
                for c in range(nblk):
                    cs_ = slice(c * 128, (c + 1) * 128)
                    for h in range(4):
                        p, hh = h // 2, h % 2
                        i = h % 2
                        S.op("pe", lambda e: e.matmul(ps[i][:, 0:256], lhsT=btlm[p][hh][:, cs_], rhs=arT[p][:, :, cs_], start=True, stop=True),
                             reads=[tbtlm[p][hh], tarT[p]], writes=[tps[i]])
                        S.op("pe", lambda e: e.matmul(ps[i][:, 256:512], lhsT=ktlm[p][hh][:, cs_], rhs=arT[p][:, :, cs_], start=True, stop=True),
                             reads=[tktlm[p][hh], tarT[p]], writes=[tps[i]])
                        chk('c0%d' % ti)
                        tt("dve", AT[h].rearrange("p a n -> p (a n)"), ps[i][:, :], maskT[:], ALU.mult, [tps[i], tC], [tAT[h]])
                        chk('c1%d' % ti)
                        S.op("pe", lambda e: e.matmul(ps[2][:, h * 128:(h + 1) * 128], lhsT=arT[p][:, 0, cs_], rhs=btlm[p][hh][:, cs_], start=True, stop=True),
                             reads=[tbtlm[p][hh], tarT[p]], writes=[tps[2]])
                    chk('ca%d' % ti)
                    for h in range(4):
                        tt("dve", Mc[:, h, :], ps[2][:, h * 128:(h + 1) * 128], masklow[:], ALU.mult, [tps[2], tC], [tMc])
                        S.op("pool", lambda e: e.tensor_copy(out=Mtc[:, h, :], in_=AT[h][:, 0, :]), reads=[tAT[h]], writes=[tMtc])
                        tt("pool", Ptc[:, h, :], AT[h][:, 0, :], ident[:], ALU.add, [tAT[h], tC], [tPtc])
                    chk('cb%d' % ti)
                    for lvl in range(7):
                        if lvl >= 1:
                            for h in range(4):
                                S.op("pe", lambda e: e.matmul(ps[3][:, h * 128:(h + 1) * 128], lhsT=Mc[:, h, :], rhs=Ptc[:, h, :], start=True, stop=True),
                                     reads=[tMc, tPtc], writes=[tps[3]])
                        if lvl < 6:
                            for h in range(4):
                                S.op("pe", lambda e: e.matmul(ps[4][:, h * 128:(h + 1) * 128], lhsT=Mtc[:, h, :], rhs=Mc[:, h, :], start=True, stop=True),
                                     reads=[tMc, tMtc], writes=[tps[4]])
                            if lvl < 5:
                                for h in range(4):
                                    S.op("pe", lambda e: e.matmul(ps[5][:, h * 128:(h + 1) * 128], lhsT=Mc[:, h, :], rhs=Mtc[:, h, :], start=True, stop=True),
                                         reads=[tMc, tMtc], writes=[tps[5]])
                        if lvl >= 1:
                            tt("dve", Ptc.rearrange("p h n -> p (h n)"), ps[3][:, :], Ptc.rearrange("p h n -> p (h n)"), ALU.add, [tps[3], tPtc], [tPtc])
                        if lvl < 6:
                            S.op("act", lambda e: e.activation(out=Mc.rearrange("p h n -> p (h n)"), in_=ps[4][:, :], func=AF.Copy), reads=[tps[4]], writes=[tMc])
                            if lvl < 5:
                                S.op("act", lambda e: e.activation(out=Mtc.rearrange("p h n -> p (h n)"), in_=ps[5][:, :], func=AF.Copy), reads=[tps[5]], writes=[tMtc])
                    chk('cc%d' % ti)
                    for p in range(2):
                        S.op("pe", lambda e: e.matmul(ps[0][:, p * 128:(p + 1) * 128], lhsT=arT[p][:, 0, cs_], rhs=Sbf[:, p, :], start=True, stop=False),
                             reads=[tarT[p], tSbf], writes=[tps[0]])
                        for hh in range(2):
                            h = 2 * p + hh
                            S.op("pe", lambda e: e.matmul(ps[0][:, h * 64:(h + 1) * 64], lhsT=AT[h][:, 2, :], rhs=Vtok[:, c, h * 64:(h + 1) * 64],
                                                          start=False, stop=(hh == 1)), reads=[tAT[h], tVtok], writes=[tps[0]])
                    S.op("act", lambda e: e.activation(out=Xsb.rearrange("p h n -> p (h n)"), in_=ps[0][:, 0:256], func=AF.Copy), reads=[tps[0]], writes=[tXsb])
                    for h in range(4):
                        S.op("pe", lambda e: e.matmul(ps[1][:, h * 64:(h + 1) * 64], lhsT=Ptc[:, h, :], rhs=Xsb[:, h, :], start=True, stop=True),
                             reads=[tPtc, tXsb], writes=[tps[1]])
                    S.op("act", lambda e: e.activation(out=Usb.rearrange("p h n -> p (h n)"), in_=ps[1][:, 0:256], func=AF.Copy), reads=[tps[1]], writes=[tUsb])
                    for hh in range(2):
                        src = ps[1][:, 0:256].rearrange("p (a b n) -> p a b n", a=2, b=2)[:, :, hh, :]
                        dst = Uz.rearrange("p (a b) n -> p a b n", b=2)[:, :, hh, hh * 64:(hh + 1) * 64]
                        S.op("act", lambda e: e.activation(out=dst, in_=src, func=AF.Copy), reads=[tps[1]], writes=[tUz])
                    chk('cd%d' % ti)
                    for p in range(2):
                        S.op("pe", lambda e: e.matmul(ps[2 + p][:, 0:128], lhsT=Sbf[:, p, :], rhs=arT[p][:, 1, cs_], start=True, stop=False),
                             reads=[tSbf, tarT[p]], writes=[tps[2 + p]])
                        for hh in range(2):
                            h = 2 * p + hh
                            S.op("pe", lambda e: e.matmul(ps[2 + p][:, 0:128], lhsT=Uz[:, h, :], rhs=AT[h][:, 1, :], start=False, stop=False),
                                 reads=[tUz, tAT[h]], writes=[tps[2 + p]])
                            S.op("pe", lambda e: e.matmul(ps[2 + p][:, 0:128], lhsT=Vz[:, c, h, :], rhs=AT[h][:, 3, :], start=False, stop=(hh == 1)),
                                 reads=[tVz, tAT[h]], writes=[tps[2 + p]])
                        if ti >= 0:
                            evac_copy(yT[p][:, cs_], ps[2 + p][:, 0:128], [tps[2 + p]], [tyT[p]])
                    for p in range(2):
                        S.op("pe", lambda e: e.matmul(ps[4 + p][:, 0:128], lhsT=Bhat[:, c, p * 128:(p + 1) * 128],
                                                      rhs=Usb.rearrange("p h n -> p (h n)")[:, p * 128:(p + 1) * 128], start=True, stop=False),
                             reads=[tBhat, tUsb], writes=[tps[4 + p]])
                        S.op("pe", lambda e: e.matmul(ps[4 + p][:, 0:128], lhsT=Khat[:, c, p * 128:(p + 1) * 128], rhs=Vtok[:, c, p * 128:(p + 1) * 128],
                                                      start=False, stop=True), reads=[tKhat, tVtok], writes=[tps[4 + p]])
                        tt("dve", t3[:, 0:128], ps[4 + p][:, 0:128], blockones[:], ALU.mult, [tps[4 + p], tC], [tt3])
                        stt(S32[:, p, :], S32[:, p, :], WC[:, p * 2 + c:p * 2 + c + 1], t3[:, 0:128], ALU.mult, ALU.add, [tS32, tWC, tt3], [tS32])
                    S.op("act", lambda e: e.activation(out=Sbf.rearrange("p a n -> p (a n)"), in_=S32.rearrange("p a n -> p (a n)"), func=AF.Copy),
                         reads=[tS32], writes=[tSbf])

                chk('chain%d' % ti)
                if ti < 0:
                    continue

                ob = rr["oT"] % 2
                rr["oT"] += 1
                oTt = oT[ob]
                toTt = toT[ob]

                for p in range(2):
                    y = yT[p]
                    S.op("pe", lambda e, y=y: e.matmul(ps[0][:, 0:n], lhsT=blockones[:], rhs=y[:, 0:n], start=True, stop=True),
                         reads=[tC, tyT[p]], writes=[tps[0]])
                    stt(t1[:, 0:n], ps[0][:, 0:n], -1.0 / 64.0, y[:, 0:n], ALU.mult, ALU.add, [tps[0], tyT[p]], [tt1])
                    tt("pool", t2[:, 0:n], t1[:, 0:n], t1[:, 0:n], ALU.mult, [tt1], [tt2])
                    S.op("pe", lambda e: e.matmul(ps[1][:, 0:n], lhsT=blockones[:], rhs=t2[:, 0:n], start=True, stop=True),
                         reads=[tC, tt2], writes=[tps[1]])
                    act(t2[:, 0:n], ps[1][:, 0:n], AF.Ln, [tps[1], tC], [tt2], bias=eps_ap(64e-5), scale=1.0 / 64.0)
                    act(t2[:, 0:n], t2[:, 0:n], AF.Exp, [tt2], [tt2], scale=-0.5)
                    tt("dve", t1[:, 0:n], t1[:, 0:n], t2[:, 0:n], ALU.mult, [tt1, tt2], [tt1])
                    ts("dve", t1[:, 0:n], t1[:, 0:n], pvec[:, PV_GNG + p:PV_GNG + p + 1], pvec[:, PV_GNB + p:PV_GNB + p + 1],
                       ALU.mult, ALU.add, [tt1, tC], [tt1])
                    tt("pool", t1[:, 0:n], t1[:, 0:n], bonus[p][:, 0:n], ALU.add, [tt1, tbonus[p]], [tt1])
                    tt("dve", oTt[:, 2 + p, 0:n], t1[:, 0:n], gate_r[:, p, 0:n], ALU.mult, [tt1, tgr], [toTt])

                chk('rwpost%d' % ti)
                qb0 = blk0
                OO = ps[4][:, :].rearrange("p (m n) -> p m n", m=2)
                SS = ps[5][:, :].rearrange("p (m n) -> p m n", m=2)
                for h in range(2):
                    kbs = list(range(0, qb0 + 2))

                    def geom(kb):
                        delta = kb - qb0
                        q_lo = 128 if delta == 1 else 0
                        return delta, q_lo, n - q_lo, kb % 2

                    def stage_qk(kb):
                        delta, q_lo, nq, par = geom(kb)
                        for m in range(2):
                            rs = slice(m * 64, m * 64 + 64)
                            pi = 2 * par + m
                            S.op("pe", lambda e: e.matmul(ps[pi][:, 0:nq], lhsT=KT[rs, h, kb * 128:(kb + 1) * 128], rhs=QT[rs, h, q_lo:n],
                                                          start=True, stop=True), reads=[tKT, tQT], writes=[tps[pi]])

                    def stage_rest(kb):
                        delta, q_lo, nq, par = geom(kb)
                        near = delta >= -1
                        E = Eb[par]
                        tE = tEb[par]
                        for m in range(2):
                            pi = 2 * par + m
                            if near:
                                boff = 0 if delta >= 0 else 128
                                stt(Ssb[m][:, 0:nq], ps[pi][:, 0:nq], 0.125, bext[h][:, boff:boff + nq], ALU.mult, ALU.add,
                                    [tps[pi], tC], [tSsb[m]])
                                if kb == 0:
                                    S.op("pool", lambda e: e.memset(Ssb[m][0:112, 0:nq], NEG), reads=[], writes=[tSsb[m]])
                                act(E[:, m, 0:nq], Ssb[m][:, 0:nq], AF.Exp, [tSsb[m]], [tE])
                            else:
                                bc = bcol[:, 2 + h:3 + h] if kb == 0 else bcol[:, h:h + 1]
                                act(E[:, m, 0:nq], ps[pi][:, 0:nq], AF.Exp, [tps[pi], tC], [tE], bias=bc, scale=0.125)
                        first = (kb == kbs[0])
                        last = (kb == kbs[-1])
                        if q_lo == 0:
                            Ef = E.rearrange("p m n -> p (m n)")
                            S.op("pe", lambda e: e.matmul(ps[4][:, :], lhsT=Vt[:, kb, h * 128:(h + 1) * 128], rhs=Ef, start=first, stop=last),
                                 reads=[tV, tE], writes=[tps[4]])
                            S.op("pe", lambda e: e.matmul(ps[5][:, :], lhsT=ones_bf[:], rhs=Ef, start=first, stop=last),
                                 reads=[tC, tE], writes=[tps[5]])
                        else:
                            for m in range(2):
                                S.op("pe", lambda e: e.matmul(OO[:, m, q_lo:n], lhsT=Vt[:, kb, h * 128:(h + 1) * 128], rhs=E[:, m, 0:nq],
                                                              start=first, stop=(last and m == 1)), reads=[tV, tE], writes=[tps[4]])
                                S.op("pe", lambda e: e.matmul(SS[:, m, q_lo:n], lhsT=ones_bf[:], rhs=E[:, m, 0:nq],
                                                              start=first, stop=(last and m == 1)), reads=[tC, tE], writes=[tps[5]])

                    def post_a():
                        S.op("dve", lambda e: e.reciprocal(out=t1[:, 0:n], in_=SS[:, 0, 0:n]), reads=[tps[5]], writes=[tt1])
                        S.op("dve", lambda e: e.reciprocal(out=t2[:, 0:n], in_=SS[:, 1, 0:n]), reads=[tps[5]], writes=[tt2])
                        tt("dve", t1[:, 0:n], OO[:, 0, 0:n], t1[:, 0:n], ALU.mult, [tps[4], tt1], [tt1])
                        tt("dve", t2[:, 0:n], OO[:, 1, 0:n], t2[:, 0:n], ALU.mult, [tps[4], tt2], [tt2])

                    def post_b(hp):
                        stt(t1[:, 0:n], t2[:, 0:n], pder[:, 3:4], t1[:, 0:n], ALU.mult, ALU.add, [tt1, tt2, tC], [tt1])
                        tt("pool", t2[:, 0:n], t1[:, 0:n], t1[:, 0:n], ALU.mult, [tt1], [tt2])
                        S.op("pe", lambda e: e.matmul(ps[6][:, 0:n], lhsT=ones_f[:], rhs=t2[:, 0:n], start=True, stop=True),
                             reads=[tC, tt2], writes=[tps[6]])
                        act(t2[:, 0:n], ps[6][:, 0:n], AF.Ln, [tps[6], tC], [tt2], bias=eps_ap(1e-5), scale=1.0 / 128.0)
                        act(t2[:, 0:n], t2[:, 0:n], AF.Exp, [tt2], [tt2], scale=-0.5)
                        tt("dve", t1[:, 0:n], t1[:, 0:n], t2[:, 0:n], ALU.mult, [tt1, tt2], [tt1])
                        stt(oTt[:, hp, 0:n], t1[:, 0:n], pder[:, 2:3], gate_a[:, hp, 0:n], ALU.mult, ALU.mult, [tt1, tga, tC], [toTt])

                    stage_qk(kbs[0])
                    for i_, kb in enumerate(kbs):
                        if i_ + 1 < len(kbs):
                            stage_qk(kbs[i_ + 1])
                        stage_rest(kb)
                        if h == 1 and i_ == min(1, len(kbs) - 1):
                            post_b(0)
                    post_a()
                    if h == 1:
                        post_b(1)

                chk('attn%d' % ti)
                for cc in range(4):
                    S.dma(lambda e: e.dma_start(out=agin[cc].ap()[:, ti * 256:ti * 256 + 256], in_=oTt[:, cc, 0:256]), reads=[toTt], writes=[tagin[cc]],
                          key="oT%d" % ob)
                if DEBUG:
                    dd = dbg_d.rearrange("(c p) n -> p c n", p=128)[:, :, ti * 256:ti * 256 + 256]
                    S.dma(lambda e, dd=dd, oTt=oTt: e.dma_start(out=dd, in_=oTt[:, :, 0:256]), reads=[toTt], key="oT%d" % ob)

            chk('main')
            for cc in range(4):
                S.dma(lambda e: e.collective_compute("AllGather", ALU.bypass, replica_groups=[[0, 1, 2, 3], [4, 5, 6, 7]],
                                                     ins=[agin[cc].ap()], outs=[agout[cc].ap()]),
                      reads=[tagin[cc]], writes=[tagout[cc]], key="cc%d" % cc, queue="pool", inc=1)
            chk('ag')
            S.barrier()
            apos[0] = 0
            oq = carve(16 * 1024 * 2, BF16, "p (c n) -> p c n", c=16)
            toq = S.tile("oq")
            gbt = carve(4 * D * 4, F32, "p (a n) -> p a n", a=4)
            tgb = S.tile("gbt")
            rsd = [carve(D * 4, F32) for _ in range(2)]
            trsd = [S.tile("rsd%d" % i) for i in range(2)]
            fstE = [carve(4 * 6 * 4, F32, "p (a n) -> p a n", a=4) for _ in range(2)]
            fmvE = [carve(8 * 4, F32) for _ in range(2)]
            tfE = [S.tile("fstE%d" % i) for i in range(2)]
            fstP = [carve(4 * 6 * 4, F32, "p (a n) -> p a n", a=4) for _ in range(2)]
            fmvP = [carve(8 * 4, F32) for _ in range(2)]
            tfP = [S.tile("fstP%d" % i) for i in range(2)]
            assert apos[0] <= arena_w

            for kc in range(16):
                b = kc % 2
                S.dma(lambda e, kc=kc, b=b: e.dma_start(out=xb[b][:, 0:D], in_=wout_d[kc * 128:(kc + 1) * 128, :]),
                      writes=[txb[b]], key="xb%d" % b)
                eng = ("dve", "pool", "act")[kc % 3]
                if eng == "act":
                    S.op("act", lambda e, kc=kc, b=b: e.activation(out=W[:, kc, 0:D], in_=xb[b][:, 0:D], func=AF.Copy),
                         reads=[txb[b]], writes=[tW])
                else:
                    S.op(eng, lambda e, kc=kc, b=b: e.tensor_copy(out=W[:, kc, 0:D], in_=xb[b][:, 0:D]), reads=[txb[b]], writes=[tW])
            gsrc = bass.AP(tensor=gb_d.tensor, offset=0, ap=[[0, 128], [D, 4], [1, D]])
            S.dma(lambda e: e.dma_start(out=gbt, in_=gsrc), writes=[tgb], key="gbt")

            idxt = sbt("idxt", [128, 4], mybir.dt.int32)
            tidx = S.tile("idxt")
            S.dma(lambda e: e.dma_start(out=idxt[:], in_=idx_d), writes=[tidx], key="idxt")
            for kc in range(16):
                r_, c_ = kc // 4, kc % 4
                agv = agout[c_].ap().rearrange("e (q n) -> (e q) n", q=4)
                S.dma(lambda e: e.indirect_dma_start(out=oq[:, kc, :], out_offset=None, in_=agv,
                                                     in_offset=bass.IndirectOffsetOnAxis(ap=idxt[:, r_:r_ + 1], axis=0)),
                      reads=[tagout[c_], tidx], writes=[toq], key="oq", queue="pool")
            alpha = 2.0 ** 0.25
            def ln_emb(blk):
                b = blk % 2
                xt, r, tr = xb[b], rsd[b], trsd[b]
                fst, fmv, tf = fstE[b], fmvE[b], tfE[b]
                S.dma(lambda e: e.dma_start(out=xt[:, 0:D], in_=xq_d[blk * 128:(blk + 1) * 128, :]), writes=[txb[b]], key="xb%d" % b)
                for q in range(4):
                    S.op("dve", lambda e: e.bn_stats(out=fst[:, q, :], in_=xt[:, q * 512:(q + 1) * 512]), reads=[txb[b]], writes=[tf])
                S.op("dve", lambda e: e.bn_aggr(out=fmv[:, 0:2], in_=fst.rearrange("p a n -> p (a n)")), reads=[tf], writes=[tf])
                act(fmv[:, 2:3], fmv[:, 1:2], AF.Ln, [tf], [tf], bias=eps_ap(1e-5))
                act(fmv[:, 2:3], fmv[:, 2:3], AF.Exp, [tf], [tf], scale=-0.5)
                stt(fmv[:, 3:4], fmv[:, 0:1], -1.0, fmv[:, 2:3], ALU.mult, ALU.mult, [tf], [tf])
                S.op("act", lambda e: e.activation(out=r, in_=xt[:, 0:D], func=AF.Identity, bias=fmv[:, 3:4], scale=fmv[:, 2:3]),
                     reads=[txb[b], tf], writes=[tr])
                tt("pool", r, r, gbt[:, 0, :], ALU.mult, [tr, tgb], [tr])
                tt("pool", r, r, gbt[:, 1, :], ALU.add, [tr, tgb], [tr])

            def proj(blk):
                b = blk % 2
                r, tr = rsd[b], trsd[b]
                for ct in range(4):
                    for kc in range(16):
                        S.op("pe", lambda e: e.matmul(ps[ct][:, :], lhsT=oq[:, kc, blk * 128:(blk + 1) * 128],
                                                      rhs=W[:, kc, ct * 512:(ct + 1) * 512], start=(kc == 0), stop=(kc == 15)),
                             reads=[toq, tW], writes=[tps[ct]])
                    stt(r[:, ct * 512:(ct + 1) * 512], r[:, ct * 512:(ct + 1) * 512], alpha, ps[ct][:, :], ALU.mult, ALU.add,
                        [tr, tps[ct]], [tr])

            def post_ln(blk):
                b = blk % 2
                r, tr = rsd[b], trsd[b]
                fst, fmv, tf = fstP[b], fmvP[b], tfP[b]
                for q in range(4):
                    S.op("dve", lambda e: e.bn_stats(out=fst[:, q, :], in_=r[:, q * 512:(q + 1) * 512]), reads=[tr], writes=[tf])
                S.op("dve", lambda e: e.bn_aggr(out=fmv[:, 4:6], in_=fst.rearrange("p a n -> p (a n)")), reads=[tf], writes=[tf])
                act(fmv[:, 6:7], fmv[:, 5:6], AF.Ln, [tf], [tf], bias=eps_ap(1e-5))
                act(fmv[:, 6:7], fmv[:, 6:7], AF.Exp, [tf], [tf], scale=-0.5)
                stt(fmv[:, 7:8], fmv[:, 4:5], -1.0, fmv[:, 6:7], ALU.mult, ALU.mult, [tf], [tf])
                S.op("act", lambda e: e.activation(out=r, in_=r, func=AF.Identity, bias=fmv[:, 7:8], scale=fmv[:, 6:7]),
                     reads=[tr, tf], writes=[tr])
                tt("dve", r, r, gbt[:, 2, :], ALU.mult, [tr, tgb], [tr])
                tt("pool", r, r, gbt[:, 3, :], ALU.add, [tr, tgb], [tr])
                S.dma(lambda e: e.dma_start(out=out_d[blk * 128:(blk + 1) * 128, :], in_=r), reads=[tr], key="rsd%d" % b)

            ln_emb(0)
            for blk in range(8):
                if blk + 1 < 8:
                    ln_emb(blk + 1)
                proj(blk)
                post_ln(blk)

        except _Stop:
            pass
        S.emit()
    return nc


_NC_CACHE = {}


def kernel(**inp):
    maps = _prep_inputs(inp)
    x = np.asarray(inp["x"], np.float32)
    for c in range(8):
        b, r = c // 4, c % 4
        maps[c]["xq"] = np.ascontiguousarray(x[b, r * 1024:(r + 1) * 1024])
        p = np.arange(128)[:, None]
        kc = np.arange(4)[None, :]
        maps[c]["idx"] = ((kc * 128 + p) * 4 + r).astype(np.int32)
    if "nc" not in _NC_CACHE:
        _NC_CACHE["nc"] = build_nc()
    nc = _NC_CACHE["nc"]
    res = run_bass_kernel_spmd(nc, maps, core_ids=list(range(8)))
    out = np.zeros((2, SEQ, D), np.float32)
    for c in range(8):
        b, r = c // 4, c % 4
        out[b, r * 1024:(r + 1) * 1024] = res.results[c]["out"]
    if DEBUG:
        kernel.dbg = [res.results[c]["dbg"] for c in range(8)]
    return out
```

```python
import math
from contextlib import ExitStack

import numpy as np
import ml_dtypes

import concourse.bass as bass
import concourse.mybir as mybir
from concourse.bass_utils import run_bass_kernel_spmd

F32 = mybir.dt.float32
BF16 = mybir.dt.bfloat16
AF = mybir.ActivationFunctionType
ALU = mybir.AluOpType

D = 2048
SEQ = 4096
NB = 33
P_TOK = NB * 128
NCOL = 2240
C0 = math.exp(-0.5)
NEG = -1.0e4
TABW = 511
DEBUG = False
STOP = None


class _Stop(Exception):
    pass


_HITS = {}


def chk(tag):
    import os
    if STOP == tag:
        _HITS[tag] = _HITS.get(tag, 0) + 1
        if _HITS[tag] >= int(os.environ.get("NTH", "1")):
            raise _Stop()


class T:
    __slots__ = ("name", "w", "r", "parent")

    def __init__(self, name, init_r=None, parent=None):
        self.name = name
        self.w = None
        self.r = dict(init_r) if init_r else {}
        self.parent = parent


class _Rec:
    def __init__(self):
        self.call = None

    def __getattr__(self, name):
        def f(*a, **k):
            assert self.call is None
            self.call = (name, a, k)
            return self
        return f


def _freeze(fn):
    r = _Rec()
    fn(r)
    name, a, k = r.call
    return lambda e: getattr(e, name)(*a, **k)


class Sched:
    ENG = ("pe", "act", "dve", "pool", "sp")

    def __init__(self, nc, es):
        self.nc = nc
        self.es = es
        self.ops = {e: [] for e in self.ENG}
        self.seen = {e: {} for e in self.ENG}
        self.dsems = {}
        self.esem = {}
        self.bar = {}

    def tile(self, name):
        return T(name, self.bar)

    def barrier(self):
        b = {}
        for e in self.ENG:
            if e != "sp" and self.ops[e]:
                for i in range(len(self.ops[e]) - 1, -1, -1):
                    if self.ops[e][i]["dma"] is None:
                        b[('E', e)] = ('E', e, i)
                        break
        for k, v in self.dsems.items():
            if v[1] > 0:
                b[('D', k)] = ('D', k, v[1])
        self.bar = b

    @staticmethod
    def _dep(waits, ev):
        key = ev[:2]
        if waits.get(key, -1) < ev[2]:
            waits[key] = ev[2]

    def _collect(self, eng, reads, writes):
        waits = {}
        for t in reads:
            if t.w is not None:
                self._dep(waits, t.w)
        for t in writes:
            if t.w is not None:
                self._dep(waits, t.w)
            for ev in t.r.values():
                self._dep(waits, ev)
        wl = []
        for key, val in waits.items():
            if key[0] == 'E' and key[1] == eng and eng == 'pe':
                continue
            if self.seen[eng].get(key, -1) >= val:
                continue
            self.seen[eng][key] = val
            wl.append((key, val))
        return wl

    def op(self, eng, fn, reads=(), writes=()):
        fn = _freeze(fn)
        extra = [t.parent for t in list(reads) + list(writes) if t.parent is not None]
        if extra:
            reads = list(reads) + extra
        wl = self._collect(eng, reads, writes)
        idx = len(self.ops[eng])
        ev = ('E', eng, idx)
        self.ops[eng].append(dict(waits=wl, fn=fn, sig=False, dma=None))
        for t in reads:
            t.r[('E', eng)] = ev
        for t in writes:
            t.w = ev
            t.r = {}
        return ev

    def dsem(self, key):
        if key not in self.dsems:
            h = self.es.enter_context(self.nc.semaphore("d_" + key))
            self.dsems[key] = [h, 0]
        return self.dsems[key]

    def dma(self, fn, reads=(), writes=(), key=None, queue="sp", inc=16):
        fn = _freeze(fn)
        wl = self._collect(queue, reads, writes)
        ds = self.dsem(key)
        ds[1] += inc
        ev = ('D', key, ds[1])
        self.ops[queue].append(dict(waits=wl, fn=fn, sig=False, dma=(key, inc)))
        for t in reads:
            t.r[('D', key)] = ev
        for t in writes:
            t.w = ev
            t.r = {}
        return ev

    def emit(self):
        nc = self.nc
        for e in self.ENG:
            for o in self.ops[e]:
                for key, val in o["waits"]:
                    if key[0] == 'E':
                        self.ops[key[1]][val]["sig"] = True
        cnt = {}
        for e in self.ENG:
            c = 0
            lst = []
            for o in self.ops[e]:
                if o["sig"]:
                    c += 1
                lst.append(c)
            cnt[e] = lst
            self.esem[e] = self.es.enter_context(nc.semaphore("e_" + e))
        finals = [(('D', k), v[1]) for k, v in self.dsems.items() if v[1] > 0]

        def resolve(key, val):
            if key[0] == 'E':
                return self.esem[key[1]], cnt[key[1]][val]
            return self.dsems[key[1]][0], val

        def body(e):
            def run(eng):
                for o in self.ops[e]:
                    for key, val in o["waits"]:
                        s, v = resolve(key, val)
                        eng.wait_ge(s, v)
                    ins = o["fn"](eng)
                    if o["dma"] is not None:
                        ins.then_inc(self.dsems[o["dma"][0]][0], o["dma"][1])
                    elif o["sig"]:
                        ins.then_inc(self.esem[e], 1)
                if e == "sp":
                    for key, val in finals:
                        s, v = resolve(key, val)
                        eng.wait_ge(s, v)
            return run

        with nc.Block() as block:
            block.tensor(body("pe"))
            block.scalar(body("act"))
            block.vector(body("dve"))
            block.gpsimd(body("pool"))
            block.sync(body("sp"))


def _bucket(n):
    n = np.maximum(n, 0)
    nf = np.maximum(n, 1).astype(np.float32)
    large = 16 + (np.log(nf / np.float32(16)) / np.float32(math.log(128 / 16)) * np.float32(16)).astype(np.int32)
    large = np.minimum(large, 31)
    return np.where(n < 16, n, large)


PV_MU = 0
PV_MUWD = 6
PV_MUAD = 7
PV_W0 = 8
PV_A0 = 10
PV_KK = 12
PV_KA = 14
PV_RK = 16
PV_GNG = 18
PV_GNB = 20
PV_SUBLN = 22
PV_LNG = 23
PV_LNB = 39
PV_N = 55


def _consts():
    c = {}
    c["ident"] = np.eye(128, dtype=np.float32).astype(ml_dtypes.bfloat16)
    c["ones_bf"] = np.ones((128, 128), dtype=ml_dtypes.bfloat16)
    c["ones_f"] = np.ones((128, 128), dtype=np.float32)
    bo = np.zeros((128, 128), np.float32)
    bo[:64, :64] = 1.0
    bo[64:, 64:] = 1.0
    c["blockones"] = bo
    s = np.arange(128)[:, None]
    t = np.arange(128)[None, :]
    strict = (s < t).astype(np.float32)
    incl = (s <= t).astype(np.float32)
    m = np.zeros((128, 2, 2, 128), np.float32)
    m[:, :, 0, :] = strict[:, None, :]
    m[:, :, 1, :] = incl[:, None, :]
    c["maskT"] = m.reshape(128, 512)
    c["masklow"] = (s > t).astype(np.float32)
    d = np.arange(TABW) - 127
    oh = np.zeros((33, TABW), np.float32)
    b = _bucket(d)
    for i in range(TABW):
        if d[i] >= 0:
            oh[b[i], i] = 1.0
        else:
            oh[32, i] = NEG
    c["oh"] = oh
    return c


def _prep_inputs(inp):
    f = np.float32
    w_in = np.asarray(inp["w_in"][0], f)
    w_out = np.asarray(inp["w_out"][0], f)
    x = np.asarray(inp["x"], f)
    consts = _consts()
    lam4 = np.stack([inp["lambda_q1"][0], inp["lambda_k1"][0], inp["lambda_q2"][0], inp["lambda_k2"][0]]).astype(f)
    rows = []
    for r in range(4):
        rows += list(range((2 * r) * 128, (2 * r + 2) * 128))
        rows += list(range(1024 + (4 * r) * 64, 1024 + (4 * r + 4) * 64))
    w_out_p = np.ascontiguousarray(w_out[rows])
    gb = np.stack([inp["ln_emb_g"], inp["ln_emb_b"], inp["ln_post_g"][0], inp["ln_post_b"][0]]).astype(f)
    maps = []
    for c in range(8):
        b, hg = c // 4, c % 4
        h0 = 2 * hg
        cols = []
        cols += list(range(h0 * 128, (h0 + 2) * 128))
        cols += list(range(1024 + h0 * 128, 1024 + (h0 + 2) * 128))
        cols += list(range(3072 + h0 * 128, 3072 + (h0 + 2) * 128))
        rb = 4096 + hg * 256
        cols += list(range(rb, rb + 256))
        cols += list(range(rb + 1024, rb + 1024 + 256))
        cols += list(range(rb + 2048, rb + 2048 + 256))
        cols += list(range(7360 + hg * 256, 7360 + hg * 256 + 256))
        cols += list(range(4096 + 3072, 4096 + 3072 + 192))
        cols += list(range(2048 + h0 * 128, 2048 + (h0 + 2) * 128))
        assert len(cols) == NCOL
        wi = np.ascontiguousarray(w_in[:, cols])
        pv = np.zeros((128, PV_N), f)
        mu = inp["rw_mu"][0]
        rs = slice(hg * 256, hg * 256 + 256)
        for p in range(2):
            ps_ = slice(hg * 256 + p * 128, hg * 256 + (p + 1) * 128)
            pv[:, PV_MU + 0 + p] = mu[0:1024][ps_]
            pv[:, PV_MU + 2 + p] = mu[1024:2048][ps_]
            pv[:, PV_MU + 4 + p] = mu[2048:3072][ps_]
            pv[:, PV_W0 + p] = inp["rw_w0"][0][ps_]
            pv[:, PV_A0 + p] = inp["rw_a0"][0][ps_]
            pv[:, PV_KK + p] = inp["rw_k_k"][0][ps_]
            pv[:, PV_KA + p] = inp["rw_k_a"][0][ps_]
            pv[:, PV_RK + p] = inp["rw_r_k"][0].reshape(-1)[ps_]
            pv[:, PV_GNG + p] = inp["rw_gn_g"][0][ps_]
            pv[:, PV_GNB + p] = inp["rw_gn_b"][0][ps_]
        pv[:96, PV_MUWD] = mu[3072:3168]
        pv[:96, PV_MUAD] = mu[3168:3264]
        pv[:, PV_SUBLN] = inp["subln_g"][0]
        pv[:, PV_LNG:PV_LNG + 16] = np.asarray(inp["ln_emb_g"], f).reshape(16, 128).T
        pv[:, PV_LNB:PV_LNB + 16] = np.asarray(inp["ln_emb_b"], f).reshape(16, 128).T
        relb = np.ones((33, 2, 128), f)
        for j in range(2):
            relb[:32, j, :] = np.asarray(inp["rel_bias"], f)[:, h0 + j][:, None]
        lora = np.zeros((96, 2, 256), f)
        lora[:, 0, :] = inp["rw_w_up"][0][:, rs]
        lora[:, 1, :] = inp["rw_a_up"][0][:, rs]
        m = {
            "x": np.ascontiguousarray(x[b]),
            "meta": np.asarray(inp["meta_tokens"], f),
            "w_in": wi,
            "w_out": w_out_p,
            "pvec": pv,
            "relb": relb,
            "lora": lora,
            "lam4": lam4,
            "gb": gb,
        }
        m.update(consts)
        maps.append(m)
    return maps


def build_nc():
    nc = bass.Bass("TRN2", target_bir_lowering=False)

    def din(name, shape, dt=F32):
        return nc.dram_tensor(name, list(shape), dt, kind="ExternalInput").ap()

    x_d = din("x", [SEQ, D])
    meta_d = din("meta", [16, D])
    win_d = din("w_in", [D, NCOL])
    wout_d = din("w_out", [D, D])
    pvec_d = din("pvec", [128, PV_N])
    relb_d = din("relb", [33, 2, 128])
    lora_d = din("lora", [96, 2, 256])
    lam4_d = din("lam4", [4, 64])
    gb_d = din("gb", [4, D])
    ident_d = din("ident", [128, 128], BF16)
    onesbf_d = din("ones_bf", [128, 128], BF16)
    onesf_d = din("ones_f", [128, 128])
    bo_d = din("blockones", [128, 128])
    maskT_d = din("maskT", [128, 512])
    masklow_d = din("masklow", [128, 128])
    oh_d = din("oh", [33, TABW])
    xq_d = din("xq", [1024, D])
    idx_d = din("idx", [128, 4], mybir.dt.int32)
    out_d = nc.dram_tensor("out", [1024, D], F32, kind="ExternalOutput").ap()
    if DEBUG:
        dbg_d = nc.dram_tensor("dbg", [512, SEQ], BF16, kind="ExternalOutput").ap()
    agin = [nc.dram_tensor("agin%d" % c, [128, SEQ], BF16) for c in range(4)]
    agout = [nc.dram_tensor("agout%d" % c, [512, SEQ], BF16) for c in range(4)]
    tab_d = [nc.dram_tensor("tab%d" % j, [128, TABW], F32) for j in range(2)]

    with ExitStack() as es:
        S = Sched(nc, es)

        def sbt(name, shape, dt=F32):
            return es.enter_context(nc.sbuf_tensor("s_" + name, list(shape), dt))

        W = sbt("W", [128, 16, NCOL], BF16)
        tW = S.tile("W")
        xb = [sbt("xb%d" % i, [128, NCOL]) for i in range(2)]
        txb = [S.tile("xb%d" % i) for i in range(2)]
        pvec = sbt("pvec", [128, PV_N])
        pder = sbt("pder", [128, 8])
        ident = sbt("ident", [128, 128], BF16)
        ones_bf = sbt("ones_bf", [128, 128], BF16)
        ones_f = sbt("ones_f", [128, 128])
        blockones = sbt("blockones", [128, 128])
        maskT = sbt("maskT", [128, 512])
        masklow = sbt("masklow", [128, 128])
        lora32 = sbt("lora32", [96, 2, 256])
        lora = sbt("lora", [96, 2, 256], BF16)
        bext = [sbt("bext%d" % j, [128, 384]) for j in range(2)]
        bcol = sbt("bcol", [128, 4])
        tC = S.tile("consts")
        arena_w = 28100
        arena = sbt("arena", [128, arena_w])
        apos = [0]

        def carve(nbytes, dt, shape_str=None, **kw):
            n32 = (nbytes + 3) // 4
            a = apos[0]
            apos[0] += n32
            assert apos[0] <= arena_w, apos[0]
            ap = arena[:, a:a + n32]
            if dt != F32:
                ap = ap.bitcast(dt)
            if shape_str:
                ap = ap.rearrange(shape_str, **kw)
            return ap

        ps = [es.enter_context(nc.psum_tensor("ps%d" % i, [128, 512], F32)) for i in range(7)]
        tps = [S.tile("ps%d" % i) for i in range(7)]
        psb = es.enter_context(nc.psum_tensor("psb", [128, 1024], BF16))
        tpsb = S.tile("psb")

        KT = carve(2 * P_TOK * 2, BF16, "p (h n) -> p h n", h=2)
        tKT = S.tile("KT")
        Vt = carve(NB * 256 * 2, BF16, "p (b n) -> p b n", b=NB)
        tV = S.tile("V")
        xn_off = apos[0]
        xn = carve(D * 2, BF16)
        txn = S.tile("xn")
        hT_off = apos[0]
        hT = carve(16 * 256 * 2, BF16, "p (c n) -> p c n", c=16)
        thT = S.tile("hT")
        wk2 = [arena[:, hT_off + i * 256:hT_off + (i + 1) * 256] for i in range(8)] + \
              [arena[:, xn_off + i * 256:xn_off + (i + 1) * 256] for i in range(4)]
        twk2 = [T("wk2_%d" % i, parent=thT) for i in range(8)] + [T("wk2_%d" % (8 + i), parent=txn) for i in range(4)]
        QT = carve(2 * 256 * 2, BF16, "p (h n) -> p h n", h=2)
        tQT = S.tile("QT")
        gate_a = carve(2 * 256 * 2, BF16, "p (h n) -> p h n", h=2)
        tga = S.tile("gate_a")
        gate_r = carve(2 * 256 * 2, BF16, "p (h n) -> p h n", h=2)
        tgr = S.tile("gate_r")
        zb = [carve(257 * 4, F32) for _ in range(8)]
        tzb = [S.tile("z%d" % i) for i in range(8)]
        NWK = 12
        wk = [carve(256 * 4, F32) for _ in range(NWK)]
        twk = [S.tile("wk%d" % i) for i in range(NWK)]
        bonus = [carve(256 * 4, F32) for _ in range(2)]
        tbonus = [S.tile("bonus%d" % i) for i in range(2)]
        yT = [wk[1], wk[5]]
        tyT = [twk[1], twk[5]]
        twb = carve(256 * 2, BF16)
        adb = carve(256 * 2, BF16)
        ttwb = S.tile("twb")
        tadb = S.tile("adb")
        arT = [carve(2 * 256 * 2, BF16, "p (a n) -> p a n", a=2) for _ in range(2)]
        tarT = [S.tile("arT%d" % i) for i in range(2)]
        ktl = [carve(256 * 2, BF16) for _ in range(2)]
        btl = [carve(256 * 2, BF16) for _ in range(2)]
        tktl = [S.tile("ktl%d" % i) for i in range(2)]
        tbtl = [S.tile("btl%d" % i) for i in range(2)]
        fm3 = [carve(256 * 2, BF16) for _ in range(3)]
        tfm3 = [S.tile("fm3_%d" % i) for i in range(3)]
        fm3b = [carve(256 * 2, BF16) for _ in range(3)]
        tfm3b = [S.tile("fm3b_%d" % i) for i in range(3)]
        tnb2 = [S.tile("nbias%d" % i) for i in range(2)]
        tok3 = [carve(2 * 256 * 2, BF16, "p (c n) -> p c n", c=2) for _ in range(3)]
        ttok3 = [S.tile("tok3_%d" % i) for i in range(3)]
        WC = carve(4 * 4, F32)
        tWC = S.tile("WC")
        nbias = carve(4 * 4, F32)
        tnbias = S.tile("nbias")
        AT = [carve(4 * 128 * 2, BF16, "p (a n) -> p a n", a=4) for _ in range(4)]
        tAT = [S.tile("AT%d" % i) for i in range(4)]
        Mc = carve(4 * 128 * 2, BF16, "p (h n) -> p h n", h=4)
        Mtc = carve(4 * 128 * 2, BF16, "p (h n) -> p h n", h=4)
        Ptc = carve(4 * 128 * 2, BF16, "p (h n) -> p h n", h=4)
        tMc = S.tile("Mc")
        tMtc = S.tile("Mtc")
        tPtc = S.tile("Ptc")
        btlm = [[carve(256 * 2, BF16) for _ in range(2)] for _ in range(2)]
        ktlm = [[carve(256 * 2, BF16) for _ in range(2)] for _ in range(2)]
        tbtlm = [[S.tile("btlm%d%d" % (i, j)) for j in range(2)] for i in range(2)]
        tktlm = [[S.tile("ktlm%d%d" % (i, j)) for j in range(2)] for i in range(2)]
        Vz = carve(2 * 4 * 128 * 2, BF16, "p (c h n) -> p c h n", c=2, h=4)
        tVz = S.tile("Vz")
        Uz = carve(4 * 128 * 2, BF16, "p (h n) -> p h n", h=4)
        tUz = S.tile("Uz")
        Xsb = carve(256 * 2, BF16, "p (h n) -> p h n", h=4)
        Usb = carve(256 * 2, BF16, "p (h n) -> p h n", h=4)
        tXsb = S.tile("Xsb")
        tUsb = S.tile("Usb")
        S32 = carve(2 * 128 * 4, F32, "p (a n) -> p a n", a=2)
        Sbf = carve(2 * 128 * 2, BF16, "p (a n) -> p a n", a=2)
        tS32 = S.tile("S32")
        tSbf = S.tile("Sbf")
        Ssb = [wk[0], wk[2]]
        tSsb = [twk[0], twk[2]]
        Eb = [carve(2 * 256 * 2, BF16, "p (m n) -> p m n", m=2) for _ in range(2)]
        tEb = [S.tile("Eb%d" % i) for i in range(2)]
        oT = [carve(4 * 256 * 2, BF16, "p (c n) -> p c n", c=4) for _ in range(2)]
        toT = [S.tile("oT%d" % i) for i in range(2)]
        stats = carve(4 * 6 * 4, F32, "p (a n) -> p a n", a=4)
        mv = carve(8 * 4, F32)
        tst = S.tile("stats")
        lamw = carve(4 * 64 * 4, F32, "p (a n) -> p a n", a=4)
        tabsb = carve(TABW * 4, F32)
        ttab = S.tile("tabsb")
        main_end = apos[0]
        print("arena main words", main_end, "of", arena_w)

        tagin = [S.tile("agin%d" % c) for c in range(4)]
        tagout = [S.tile("agout%d" % c) for c in range(4)]
        ttabd = [S.tile("tabd%d" % j) for j in range(2)]

        try:
            def cload(dst, src):
                S.dma(lambda e: e.dma_start(out=dst, in_=src), writes=[tC], key="consts")

            cload(pvec[:], pvec_d)
            cload(ident[:], ident_d)
            cload(ones_bf[:], onesbf_d)
            cload(ones_f[:], onesf_d)
            cload(blockones[:], bo_d)
            cload(maskT[:], maskT_d)
            cload(masklow[:], masklow_d)
            cload(lora32[:], lora_d)
            relb = carve(2 * 128 * 4, F32, "p (a n) -> p a n", a=2)
            ohsb = carve(TABW * 4, F32)
            cload(relb[0:33], relb_d)
            cload(ohsb[0:33], oh_d)
            lam_src = bass.AP(tensor=lam4_d.tensor, offset=0, ap=[[0, 128], [64, 4], [1, 64]])
            cload(lamw, lam_src)

            S.op("dve", lambda e: e.tensor_copy(out=lora[:], in_=lora32[:]), reads=[tC], writes=[tC])
            S.op("dve", lambda e: e.tensor_scalar(out=pder[:, 0:2], in0=pvec[:, PV_KA:PV_KA + 2], scalar1=-1.0, scalar2=1.0,
                                                  op0=ALU.mult, op1=ALU.add), reads=[tC], writes=[tC])
            S.op("dve", lambda e: e.tensor_scalar(out=pder[:, 2:3], in0=pvec[:, PV_SUBLN:PV_SUBLN + 1], scalar1=0.8, scalar2=None,
                                                  op0=ALU.mult), reads=[tC], writes=[tC])
            S.op("dve", lambda e: e.tensor_tensor(out=lamw[:, 0, :], in0=lamw[:, 0, :], in1=lamw[:, 1, :], op=ALU.mult),
                 reads=[tC], writes=[tC])
            S.op("dve", lambda e: e.tensor_tensor(out=lamw[:, 2, :], in0=lamw[:, 2, :], in1=lamw[:, 3, :], op=ALU.mult),
                 reads=[tC], writes=[tC])
            S.op("dve", lambda e: e.reduce_sum(out=pder[:, 4:5], in_=lamw[:, 0, :], axis=mybir.AxisListType.X),
                 reads=[tC], writes=[tC])
            S.op("dve", lambda e: e.reduce_sum(out=pder[:, 5:6], in_=lamw[:, 2, :], axis=mybir.AxisListType.X),
                 reads=[tC], writes=[tC])
            S.op("act", lambda e: e.activation(out=pder[:, 4:6], in_=pder[:, 4:6], func=AF.Exp), reads=[tC], writes=[tC])
            S.op("dve", lambda e: e.tensor_tensor(out=pder[:, 3:4], in0=pder[:, 5:6], in1=pder[:, 4:5], op=ALU.subtract),
                 reads=[tC], writes=[tC])
            S.op("dve", lambda e: e.tensor_scalar(out=pder[:, 3:4], in0=pder[:, 3:4], scalar1=-0.2, scalar2=None, op0=ALU.add),
                 reads=[tC], writes=[tC])
            S.op("pool", lambda e: e.memset(S32.rearrange("p a n -> p (a n)"), 0.0), writes=[tS32])
            S.op("pool", lambda e: e.memset(Sbf.rearrange("p a n -> p (a n)"), 0.0), writes=[tSbf])
            S.op("pool", lambda e: e.memset(Vz.rearrange("p c h n -> p (c h n)"), 0.0), writes=[tVz])
            S.op("pool", lambda e: e.memset(Uz.rearrange("p h n -> p (h n)"), 0.0), writes=[tUz])
            for i in range(8):
                S.op("pool", lambda e, i=i: e.memset(zb[i], 0.0), writes=[tzb[i]])

            for j in range(2):
                S.op("pe", lambda e, j=j: e.matmul(ps[0][:, 0:256], lhsT=relb[0:33, j, :], rhs=ohsb[0:33, 0:256], start=True, stop=True),
                     reads=[tC], writes=[tps[0]])
                S.op("pe", lambda e, j=j: e.matmul(ps[1][:, 0:255], lhsT=relb[0:33, j, :], rhs=ohsb[0:33, 256:511], start=True, stop=True),
                     reads=[tC], writes=[tps[1]])
                S.op("dve", lambda e: e.tensor_copy(out=tabsb[:, 0:256], in_=ps[0][:, 0:256]), reads=[tps[0]], writes=[ttab])
                S.op("dve", lambda e: e.tensor_copy(out=tabsb[:, 256:511], in_=ps[1][:, 0:255]), reads=[tps[1]], writes=[ttab])
                S.op("dve", lambda e, j=j: e.tensor_copy(out=bcol[:, j:j + 1], in_=tabsb[:, 510:511]), reads=[ttab], writes=[tC])
                S.op("dve", lambda e, j=j: e.tensor_copy(out=bcol[:, 2 + j:3 + j], in_=tabsb[:, 510:511]), reads=[ttab], writes=[tC])
                S.op("dve", lambda e, j=j: e.memset(bcol[0:112, 2 + j:3 + j], NEG), reads=[], writes=[tC])
                S.dma(lambda e, j=j: e.dma_start(out=tab_d[j].ap(), in_=tabsb), reads=[ttab], writes=[ttabd[j]], key="tabst")
                src = bass.AP(tensor=tab_d[j].ap().tensor, offset=127, ap=[[TABW - 1, 128], [1, 384]])
                S.dma(lambda e, j=j, src=src: e.dma_start(out=bext[j][:], in_=src), reads=[ttabd[j]], writes=[tC], key="consts")

            for kc in range(16):
                b = kc % 2
                S.dma(lambda e, kc=kc, b=b: e.dma_start(out=xb[b][:], in_=win_d[kc * 128:(kc + 1) * 128, :]),
                      writes=[txb[b]], key="xb%d" % b)
                eng = ("dve", "pool", "act")[kc % 3]
                if eng == "act":
                    S.op("act", lambda e, kc=kc, b=b: e.activation(out=W[:, kc, :], in_=xb[b][:], func=AF.Copy),
                         reads=[txb[b]], writes=[tW])
                else:
                    S.op(eng, lambda e, kc=kc, b=b: e.tensor_copy(out=W[:, kc, :], in_=xb[b][:]), reads=[txb[b]], writes=[tW])

            chk('setup')
            rr = {"ps": 0, "ev": 0, "xb": 0, "oT": 0}

            def next_ps(n=4):
                i = rr["ps"] % n
                rr["ps"] += 1
                return i

            def evac_copy(out, in_, reads, writes, bf=False):
                k = 1
                if k == 0:
                    S.op("dve", lambda e: e.tensor_scalar(out=out, in0=in_, scalar1=1.0, scalar2=None, op0=ALU.mult), reads=reads, writes=writes)
                else:
                    S.op("act", lambda e: e.activation(out=out, in_=in_, func=AF.Copy), reads=reads, writes=writes)

            def tt(eng, out, a, b, op, reads, writes):
                S.op(eng, lambda e: e.tensor_tensor(out=out, in0=a, in1=b, op=op), reads=reads, writes=writes)

            def ts(eng, out, a, s1, s2, op0, op1, reads, writes):
                if s2 is None:
                    S.op(eng, lambda e: e.tensor_scalar(out=out, in0=a, scalar1=s1, scalar2=None, op0=op0), reads=reads, writes=writes)
                else:
                    S.op(eng, lambda e: e.tensor_scalar(out=out, in0=a, scalar1=s1, scalar2=s2, op0=op0, op1=op1),
                         reads=reads, writes=writes)

            def stt(out, a, sc, b, op0, op1, reads, writes):
                S.op("dve", lambda e: e.scalar_tensor_tensor(out=out, in0=a, scalar=sc, in1=b, op0=op0, op1=op1),
                     reads=reads, writes=writes)

            def act(out, in_, func, reads, writes, bias=0.0, scale=1.0):
                S.op("act", lambda e: e.activation(out=out, in_=in_, func=func, bias=bias, scale=scale), reads=reads, writes=writes)

            def rsqrt_inplace(t, tt_, n, scale, eps):
                act(t[:, 0:n], t[:, 0:n], AF.Ln, [tt_], [tt_], bias=eps_ap(eps), scale=scale)
                act(t[:, 0:n], t[:, 0:n], AF.Exp, [tt_], [tt_], scale=-0.5)

            epsc = {}

            def eps_ap(v):
                if v == 0.0:
                    return 0.0
                if v not in epsc:
                    col = 6 + len(epsc)
                    S.op("pool", lambda e, col=col, v=v: e.memset(pder[:, col:col + 1], float(v)), writes=[tC])
                    epsc[v] = pder[:, col:col + 1]
                return epsc[v]

            eps_ap(1e-5)
            eps_ap(64e-5)

            def layer_norm_block(src_ap, is_meta, blk_slot):
                b = rr["xb"] % 2
                rr["xb"] += 1
                xt = xb[b]
                if is_meta:
                    S.op("pool", lambda e: e.memset(xt[:, 0:D], 0.0), writes=[txb[b]])
                    S.dma(lambda e: e.dma_start(out=xt[112:128, 0:D], in_=meta_d), writes=[txb[b]], key="xb%d" % b)
                else:
                    S.dma(lambda e: e.dma_start(out=xt[:, 0:D], in_=src_ap), writes=[txb[b]], key="xb%d" % b)
                for q in range(4):
                    S.op("dve", lambda e, q=q: e.bn_stats(out=stats[:, q, :], in_=xt[:, q * 512:(q + 1) * 512]),
                         reads=[txb[b]], writes=[tst])
                S.op("dve", lambda e: e.bn_aggr(out=mv[:, 0:2], in_=stats.rearrange("p a n -> p (a n)")), reads=[tst], writes=[tst])
                act(mv[:, 2:3], mv[:, 1:2], AF.Ln, [tst], [tst], bias=eps_ap(1e-5))
                act(mv[:, 2:3], mv[:, 2:3], AF.Exp, [tst], [tst], scale=-0.5)
                stt(mv[:, 3:4], mv[:, 0:1], -1.0, mv[:, 2:3], ALU.mult, ALU.mult, [tst], [tst])
                S.op("act", lambda e: e.activation(out=xn, in_=xt[:, 0:D], func=AF.Identity, bias=mv[:, 3:4], scale=mv[:, 2:3]),
                     reads=[txb[b], tst], writes=[txn])
                for half in range(2):
                    for dc in range(8):
                        c = half * 8 + dc
                        S.op("pe", lambda e, c=c, dc=dc: e.transpose(psb[:, dc * 128:(dc + 1) * 128], xn[:, c * 128:(c + 1) * 128], ident[:]),
                             reads=[txn, tC], writes=[tpsb])
                    for dc in range(8):
                        c = half * 8 + dc
                        dst = hT[:, c, blk_slot * 128:(blk_slot + 1) * 128]
                        src = psb[:, dc * 128:(dc + 1) * 128]
                        if dc % 2 == 0:
                            ts("dve", dst, src, pvec[:, PV_LNG + c:PV_LNG + c + 1], pvec[:, PV_LNB + c:PV_LNB + c + 1],
                               ALU.mult, ALU.add, [tpsb, tC], [thT])
                        else:
                            S.op("act", lambda e, dst=dst, src=src, c=c: e.activation(
                                out=dst, in_=src, func=AF.Identity, bias=pvec[:, PV_LNB + c:PV_LNB + c + 1],
                                scale=pvec[:, PV_LNG + c:PV_LNG + c + 1]), reads=[tpsb, tC], writes=[thT])
                if is_meta:
                    S.op("pool", lambda e: e.memset(hT[:, :, 0:112], 0.0), writes=[thT])

            def inproj_fm(ct_off, m, nt, sink):
                i = next_ps()
                for kc in range(16):
                    S.op("pe", lambda e, kc=kc, i=i: e.matmul(ps[i][0:m, 0:nt], lhsT=W[:, kc, ct_off:ct_off + m], rhs=hT[:, kc, 0:nt],
                                                            start=(kc == 0), stop=(kc == 15)), reads=[tW, thT], writes=[tps[i]])
                sink(ps[i][0:m, 0:nt], tps[i])

            tiles = [(-1, 128)] + [(i, 256) for i in range(16)]
            for (ti, nt) in tiles:
                nblk = nt // 128
                blk0 = 0 if ti < 0 else 1 + 2 * ti
                pos0 = blk0 * 128
                for j in range(nblk):
                    if ti < 0:
                        layer_norm_block(None, True, 0)
                    else:
                        r0 = ti * 256 + j * 128
                        layer_norm_block(x_d[r0:r0 + 128, :], False, j)
                if ti >= 0:
                    for h in range(2):
                        inproj_fm(h * 128, 128, nt, lambda p, tp, h=h: evac_copy(QT[:, h, 0:nt], p, [tp], [tQT]))
                for h in range(2):
                    inproj_fm(256 + h * 128, 128, nt, lambda p, tp, h=h: evac_copy(KT[:, h, pos0:pos0 + nt], p, [tp], [tKT]))
                if ti >= 0:
                    for h in range(2):
                        inproj_fm(512 + h * 128, 128, nt, lambda p, tp, h=h: act(gate_a[:, h, 0:nt], p, AF.Silu, [tp], [tga]))
                    for p_ in range(2):
                        inproj_fm(1536 + p_ * 128, 128, nt, lambda p, tp, p_=p_: act(gate_r[:, p_, 0:nt], p, AF.Silu, [tp], [tgr]))
                for zi in range(6):
                    inproj_fm(768 + zi * 128, 128, nt, lambda p, tp, zi=zi: evac_copy(zb[zi][:, 1:1 + nt], p, [tp], [tzb[zi]]))
                for zi in range(2):
                    inproj_fm(1792 + zi * 96, 96, nt, lambda p, tp, zi=zi: evac_copy(zb[6 + zi][0:96, 1:1 + nt], p, [tp], [tzb[6 + zi]]))
                for j in range(nblk):
                    i = next_ps()
                    for kc in range(16):
                        S.op("pe", lambda e, kc=kc, i=i, j=j: e.matmul(ps[i][:, 0:256], lhsT=hT[:, kc, j * 128:(j + 1) * 128],
                                                                     rhs=W[:, kc, 1984:2240], start=(kc == 0), stop=(kc == 15)),
                             reads=[tW, thT], writes=[tps[i]])
                    evac_copy(Vt[:, blk0 + j, :], ps[i][:, 0:256], [tps[i]], [tV])

                chk('inproj%d' % ti)
                r32, k32, v32, t1, t2, sg, aic, kkn, k2, bvec, cs, t3 = wk
                tr32, tk32, tv32, tt1, tt2, tsg, taic, tkkn, tk2, tbvec, tcs, tt3 = twk
                n = nt

                def shift(zi, mucol, out, tout, rows=128):
                    z = zb[zi]
                    tz = tzb[zi]
                    tt("pool", t3[0:rows, 0:n], z[0:rows, 0:n], z[0:rows, 1:1 + n], ALU.subtract, [tz], [tt3])
                    stt(out[0:rows, 0:n], t3[0:rows, 0:n], pvec[0:rows, mucol:mucol + 1], z[0:rows, 1:1 + n], ALU.mult, ALU.add,
                        [tt3, tz, tC], [tout])
                    S.op("pool", lambda e: e.tensor_copy(out=z[0:rows, 0:1], in_=z[0:rows, n:n + 1]), reads=[tz], writes=[tz])

                shift(6, PV_MUWD, t1, tt1, rows=96)
                act(twb[0:96, 0:n], t1[0:96, 0:n], AF.Tanh, [tt1], [ttwb])
                shift(7, PV_MUAD, t2, tt2, rows=96)
                S.op("dve", lambda e: e.tensor_scalar(out=adb[0:96, 0:n], in0=t2[0:96, 0:n], scalar1=1.0, scalar2=None, op0=ALU.mult), reads=[tt2], writes=[tadb])

                import os as _os
                pe_last = ('E', 'pe', len(S.ops['pe']) - 1)
                for t_ in twk2:
                    t_.r[('E', 'pe')] = pe_last

                def shift2(zi, mucol, out, tout, tmp, ttmp):
                    z = zb[zi]
                    tz = tzb[zi]
                    tt("pool", tmp[:, 0:n], z[:, 0:n], z[:, 1:1 + n], ALU.subtract, [tz], [ttmp])
                    stt(out[:, 0:n], tmp[:, 0:n], pvec[:, mucol:mucol + 1], z[:, 1:1 + n], ALU.mult, ALU.add, [ttmp, tz, tC], [tout])
                    S.op("pool", lambda e: e.tensor_copy(out=z[:, 0:1], in_=z[:, n:n + 1]), reads=[tz], writes=[tz])

                def pre_pair(p, wks, twks, f3, tf3, pb):
                    r32, k32, v32, t1, t2, sg, aic, kkn, k2, bvec, cs, t3 = wks
                    tr32, tk32, tv32, tt1, tt2, tsg, taic, tkkn, tk2, tbvec, tcs, tt3 = twks
                    b0, b1, b2 = pb
                    tnb = tnb2[p]
                    shift2(0 + p, PV_MU + 0 + p, r32, tr32, t3, tt3)
                    yield
                    shift2(2 + p, PV_MU + 2 + p, k32, tk32, t3, tt3)
                    yield
                    shift2(4 + p, PV_MU + 4 + p, v32, tv32, t3, tt3)
                    yield
                    S.op("pe", lambda e: e.matmul(ps[b0][:, 0:n], lhsT=lora[0:96, 0, p * 128:(p + 1) * 128], rhs=twb[0:96, 0:n],
                                                  start=True, stop=True), reads=[tC, ttwb], writes=[tps[b0]])
                    act(sg[:, 0:n], ps[b0][:, 0:n], AF.Sigmoid, [tps[b0], tC], [tsg], bias=pvec[:, PV_W0 + p:PV_W0 + p + 1])
                    yield
                    S.op("pe", lambda e: e.matmul(ps[b1][:, 0:n], lhsT=lora[0:96, 1, p * 128:(p + 1) * 128], rhs=adb[0:96, 0:n],
                                                  start=True, stop=True), reads=[tC, tadb], writes=[tps[b1]])
                    act(aic[:, 0:n], ps[b1][:, 0:n], AF.Sigmoid, [tps[b1], tC], [taic], bias=pvec[:, PV_A0 + p:PV_A0 + p + 1])
                    yield
                    ts("dve", t1[:, 0:n], k32[:, 0:n], pvec[:, PV_KK + p:PV_KK + p + 1], None, ALU.mult, None, [tk32, tC], [tt1])
                    tt("pool", t2[:, 0:n], t1[:, 0:n], t1[:, 0:n], ALU.mult, [tt1], [tt2])
                    yield
                    S.op("pe", lambda e: e.matmul(ps[b2][:, 0:n], lhsT=blockones[:], rhs=t2[:, 0:n], start=True, stop=True),
                         reads=[tC, tt2], writes=[tps[b2]])
                    ts("dve", t2[:, 0:n], ps[b2][:, 0:n], 1e-24, None, ALU.max, None, [tps[b2]], [tt2])
                    yield
                    act(t2[:, 0:n], t2[:, 0:n], AF.Ln, [tt2], [tt2])
                    yield
                    act(t2[:, 0:n], t2[:, 0:n], AF.Exp, [tt2], [tt2], scale=-0.5)
                    yield
                    tt("dve", kkn[:, 0:n], t1[:, 0:n], t2[:, 0:n], ALU.mult, [tt1, tt2], [tkkn])
                    yield
                    ts("dve", t1[:, 0:n], aic[:, 0:n], pvec[:, PV_KA + p:PV_KA + p + 1], pder[:, p:p + 1], ALU.mult, ALU.add,
                       [taic, tC], [tt1])
                    yield
                    tt("pool", k2[:, 0:n], k32[:, 0:n], t1[:, 0:n], ALU.mult, [tk32, tt1], [tk2])
                    tt("pool", bvec[:, 0:n], kkn[:, 0:n], aic[:, 0:n], ALU.mult, [tkkn, taic], [tbvec])
                    yield
                    stt(t1[:, 0:n], r32[:, 0:n], pvec[:, PV_RK + p:PV_RK + p + 1], k2[:, 0:n], ALU.mult, ALU.mult, [tr32, tk2, tC], [tt1])
                    yield
                    S.op("pe", lambda e: e.matmul(ps[b0][:, 0:n], lhsT=blockones[:], rhs=t1[:, 0:n], start=True, stop=True),
                         reads=[tC, tt1], writes=[tps[b0]])
                    tt("dve", bonus[p][:, 0:n], ps[b0][:, 0:n], v32[:, 0:n], ALU.mult, [tps[b0], tv32], [tbonus[p]])
                    yield
                    for c in range(nblk):
                        S.op("dve", lambda e: e.tensor_tensor_scan(out=cs[:, c * 128:(c + 1) * 128], data0=ones_f[:, 0:128],
                                                                   data1=sg[:, c * 128:(c + 1) * 128], initial=0.0,
                                                                   op0=ALU.mult, op1=ALU.add), reads=[tsg, tC], writes=[tcs])
                    yield
                    act(t1[:, 0:n], cs[:, 0:n], AF.Exp, [tcs], [tt1], scale=-C0)
                    yield
                    tt("dve", arT[p][:, 1, 0:n], r32[:, 0:n], t1[:, 0:n], ALU.mult, [tr32, tt1], [tarT[p]])
                    for c in range(nblk):
                        S.op("pool", lambda e: e.tensor_copy(out=WC[:, p * 2 + c:p * 2 + c + 1], in_=t1[:, c * 128 + 127:c * 128 + 128]),
                             reads=[tt1], writes=[tWC])
                        ts("dve", nbias[:, p * 2 + c:p * 2 + c + 1], cs[:, c * 128 + 127:c * 128 + 128], -C0, None, ALU.mult, None, [tcs], [tnb])
                    yield
                    tt("pool", t2[:, 0:n], cs[:, 0:n], sg[:, 0:n], ALU.subtract, [tcs, tsg], [tt2])
                    yield
                    act(t2[:, 0:n], t2[:, 0:n], AF.Exp, [tt2], [tt2], scale=-C0)
                    yield
                    stt(arT[p][:, 0, 0:n], kkn[:, 0:n], -1.0, t2[:, 0:n], ALU.mult, ALU.mult, [tkkn, tt2], [tarT[p]])
                    act(t1[:, 0:n], cs[:, 0:n], AF.Exp, [tcs], [tt1], scale=C0)
                    yield
                    tt("dve", ktl[p][:, 0:n], k2[:, 0:n], t1[:, 0:n], ALU.mult, [tk2, tt1], [tktl[p]])
                    tt("pool", btl[p][:, 0:n], bvec[:, 0:n], t1[:, 0:n], ALU.mult, [tbvec, tt1], [tbtl[p]])
                    yield
                    for hh in range(2):
                        hm = blockones[:, hh * 64:hh * 64 + 1]
                        ts("dve" if hh == 0 else "pool", btlm[p][hh][:, 0:n], btl[p][:, 0:n], hm, None, ALU.mult, None, [tbtl[p], tC], [tbtlm[p][hh]])
                        ts("pool" if hh == 0 else "dve", ktlm[p][hh][:, 0:n], ktl[p][:, 0:n], hm, None, ALU.mult, None, [tktl[p], tC], [tktlm[p][hh]])
                    yield
                    for c in range(nblk):
                        S.op("act", lambda e: e.activation(out=t2[:, c * 128:(c + 1) * 128], in_=cs[:, c * 128:(c + 1) * 128],
                                                           func=AF.Exp, bias=nbias[:, p * 2 + c:p * 2 + c + 1], scale=C0),
                             reads=[tcs, tnb], writes=[tt2])
                    yield
                    tt("dve", f3[0][:, 0:n], k2[:, 0:n], t2[:, 0:n], ALU.mult, [tk2, tt2], [tf3[0]])
                    tt("pool", f3[1][:, 0:n], bvec[:, 0:n], t2[:, 0:n], ALU.mult, [tbvec, tt2], [tf3[1]])
                    S.op("act", lambda e: e.activation(out=f3[2][:, 0:n], in_=v32[:, 0:n], func=AF.Copy), reads=[tv32], writes=[tf3[2]])
                    yield
                    for c in range(nblk):
                        for q in range(3):
                            S.op("pe", lambda e: e.matmul(ps[pb[q]][:, 0:128], lhsT=f3[q][:, c * 128:(c + 1) * 128], rhs=ident[:],
                                                          start=True, stop=True), reads=[tf3[q], tC], writes=[tps[pb[q]]])
                        yield
                        for q in range(3):
                            evac_copy(tok3[q][:, c, p * 128:(p + 1) * 128], ps[pb[q]][:, 0:128], [tps[pb[q]]], [ttok3[q]])
                        for hh in range(2):
                            S.op('act', lambda e: e.activation(out=Vz[:, c, 2 * p + hh, hh * 64:(hh + 1) * 64], in_=ps[b2][:, hh * 64:(hh + 1) * 64], func=AF.Copy),
                                 reads=[tps[b2]], writes=[tVz])
                        yield

                gens = [pre_pair(0, wk, twk, fm3, tfm3, (4, 5, 6)), pre_pair(1, wk2, twk2, fm3b, tfm3b, (1, 2, 3))]
                while gens:
                    for g_ in list(gens):
                        try:
                            next(g_)
                        except StopIteration:
                            gens.remove(g_)
                chk('rwpre%d' % ti)
                Khat, Bhat, Vtok = tok3
                tKhat, tBhat, tVtok = ttok3
                for c in range(nblk):
                    cs_ = slice(c * 128, (c + 1) * 128)
                    for h in range(4):
                        p, hh = h // 2, h % 2
                        i = h % 2
                        S.op("pe", lambda e: e.matmul(ps[i][:, 0:256], lhsT=btlm[p][hh][:, cs_], rhs=arT[p][:, :, cs_], start=True, stop=True),
                             reads=[tbtlm[p][hh], tarT[p]], writes=[tps[i]])
                        S.op("pe", lambda e: e.matmul(ps[i][:, 256:512], lhsT=ktlm[p][hh][:, cs_], rhs=arT[p][:, :, cs_], start=True, stop=True),
                             reads=[tktlm[p][hh], tarT[p]], writes=[tps[i]])
                        chk('c0%d' % ti)
                        tt("dve", AT[h].rearrange("p a n -> p (a n)"), ps[i][:, :], maskT[:], ALU.mult, [tps[i], tC], [tAT[h]])
                        chk('c1%d' % ti)
                        S.op("pe", lambda e: e.matmul(ps[2][:, h * 128:(h + 1) * 128], lhsT=arT[p][:, 0, cs_], rhs=btlm[p][hh][:, cs_], start=True, stop=True),
                             reads=[tbtlm[p][hh], tarT[p]], writes=[tps[2]])
                    chk('ca%d' % ti)
                    for h in range(4):
                        tt("dve", Mc[:, h, :], ps[2][:, h * 128:(h + 1) * 128], masklow[:], ALU.mult, [tps[2], tC], [tMc])
                        S.op("pool", lambda e: e.tensor_copy(out=Mtc[:, h, :], in_=AT[h][:, 0, :]), reads=[tAT[h]], writes=[tMtc])
                        tt("pool", Ptc[:, h, :], AT[h][:, 0, :], ident[:], ALU.add, [tAT[h], tC], [tPtc])
                    chk('cb%d' % ti)
                    for lvl in range(7):
                        if lvl >= 1:
                            for h in range(4):
                                S.op("pe", lambda e: e.matmul(ps[3][:, h * 128:(h + 1) * 128], lhsT=Mc[:, h, :], rhs=Ptc[:, h, :], start=True, stop=True),
                                     reads=[tMc, tPtc], writes=[tps[3]])
                        if lvl < 6:
                            for h in range(4):
                                S.op("pe", lambda e: e.matmul(ps[4][:, h * 128:(h + 1) * 128], lhsT=Mtc[:, h, :], rhs=Mc[:, h, :], start=True, stop=True),
                                     reads=[tMc, tMtc], writes=[tps[4]])
                            if lvl < 5:
                                for h in range(4):
                                    S.op("pe", lambda e: e.matmul(ps[5][:, h * 128:(h + 1) * 128], lhsT=Mc[:, h, :], rhs=Mtc[:, h, :], start=True, stop=True),
                                         reads=[tMc, tMtc], writes=[tps[5]])
                        if lvl >= 1:
                            tt("dve", Ptc.rearrange("p h n -> p (h n)"), ps[3][:, :], Ptc.rearrange("p h n -> p (h n)"), ALU.add, [tps[3], tPtc], [tPtc])
                        if lvl < 6:
                            S.op("act", lambda e: e.activation(out=Mc.rearrange("p h n -> p (h n)"), in_=ps[4][:, :], func=AF.Copy), reads=[tps[4]], writes=[tMc])
                            if lvl < 5:
                                S.op("act", lambda e: e.activation(out=Mtc.rearrange("p h n -> p (h n)"), in_=ps[5][:, :], func=AF.Copy), reads=[tps[5]], writes=[tMtc])
                    chk('cc%d' % ti)
                    for p in range(2):
                        S.op("pe", lambda e: e.matmul(ps[0][:, p * 128:(p + 1) * 128], lhsT=arT[p][:, 0, cs_], rhs=Sbf[:, p, :], start=True, stop=False),
                             reads=[tarT[p], tSbf], writes=[tps[0]])
                        for hh in range(2):
                            h = 2 * p + hh
                            S.op("pe", lambda e: e.matmul(ps[0][:, h * 64:(h + 1) * 64], lhsT=AT[h][:, 2, :], rhs=Vtok[:, c, h * 64:(h + 1) * 64],
                                                          start=False, stop=(hh == 1)), reads=[tAT[h], tVtok], writes=[tps[0]])
                    S.op("act", lambda e: e.activation(out=Xsb.rearrange("p h n -> p (h n)"), in_=ps[0][:, 0:256], func=AF.Copy), reads=[tps[0]], writes=[tXsb])
                    for h in range(4):
                        S.op("pe", lambda e: e.matmul(ps[1][:, h * 64:(h + 1) * 64], lhsT=Ptc[:, h, :], rhs=Xsb[:, h, :], start=True, stop=True),
                             reads=[tPtc, tXsb], writes=[tps[1]])
                    S.op("act", lambda e: e.activation(out=Usb.rearrange("p h n -> p (h n)"), in_=ps[1][:, 0:256], func=AF.Copy), reads=[tps[1]], writes=[tUsb])
                    for hh in range(2):
                        src = ps[1][:, 0:256].rearrange("p (a b n) -> p a b n", a=2, b=2)[:, :, hh, :]
                        dst = Uz.rearrange("p (a b) n -> p a b n", b=2)[:, :, hh, hh * 64:(hh + 1) * 64]
                        S.op("act", lambda e: e.activation(out=dst, in_=src, func=AF.Copy), reads=[tps[1]], writes=[tUz])
                    chk('cd%d' % ti)
                    for p in range(2):
                        S.op("pe", lambda e: e.matmul(ps[2 + p][:, 0:128], lhsT=Sbf[:, p, :], rhs=arT[p][:, 1, cs_], start=True, stop=False),
                             reads=[tSbf, tarT[p]], writes=[tps[2 + p]])
                        for hh in range(2):
                            h = 2 * p + hh
                            S.op("pe", lambda e: e.matmul(ps[2 + p][:, 0:128], lhsT=Uz[:, h, :], rhs=AT[h][:, 1, :], start=False, stop=False),
                                 reads=[tUz, tAT[h]], writes=[tps[2 + p]])
                            S.op("pe", lambda e: e.matmul(ps[2 + p][:, 0:128], lhsT=Vz[:, c, h, :], rhs=AT[h][:, 3, :], start=False, stop=(hh == 1)),
                                 reads=[tVz, tAT[h]], writes=[tps[2 + p]])
                        if ti >= 0:
                            evac_copy(yT[p][:, cs_], ps[2 + p][:, 0:128], [tps[2 + p]], [tyT[p]])
                    for p in range(2):
                        S.op("pe", lambda e: e.matmul(ps[4 + p][:, 0:128], lhsT=Bhat[:, c, p * 128:(p + 1) * 128],
                                                      rhs=Usb.rearrange("p h n -> p (h n)")[:, p * 128:(p + 1) * 128], start=True, stop=False),
                             reads=[tBhat, tUsb], writes=[tps[4 + p]])
                        S.op("pe", lambda e: e.matmul(ps[4 + p][:, 0:128], lhsT=Khat[:, c, p * 128:(p + 1) * 128], rhs=Vtok[:, c, p * 128:(p + 1) * 128],
                                                      start=False, stop=True), reads=[tKhat, tVtok], writes=[tps[4 + p]])
                        tt("dve", t3[:, 0:128], ps[4 + p][:, 0:128], blockones[:], ALU.mult, [tps[4 + p], tC], [tt3])
                        stt(S32[:, p, :], S32[:, p, :], WC[:, p * 2 + c:p * 2 + c + 1], t3[:, 0:128], ALU.mult, ALU.add, [tS32, tWC, tt3], [tS32])
                    S.op("act", lambda e: e.activation(out=Sbf.rearrange("p a n -> p (a n)"), in_=S32.rearrange("p a n -> p (a n)"), func=AF.Copy),
                         reads=[tS32], writes=[tSbf])

                chk('chain%d' % ti)
                if ti < 0:
                    continue

                ob = rr["oT"] % 2
                rr["oT"] += 1
                oTt = oT[ob]
                toTt = toT[ob]

                for p in range(2):
                    y = yT[p]
                    S.op("pe", lambda e, y=y: e.matmul(ps[0][:, 0:n], lhsT=blockones[:], rhs=y[:, 0:n], start=True, stop=True),
                         reads=[tC, tyT[p]], writes=[tps[0]])
                    stt(t1[:, 0:n], ps[0][:, 0:n], -1.0 / 64.0, y[:, 0:n], ALU.mult, ALU.add, [tps[0], tyT[p]], [tt1])
                    tt("pool", t2[:, 0:n], t1[:, 0:n], t1[:, 0:n], ALU.mult, [tt1], [tt2])
                    S.op("pe", lambda e: e.matmul(ps[1][:, 0:n], lhsT=blockones[:], rhs=t2[:, 0:n], start=True, stop=True),
                         reads=[tC, tt2], writes=[tps[1]])
                    act(t2[:, 0:n], ps[1][:, 0:n], AF.Ln, [tps[1], tC], [tt2], bias=eps_ap(64e-5), scale=1.0 / 64.0)
                    act(t2[:, 0:n], t2[:, 0:n], AF.Exp, [tt2], [tt2], scale=-0.5)
                    tt("dve", t1[:, 0:n], t1[:, 0:n], t2[:, 0:n], ALU.mult, [tt1, tt2], [tt1])
                    ts("dve", t1[:, 0:n], t1[:, 0:n], pvec[:, PV_GNG + p:PV_GNG + p + 1], pvec[:, PV_GNB + p:PV_GNB + p + 1],
                       ALU.mult, ALU.add, [tt1, tC], [tt1])
                    tt("pool", t1[:, 0:n], t1[:, 0:n], bonus[p][:, 0:n], ALU.add, [tt1, tbonus[p]], [tt1])
                    tt("dve", oTt[:, 2 + p, 0:n], t1[:, 0:n], gate_r[:, p, 0:n], ALU.mult, [tt1, tgr], [toTt])

                chk('rwpost%d' % ti)
                qb0 = blk0
                OO = ps[4][:, :].rearrange("p (m n) -> p m n", m=2)
                SS = ps[5][:, :].rearrange("p (m n) -> p m n", m=2)
                for h in range(2):
                    kbs = list(range(0, qb0 + 2))

                    def geom(kb):
                        delta = kb - qb0
                        q_lo = 128 if delta == 1 else 0
                        return delta, q_lo, n - q_lo, kb % 2

                    def stage_qk(kb):
                        delta, q_lo, nq, par = geom(kb)
                        for m in range(2):
                            rs = slice(m * 64, m * 64 + 64)
                            pi = 2 * par + m
                            S.op("pe", lambda e: e.matmul(ps[pi][:, 0:nq], lhsT=KT[rs, h, kb * 128:(kb + 1) * 128], rhs=QT[rs, h, q_lo:n],
                                                          start=True, stop=True), reads=[tKT, tQT], writes=[tps[pi]])

                    def stage_rest(kb):
                        delta, q_lo, nq, par = geom(kb)
                        near = delta >= -1
                        E = Eb[par]
                        tE = tEb[par]
                        for m in range(2):
                            pi = 2 * par + m
                            if near:
                                boff = 0 if delta >= 0 else 128
                                stt(Ssb[m][:, 0:nq], ps[pi][:, 0:nq], 0.125, bext[h][:, boff:boff + nq], ALU.mult, ALU.add,
                                    [tps[pi], tC], [tSsb[m]])
                                if kb == 0:
                                    S.op("pool", lambda e: e.memset(Ssb[m][0:112, 0:nq], NEG), reads=[], writes=[tSsb[m]])
                                act(E[:, m, 0:nq], Ssb[m][:, 0:nq], AF.Exp, [tSsb[m]], [tE])
                            else:
                                bc = bcol[:, 2 + h:3 + h] if kb == 0 else bcol[:, h:h + 1]
                                act(E[:, m, 0:nq], ps[pi][:, 0:nq], AF.Exp, [tps[pi], tC], [tE], bias=bc, scale=0.125)
                        first = (kb == kbs[0])
                        last = (kb == kbs[-1])
                        if q_lo == 0:
                            Ef = E.rearrange("p m n -> p (m n)")
                            S.op("pe", lambda e: e.matmul(ps[4][:, :], lhsT=Vt[:, kb, h * 128:(h + 1) * 128], rhs=Ef, start=first, stop=last),
                                 reads=[tV, tE], writes=[tps[4]])
                            S.op("pe", lambda e: e.matmul(ps[5][:, :], lhsT=ones_bf[:], rhs=Ef, start=first, stop=last),
                                 reads=[tC, tE], writes=[tps[5]])
                        else:
                            for m in range(2):
                                S.op("pe", lambda e: e.matmul(OO[:, m, q_lo:n], lhsT=Vt[:, kb, h * 128:(h + 1) * 128], rhs=E[:, m, 0:nq],
                                                              start=first, stop=(last and m == 1)), reads=[tV, tE], writes=[tps[4]])
                                S.op("pe", lambda e: e.matmul(SS[:, m, q_lo:n], lhsT=ones_bf[:], rhs=E[:, m, 0:nq],
                                                              start=first, stop=(last and m == 1)), reads=[tC, tE], writes=[tps[5]])

                    def post_a():
                        S.op("dve", lambda e: e.reciprocal(out=t1[:, 0:n], in_=SS[:, 0, 0:n]), reads=[tps[5]], writes=[tt1])
                        S.op("dve", lambda e: e.reciprocal(out=t2[:, 0:n], in_=SS[:, 1, 0:n]), reads=[tps[5]], writes=[tt2])
                        tt("dve", t1[:, 0:n], OO[:, 0, 0:n], t1[:, 0:n], ALU.mult, [tps[4], tt1], [tt1])
                        tt("dve", t2[:, 0:n], OO[:, 1, 0:n], t2[:, 0:n], ALU.mult, [tps[4], tt2], [tt2])

                    def post_b(hp):
                        stt(t1[:, 0:n], t2[:, 0:n], pder[:, 3:4], t1[:, 0:n], ALU.mult, ALU.add, [tt1, tt2, tC], [tt1])
                        tt("pool", t2[:, 0:n], t1[:, 0:n], t1[:, 0:n], ALU.mult, [tt1], [tt2])
                        S.op("pe", lambda e: e.matmul(ps[6][:, 0:n], lhsT=ones_f[:], rhs=t2[:, 0:n], start=True, stop=True),
                             reads=[tC, tt2], writes=[tps[6]])
                        act(t2[:, 0:n], ps[6][:, 0:n], AF.Ln, [tps[6], tC], [tt2], bias=eps_ap(1e-5), scale=1.0 / 128.0)
                        act(t2[:, 0:n], t2[:, 0:n], AF.Exp, [tt2], [tt2], scale=-0.5)
                        tt("dve", t1[:, 0:n], t1[:, 0:n], t2[:, 0:n], ALU.mult, [tt1, tt2], [tt1])
                        stt(oTt[:, hp, 0:n], t1[:, 0:n], pder[:, 2:3], gate_a[:, hp, 0:n], ALU.mult, ALU.mult, [tt1, tga, tC], [toTt])

                    stage_qk(kbs[0])
                    for i_, kb in enumerate(kbs):
                        if i_ + 1 < len(kbs):
                            stage_qk(kbs[i_ + 1])
                        stage_rest(kb)
                        if h == 1 and i_ == min(1, len(kbs) - 1):
                            post_b(0)
                    post_a()
                    if h == 1:
                        post_b(1)

                chk('attn%d' % ti)
                for cc in range(4):
                    S.dma(lambda e: e.dma_start(out=agin[cc].ap()[:, ti * 256:ti * 256 + 256], in_=oTt[:, cc, 0:256]), reads=[toTt], writes=[tagin[cc]],
                          key="oT%d" % ob)
                if DEBUG:
                    dd = dbg_d.rearrange("(c p) n -> p c n", p=128)[:, :, ti * 256:ti * 256 + 256]
                    S.dma(lambda e, dd=dd, oTt=oTt: e.dma_start(out=dd, in_=oTt[:, :, 0:256]), reads=[toTt], key="oT%d" % ob)

            chk('main')
            for cc in range(4):
                S.dma(lambda e: e.collective_compute("AllGather", ALU.bypass, replica_groups=[[0, 1, 2, 3], [4, 5, 6, 7]],
                                                     ins=[agin[cc].ap()], outs=[agout[cc].ap()]),
                      reads=[tagin[cc]], writes=[tagout[cc]], key="cc%d" % cc, queue="pool", inc=1)
            chk('ag')
            S.barrier()
            apos[0] = 0
            oq = carve(16 * 1024 * 2, BF16, "p (c n) -> p c n", c=16)
            toq = S.tile("oq")
            gbt = carve(4 * D * 4, F32, "p (a n) -> p a n", a=4)
            tgb = S.tile("gbt")
            rsd = [carve(D * 4, F32) for _ in range(2)]
            trsd = [S.tile("rsd%d" % i) for i in range(2)]
            fstE = [carve(4 * 6 * 4, F32, "p (a n) -> p a n", a=4) for _ in range(2)]
            fmvE = [carve(8 * 4, F32) for _ in range(2)]
            tfE = [S.tile("fstE%d" % i) for i in range(2)]
            fstP = [carve(4 * 6 * 4, F32, "p (a n) -> p a n", a=4) for _ in range(2)]
            fmvP = [carve(8 * 4, F32) for _ in range(2)]
            tfP = [S.tile("fstP%d" % i) for i in range(2)]
            assert apos[0] <= arena_w

            for kc in range(16):
                b = kc % 2
                S.dma(lambda e, kc=kc, b=b: e.dma_start(out=xb[b][:, 0:D], in_=wout_d[kc * 128:(kc + 1) * 128, :]),
                      writes=[txb[b]], key="xb%d" % b)
                eng = ("dve", "pool", "act")[kc % 3]
                if eng == "act":
                    S.op("act", lambda e, kc=kc, b=b: e.activation(out=W[:, kc, 0:D], in_=xb[b][:, 0:D], func=AF.Copy),
                         reads=[txb[b]], writes=[tW])
                else:
                    S.op(eng, lambda e, kc=kc, b=b: e.tensor_copy(out=W[:, kc, 0:D], in_=xb[b][:, 0:D]), reads=[txb[b]], writes=[tW])
            gsrc = bass.AP(tensor=gb_d.tensor, offset=0, ap=[[0, 128], [D, 4], [1, D]])
            S.dma(lambda e: e.dma_start(out=gbt, in_=gsrc), writes=[tgb], key="gbt")

            idxt = sbt("idxt", [128, 4], mybir.dt.int32)
            tidx = S.tile("idxt")
            S.dma(lambda e: e.dma_start(out=idxt[:], in_=idx_d), writes=[tidx], key="idxt")
            for kc in range(16):
                r_, c_ = kc // 4, kc % 4
                agv = agout[c_].ap().rearrange("e (q n) -> (e q) n", q=4)
                S.dma(lambda e: e.indirect_dma_start(out=oq[:, kc, :], out_offset=None, in_=agv,
                                                     in_offset=bass.IndirectOffsetOnAxis(ap=idxt[:, r_:r_ + 1], axis=0)),
                      reads=[tagout[c_], tidx], writes=[toq], key="oq", queue="pool")
            alpha = 2.0 ** 0.25
            def ln_emb(blk):
                b = blk % 2
                xt, r, tr = xb[b], rsd[b], trsd[b]
                fst, fmv, tf = fstE[b], fmvE[b], tfE[b]
                S.dma(lambda e: e.dma_start(out=xt[:, 0:D], in_=xq_d[blk * 128:(blk + 1) * 128, :]), writes=[txb[b]], key="xb%d" % b)
                for q in range(4):
                    S.op("dve", lambda e: e.bn_stats(out=fst[:, q, :], in_=xt[:, q * 512:(q + 1) * 512]), reads=[txb[b]], writes=[tf])
                S.op("dve", lambda e: e.bn_aggr(out=fmv[:, 0:2], in_=fst.rearrange("p a n -> p (a n)")), reads=[tf], writes=[tf])
                act(fmv[:, 2:3], fmv[:, 1:2], AF.Ln, [tf], [tf], bias=eps_ap(1e-5))
                act(fmv[:, 2:3], fmv[:, 2:3], AF.Exp, [tf], [tf], scale=-0.5)
                stt(fmv[:, 3:4], fmv[:, 0:1], -1.0, fmv[:, 2:3], ALU.mult, ALU.mult, [tf], [tf])
                S.op("act", lambda e: e.activation(out=r, in_=xt[:, 0:D], func=AF.Identity, bias=fmv[:, 3:4], scale=fmv[:, 2:3]),
                     reads=[txb[b], tf], writes=[tr])
                tt("pool", r, r, gbt[:, 0, :], ALU.mult, [tr, tgb], [tr])
                tt("pool", r, r, gbt[:, 1, :], ALU.add, [tr, tgb], [tr])

            def proj(blk):
                b = blk % 2
                r, tr = rsd[b], trsd[b]
                for ct in range(4):
                    for kc in range(16):
                        S.op("pe", lambda e: e.matmul(ps[ct][:, :], lhsT=oq[:, kc, blk * 128:(blk + 1) * 128],
                                                      rhs=W[:, kc, ct * 512:(ct + 1) * 512], start=(kc == 0), stop=(kc == 15)),
                             reads=[toq, tW], writes=[tps[ct]])
                    stt(r[:, ct * 512:(ct + 1) * 512], r[:, ct * 512:(ct + 1) * 512], alpha, ps[ct][:, :], ALU.mult, ALU.add,
                        [tr, tps[ct]], [tr])

            def post_ln(blk):
                b = blk % 2
                r, tr = rsd[b], trsd[b]
                fst, fmv, tf = fstP[b], fmvP[b], tfP[b]
                for q in range(4):
                    S.op("dve", lambda e: e.bn_stats(out=fst[:, q, :], in_=r[:, q * 512:(q + 1) * 512]), reads=[tr], writes=[tf])
                S.op("dve", lambda e: e.bn_aggr(out=fmv[:, 4:6], in_=fst.rearrange("p a n -> p (a n)")), reads=[tf], writes=[tf])
                act(fmv[:, 6:7], fmv[:, 5:6], AF.Ln, [tf], [tf], bias=eps_ap(1e-5))
                act(fmv[:, 6:7], fmv[:, 6:7], AF.Exp, [tf], [tf], scale=-0.5)
                stt(fmv[:, 7:8], fmv[:, 4:5], -1.0, fmv[:, 6:7], ALU.mult, ALU.mult, [tf], [tf])
                S.op("act", lambda e: e.activation(out=r, in_=r, func=AF.Identity, bias=fmv[:, 7:8], scale=fmv[:, 6:7]),
                     reads=[tr, tf], writes=[tr])
                tt("dve", r, r, gbt[:, 2, :], ALU.mult, [tr, tgb], [tr])
                tt("pool", r, r, gbt[:, 3, :], ALU.add, [tr, tgb], [tr])
                S.dma(lambda e: e.dma_start(out=out_d[blk * 128:(blk + 1) * 128, :], in_=r), reads=[tr], key="rsd%d" % b)

            ln_emb(0)
            for blk in range(8):
                if blk + 1 < 8:
                    ln_emb(blk + 1)
                proj(blk)
                post_ln(blk)

        except _Stop:
            pass
        S.emit()
    return nc


_NC_CACHE = {}


def kernel(**inp):
    maps = _prep_inputs(inp)
    x = np.asarray(inp["x"], np.float32)
    for c in range(8):
        b, r = c // 4, c % 4
        maps[c]["xq"] = np.ascontiguousarray(x[b, r * 1024:(r + 1) * 1024])
        p = np.arange(128)[:, None]
        kc = np.arange(4)[None, :]
        maps[c]["idx"] = ((kc * 128 + p) * 4 + r).astype(np.int32)
    if "nc" not in _NC_CACHE:
        _NC_CACHE["nc"] = build_nc()
    nc = _NC_CACHE["nc"]
    res = run_bass_kernel_spmd(nc, maps, core_ids=list(range(8)))
    out = np.zeros((2, SEQ, D), np.float32)
    for c in range(8):
        b, r = c // 4, c % 4
        out[b, r * 1024:(r + 1) * 1024] = res.results[c]["out"]
    if DEBUG:
        kernel.dbg = [res.results[c]["dbg"] for c in range(8)]
    return out
```

```python
import math
from contextlib import ExitStack

import numpy as np
import ml_dtypes

import concourse.bass as bass
import concourse.mybir as mybir
from concourse.bass_utils import run_bass_kernel_spmd

F32 = mybir.dt.float32
BF16 = mybir.dt.bfloat16
AF = mybir.ActivationFunctionType
ALU = mybir.AluOpType

D = 2048
SEQ = 4096
NB = 33
P_TOK = NB * 128
NCOL = 2240
C0 = math.exp(-0.5)
NEG = -1.0e4
TABW = 511
DEBUG = False
STOP = None


class _Stop(Exception):
    pass


_HITS = {}


def chk(tag):
    import os
    if STOP == tag:
        _HITS[tag] = _HITS.get(tag, 0) + 1
        if _HITS[tag] >= int(os.environ.get("NTH", "1")):
            raise _Stop()


class T:
    __slots__ = ("name", "w", "r", "parent")

    def __init__(self, name, init_r=None, parent=None):
        self.name = name
        self.w = None
        self.r = dict(init_r) if init_r else {}
        self.parent = parent


class _Rec:
    def __init__(self):
        self.call = None

    def __getattr__(self, name):
        def f(*a, **k):
            assert self.call is None
            self.call = (name, a, k)
            return self
        return f


def _freeze(fn):
    r = _Rec()
    fn(r)
    name, a, k = r.call
    return lambda e: getattr(e, name)(*a, **k)


class Sched:
    ENG = ("pe", "act", "dve", "pool", "sp")

    def __init__(self, nc, es):
        self.nc = nc
        self.es = es
        self.ops = {e: [] for e in self.ENG}
        self.seen = {e: {} for e in self.ENG}
        self.dsems = {}
        self.esem = {}
        self.bar = {}
        self.groups = set()

    def tile(self, name):
        return T(name, self.bar)

    def barrier(self):
        b = {}
        for e in self.ENG:
            if e != "sp" and self.ops[e]:
                for i in range(len(self.ops[e]) - 1, -1, -1):
                    if self.ops[e][i]["dma"] is None:
                        b[('E', e)] = ('E', e, i)
                        break
        for k, v in self.dsems.items():
            if v[1] > 0:
                b[('D', k)] = ('D', k, v[1])
        self.bar = b

    @staticmethod
    def _dep(waits, ev):
        key = ev[:2]
        if waits.get(key, -1) < ev[2]:
            waits[key] = ev[2]

    def _collect(self, eng, reads, writes):
        waits = {}
        for t in reads:
            if t.w is not None:
                self._dep(waits, t.w)
        for t in writes:
            if t.w is not None:
                self._dep(waits, t.w)
            for ev in t.r.values():
                self._dep(waits, ev)
        wl = []
        for key, val in waits.items():
            if key[0] == 'E' and key[1] == eng and eng == 'pe':
                continue
            if self.seen[eng].get(key, -1) >= val:
                continue
            self.seen[eng][key] = val
            wl.append((key, val))
        return wl

    def op(self, eng, fn, reads=(), writes=()):
        fn = _freeze(fn)
        extra = [t.parent for t in list(reads) + list(writes) if t.parent is not None]
        if extra:
            reads = list(reads) + extra
        wl = self._collect(eng, reads, writes)
        idx = len(self.ops[eng])
        ev = ('E', eng, idx)
        self.ops[eng].append(dict(waits=wl, fn=fn, sig=False, dma=None))
        for t in reads:
            t.r[('E', eng)] = ev
        for t in writes:
            t.w = ev
            t.r = {}
        return ev

    def dsem(self, key):
        if key not in self.dsems:
            h = self.es.enter_context(self.nc.semaphore("d_" + key))
            self.dsems[key] = [h, 0]
        return self.dsems[key]

    def dma(self, fn, reads=(), writes=(), key=None, queue="sp", inc=16):
        fn = _freeze(fn)
        wl = self._collect(queue, reads, writes)
        if key in self.groups:
            wl = [w for w in wl if w[0] != ('D', key)]
        ds = self.dsem(key)
        ds[1] += inc
        ev = ('D', key, ds[1])
        self.ops[queue].append(dict(waits=wl, fn=fn, sig=False, dma=(key, inc)))
        for t in reads:
            t.r[('D', key)] = ev
        for t in writes:
            t.w = ev
            t.r = {}
        return ev

    def emit(self):
        nc = self.nc
        for e in self.ENG:
            for o in self.ops[e]:
                for key, val in o["waits"]:
                    if key[0] == 'E':
                        self.ops[key[1]][val]["sig"] = True
        cnt = {}
        for e in self.ENG:
            c = 0
            lst = []
            for o in self.ops[e]:
                if o["sig"]:
                    c += 1
                lst.append(c)
            cnt[e] = lst
            self.esem[e] = self.es.enter_context(nc.semaphore("e_" + e))
        finals = [(('D', k), v[1]) for k, v in self.dsems.items() if v[1] > 0]

        def resolve(key, val):
            if key[0] == 'E':
                return self.esem[key[1]], cnt[key[1]][val]
            if key[1] in self.groups:
                val = self.dsems[key[1]][1]
            return self.dsems[key[1]][0], val

        def body(e):
            def run(eng):
                for o in self.ops[e]:
                    for key, val in o["waits"]:
                        s, v = resolve(key, val)
                        eng.wait_ge(s, v)
                    ins = o["fn"](eng)
                    if o["dma"] is not None:
                        ins.then_inc(self.dsems[o["dma"][0]][0], o["dma"][1])
                    elif o["sig"]:
                        ins.then_inc(self.esem[e], 1)
                if e == "sp":
                    for key, val in finals:
                        s, v = resolve(key, val)
                        eng.wait_ge(s, v)
            return run

        with nc.Block() as block:
            block.tensor(body("pe"))
            block.scalar(body("act"))
            block.vector(body("dve"))
            block.gpsimd(body("pool"))
            block.sync(body("sp"))


def _bucket(n):
    n = np.maximum(n, 0)
    nf = np.maximum(n, 1).astype(np.float32)
    large = 16 + (np.log(nf / np.float32(16)) / np.float32(math.log(128 / 16)) * np.float32(16)).astype(np.int32)
    large = np.minimum(large, 31)
    return np.where(n < 16, n, large)


PV_MU = 0
PV_MUWD = 6
PV_MUAD = 7
PV_W0 = 8
PV_A0 = 10
PV_KK = 12
PV_KA = 14
PV_RK = 16
PV_GNG = 18
PV_GNB = 20
PV_SUBLN = 22
PV_LNG = 23
PV_LNB = 39
PV_N = 55


def _consts():
    c = {}
    c["ident"] = np.eye(128, dtype=np.float32).astype(ml_dtypes.bfloat16)
    c["ones_bf"] = np.ones((128, 128), dtype=ml_dtypes.bfloat16)
    c["ones_f"] = np.ones((128, 128), dtype=np.float32)
    bo = np.zeros((128, 128), np.float32)
    bo[:64, :64] = 1.0
    bo[64:, 64:] = 1.0
    c["blockones"] = bo
    s = np.arange(128)[:, None]
    t = np.arange(128)[None, :]
    strict = (s < t).astype(np.float32)
    incl = (s <= t).astype(np.float32)
    m = np.zeros((128, 2, 2, 128), np.float32)
    m[:, :, 0, :] = strict[:, None, :]
    m[:, :, 1, :] = incl[:, None, :]
    c["maskT"] = m.reshape(128, 512)
    c["masklow"] = (s > t).astype(np.float32)
    d = np.arange(TABW) - 127
    oh = np.zeros((33, TABW), np.float32)
    b = _bucket(d)
    for i in range(TABW):
        if d[i] >= 0:
            oh[b[i], i] = 1.0
        else:
            oh[32, i] = NEG
    c["oh"] = oh
    return c


def _prep_inputs(inp):
    f = np.float32
    w_in = np.asarray(inp["w_in"][0], f)
    w_out = np.asarray(inp["w_out"][0], f)
    x = np.asarray(inp["x"], f)
    consts = _consts()
    lam4 = np.stack([inp["lambda_q1"][0], inp["lambda_k1"][0], inp["lambda_q2"][0], inp["lambda_k2"][0]]).astype(f)
    rows = []
    for r in range(4):
        rows += list(range((2 * r) * 128, (2 * r + 2) * 128))
        rows += list(range(1024 + (4 * r) * 64, 1024 + (4 * r + 4) * 64))
    w_out_p = np.ascontiguousarray(w_out[rows])
    gb = np.stack([inp["ln_emb_g"], inp["ln_emb_b"], inp["ln_post_g"][0], inp["ln_post_b"][0]]).astype(f)
    maps = []
    for c in range(8):
        b, hg = c // 4, c % 4
        h0 = 2 * hg
        cols = []
        cols += list(range(h0 * 128, (h0 + 2) * 128))
        cols += list(range(1024 + h0 * 128, 1024 + (h0 + 2) * 128))
        cols += list(range(3072 + h0 * 128, 3072 + (h0 + 2) * 128))
        rb = 4096 + hg * 256
        cols += list(range(rb, rb + 256))
        cols += list(range(rb + 1024, rb + 1024 + 256))
        cols += list(range(rb + 2048, rb + 2048 + 256))
        cols += list(range(7360 + hg * 256, 7360 + hg * 256 + 256))
        cols += list(range(4096 + 3072, 4096 + 3072 + 192))
        cols += list(range(2048 + h0 * 128, 2048 + (h0 + 2) * 128))
        assert len(cols) == NCOL
        wi = np.ascontiguousarray(w_in[:, cols])
        pv = np.zeros((128, PV_N), f)
        mu = inp["rw_mu"][0]
        rs = slice(hg * 256, hg * 256 + 256)
        for p in range(2):
            ps_ = slice(hg * 256 + p * 128, hg * 256 + (p + 1) * 128)
            pv[:, PV_MU + 0 + p] = mu[0:1024][ps_]
            pv[:, PV_MU + 2 + p] = mu[1024:2048][ps_]
            pv[:, PV_MU + 4 + p] = mu[2048:3072][ps_]
            pv[:, PV_W0 + p] = inp["rw_w0"][0][ps_]
            pv[:, PV_A0 + p] = inp["rw_a0"][0][ps_]
            pv[:, PV_KK + p] = inp["rw_k_k"][0][ps_]
            pv[:, PV_KA + p] = inp["rw_k_a"][0][ps_]
            pv[:, PV_RK + p] = inp["rw_r_k"][0].reshape(-1)[ps_]
            pv[:, PV_GNG + p] = inp["rw_gn_g"][0][ps_]
            pv[:, PV_GNB + p] = inp["rw_gn_b"][0][ps_]
        pv[:96, PV_MUWD] = mu[3072:3168]
        pv[:96, PV_MUAD] = mu[3168:3264]
        pv[:, PV_SUBLN] = inp["subln_g"][0]
        pv[:, PV_LNG:PV_LNG + 16] = np.asarray(inp["ln_emb_g"], f).reshape(16, 128).T
        pv[:, PV_LNB:PV_LNB + 16] = np.asarray(inp["ln_emb_b"], f).reshape(16, 128).T
        relb = np.ones((33, 2, 128), f)
        for j in range(2):
            relb[:32, j, :] = np.asarray(inp["rel_bias"], f)[:, h0 + j][:, None]
        lora = np.zeros((96, 2, 256), f)
        lora[:, 0, :] = inp["rw_w_up"][0][:, rs]
        lora[:, 1, :] = inp["rw_a_up"][0][:, rs]
        m = {
            "x": np.ascontiguousarray(x[b]),
            "meta": np.asarray(inp["meta_tokens"], f),
            "w_in": wi,
            "w_out": w_out_p,
            "pvec": pv,
            "relb": relb,
            "lora": lora,
            "lam4": lam4,
            "gb": gb,
        }
        m.update(consts)
        maps.append(m)
    return maps


def build_nc():
    nc = bass.Bass("TRN2", target_bir_lowering=False)

    def din(name, shape, dt=F32):
        return nc.dram_tensor(name, list(shape), dt, kind="ExternalInput").ap()

    x_d = din("x", [SEQ, D])
    meta_d = din("meta", [16, D])
    win_d = din("w_in", [D, NCOL])
    wout_d = din("w_out", [D, D])
    pvec_d = din("pvec", [128, PV_N])
    relb_d = din("relb", [33, 2, 128])
    lora_d = din("lora", [96, 2, 256])
    lam4_d = din("lam4", [4, 64])
    gb_d = din("gb", [4, D])
    ident_d = din("ident", [128, 128], BF16)
    onesbf_d = din("ones_bf", [128, 128], BF16)
    onesf_d = din("ones_f", [128, 128])
    bo_d = din("blockones", [128, 128])
    maskT_d = din("maskT", [128, 512])
    masklow_d = din("masklow", [128, 128])
    oh_d = din("oh", [33, TABW])
    xq_d = din("xq", [1024, D])
    idx_d = din("idx", [128, 4], mybir.dt.int32)
    out_d = nc.dram_tensor("out", [1024, D], F32, kind="ExternalOutput").ap()
    if DEBUG:
        dbg_d = nc.dram_tensor("dbg", [512, SEQ], BF16, kind="ExternalOutput").ap()
    agin = [nc.dram_tensor("agin%d" % c, [128, SEQ], BF16) for c in range(4)]
    agout = [nc.dram_tensor("agout%d" % c, [512, SEQ], BF16) for c in range(4)]
    tab_d = [nc.dram_tensor("tab%d" % j, [128, TABW], F32) for j in range(2)]

    with ExitStack() as es:
        S = Sched(nc, es)

        def sbt(name, shape, dt=F32):
            return es.enter_context(nc.sbuf_tensor("s_" + name, list(shape), dt))

        W = sbt("W", [128, 16, NCOL], BF16)
        tW = S.tile("W")
        xb = [sbt("xb%d" % i, [128, NCOL]) for i in range(2)]
        txb = [S.tile("xb%d" % i) for i in range(2)]
        pvec = sbt("pvec", [128, PV_N])
        pder = sbt("pder", [128, 8])
        ident = sbt("ident", [128, 128], BF16)
        ones_bf = sbt("ones_bf", [128, 128], BF16)
        ones_f = sbt("ones_f", [128, 128])
        blockones = sbt("blockones", [128, 128])
        maskT = sbt("maskT", [128, 512])
        masklow = sbt("masklow", [128, 128])
        lora32 = sbt("lora32", [96, 2, 256])
        lora = sbt("lora", [96, 2, 256], BF16)
        bext = [sbt("bext%d" % j, [128, 384]) for j in range(2)]
        bcol = sbt("bcol", [128, 4])
        tC = S.tile("consts")
        arena_w = 28100
        arena = sbt("arena", [128, arena_w])
        apos = [0]

        def carve(nbytes, dt, shape_str=None, **kw):
            n32 = (nbytes + 3) // 4
            a = apos[0]
            apos[0] += n32
            assert apos[0] <= arena_w, apos[0]
            ap = arena[:, a:a + n32]
            if dt != F32:
                ap = ap.bitcast(dt)
            if shape_str:
                ap = ap.rearrange(shape_str, **kw)
            return ap

        ps = [es.enter_context(nc.psum_tensor("ps%d" % i, [128, 512], F32)) for i in range(7)]
        tps = [S.tile("ps%d" % i) for i in range(7)]
        psb = es.enter_context(nc.psum_tensor("psb", [128, 1024], BF16))
        tpsb = S.tile("psb")

        KT = carve(2 * P_TOK * 2, BF16, "p (h n) -> p h n", h=2)
        tKT = S.tile("KT")
        Vt = carve(NB * 256 * 2, BF16, "p (b n) -> p b n", b=NB)
        tV = S.tile("V")
        xn_off = apos[0]
        xn = carve(D * 2, BF16)
        txn = S.tile("xn")
        hT_off = apos[0]
        hT = carve(16 * 256 * 2, BF16, "p (c n) -> p c n", c=16)
        thT = S.tile("hT")
        wk2 = [arena[:, hT_off + i * 256:hT_off + (i + 1) * 256] for i in range(8)] + \
              [arena[:, xn_off + i * 256:xn_off + (i + 1) * 256] for i in range(4)]
        twk2 = [T("wk2_%d" % i, parent=thT) for i in range(8)] + [T("wk2_%d" % (8 + i), parent=txn) for i in range(4)]
        QT = carve(2 * 256 * 2, BF16, "p (h n) -> p h n", h=2)
        tQT = S.tile("QT")
        gate_a = carve(2 * 256 * 2, BF16, "p (h n) -> p h n", h=2)
        tga = S.tile("gate_a")
        gate_r = carve(2 * 256 * 2, BF16, "p (h n) -> p h n", h=2)
        tgr = S.tile("gate_r")
        zb = [carve(257 * 4, F32) for _ in range(8)]
        tzb = [S.tile("z%d" % i) for i in range(8)]
        NWK = 12
        wk = [carve(256 * 4, F32) for _ in range(NWK)]
        twk = [S.tile("wk%d" % i) for i in range(NWK)]
        bonus = [carve(256 * 4, F32) for _ in range(2)]
        tbonus = [S.tile("bonus%d" % i) for i in range(2)]
        yT = [wk[1], wk[5]]
        tyT = [twk[1], twk[5]]
        twb = carve(256 * 2, BF16)
        adb = carve(256 * 2, BF16)
        ttwb = S.tile("twb")
        tadb = S.tile("adb")
        arT = [carve(2 * 256 * 2, BF16, "p (a n) -> p a n", a=2) for _ in range(2)]
        tarT = [S.tile("arT%d" % i) for i in range(2)]
        ktl = [carve(256 * 2, BF16) for _ in range(2)]
        btl = [carve(256 * 2, BF16) for _ in range(2)]
        tktl = [S.tile("ktl%d" % i) for i in range(2)]
        tbtl = [S.tile("btl%d" % i) for i in range(2)]
        fm3 = [carve(256 * 2, BF16) for _ in range(3)]
        tfm3 = [S.tile("fm3_%d" % i) for i in range(3)]
        fm3b = [carve(256 * 2, BF16) for _ in range(3)]
        tfm3b = [S.tile("fm3b_%d" % i) for i in range(3)]
        tnb2 = [S.tile("nbias%d" % i) for i in range(2)]
        tok3 = [carve(2 * 256 * 2, BF16, "p (c n) -> p c n", c=2) for _ in range(3)]
        ttok3 = [S.tile("tok3_%d" % i) for i in range(3)]
        WC = carve(4 * 4, F32)
        tWC = S.tile("WC")
        nbias = carve(4 * 4, F32)
        tnbias = S.tile("nbias")
        AT = [carve(4 * 128 * 2, BF16, "p (a n) -> p a n", a=4) for _ in range(4)]
        tAT = [S.tile("AT%d" % i) for i in range(4)]
        Mc = carve(4 * 128 * 2, BF16, "p (h n) -> p h n", h=4)
        Mtc = carve(4 * 128 * 2, BF16, "p (h n) -> p h n", h=4)
        Ptc = carve(4 * 128 * 2, BF16, "p (h n) -> p h n", h=4)
        tMc = S.tile("Mc")
        tMtc = S.tile("Mtc")
        tPtc = S.tile("Ptc")
        btlm = [[carve(256 * 2, BF16) for _ in range(2)] for _ in range(2)]
        ktlm = [[carve(256 * 2, BF16) for _ in range(2)] for _ in range(2)]
        tbtlm = [[S.tile("btlm%d%d" % (i, j)) for j in range(2)] for i in range(2)]
        tktlm = [[S.tile("ktlm%d%d" % (i, j)) for j in range(2)] for i in range(2)]
        Vz = carve(2 * 4 * 128 * 2, BF16, "p (c h n) -> p c h n", c=2, h=4)
        tVz = S.tile("Vz")
        Uz = carve(4 * 128 * 2, BF16, "p (h n) -> p h n", h=4)
        tUz = S.tile("Uz")
        Xsb = carve(256 * 2, BF16, "p (h n) -> p h n", h=4)
        Usb = carve(256 * 2, BF16, "p (h n) -> p h n", h=4)
        tXsb = S.tile("Xsb")
        tUsb = S.tile("Usb")
        S32 = carve(2 * 128 * 4, F32, "p (a n) -> p a n", a=2)
        Sbf = carve(2 * 128 * 2, BF16, "p (a n) -> p a n", a=2)
        tS32 = S.tile("S32")
        tSbf = S.tile("Sbf")
        Ssb = [wk[0], wk[2]]
        tSsb = [twk[0], twk[2]]
        Eb = [carve(2 * 256 * 2, BF16, "p (m n) -> p m n", m=2) for _ in range(2)]
        tEb = [S.tile("Eb%d" % i) for i in range(2)]
        oT = [carve(4 * 256 * 2, BF16, "p (c n) -> p c n", c=4) for _ in range(2)]
        toT = [S.tile("oT%d" % i) for i in range(2)]
        stats = carve(4 * 6 * 4, F32, "p (a n) -> p a n", a=4)
        mv = carve(8 * 4, F32)
        tst = S.tile("stats")
        lamw = carve(4 * 64 * 4, F32, "p (a n) -> p a n", a=4)
        tabsb = carve(TABW * 4, F32)
        ttab = S.tile("tabsb")
        main_end = apos[0]
        print("arena main words", main_end, "of", arena_w)

        tagin = [S.tile("agin%d" % c) for c in range(4)]
        tagout = [S.tile("agout%d" % c) for c in range(4)]
        ttabd = [S.tile("tabd%d" % j) for j in range(2)]

        try:
            S.groups.add("consts")

            def cload(dst, src):
                S.dma(lambda e: e.dma_start(out=dst, in_=src), writes=[tC], key="consts")

            cload(pvec[:], pvec_d)
            cload(ident[:], ident_d)
            cload(ones_bf[:], onesbf_d)
            cload(ones_f[:], onesf_d)
            cload(blockones[:], bo_d)
            cload(maskT[:], maskT_d)
            cload(masklow[:], masklow_d)
            cload(lora32[:], lora_d)
            relb = carve(2 * 128 * 4, F32, "p (a n) -> p a n", a=2)
            ohsb = carve(TABW * 4, F32)
            cload(relb[0:33], relb_d)
            cload(ohsb[0:33], oh_d)
            lam_src = bass.AP(tensor=lam4_d.tensor, offset=0, ap=[[0, 128], [64, 4], [1, 64]])
            cload(lamw, lam_src)

            S.op("dve", lambda e: e.tensor_copy(out=lora[:], in_=lora32[:]), reads=[tC], writes=[tC])
            S.op("dve", lambda e: e.tensor_scalar(out=pder[:, 0:2], in0=pvec[:, PV_KA:PV_KA + 2], scalar1=-1.0, scalar2=1.0,
                                                  op0=ALU.mult, op1=ALU.add), reads=[tC], writes=[tC])
            S.op("dve", lambda e: e.tensor_scalar(out=pder[:, 2:3], in0=pvec[:, PV_SUBLN:PV_SUBLN + 1], scalar1=0.8, scalar2=None,
                                                  op0=ALU.mult), reads=[tC], writes=[tC])
            S.op("dve", lambda e: e.tensor_tensor(out=lamw[:, 0, :], in0=lamw[:, 0, :], in1=lamw[:, 1, :], op=ALU.mult),
                 reads=[tC], writes=[tC])
            S.op("dve", lambda e: e.tensor_tensor(out=lamw[:, 2, :], in0=lamw[:, 2, :], in1=lamw[:, 3, :], op=ALU.mult),
                 reads=[tC], writes=[tC])
            S.op("dve", lambda e: e.reduce_sum(out=pder[:, 4:5], in_=lamw[:, 0, :], axis=mybir.AxisListType.X),
                 reads=[tC], writes=[tC])
            S.op("dve", lambda e: e.reduce_sum(out=pder[:, 5:6], in_=lamw[:, 2, :], axis=mybir.AxisListType.X),
                 reads=[tC], writes=[tC])
            S.op("act", lambda e: e.activation(out=pder[:, 4:6], in_=pder[:, 4:6], func=AF.Exp), reads=[tC], writes=[tC])
            S.op("dve", lambda e: e.tensor_tensor(out=pder[:, 3:4], in0=pder[:, 5:6], in1=pder[:, 4:5], op=ALU.subtract),
                 reads=[tC], writes=[tC])
            S.op("dve", lambda e: e.tensor_scalar(out=pder[:, 3:4], in0=pder[:, 3:4], scalar1=-0.2, scalar2=None, op0=ALU.add),
                 reads=[tC], writes=[tC])
            S.op("pool", lambda e: e.memset(S32.rearrange("p a n -> p (a n)"), 0.0), writes=[tS32])
            S.op("pool", lambda e: e.memset(Sbf.rearrange("p a n -> p (a n)"), 0.0), writes=[tSbf])
            S.op("pool", lambda e: e.memset(Vz.rearrange("p c h n -> p (c h n)"), 0.0), writes=[tVz])
            S.op("pool", lambda e: e.memset(Uz.rearrange("p h n -> p (h n)"), 0.0), writes=[tUz])
            for i in range(8):
                S.op("pool", lambda e, i=i: e.memset(zb[i], 0.0), writes=[tzb[i]])

            for j in range(2):
                S.op("pe", lambda e, j=j: e.matmul(ps[0][:, 0:256], lhsT=relb[0:33, j, :], rhs=ohsb[0:33, 0:256], start=True, stop=True),
                     reads=[tC], writes=[tps[0]])
                S.op("pe", lambda e, j=j: e.matmul(ps[1][:, 0:255], lhsT=relb[0:33, j, :], rhs=ohsb[0:33, 256:511], start=True, stop=True),
                     reads=[tC], writes=[tps[1]])
                S.op("dve", lambda e: e.tensor_copy(out=tabsb[:, 0:256], in_=ps[0][:, 0:256]), reads=[tps[0]], writes=[ttab])
                S.op("dve", lambda e: e.tensor_copy(out=tabsb[:, 256:511], in_=ps[1][:, 0:255]), reads=[tps[1]], writes=[ttab])
                S.op("dve", lambda e, j=j: e.tensor_copy(out=bcol[:, j:j + 1], in_=tabsb[:, 510:511]), reads=[ttab], writes=[tC])
                S.op("dve", lambda e, j=j: e.tensor_copy(out=bcol[:, 2 + j:3 + j], in_=tabsb[:, 510:511]), reads=[ttab], writes=[tC])
                S.op("dve", lambda e, j=j: e.memset(bcol[0:112, 2 + j:3 + j], NEG), reads=[], writes=[tC])
                S.dma(lambda e, j=j: e.dma_start(out=tab_d[j].ap(), in_=tabsb), reads=[ttab], writes=[ttabd[j]], key="tabst")
                src = bass.AP(tensor=tab_d[j].ap().tensor, offset=127, ap=[[TABW - 1, 128], [1, 384]])
                S.dma(lambda e, j=j, src=src: e.dma_start(out=bext[j][:], in_=src), reads=[ttabd[j]], writes=[tC], key="bext%d" % j)

            for kc in range(16):
                b = kc % 2
                S.dma(lambda e, kc=kc, b=b: e.dma_start(out=xb[b][:], in_=win_d[kc * 128:(kc + 1) * 128, :]),
                      writes=[txb[b]], key="xb%d" % b)
                eng = ("dve", "pool", "act")[kc % 3]
                if eng == "act":
                    S.op("act", lambda e, kc=kc, b=b: e.activation(out=W[:, kc, :], in_=xb[b][:], func=AF.Copy),
                         reads=[txb[b]], writes=[tW])
                else:
                    S.op(eng, lambda e, kc=kc, b=b: e.tensor_copy(out=W[:, kc, :], in_=xb[b][:]), reads=[txb[b]], writes=[tW])

            chk('setup')
            rr = {"ps": 0, "ev": 0, "xb": 0, "oT": 0}

            def next_ps(n=4):
                i = rr["ps"] % n
                rr["ps"] += 1
                return i

            def evac_copy(out, in_, reads, writes, bf=False):
                k = 1
                if k == 0:
                    S.op("dve", lambda e: e.tensor_scalar(out=out, in0=in_, scalar1=1.0, scalar2=None, op0=ALU.mult), reads=reads, writes=writes)
                else:
                    S.op("act", lambda e: e.activation(out=out, in_=in_, func=AF.Copy), reads=reads, writes=writes)

            def tt(eng, out, a, b, op, reads, writes):
                S.op(eng, lambda e: e.tensor_tensor(out=out, in0=a, in1=b, op=op), reads=reads, writes=writes)

            def ts(eng, out, a, s1, s2, op0, op1, reads, writes):
                if s2 is None:
                    S.op(eng, lambda e: e.tensor_scalar(out=out, in0=a, scalar1=s1, scalar2=None, op0=op0), reads=reads, writes=writes)
                else:
                    S.op(eng, lambda e: e.tensor_scalar(out=out, in0=a, scalar1=s1, scalar2=s2, op0=op0, op1=op1),
                         reads=reads, writes=writes)

            def stt(out, a, sc, b, op0, op1, reads, writes):
                S.op("dve", lambda e: e.scalar_tensor_tensor(out=out, in0=a, scalar=sc, in1=b, op0=op0, op1=op1),
                     reads=reads, writes=writes)

            def act(out, in_, func, reads, writes, bias=0.0, scale=1.0):
                S.op("act", lambda e: e.activation(out=out, in_=in_, func=func, bias=bias, scale=scale), reads=reads, writes=writes)

            def rsqrt_inplace(t, tt_, n, scale, eps):
                act(t[:, 0:n], t[:, 0:n], AF.Ln, [tt_], [tt_], bias=eps_ap(eps), scale=scale)
                act(t[:, 0:n], t[:, 0:n], AF.Exp, [tt_], [tt_], scale=-0.5)

            epsc = {}

            def eps_ap(v):
                if v == 0.0:
                    return 0.0
                if v not in epsc:
                    col = 6 + len(epsc)
                    S.op("pool", lambda e, col=col, v=v: e.memset(pder[:, col:col + 1], float(v)), writes=[tC])
                    epsc[v] = pder[:, col:col + 1]
                return epsc[v]

            eps_ap(1e-5)
            eps_ap(64e-5)

            def layer_norm_block(src_ap, is_meta, blk_slot):
                b = rr["xb"] % 2
                rr["xb"] += 1
                xt = xb[b]
                if is_meta:
                    S.op("pool", lambda e: e.memset(xt[:, 0:D], 0.0), writes=[txb[b]])
                    S.dma(lambda e: e.dma_start(out=xt[112:128, 0:D], in_=meta_d), writes=[txb[b]], key="xb%d" % b)
                else:
                    S.dma(lambda e: e.dma_start(out=xt[:, 0:D], in_=src_ap), writes=[txb[b]], key="xb%d" % b)
                for q in range(4):
                    S.op("dve", lambda e, q=q: e.bn_stats(out=stats[:, q, :], in_=xt[:, q * 512:(q + 1) * 512]),
                         reads=[txb[b]], writes=[tst])
                S.op("dve", lambda e: e.bn_aggr(out=mv[:, 0:2], in_=stats.rearrange("p a n -> p (a n)")), reads=[tst], writes=[tst])
                act(mv[:, 2:3], mv[:, 1:2], AF.Ln, [tst], [tst], bias=eps_ap(1e-5))
                act(mv[:, 2:3], mv[:, 2:3], AF.Exp, [tst], [tst], scale=-0.5)
                stt(mv[:, 3:4], mv[:, 0:1], -1.0, mv[:, 2:3], ALU.mult, ALU.mult, [tst], [tst])
                S.op("act", lambda e: e.activation(out=xn, in_=xt[:, 0:D], func=AF.Identity, bias=mv[:, 3:4], scale=mv[:, 2:3]),
                     reads=[txb[b], tst], writes=[txn])
                for half in range(2):
                    for dc in range(8):
                        c = half * 8 + dc
                        S.op("pe", lambda e, c=c, dc=dc: e.transpose(psb[:, dc * 128:(dc + 1) * 128], xn[:, c * 128:(c + 1) * 128], ident[:]),
                             reads=[txn, tC], writes=[tpsb])
                    for dc in range(8):
                        c = half * 8 + dc
                        dst = hT[:, c, blk_slot * 128:(blk_slot + 1) * 128]
                        src = psb[:, dc * 128:(dc + 1) * 128]
                        if dc % 2 == 0:
                            ts("dve", dst, src, pvec[:, PV_LNG + c:PV_LNG + c + 1], pvec[:, PV_LNB + c:PV_LNB + c + 1],
                               ALU.mult, ALU.add, [tpsb, tC], [thT])
                        else:
                            S.op("act", lambda e, dst=dst, src=src, c=c: e.activation(
                                out=dst, in_=src, func=AF.Identity, bias=pvec[:, PV_LNB + c:PV_LNB + c + 1],
                                scale=pvec[:, PV_LNG + c:PV_LNG + c + 1]), reads=[tpsb, tC], writes=[thT])
                if is_meta:
                    S.op("pool", lambda e: e.memset(hT[:, :, 0:112], 0.0), writes=[thT])

            def inproj_fm(ct_off, m, nt, sink):
                i = next_ps()
                for kc in range(16):
                    S.op("pe", lambda e, kc=kc, i=i: e.matmul(ps[i][0:m, 0:nt], lhsT=W[:, kc, ct_off:ct_off + m], rhs=hT[:, kc, 0:nt],
                                                            start=(kc == 0), stop=(kc == 15)), reads=[tW, thT], writes=[tps[i]])
                sink(ps[i][0:m, 0:nt], tps[i])

            tiles = [(-1, 128)] + [(i, 256) for i in range(16)]
            for (ti, nt) in tiles:
                nblk = nt // 128
                blk0 = 0 if ti < 0 else 1 + 2 * ti
                pos0 = blk0 * 128
                for j in range(nblk):
                    if ti < 0:
                        layer_norm_block(None, True, 0)
                    else:
                        r0 = ti * 256 + j * 128
                        layer_norm_block(x_d[r0:r0 + 128, :], False, j)
                if ti >= 0:
                    for h in range(2):
                        inproj_fm(h * 128, 128, nt, lambda p, tp, h=h: evac_copy(QT[:, h, 0:nt], p, [tp], [tQT]))
                for h in range(2):
                    inproj_fm(256 + h * 128, 128, nt, lambda p, tp, h=h: evac_copy(KT[:, h, pos0:pos0 + nt], p, [tp], [tKT]))
                if ti >= 0:
                    for h in range(2):
                        inproj_fm(512 + h * 128, 128, nt, lambda p, tp, h=h: act(gate_a[:, h, 0:nt], p, AF.Silu, [tp], [tga]))
                    for p_ in range(2):
                        inproj_fm(1536 + p_ * 128, 128, nt, lambda p, tp, p_=p_: act(gate_r[:, p_, 0:nt], p, AF.Silu, [tp], [tgr]))
                for zi in range(6):
                    inproj_fm(768 + zi * 128, 128, nt, lambda p, tp, zi=zi: evac_copy(zb[zi][:, 1:1 + nt], p, [tp], [tzb[zi]]))
                for zi in range(2):
                    inproj_fm(1792 + zi * 96, 96, nt, lambda p, tp, zi=zi: evac_copy(zb[6 + zi][0:96, 1:1 + nt], p, [tp], [tzb[6 + zi]]))
                for j in range(nblk):
                    i = next_ps()
                    for kc in range(16):
                        S.op("pe", lambda e, kc=kc, i=i, j=j: e.matmul(ps[i][:, 0:256], lhsT=hT[:, kc, j * 128:(j + 1) * 128],
                                                                     rhs=W[:, kc, 1984:2240], start=(kc == 0), stop=(kc == 15)),
                             reads=[tW, thT], writes=[tps[i]])
                    evac_copy(Vt[:, blk0 + j, :], ps[i][:, 0:256], [tps[i]], [tV])

                chk('inproj%d' % ti)
                r32, k32, v32, t1, t2, sg, aic, kkn, k2, bvec, cs, t3 = wk
                tr32, tk32, tv32, tt1, tt2, tsg, taic, tkkn, tk2, tbvec, tcs, tt3 = twk
                n = nt

                def shift(zi, mucol, out, tout, rows=128):
                    z = zb[zi]
                    tz = tzb[zi]
                    tt("pool", t3[0:rows, 0:n], z[0:rows, 0:n], z[0:rows, 1:1 + n], ALU.subtract, [tz], [tt3])
                    stt(out[0:rows, 0:n], t3[0:rows, 0:n], pvec[0:rows, mucol:mucol + 1], z[0:rows, 1:1 + n], ALU.mult, ALU.add,
                        [tt3, tz, tC], [tout])
                    S.op("pool", lambda e: e.tensor_copy(out=z[0:rows, 0:1], in_=z[0:rows, n:n + 1]), reads=[tz], writes=[tz])

                shift(6, PV_MUWD, t1, tt1, rows=96)
                act(twb[0:96, 0:n], t1[0:96, 0:n], AF.Tanh, [tt1], [ttwb])
                shift(7, PV_MUAD, t2, tt2, rows=96)
                S.op("dve", lambda e: e.tensor_scalar(out=adb[0:96, 0:n], in0=t2[0:96, 0:n], scalar1=1.0, scalar2=None, op0=ALU.mult), reads=[tt2], writes=[tadb])

                import os as _os
                pe_last = ('E', 'pe', len(S.ops['pe']) - 1)
                for t_ in twk2:
                    t_.r[('E', 'pe')] = pe_last

                def shift2(zi, mucol, out, tout, tmp, ttmp):
                    z = zb[zi]
                    tz = tzb[zi]
                    tt("pool", tmp[:, 0:n], z[:, 0:n], z[:, 1:1 + n], ALU.subtract, [tz], [ttmp])
                    stt(out[:, 0:n], tmp[:, 0:n], pvec[:, mucol:mucol + 1], z[:, 1:1 + n], ALU.mult, ALU.add, [ttmp, tz, tC], [tout])
                    S.op("pool", lambda e: e.tensor_copy(out=z[:, 0:1], in_=z[:, n:n + 1]), reads=[tz], writes=[tz])

                def pre_pair(p, wks, twks, f3, tf3, pb):
                    r32, k32, v32, t1, t2, sg, aic, kkn, k2, bvec, cs, t3 = wks
                    tr32, tk32, tv32, tt1, tt2, tsg, taic, tkkn, tk2, tbvec, tcs, tt3 = twks
                    b0, b1, b2 = pb
                    tnb = tnb2[p]
                    shift2(0 + p, PV_MU + 0 + p, r32, tr32, t3, tt3)
                    yield
                    shift2(2 + p, PV_MU + 2 + p, k32, tk32, t3, tt3)
                    yield
                    shift2(4 + p, PV_MU + 4 + p, v32, tv32, t3, tt3)
                    yield
                    S.op("pe", lambda e: e.matmul(ps[b0][:, 0:n], lhsT=lora[0:96, 0, p * 128:(p + 1) * 128], rhs=twb[0:96, 0:n],
                                                  start=True, stop=True), reads=[tC, ttwb], writes=[tps[b0]])
                    act(sg[:, 0:n], ps[b0][:, 0:n], AF.Sigmoid, [tps[b0], tC], [tsg], bias=pvec[:, PV_W0 + p:PV_W0 + p + 1])
                    yield
                    S.op("pe", lambda e: e.matmul(ps[b1][:, 0:n], lhsT=lora[0:96, 1, p * 128:(p + 1) * 128], rhs=adb[0:96, 0:n],
                                                  start=True, stop=True), reads=[tC, tadb], writes=[tps[b1]])
                    act(aic[:, 0:n], ps[b1][:, 0:n], AF.Sigmoid, [tps[b1], tC], [taic], bias=pvec[:, PV_A0 + p:PV_A0 + p + 1])
                    yield
                    ts("dve", t1[:, 0:n], k32[:, 0:n], pvec[:, PV_KK + p:PV_KK + p + 1], None, ALU.mult, None, [tk32, tC], [tt1])
                    tt("pool", t2[:, 0:n], t1[:, 0:n], t1[:, 0:n], ALU.mult, [tt1], [tt2])
                    yield
                    S.op("pe", lambda e: e.matmul(ps[b2][:, 0:n], lhsT=blockones[:], rhs=t2[:, 0:n], start=True, stop=True),
                         reads=[tC, tt2], writes=[tps[b2]])
                    ts("dve", t2[:, 0:n], ps[b2][:, 0:n], 1e-24, None, ALU.max, None, [tps[b2]], [tt2])
                    yield
                    act(t2[:, 0:n], t2[:, 0:n], AF.Ln, [tt2], [tt2])
                    yield
                    act(t2[:, 0:n], t2[:, 0:n], AF.Exp, [tt2], [tt2], scale=-0.5)
                    yield
                    tt("dve", kkn[:, 0:n], t1[:, 0:n], t2[:, 0:n], ALU.mult, [tt1, tt2], [tkkn])
                    yield
                    ts("dve", t1[:, 0:n], aic[:, 0:n], pvec[:, PV_KA + p:PV_KA + p + 1], pder[:, p:p + 1], ALU.mult, ALU.add,
                       [taic, tC], [tt1])
                    yield
                    tt("pool", k2[:, 0:n], k32[:, 0:n], t1[:, 0:n], ALU.mult, [tk32, tt1], [tk2])
                    tt("pool", bvec[:, 0:n], kkn[:, 0:n], aic[:, 0:n], ALU.mult, [tkkn, taic], [tbvec])
                    yield
                    stt(t1[:, 0:n], r32[:, 0:n], pvec[:, PV_RK + p:PV_RK + p + 1], k2[:, 0:n], ALU.mult, ALU.mult, [tr32, tk2, tC], [tt1])
                    yield
                    S.op("pe", lambda e: e.matmul(ps[b0][:, 0:n], lhsT=blockones[:], rhs=t1[:, 0:n], start=True, stop=True),
                         reads=[tC, tt1], writes=[tps[b0]])
                    tt("dve", bonus[p][:, 0:n], ps[b0][:, 0:n], v32[:, 0:n], ALU.mult, [tps[b0], tv32], [tbonus[p]])
                    yield
                    for c in range(nblk):
                        S.op("dve", lambda e: e.tensor_tensor_scan(out=cs[:, c * 128:(c + 1) * 128], data0=ones_f[:, 0:128],
                                                                   data1=sg[:, c * 128:(c + 1) * 128], initial=0.0,
                                                                   op0=ALU.mult, op1=ALU.add), reads=[tsg, tC], writes=[tcs])
                    yield
                    act(t1[:, 0:n], cs[:, 0:n], AF.Exp, [tcs], [tt1], scale=-C0)
                    yield
                    tt("dve", arT[p][:, 1, 0:n], r32[:, 0:n], t1[:, 0:n], ALU.mult, [tr32, tt1], [tarT[p]])
                    for c in range(nblk):
                        S.op("pool", lambda e: e.tensor_copy(out=WC[:, p * 2 + c:p * 2 + c + 1], in_=t1[:, c * 128 + 127:c * 128 + 128]),
                             reads=[tt1], writes=[tWC])
                        ts("dve", nbias[:, p * 2 + c:p * 2 + c + 1], cs[:, c * 128 + 127:c * 128 + 128], -C0, None, ALU.mult, None, [tcs], [tnb])
                    yield
                    tt("pool", t2[:, 0:n], cs[:, 0:n], sg[:, 0:n], ALU.subtract, [tcs, tsg], [tt2])
                    yield
                    act(t2[:, 0:n], t2[:, 0:n], AF.Exp, [tt2], [tt2], scale=-C0)
                    yield
                    stt(arT[p][:, 0, 0:n], kkn[:, 0:n], -1.0, t2[:, 0:n], ALU.mult, ALU.mult, [tkkn, tt2], [tarT[p]])
                    act(t1[:, 0:n], cs[:, 0:n], AF.Exp, [tcs], [tt1], scale=C0)
                    yield
                    tt("dve", ktl[p][:, 0:n], k2[:, 0:n], t1[:, 0:n], ALU.mult, [tk2, tt1], [tktl[p]])
                    tt("pool", btl[p][:, 0:n], bvec[:, 0:n], t1[:, 0:n], ALU.mult, [tbvec, tt1], [tbtl[p]])
                    yield
                    for hh in range(2):
                        hm = blockones[:, hh * 64:hh * 64 + 1]
                        ts("dve" if hh == 0 else "pool", btlm[p][hh][:, 0:n], btl[p][:, 0:n], hm, None, ALU.mult, None, [tbtl[p], tC], [tbtlm[p][hh]])
                        ts("pool" if hh == 0 else "dve", ktlm[p][hh][:, 0:n], ktl[p][:, 0:n], hm, None, ALU.mult, None, [tktl[p], tC], [tktlm[p][hh]])
                    yield
                    for c in range(nblk):
                        S.op("act", lambda e: e.activation(out=t2[:, c * 128:(c + 1) * 128], in_=cs[:, c * 128:(c + 1) * 128],
                                                           func=AF.Exp, bias=nbias[:, p * 2 + c:p * 2 + c + 1], scale=C0),
                             reads=[tcs, tnb], writes=[tt2])
                    yield
                    tt("dve", f3[0][:, 0:n], k2[:, 0:n], t2[:, 0:n], ALU.mult, [tk2, tt2], [tf3[0]])
                    tt("pool", f3[1][:, 0:n], bvec[:, 0:n], t2[:, 0:n], ALU.mult, [tbvec, tt2], [tf3[1]])
                    S.op("act", lambda e: e.activation(out=f3[2][:, 0:n], in_=v32[:, 0:n], func=AF.Copy), reads=[tv32], writes=[tf3[2]])
                    yield
                    for c in range(nblk):
                        for q in range(3):
                            S.op("pe", lambda e: e.matmul(ps[pb[q]][:, 0:128], lhsT=f3[q][:, c * 128:(c + 1) * 128], rhs=ident[:],
                                                          start=True, stop=True), reads=[tf3[q], tC], writes=[tps[pb[q]]])
                        yield
                        for q in range(3):
                            evac_copy(tok3[q][:, c, p * 128:(p + 1) * 128], ps[pb[q]][:, 0:128], [tps[pb[q]]], [ttok3[q]])
                        for hh in range(2):
                            S.op('act', lambda e: e.activation(out=Vz[:, c, 2 * p + hh, hh * 64:(hh + 1) * 64], in_=ps[b2][:, hh * 64:(hh + 1) * 64], func=AF.Copy),
                                 reads=[tps[b2]], writes=[tVz])
                        yield

                gens = [pre_pair(0, wk, twk, fm3, tfm3, (4, 5, 6)), pre_pair(1, wk2, twk2, fm3b, tfm3b, (1, 2, 3))]
                while gens:
                    for g_ in list(gens):
                        try:
                            next(g_)
                        except StopIteration:
                            gens.remove(g_)
                chk('rwpre%d' % ti)
                Khat, Bhat, Vtok = tok3
                tKhat, tBhat, tVtok = ttok3
                for c in range(nblk):
                    cs_ = slice(c * 128, (c + 1) * 128)
                    for h in range(4):
                        p, hh = h // 2, h % 2
                        i = h % 2
                        S.op("pe", lambda e: e.matmul(ps[i][:, 0:256], lhsT=btlm[p][hh][:, cs_], rhs=arT[p][:, :, cs_], start=True, stop=True),
                             reads=[tbtlm[p][hh], tarT[p]], writes=[tps[i]])
                        S.op("pe", lambda e: e.matmul(ps[i][:, 256:512], lhsT=ktlm[p][hh][:, cs_], rhs=arT[p][:, :, cs_], start=True, stop=True),
                             reads=[tktlm[p][hh], tarT[p]], writes=[tps[i]])
                        chk('c0%d' % ti)
                        tt("dve", AT[h].rearrange("p a n -> p (a n)"), ps[i][:, :], maskT[:], ALU.mult, [tps[i], tC], [tAT[h]])
                        chk('c1%d' % ti)
                        S.op("pe", lambda e: e.matmul(ps[2][:, h * 128:(h + 1) * 128], lhsT=arT[p][:, 0, cs_], rhs=btlm[p][hh][:, cs_], start=True, stop=True),
                             reads=[tbtlm[p][hh], tarT[p]], writes=[tps[2]])
                    chk('ca%d' % ti)
                    for h in range(4):
                        tt("dve", Mc[:, h, :], ps[2][:, h * 128:(h + 1) * 128], masklow[:], ALU.mult, [tps[2], tC], [tMc])
                        S.op("pool", lambda e: e.tensor_copy(out=Mtc[:, h, :], in_=AT[h][:, 0, :]), reads=[tAT[h]], writes=[tMtc])
                        tt("pool", Ptc[:, h, :], AT[h][:, 0, :], ident[:], ALU.add, [tAT[h], tC], [tPtc])
                    chk('cb%d' % ti)
                    for lvl in range(7):
                        if lvl >= 1:
                            for h in range(4):
                                S.op("pe", lambda e: e.matmul(ps[3][:, h * 128:(h + 1) * 128], lhsT=Mc[:, h, :], rhs=Ptc[:, h, :], start=True, stop=True),
                                     reads=[tMc, tPtc], writes=[tps[3]])
                        if lvl < 6:
                            for h in range(4):
                                S.op("pe", lambda e: e.matmul(ps[4][:, h * 128:(h + 1) * 128], lhsT=Mtc[:, h, :], rhs=Mc[:, h, :], start=True, stop=True),
                                     reads=[tMc, tMtc], writes=[tps[4]])
                            if lvl < 5:
                                for h in range(4):
                                    S.op("pe", lambda e: e.matmul(ps[5][:, h * 128:(h + 1) * 128], lhsT=Mc[:, h, :], rhs=Mtc[:, h, :], start=True, stop=True),
                                         reads=[tMc, tMtc], writes=[tps[5]])
                        if lvl >= 1:
                            tt("dve", Ptc.rearrange("p h n -> p (h n)"), ps[3][:, :], Ptc.rearrange("p h n -> p (h n)"), ALU.add, [tps[3], tPtc], [tPtc])
                        if lvl < 6:
                            S.op("act", lambda e: e.activation(out=Mc.rearrange("p h n -> p (h n)"), in_=ps[4][:, :], func=AF.Copy), reads=[tps[4]], writes=[tMc])
                            if lvl < 5:
                                S.op("act", lambda e: e.activation(out=Mtc.rearrange("p h n -> p (h n)"), in_=ps[5][:, :], func=AF.Copy), reads=[tps[5]], writes=[tMtc])
                    chk('cc%d' % ti)
                    for p in range(2):
                        S.op("pe", lambda e: e.matmul(ps[0][:, p * 128:(p + 1) * 128], lhsT=arT[p][:, 0, cs_], rhs=Sbf[:, p, :], start=True, stop=False),
                             reads=[tarT[p], tSbf], writes=[tps[0]])
                        for hh in range(2):
                            h = 2 * p + hh
                            S.op("pe", lambda e: e.matmul(ps[0][:, h * 64:(h + 1) * 64], lhsT=AT[h][:, 2, :], rhs=Vtok[:, c, h * 64:(h + 1) * 64],
                                                          start=False, stop=(hh == 1)), reads=[tAT[h], tVtok], writes=[tps[0]])
                    S.op("act", lambda e: e.activation(out=Xsb.rearrange("p h n -> p (h n)"), in_=ps[0][:, 0:256], func=AF.Copy), reads=[tps[0]], writes=[tXsb])
                    for h in range(4):
                        S.op("pe", lambda e: e.matmul(ps[1][:, h * 64:(h + 1) * 64], lhsT=Ptc[:, h, :], rhs=Xsb[:, h, :], start=True, stop=True),
                             reads=[tPtc, tXsb], writes=[tps[1]])
                    S.op("act", lambda e: e.activation(out=Usb.rearrange("p h n -> p (h n)"), in_=ps[1][:, 0:256], func=AF.Copy), reads=[tps[1]], writes=[tUsb])
                    for hh in range(2):
                        src = ps[1][:, 0:256].rearrange("p (a b n) -> p a b n", a=2, b=2)[:, :, hh, :]
                        dst = Uz.rearrange("p (a b) n -> p a b n", b=2)[:, :, hh, hh * 64:(hh + 1) * 64]
                        S.op("act", lambda e: e.activation(out=dst, in_=src, func=AF.Copy), reads=[tps[1]], writes=[tUz])
                    chk('cd%d' % ti)
                    for p in range(2):
                        S.op("pe", lambda e: e.matmul(ps[2 + p][:, 0:128], lhsT=Sbf[:, p, :], rhs=arT[p][:, 1, cs_], start=True, stop=False),
                             reads=[tSbf, tarT[p]], writes=[tps[2 + p]])
                        for hh in range(2):
                            h = 2 * p + hh
                            S.op("pe", lambda e: e.matmul(ps[2 + p][:, 0:128], lhsT=Uz[:, h, :], rhs=AT[h][:, 1, :], start=False, stop=False),
                                 reads=[tUz, tAT[h]], writes=[tps[2 + p]])
                            S.op("pe", lambda e: e.matmul(ps[2 + p][:, 0:128], lhsT=Vz[:, c, h, :], rhs=AT[h][:, 3, :], start=False, stop=(hh == 1)),
                                 reads=[tVz, tAT[h]], writes=[tps[2 + p]])
                        if ti >= 0:
                            evac_copy(yT[p][:, cs_], ps[2 + p][:, 0:128], [tps[2 + p]], [tyT[p]])
                    for p in range(2):
                        S.op("pe", lambda e: e.matmul(ps[4 + p][:, 0:128], lhsT=Bhat[:, c, p * 128:(p + 1) * 128],
                                                      rhs=Usb.rearrange("p h n -> p (h n)")[:, p * 128:(p + 1) * 128], start=True, stop=False),
                             reads=[tBhat, tUsb], writes=[tps[4 + p]])
                        S.op("pe", lambda e: e.matmul(ps[4 + p][:, 0:128], lhsT=Khat[:, c, p * 128:(p + 1) * 128], rhs=Vtok[:, c, p * 128:(p + 1) * 128],
                                                      start=False, stop=True), reads=[tKhat, tVtok], writes=[tps[4 + p]])
                        tt("dve", t3[:, 0:128], ps[4 + p][:, 0:128], blockones[:], ALU.mult, [tps[4 + p], tC], [tt3])
                        stt(S32[:, p, :], S32[:, p, :], WC[:, p * 2 + c:p * 2 + c + 1], t3[:, 0:128], ALU.mult, ALU.add, [tS32, tWC, tt3], [tS32])
                    S.op("act", lambda e: e.activation(out=Sbf.rearrange("p a n -> p (a n)"), in_=S32.rearrange("p a n -> p (a n)"), func=AF.Copy),
                         reads=[tS32], writes=[tSbf])

                chk('chain%d' % ti)
                if ti < 0:
                    continue

                ob = rr["oT"] % 2
                rr["oT"] += 1
                oTt = oT[ob]
                toTt = toT[ob]

                for p in range(2):
                    y = yT[p]
                    S.op("pe", lambda e, y=y: e.matmul(ps[0][:, 0:n], lhsT=blockones[:], rhs=y[:, 0:n], start=True, stop=True),
                         reads=[tC, tyT[p]], writes=[tps[0]])
                    stt(t1[:, 0:n], ps[0][:, 0:n], -1.0 / 64.0, y[:, 0:n], ALU.mult, ALU.add, [tps[0], tyT[p]], [tt1])
                    tt("pool", t2[:, 0:n], t1[:, 0:n], t1[:, 0:n], ALU.mult, [tt1], [tt2])
                    S.op("pe", lambda e: e.matmul(ps[1][:, 0:n], lhsT=blockones[:], rhs=t2[:, 0:n], start=True, stop=True),
                         reads=[tC, tt2], writes=[tps[1]])
                    act(t2[:, 0:n], ps[1][:, 0:n], AF.Ln, [tps[1], tC], [tt2], bias=eps_ap(64e-5), scale=1.0 / 64.0)
                    act(t2[:, 0:n], t2[:, 0:n], AF.Exp, [tt2], [tt2], scale=-0.5)
                    tt("dve", t1[:, 0:n], t1[:, 0:n], t2[:, 0:n], ALU.mult, [tt1, tt2], [tt1])
                    ts("dve", t1[:, 0:n], t1[:, 0:n], pvec[:, PV_GNG + p:PV_GNG + p + 1], pvec[:, PV_GNB + p:PV_GNB + p + 1],
                       ALU.mult, ALU.add, [tt1, tC], [tt1])
                    tt("pool", t1[:, 0:n], t1[:, 0:n], bonus[p][:, 0:n], ALU.add, [tt1, tbonus[p]], [tt1])
                    tt("dve", oTt[:, 2 + p, 0:n], t1[:, 0:n], gate_r[:, p, 0:n], ALU.mult, [tt1, tgr], [toTt])

                chk('rwpost%d' % ti)
                qb0 = blk0
                OO = ps[4][:, :].rearrange("p (m n) -> p m n", m=2)
                SS = ps[5][:, :].rearrange("p (m n) -> p m n", m=2)
                for h in range(2):
                    kbs = list(range(0, qb0 + 2))

                    def geom(kb):
                        delta = kb - qb0
                        q_lo = 128 if delta == 1 else 0
                        return delta, q_lo, n - q_lo, kb % 2

                    def stage_qk(kb):
                        delta, q_lo, nq, par = geom(kb)
                        for m in range(2):
                            rs = slice(m * 64, m * 64 + 64)
                            pi = 2 * par + m
                            S.op("pe", lambda e: e.matmul(ps[pi][:, 0:nq], lhsT=KT[rs, h, kb * 128:(kb + 1) * 128], rhs=QT[rs, h, q_lo:n],
                                                          start=True, stop=True), reads=[tKT, tQT], writes=[tps[pi]])

                    def stage_rest(kb):
                        delta, q_lo, nq, par = geom(kb)
                        near = delta >= -1
                        E = Eb[par]
                        tE = tEb[par]
                        for m in range(2):
                            pi = 2 * par + m
                            if near:
                                boff = 0 if delta >= 0 else 128
                                stt(Ssb[m][:, 0:nq], ps[pi][:, 0:nq], 0.125, bext[h][:, boff:boff + nq], ALU.mult, ALU.add,
                                    [tps[pi], tC], [tSsb[m]])
                                if kb == 0:
                                    S.op("pool", lambda e: e.memset(Ssb[m][0:112, 0:nq], NEG), reads=[], writes=[tSsb[m]])
                                act(E[:, m, 0:nq], Ssb[m][:, 0:nq], AF.Exp, [tSsb[m]], [tE])
                            else:
                                bc = bcol[:, 2 + h:3 + h] if kb == 0 else bcol[:, h:h + 1]
                                act(E[:, m, 0:nq], ps[pi][:, 0:nq], AF.Exp, [tps[pi], tC], [tE], bias=bc, scale=0.125)
                        first = (kb == kbs[0])
                        last = (kb == kbs[-1])
                        if q_lo == 0:
                            Ef = E.rearrange("p m n -> p (m n)")
                            S.op("pe", lambda e: e.matmul(ps[4][:, :], lhsT=Vt[:, kb, h * 128:(h + 1) * 128], rhs=Ef, start=first, stop=last),
                                 reads=[tV, tE], writes=[tps[4]])
                            S.op("pe", lambda e: e.matmul(ps[5][:, :], lhsT=ones_bf[:], rhs=Ef, start=first, stop=last),
                                 reads=[tC, tE], writes=[tps[5]])
                        else:
                            for m in range(2):
                                S.op("pe", lambda e: e.matmul(OO[:, m, q_lo:n], lhsT=Vt[:, kb, h * 128:(h + 1) * 128], rhs=E[:, m, 0:nq],
                                                              start=first, stop=(last and m == 1)), reads=[tV, tE], writes=[tps[4]])
                                S.op("pe", lambda e: e.matmul(SS[:, m, q_lo:n], lhsT=ones_bf[:], rhs=E[:, m, 0:nq],
                                                              start=first, stop=(last and m == 1)), reads=[tC, tE], writes=[tps[5]])

                    def post_a():
                        S.op("dve", lambda e: e.reciprocal(out=t1[:, 0:n], in_=SS[:, 0, 0:n]), reads=[tps[5]], writes=[tt1])
                        S.op("dve", lambda e: e.reciprocal(out=t2[:, 0:n], in_=SS[:, 1, 0:n]), reads=[tps[5]], writes=[tt2])
                        tt("dve", t1[:, 0:n], OO[:, 0, 0:n], t1[:, 0:n], ALU.mult, [tps[4], tt1], [tt1])
                        tt("dve", t2[:, 0:n], OO[:, 1, 0:n], t2[:, 0:n], ALU.mult, [tps[4], tt2], [tt2])

                    def post_b(hp):
                        stt(t1[:, 0:n], t2[:, 0:n], pder[:, 3:4], t1[:, 0:n], ALU.mult, ALU.add, [tt1, tt2, tC], [tt1])
                        tt("pool", t2[:, 0:n], t1[:, 0:n], t1[:, 0:n], ALU.mult, [tt1], [tt2])
                        S.op("pe", lambda e: e.matmul(ps[6][:, 0:n], lhsT=ones_f[:], rhs=t2[:, 0:n], start=True, stop=True),
                             reads=[tC, tt2], writes=[tps[6]])
                        act(t2[:, 0:n], ps[6][:, 0:n], AF.Ln, [tps[6], tC], [tt2], bias=eps_ap(1e-5), scale=1.0 / 128.0)
                        act(t2[:, 0:n], t2[:, 0:n], AF.Exp, [tt2], [tt2], scale=-0.5)
                        tt("dve", t1[:, 0:n], t1[:, 0:n], t2[:, 0:n], ALU.mult, [tt1, tt2], [tt1])
                        stt(oTt[:, hp, 0:n], t1[:, 0:n], pder[:, 2:3], gate_a[:, hp, 0:n], ALU.mult, ALU.mult, [tt1, tga, tC], [toTt])

                    stage_qk(kbs[0])
                    for i_, kb in enumerate(kbs):
                        if i_ + 1 < len(kbs):
                            stage_qk(kbs[i_ + 1])
                        stage_rest(kb)
                        if h == 1 and i_ == min(1, len(kbs) - 1):
                            post_b(0)
                    post_a()
                    if h == 1:
                        post_b(1)

                chk('attn%d' % ti)
                for cc in range(4):
                    S.dma(lambda e: e.dma_start(out=agin[cc].ap()[:, ti * 256:ti * 256 + 256], in_=oTt[:, cc, 0:256]), reads=[toTt], writes=[tagin[cc]],
                          key="oT%d_%d" % (ob, cc))
                if DEBUG:
                    dd = dbg_d.rearrange("(c p) n -> p c n", p=128)[:, :, ti * 256:ti * 256 + 256]
                    S.dma(lambda e, dd=dd, oTt=oTt: e.dma_start(out=dd, in_=oTt[:, :, 0:256]), reads=[toTt], key="oT%d" % ob)

            chk('main')
            for cc in range(4):
                S.dma(lambda e: e.collective_compute("AllGather", ALU.bypass, replica_groups=[[0, 1, 2, 3], [4, 5, 6, 7]],
                                                     ins=[agin[cc].ap()], outs=[agout[cc].ap()]),
                      reads=[tagin[cc]], writes=[tagout[cc]], key="cc%d" % cc, queue="pool", inc=1)
            chk('ag')
            S.barrier()
            apos[0] = 0
            oq = carve(16 * 1024 * 2, BF16, "p (c n) -> p c n", c=16)
            toq = S.tile("oq")
            gbt = carve(4 * D * 4, F32, "p (a n) -> p a n", a=4)
            tgb = S.tile("gbt")
            rsd = [carve(D * 4, F32) for _ in range(2)]
            trsd = [S.tile("rsd%d" % i) for i in range(2)]
            fstE = [carve(4 * 6 * 4, F32, "p (a n) -> p a n", a=4) for _ in range(2)]
            fmvE = [carve(8 * 4, F32) for _ in range(2)]
            tfE = [S.tile("fstE%d" % i) for i in range(2)]
            fstP = [carve(4 * 6 * 4, F32, "p (a n) -> p a n", a=4) for _ in range(2)]
            fmvP = [carve(8 * 4, F32) for _ in range(2)]
            tfP = [S.tile("fstP%d" % i) for i in range(2)]
            assert apos[0] <= arena_w

            for kc in range(16):
                b = kc % 2
                S.dma(lambda e, kc=kc, b=b: e.dma_start(out=xb[b][:, 0:D], in_=wout_d[kc * 128:(kc + 1) * 128, :]),
                      writes=[txb[b]], key="xb%d" % b)
                eng = ("dve", "pool", "act")[kc % 3]
                if eng == "act":
                    S.op("act", lambda e, kc=kc, b=b: e.activation(out=W[:, kc, 0:D], in_=xb[b][:, 0:D], func=AF.Copy),
                         reads=[txb[b]], writes=[tW])
                else:
                    S.op(eng, lambda e, kc=kc, b=b: e.tensor_copy(out=W[:, kc, 0:D], in_=xb[b][:, 0:D]), reads=[txb[b]], writes=[tW])
            gsrc = bass.AP(tensor=gb_d.tensor, offset=0, ap=[[0, 128], [D, 4], [1, D]])
            S.dma(lambda e: e.dma_start(out=gbt, in_=gsrc), writes=[tgb], key="gbt")

            idxt = sbt("idxt", [128, 4], mybir.dt.int32)
            tidx = S.tile("idxt")
            S.dma(lambda e: e.dma_start(out=idxt[:], in_=idx_d), writes=[tidx], key="idxt")
            for kc in range(16):
                r_, c_ = kc // 4, kc % 4
                agv = agout[c_].ap().rearrange("e (q n) -> (e q) n", q=4)
                S.dma(lambda e: e.indirect_dma_start(out=oq[:, kc, :], out_offset=None, in_=agv,
                                                     in_offset=bass.IndirectOffsetOnAxis(ap=idxt[:, r_:r_ + 1], axis=0)),
                      reads=[tagout[c_], tidx], writes=[toq], key="oq", queue="pool")
            alpha = 2.0 ** 0.25
            def ln_emb(blk):
                b = blk % 2
                xt, r, tr = xb[b], rsd[b], trsd[b]
                fst, fmv, tf = fstE[b], fmvE[b], tfE[b]
                S.dma(lambda e: e.dma_start(out=xt[:, 0:D], in_=xq_d[blk * 128:(blk + 1) * 128, :]), writes=[txb[b]], key="xb%d" % b)
                for q in range(4):
                    S.op("dve", lambda e: e.bn_stats(out=fst[:, q, :], in_=xt[:, q * 512:(q + 1) * 512]), reads=[txb[b]], writes=[tf])
                S.op("dve", lambda e: e.bn_aggr(out=fmv[:, 0:2], in_=fst.rearrange("p a n -> p (a n)")), reads=[tf], writes=[tf])
                act(fmv[:, 2:3], fmv[:, 1:2], AF.Ln, [tf], [tf], bias=eps_ap(1e-5))
                act(fmv[:, 2:3], fmv[:, 2:3], AF.Exp, [tf], [tf], scale=-0.5)
                stt(fmv[:, 3:4], fmv[:, 0:1], -1.0, fmv[:, 2:3], ALU.mult, ALU.mult, [tf], [tf])
                S.op("act", lambda e: e.activation(out=r, in_=xt[:, 0:D], func=AF.Identity, bias=fmv[:, 3:4], scale=fmv[:, 2:3]),
                     reads=[txb[b], tf], writes=[tr])
                tt("pool", r, r, gbt[:, 0, :], ALU.mult, [tr, tgb], [tr])
                tt("pool", r, r, gbt[:, 1, :], ALU.add, [tr, tgb], [tr])

            def proj(blk):
                b = blk % 2
                r, tr = rsd[b], trsd[b]
                for ct in range(4):
                    for kc in range(16):
                        S.op("pe", lambda e: e.matmul(ps[ct][:, :], lhsT=oq[:, kc, blk * 128:(blk + 1) * 128],
                                                      rhs=W[:, kc, ct * 512:(ct + 1) * 512], start=(kc == 0), stop=(kc == 15)),
                             reads=[toq, tW], writes=[tps[ct]])
                    stt(r[:, ct * 512:(ct + 1) * 512], r[:, ct * 512:(ct + 1) * 512], alpha, ps[ct][:, :], ALU.mult, ALU.add,
                        [tr, tps[ct]], [tr])

            def post_ln(blk):
                b = blk % 2
                r, tr = rsd[b], trsd[b]
                fst, fmv, tf = fstP[b], fmvP[b], tfP[b]
                for q in range(4):
                    S.op("dve", lambda e: e.bn_stats(out=fst[:, q, :], in_=r[:, q * 512:(q + 1) * 512]), reads=[tr], writes=[tf])
                S.op("dve", lambda e: e.bn_aggr(out=fmv[:, 4:6], in_=fst.rearrange("p a n -> p (a n)")), reads=[tf], writes=[tf])
                act(fmv[:, 6:7], fmv[:, 5:6], AF.Ln, [tf], [tf], bias=eps_ap(1e-5))
                act(fmv[:, 6:7], fmv[:, 6:7], AF.Exp, [tf], [tf], scale=-0.5)
                stt(fmv[:, 7:8], fmv[:, 4:5], -1.0, fmv[:, 6:7], ALU.mult, ALU.mult, [tf], [tf])
                S.op("act", lambda e: e.activation(out=r, in_=r, func=AF.Identity, bias=fmv[:, 7:8], scale=fmv[:, 6:7]),
                     reads=[tr, tf], writes=[tr])
                tt("dve", r, r, gbt[:, 2, :], ALU.mult, [tr, tgb], [tr])
                tt("pool", r, r, gbt[:, 3, :], ALU.add, [tr, tgb], [tr])
                S.dma(lambda e: e.dma_start(out=out_d[blk * 128:(blk + 1) * 128, :], in_=r), reads=[tr], key="rsd%d" % b)

            ln_emb(0)
            for blk in range(8):
                if blk + 1 < 8:
                    ln_emb(blk + 1)
                proj(blk)
                post_ln(blk)

        except _Stop:
            pass
        S.emit()
    return nc


_NC_CACHE = {}


def kernel(**inp):
    maps = _prep_inputs(inp)
    x = np.asarray(inp["x"], np.float32)
    for c in range(8):
        b, r = c // 4, c % 4
        maps[c]["xq"] = np.ascontiguousarray(x[b, r * 1024:(r + 1) * 1024])
        p = np.arange(128)[:, None]
        kc = np.arange(4)[None, :]
        maps[c]["idx"] = ((kc * 128 + p) * 4 + r).astype(np.int32)
    if "nc" not in _NC_CACHE:
        _NC_CACHE["nc"] = build_nc()
    nc = _NC_CACHE["nc"]
    res = run_bass_kernel_spmd(nc, maps, core_ids=list(range(8)))
    out = np.zeros((2, SEQ, D), np.float32)
    for c in range(8):
        b, r = c // 4, c % 4
        out[b, r * 1024:(r + 1) * 1024] = res.results[c]["out"]
    if DEBUG:
        kernel.dbg = [res.results[c]["dbg"] for c in range(8)]
    return out
```

```python
import math
from contextlib import ExitStack

import numpy as np
import ml_dtypes

import concourse.bass as bass
import concourse.mybir as mybir
from concourse.bass_utils import run_bass_kernel_spmd

F32 = mybir.dt.float32
BF16 = mybir.dt.bfloat16
AF = mybir.ActivationFunctionType
ALU = mybir.AluOpType

D = 2048
SEQ = 4096
NB = 33
P_TOK = NB * 128
NCOL = 2240
C0 = math.exp(-0.5)
NEG = -1.0e4
TABW = 511
DEBUG = False
STOP = None


class _Stop(Exception):
    pass


_HITS = {}


def chk(tag):
    import os
    if STOP == tag:
        _HITS[tag] = _HITS.get(tag, 0) + 1
        if _HITS[tag] >= int(os.environ.get("NTH", "1")):
            raise _Stop()


class T:
    __slots__ = ("name", "w", "r", "parent")

    def __init__(self, name, init_r=None, parent=None):
        self.name = name
        self.w = None
        self.r = dict(init_r) if init_r else {}
        self.parent = parent


class _Rec:
    def __init__(self):
        self.call = None

    def __getattr__(self, name):
        def f(*a, **k):
            assert self.call is None
            self.call = (name, a, k)
            return self
        return f


def _freeze(fn):
    r = _Rec()
    fn(r)
    name, a, k = r.call
    return lambda e: getattr(e, name)(*a, **k)


class Sched:
    ENG = ("pe", "act", "dve", "pool", "sp")

    def __init__(self, nc, es):
        self.nc = nc
        self.es = es
        self.ops = {e: [] for e in self.ENG}
        self.seen = {e: {} for e in self.ENG}
        self.dsems = {}
        self.esem = {}
        self.bar = {}
        self.groups = set()

    def tile(self, name):
        return T(name, self.bar)

    def barrier(self):
        b = {}
        for e in self.ENG:
            if e != "sp" and self.ops[e]:
                for i in range(len(self.ops[e]) - 1, -1, -1):
                    if self.ops[e][i]["dma"] is None:
                        b[('E', e)] = ('E', e, i)
                        break
        for k, v in self.dsems.items():
            if v[1] > 0:
                b[('D', k)] = ('D', k, v[1])
        self.bar = b

    @staticmethod
    def _dep(waits, ev):
        key = ev[:2]
        if waits.get(key, -1) < ev[2]:
            waits[key] = ev[2]

    def _collect(self, eng, reads, writes):
        waits = {}
        for t in reads:
            if t.w is not None:
                self._dep(waits, t.w)
        for t in writes:
            if t.w is not None:
                self._dep(waits, t.w)
            for ev in t.r.values():
                self._dep(waits, ev)
        wl = []
        for key, val in waits.items():
            if key[0] == 'E' and key[1] == eng and eng == 'pe':
                continue
            if self.seen[eng].get(key, -1) >= val:
                continue
            self.seen[eng][key] = val
            wl.append((key, val))
        return wl

    def op(self, eng, fn, reads=(), writes=()):
        fn = _freeze(fn)
        extra = [t.parent for t in list(reads) + list(writes) if t.parent is not None]
        if extra:
            reads = list(reads) + extra
        wl = self._collect(eng, reads, writes)
        idx = len(self.ops[eng])
        ev = ('E', eng, idx)
        self.ops[eng].append(dict(waits=wl, fn=fn, sig=False, dma=None))
        for t in reads:
            t.r[('E', eng)] = ev
        for t in writes:
            t.w = ev
            t.r = {}
        return ev

    def dsem(self, key):
        if key not in self.dsems:
            h = self.es.enter_context(self.nc.semaphore("d_" + key))
            self.dsems[key] = [h, 0]
        return self.dsems[key]

    def dma(self, fn, reads=(), writes=(), key=None, queue="sp", inc=16):
        fn = _freeze(fn)
        wl = self._collect(queue, reads, writes)
        if key in self.groups:
            wl = [w for w in wl if w[0] != ('D', key)]
        ds = self.dsem(key)
        ds[1] += inc
        ev = ('D', key, ds[1])
        self.ops[queue].append(dict(waits=wl, fn=fn, sig=False, dma=(key, inc)))
        for t in reads:
            t.r[('D', key)] = ev
        for t in writes:
            t.w = ev
            t.r = {}
        return ev

    def emit(self):
        nc = self.nc
        for e in self.ENG:
            for o in self.ops[e]:
                for key, val in o["waits"]:
                    if key[0] == 'E':
                        self.ops[key[1]][val]["sig"] = True
        cnt = {}
        for e in self.ENG:
            c = 0
            lst = []
            for o in self.ops[e]:
                if o["sig"]:
                    c += 1
                lst.append(c)
            cnt[e] = lst
            self.esem[e] = self.es.enter_context(nc.semaphore("e_" + e))
        finals = [(('D', k), v[1]) for k, v in self.dsems.items() if v[1] > 0]

        def resolve(key, val):
            if key[0] == 'E':
                return self.esem[key[1]], cnt[key[1]][val]
            if key[1] in self.groups:
                val = self.dsems[key[1]][1]
            return self.dsems[key[1]][0], val

        def body(e):
            def run(eng):
                for o in self.ops[e]:
                    for key, val in o["waits"]:
                        s, v = resolve(key, val)
                        eng.wait_ge(s, v)
                    ins = o["fn"](eng)
                    if o["dma"] is not None:
                        ins.then_inc(self.dsems[o["dma"][0]][0], o["dma"][1])
                    elif o["sig"]:
                        ins.then_inc(self.esem[e], 1)
                if e == "sp":
                    for key, val in finals:
                        s, v = resolve(key, val)
                        eng.wait_ge(s, v)
            return run

        with nc.Block() as block:
            block.tensor(body("pe"))
            block.scalar(body("act"))
            block.vector(body("dve"))
            block.gpsimd(body("pool"))
            block.sync(body("sp"))


def _bucket(n):
    n = np.maximum(n, 0)
    nf = np.maximum(n, 1).astype(np.float32)
    large = 16 + (np.log(nf / np.float32(16)) / np.float32(math.log(128 / 16)) * np.float32(16)).astype(np.int32)
    large = np.minimum(large, 31)
    return np.where(n < 16, n, large)


PV_MU = 0
PV_MUWD = 6
PV_MUAD = 7
PV_W0 = 8
PV_A0 = 10
PV_KK = 12
PV_KA = 14
PV_RK = 16
PV_GNG = 18
PV_GNB = 20
PV_SUBLN = 22
PV_LNG = 23
PV_LNB = 39
PV_N = 55


def _consts():
    c = {}
    c["ident"] = np.eye(128, dtype=np.float32).astype(ml_dtypes.bfloat16)
    c["ones_bf"] = np.ones((128, 128), dtype=ml_dtypes.bfloat16)
    c["ones_f"] = np.ones((128, 128), dtype=np.float32)
    bo = np.zeros((128, 128), np.float32)
    bo[:64, :64] = 1.0
    bo[64:, 64:] = 1.0
    c["blockones"] = bo
    s = np.arange(128)[:, None]
    t = np.arange(128)[None, :]
    strict = (s < t).astype(np.float32)
    incl = (s <= t).astype(np.float32)
    m = np.zeros((128, 2, 2, 128), np.float32)
    m[:, :, 0, :] = strict[:, None, :]
    m[:, :, 1, :] = incl[:, None, :]
    c["maskT"] = m.reshape(128, 512)
    c["masklow"] = (s > t).astype(np.float32)
    d = np.arange(TABW) - 127
    oh = np.zeros((33, TABW), np.float32)
    b = _bucket(d)
    for i in range(TABW):
        if d[i] >= 0:
            oh[b[i], i] = 1.0
        else:
            oh[32, i] = NEG
    c["oh"] = oh
    return c


def _prep_inputs(inp):
    f = np.float32
    w_in = np.asarray(inp["w_in"][0], f)
    w_out = np.asarray(inp["w_out"][0], f)
    x = np.asarray(inp["x"], f)
    consts = _consts()
    lam4 = np.stack([inp["lambda_q1"][0], inp["lambda_k1"][0], inp["lambda_q2"][0], inp["lambda_k2"][0]]).astype(f)
    rows = []
    for r in range(4):
        rows += list(range((2 * r) * 128, (2 * r + 2) * 128))
        rows += list(range(1024 + (4 * r) * 64, 1024 + (4 * r + 4) * 64))
    w_out_p = np.ascontiguousarray(w_out[rows])
    gb = np.stack([inp["ln_emb_g"], inp["ln_emb_b"], inp["ln_post_g"][0], inp["ln_post_b"][0]]).astype(f)
    maps = []
    for c in range(8):
        b, hg = c // 4, c % 4
        h0 = 2 * hg
        cols = []
        cols += list(range(h0 * 128, (h0 + 2) * 128))
        cols += list(range(1024 + h0 * 128, 1024 + (h0 + 2) * 128))
        cols += list(range(3072 + h0 * 128, 3072 + (h0 + 2) * 128))
        rb = 4096 + hg * 256
        cols += list(range(rb, rb + 256))
        cols += list(range(rb + 1024, rb + 1024 + 256))
        cols += list(range(rb + 2048, rb + 2048 + 256))
        cols += list(range(7360 + hg * 256, 7360 + hg * 256 + 256))
        cols += list(range(4096 + 3072, 4096 + 3072 + 192))
        cols += list(range(2048 + h0 * 128, 2048 + (h0 + 2) * 128))
        assert len(cols) == NCOL
        wi = np.ascontiguousarray(w_in[:, cols])
        pv = np.zeros((128, PV_N), f)
        mu = inp["rw_mu"][0]
        rs = slice(hg * 256, hg * 256 + 256)
        for p in range(2):
            ps_ = slice(hg * 256 + p * 128, hg * 256 + (p + 1) * 128)
            pv[:, PV_MU + 0 + p] = mu[0:1024][ps_]
            pv[:, PV_MU + 2 + p] = mu[1024:2048][ps_]
            pv[:, PV_MU + 4 + p] = mu[2048:3072][ps_]
            pv[:, PV_W0 + p] = inp["rw_w0"][0][ps_]
            pv[:, PV_A0 + p] = inp["rw_a0"][0][ps_]
            pv[:, PV_KK + p] = inp["rw_k_k"][0][ps_]
            pv[:, PV_KA + p] = inp["rw_k_a"][0][ps_]
            pv[:, PV_RK + p] = inp["rw_r_k"][0].reshape(-1)[ps_]
            pv[:, PV_GNG + p] = inp["rw_gn_g"][0][ps_]
            pv[:, PV_GNB + p] = inp["rw_gn_b"][0][ps_]
        pv[:96, PV_MUWD] = mu[3072:3168]
        pv[:96, PV_MUAD] = mu[3168:3264]
        pv[:, PV_SUBLN] = inp["subln_g"][0]
        pv[:, PV_LNG:PV_LNG + 16] = np.asarray(inp["ln_emb_g"], f).reshape(16, 128).T
        pv[:, PV_LNB:PV_LNB + 16] = np.asarray(inp["ln_emb_b"], f).reshape(16, 128).T
        relb = np.ones((33, 2, 128), f)
        for j in range(2):
            relb[:32, j, :] = np.asarray(inp["rel_bias"], f)[:, h0 + j][:, None]
        lora = np.zeros((96, 2, 256), f)
        lora[:, 0, :] = inp["rw_w_up"][0][:, rs]
        lora[:, 1, :] = inp["rw_a_up"][0][:, rs]
        m = {
            "x": np.ascontiguousarray(x[b]),
            "meta": np.asarray(inp["meta_tokens"], f),
            "w_in": wi,
            "w_out": w_out_p,
            "pvec": pv,
            "relb": relb,
            "lora": lora,
            "lam4": lam4,
            "gb": gb,
        }
        m.update(consts)
        maps.append(m)
    return maps


def build_nc():
    nc = bass.Bass("TRN2", target_bir_lowering=False)

    def din(name, shape, dt=F32):
        return nc.dram_tensor(name, list(shape), dt, kind="ExternalInput").ap()

    x_d = din("x", [SEQ, D])
    meta_d = din("meta", [16, D])
    win_d = din("w_in", [D, NCOL])
    wout_d = din("w_out", [D, D])
    pvec_d = din("pvec", [128, PV_N])
    relb_d = din("relb", [33, 2, 128])
    lora_d = din("lora", [96, 2, 256])
    lam4_d = din("lam4", [4, 64])
    gb_d = din("gb", [4, D])
    ident_d = din("ident", [128, 128], BF16)
    onesbf_d = din("ones_bf", [128, 128], BF16)
    onesf_d = din("ones_f", [128, 128])
    bo_d = din("blockones", [128, 128])
    maskT_d = din("maskT", [128, 512])
    masklow_d = din("masklow", [128, 128])
    oh_d = din("oh", [33, TABW])
    xq_d = din("xq", [1024, D])
    idx_d = din("idx", [128, 4], mybir.dt.int32)
    out_d = nc.dram_tensor("out", [1024, D], F32, kind="ExternalOutput").ap()
    if DEBUG:
        dbg_d = nc.dram_tensor("dbg", [512, SEQ], BF16, kind="ExternalOutput").ap()
    agin = [nc.dram_tensor("agin%d" % c, [128, SEQ], BF16) for c in range(4)]
    agout = [nc.dram_tensor("agout%d" % c, [512, SEQ], BF16) for c in range(4)]
    tab_d = [nc.dram_tensor("tab%d" % j, [128, TABW], F32) for j in range(2)]

    with ExitStack() as es:
        S = Sched(nc, es)

        def sbt(name, shape, dt=F32):
            return es.enter_context(nc.sbuf_tensor("s_" + name, list(shape), dt))

        W = sbt("W", [128, 16, NCOL], BF16)
        tW = S.tile("W")
        xb = [sbt("xb%d" % i, [128, NCOL]) for i in range(2)]
        txb = [S.tile("xb%d" % i) for i in range(2)]
        pvec = sbt("pvec", [128, PV_N])
        pder = sbt("pder", [128, 8])
        ident = sbt("ident", [128, 128], BF16)
        ones_bf = sbt("ones_bf", [128, 128], BF16)
        ones_f = sbt("ones_f", [128, 128])
        blockones = sbt("blockones", [128, 128])
        maskT = sbt("maskT", [128, 512])
        masklow = sbt("masklow", [128, 128])
        lora32 = sbt("lora32", [96, 2, 256])
        lora = sbt("lora", [96, 2, 256], BF16)
        bext = [sbt("bext%d" % j, [128, 384]) for j in range(2)]
        bcol = sbt("bcol", [128, 4])
        tC = S.tile("consts")
        arena_w = 28100
        arena = sbt("arena", [128, arena_w])
        apos = [0]

        def carve(nbytes, dt, shape_str=None, **kw):
            n32 = (nbytes + 3) // 4
            a = apos[0]
            apos[0] += n32
            assert apos[0] <= arena_w, apos[0]
            ap = arena[:, a:a + n32]
            if dt != F32:
                ap = ap.bitcast(dt)
            if shape_str:
                ap = ap.rearrange(shape_str, **kw)
            return ap

        ps = [es.enter_context(nc.psum_tensor("ps%d" % i, [128, 512], F32)) for i in range(7)]
        tps = [S.tile("ps%d" % i) for i in range(7)]
        psb = es.enter_context(nc.psum_tensor("psb", [128, 1024], BF16))
        tpsb = S.tile("psb")

        KT = carve(2 * P_TOK * 2, BF16, "p (h n) -> p h n", h=2)
        tKT = S.tile("KT")
        Vt = carve(NB * 256 * 2, BF16, "p (b n) -> p b n", b=NB)
        tV = S.tile("V")
        xn_off = apos[0]
        xn = carve(D * 2, BF16)
        txn = S.tile("xn")
        hT_off = apos[0]
        hT = carve(16 * 256 * 2, BF16, "p (c n) -> p c n", c=16)
        thT = S.tile("hT")
        wk2 = [arena[:, hT_off + i * 256:hT_off + (i + 1) * 256] for i in range(8)] + \
              [arena[:, xn_off + i * 256:xn_off + (i + 1) * 256] for i in range(4)]
        twk2 = [T("wk2_%d" % i, parent=thT) for i in range(8)] + [T("wk2_%d" % (8 + i), parent=txn) for i in range(4)]
        QT = carve(2 * 256 * 2, BF16, "p (h n) -> p h n", h=2)
        tQT = S.tile("QT")
        gate_a = carve(2 * 256 * 2, BF16, "p (h n) -> p h n", h=2)
        tga = S.tile("gate_a")
        gate_r = carve(2 * 256 * 2, BF16, "p (h n) -> p h n", h=2)
        tgr = S.tile("gate_r")
        zb = [carve(257 * 4, F32) for _ in range(8)]
        tzb = [S.tile("z%d" % i) for i in range(8)]
        NWK = 12
        wk = [carve(256 * 4, F32) for _ in range(NWK)]
        twk = [S.tile("wk%d" % i) for i in range(NWK)]
        bonus = [carve(256 * 4, F32) for _ in range(2)]
        tbonus = [S.tile("bonus%d" % i) for i in range(2)]
        yT = [wk[1], wk[5]]
        tyT = [twk[1], twk[5]]
        twb = carve(256 * 2, BF16)
        adb = carve(256 * 2, BF16)
        ttwb = S.tile("twb")
        tadb = S.tile("adb")
        arT = [carve(2 * 256 * 2, BF16, "p (a n) -> p a n", a=2) for _ in range(2)]
        tarT = [S.tile("arT%d" % i) for i in range(2)]
        ktl = [carve(256 * 2, BF16) for _ in range(2)]
        btl = [carve(256 * 2, BF16) for _ in range(2)]
        tktl = [S.tile("ktl%d" % i) for i in range(2)]
        tbtl = [S.tile("btl%d" % i) for i in range(2)]
        fm3 = [carve(256 * 2, BF16) for _ in range(3)]
        tfm3 = [S.tile("fm3_%d" % i) for i in range(3)]
        fm3b = [carve(256 * 2, BF16) for _ in range(3)]
        tfm3b = [S.tile("fm3b_%d" % i) for i in range(3)]
        tnb2 = [S.tile("nbias%d" % i) for i in range(2)]
        tok3 = [carve(2 * 256 * 2, BF16, "p (c n) -> p c n", c=2) for _ in range(3)]
        ttok3 = [S.tile("tok3_%d" % i) for i in range(3)]
        WC = carve(4 * 4, F32)
        tWC = S.tile("WC")
        nbias = carve(4 * 4, F32)
        tnbias = S.tile("nbias")
        AT = [carve(4 * 128 * 2, BF16, "p (a n) -> p a n", a=4) for _ in range(4)]
        tAT = [S.tile("AT%d" % i) for i in range(4)]
        Mc = carve(4 * 128 * 2, BF16, "p (h n) -> p h n", h=4)
        Mtc = carve(4 * 128 * 2, BF16, "p (h n) -> p h n", h=4)
        Ptc = carve(4 * 128 * 2, BF16, "p (h n) -> p h n", h=4)
        tMc = S.tile("Mc")
        tMtc = S.tile("Mtc")
        tPtc = S.tile("Ptc")
        btlm = [[carve(256 * 2, BF16) for _ in range(2)] for _ in range(2)]
        ktlm = [[carve(256 * 2, BF16) for _ in range(2)] for _ in range(2)]
        tbtlm = [[S.tile("btlm%d%d" % (i, j)) for j in range(2)] for i in range(2)]
        tktlm = [[S.tile("ktlm%d%d" % (i, j)) for j in range(2)] for i in range(2)]
        Vz = carve(2 * 4 * 128 * 2, BF16, "p (c h n) -> p c h n", c=2, h=4)
        tVz = S.tile("Vz")
        Uz = carve(4 * 128 * 2, BF16, "p (h n) -> p h n", h=4)
        tUz = S.tile("Uz")
        Xsb = carve(256 * 2, BF16, "p (h n) -> p h n", h=4)
        Usb = carve(256 * 2, BF16, "p (h n) -> p h n", h=4)
        tXsb = S.tile("Xsb")
        tUsb = S.tile("Usb")
        S32 = carve(2 * 128 * 4, F32, "p (a n) -> p a n", a=2)
        Sbf = carve(2 * 128 * 2, BF16, "p (a n) -> p a n", a=2)
        tS32 = S.tile("S32")
        tSbf = S.tile("Sbf")
        Ssb = [wk[0], wk[2]]
        tSsb = [twk[0], twk[2]]
        Eb = [carve(2 * 256 * 2, BF16, "p (m n) -> p m n", m=2) for _ in range(2)]
        tEb = [S.tile("Eb%d" % i) for i in range(2)]
        oT = [carve(4 * 256 * 2, BF16, "p (c n) -> p c n", c=4) for _ in range(2)]
        toT = [S.tile("oT%d" % i) for i in range(2)]
        stats = carve(4 * 6 * 4, F32, "p (a n) -> p a n", a=4)
        mv = carve(8 * 4, F32)
        tst = S.tile("stats")
        lamw = carve(4 * 64 * 4, F32, "p (a n) -> p a n", a=4)
        tabsb = carve(TABW * 4, F32)
        ttab = S.tile("tabsb")
        main_end = apos[0]
        print("arena main words", main_end, "of", arena_w)

        tagin = [S.tile("agin%d" % c) for c in range(4)]
        tagout = [S.tile("agout%d" % c) for c in range(4)]
        ttabd = [S.tile("tabd%d" % j) for j in range(2)]

        try:
            S.groups.add("consts")

            def cload(dst, src):
                S.dma(lambda e: e.dma_start(out=dst, in_=src), writes=[tC], key="consts")

            cload(pvec[:], pvec_d)
            cload(ident[:], ident_d)
            cload(ones_bf[:], onesbf_d)
            cload(ones_f[:], onesf_d)
            cload(blockones[:], bo_d)
            cload(maskT[:], maskT_d)
            cload(masklow[:], masklow_d)
            cload(lora32[:], lora_d)
            relb = carve(2 * 128 * 4, F32, "p (a n) -> p a n", a=2)
            ohsb = carve(TABW * 4, F32)
            cload(relb[0:33], relb_d)
            cload(ohsb[0:33], oh_d)
            lam_src = bass.AP(tensor=lam4_d.tensor, offset=0, ap=[[0, 128], [64, 4], [1, 64]])
            cload(lamw, lam_src)

            S.op("dve", lambda e: e.tensor_copy(out=lora[:], in_=lora32[:]), reads=[tC], writes=[tC])
            S.op("dve", lambda e: e.tensor_scalar(out=pder[:, 0:2], in0=pvec[:, PV_KA:PV_KA + 2], scalar1=-1.0, scalar2=1.0,
                                                  op0=ALU.mult, op1=ALU.add), reads=[tC], writes=[tC])
            S.op("dve", lambda e: e.tensor_scalar(out=pder[:, 2:3], in0=pvec[:, PV_SUBLN:PV_SUBLN + 1], scalar1=0.8, scalar2=None,
                                                  op0=ALU.mult), reads=[tC], writes=[tC])
            S.op("dve", lambda e: e.tensor_tensor(out=lamw[:, 0, :], in0=lamw[:, 0, :], in1=lamw[:, 1, :], op=ALU.mult),
                 reads=[tC], writes=[tC])
            S.op("dve", lambda e: e.tensor_tensor(out=lamw[:, 2, :], in0=lamw[:, 2, :], in1=lamw[:, 3, :], op=ALU.mult),
                 reads=[tC], writes=[tC])
            S.op("dve", lambda e: e.reduce_sum(out=pder[:, 4:5], in_=lamw[:, 0, :], axis=mybir.AxisListType.X),
                 reads=[tC], writes=[tC])
            S.op("dve", lambda e: e.reduce_sum(out=pder[:, 5:6], in_=lamw[:, 2, :], axis=mybir.AxisListType.X),
                 reads=[tC], writes=[tC])
            S.op("act", lambda e: e.activation(out=pder[:, 4:6], in_=pder[:, 4:6], func=AF.Exp), reads=[tC], writes=[tC])
            S.op("dve", lambda e: e.tensor_tensor(out=pder[:, 3:4], in0=pder[:, 5:6], in1=pder[:, 4:5], op=ALU.subtract),
                 reads=[tC], writes=[tC])
            S.op("dve", lambda e: e.tensor_scalar(out=pder[:, 3:4], in0=pder[:, 3:4], scalar1=-0.2, scalar2=None, op0=ALU.add),
                 reads=[tC], writes=[tC])
            S.op("pool", lambda e: e.memset(S32.rearrange("p a n -> p (a n)"), 0.0), writes=[tS32])
            S.op("pool", lambda e: e.memset(Sbf.rearrange("p a n -> p (a n)"), 0.0), writes=[tSbf])
            S.op("pool", lambda e: e.memset(Vz.rearrange("p c h n -> p (c h n)"), 0.0), writes=[tVz])
            S.op("pool", lambda e: e.memset(Uz.rearrange("p h n -> p (h n)"), 0.0), writes=[tUz])
            for i in range(8):
                S.op("pool", lambda e, i=i: e.memset(zb[i], 0.0), writes=[tzb[i]])

            for j in range(2):
                S.op("pe", lambda e, j=j: e.matmul(ps[0][:, 0:256], lhsT=relb[0:33, j, :], rhs=ohsb[0:33, 0:256], start=True, stop=True),
                     reads=[tC], writes=[tps[0]])
                S.op("pe", lambda e, j=j: e.matmul(ps[1][:, 0:255], lhsT=relb[0:33, j, :], rhs=ohsb[0:33, 256:511], start=True, stop=True),
                     reads=[tC], writes=[tps[1]])
                S.op("dve", lambda e: e.tensor_copy(out=tabsb[:, 0:256], in_=ps[0][:, 0:256]), reads=[tps[0]], writes=[ttab])
                S.op("dve", lambda e: e.tensor_copy(out=tabsb[:, 256:511], in_=ps[1][:, 0:255]), reads=[tps[1]], writes=[ttab])
                S.op("dve", lambda e, j=j: e.tensor_copy(out=bcol[:, j:j + 1], in_=tabsb[:, 510:511]), reads=[ttab], writes=[tC])
                S.op("dve", lambda e, j=j: e.tensor_copy(out=bcol[:, 2 + j:3 + j], in_=tabsb[:, 510:511]), reads=[ttab], writes=[tC])
                S.op("dve", lambda e, j=j: e.memset(bcol[0:112, 2 + j:3 + j], NEG), reads=[], writes=[tC])
                S.dma(lambda e, j=j: e.dma_start(out=tab_d[j].ap(), in_=tabsb), reads=[ttab], writes=[ttabd[j]], key="tabst")
                src = bass.AP(tensor=tab_d[j].ap().tensor, offset=127, ap=[[TABW - 1, 128], [1, 384]])
                S.dma(lambda e, j=j, src=src: e.dma_start(out=bext[j][:], in_=src), reads=[ttabd[j]], writes=[tC], key="bext%d" % j)

            for kc in range(16):
                b = kc % 2
                S.dma(lambda e, kc=kc, b=b: e.dma_start(out=xb[b][:], in_=win_d[kc * 128:(kc + 1) * 128, :]),
                      writes=[txb[b]], key="xb%d" % b)
                eng = ("dve", "pool", "act")[kc % 3]
                if eng == "act":
                    S.op("act", lambda e, kc=kc, b=b: e.activation(out=W[:, kc, :], in_=xb[b][:], func=AF.Copy),
                         reads=[txb[b]], writes=[tW])
                else:
                    S.op(eng, lambda e, kc=kc, b=b: e.tensor_copy(out=W[:, kc, :], in_=xb[b][:]), reads=[txb[b]], writes=[tW])

            chk('setup')
            rr = {"ps": 0, "ev": 0, "xb": 0, "oT": 0}

            def next_ps(n=4):
                i = rr["ps"] % n
                rr["ps"] += 1
                return i

            def evac_copy(out, in_, reads, writes, bf=False):
                k = 1
                if k == 0:
                    S.op("dve", lambda e: e.tensor_scalar(out=out, in0=in_, scalar1=1.0, scalar2=None, op0=ALU.mult), reads=reads, writes=writes)
                else:
                    S.op("act", lambda e: e.activation(out=out, in_=in_, func=AF.Copy), reads=reads, writes=writes)

            def tt(eng, out, a, b, op, reads, writes):
                S.op(eng, lambda e: e.tensor_tensor(out=out, in0=a, in1=b, op=op), reads=reads, writes=writes)

            def ts(eng, out, a, s1, s2, op0, op1, reads, writes):
                if s2 is None:
                    S.op(eng, lambda e: e.tensor_scalar(out=out, in0=a, scalar1=s1, scalar2=None, op0=op0), reads=reads, writes=writes)
                else:
                    S.op(eng, lambda e: e.tensor_scalar(out=out, in0=a, scalar1=s1, scalar2=s2, op0=op0, op1=op1),
                         reads=reads, writes=writes)

            def stt(out, a, sc, b, op0, op1, reads, writes):
                S.op("dve", lambda e: e.scalar_tensor_tensor(out=out, in0=a, scalar=sc, in1=b, op0=op0, op1=op1),
                     reads=reads, writes=writes)

            def act(out, in_, func, reads, writes, bias=0.0, scale=1.0):
                S.op("act", lambda e: e.activation(out=out, in_=in_, func=func, bias=bias, scale=scale), reads=reads, writes=writes)

            def rsqrt_inplace(t, tt_, n, scale, eps):
                act(t[:, 0:n], t[:, 0:n], AF.Ln, [tt_], [tt_], bias=eps_ap(eps), scale=scale)
                act(t[:, 0:n], t[:, 0:n], AF.Exp, [tt_], [tt_], scale=-0.5)

            epsc = {}

            def eps_ap(v):
                if v == 0.0:
                    return 0.0
                if v not in epsc:
                    col = 6 + len(epsc)
                    S.op("pool", lambda e, col=col, v=v: e.memset(pder[:, col:col + 1], float(v)), writes=[tC])
                    epsc[v] = pder[:, col:col + 1]
                return epsc[v]

            eps_ap(1e-5)
            eps_ap(64e-5)

            def layer_norm_block(src_ap, is_meta, blk_slot):
                b = rr["xb"] % 2
                rr["xb"] += 1
                xt = xb[b]
                if is_meta:
                    S.op("pool", lambda e: e.memset(xt[:, 0:D], 0.0), writes=[txb[b]])
                    S.dma(lambda e: e.dma_start(out=xt[112:128, 0:D], in_=meta_d), writes=[txb[b]], key="xb%d" % b)
                else:
                    S.dma(lambda e: e.dma_start(out=xt[:, 0:D], in_=src_ap), writes=[txb[b]], key="xb%d" % b)
                for q in range(4):
                    S.op("dve", lambda e, q=q: e.bn_stats(out=stats[:, q, :], in_=xt[:, q * 512:(q + 1) * 512]),
                         reads=[txb[b]], writes=[tst])
                S.op("dve", lambda e: e.bn_aggr(out=mv[:, 0:2], in_=stats.rearrange("p a n -> p (a n)")), reads=[tst], writes=[tst])
                act(mv[:, 2:3], mv[:, 1:2], AF.Ln, [tst], [tst], bias=eps_ap(1e-5))
                act(mv[:, 2:3], mv[:, 2:3], AF.Exp, [tst], [tst], scale=-0.5)
                stt(mv[:, 3:4], mv[:, 0:1], -1.0, mv[:, 2:3], ALU.mult, ALU.mult, [tst], [tst])
                S.op("act", lambda e: e.activation(out=xn, in_=xt[:, 0:D], func=AF.Identity, bias=mv[:, 3:4], scale=mv[:, 2:3]),
                     reads=[txb[b], tst], writes=[txn])
                for half in range(2):
                    for dc in range(8):
                        c = half * 8 + dc
                        S.op("pe", lambda e, c=c, dc=dc: e.transpose(psb[:, dc * 128:(dc + 1) * 128], xn[:, c * 128:(c + 1) * 128], ident[:]),
                             reads=[txn, tC], writes=[tpsb])
                    for dc in range(8):
                        c = half * 8 + dc
                        dst = hT[:, c, blk_slot * 128:(blk_slot + 1) * 128]
                        src = psb[:, dc * 128:(dc + 1) * 128]
                        if dc % 2 == 0:
                            ts("dve", dst, src, pvec[:, PV_LNG + c:PV_LNG + c + 1], pvec[:, PV_LNB + c:PV_LNB + c + 1],
                               ALU.mult, ALU.add, [tpsb, tC], [thT])
                        else:
                            S.op("act", lambda e, dst=dst, src=src, c=c: e.activation(
                                out=dst, in_=src, func=AF.Identity, bias=pvec[:, PV_LNB + c:PV_LNB + c + 1],
                                scale=pvec[:, PV_LNG + c:PV_LNG + c + 1]), reads=[tpsb, tC], writes=[thT])
                if is_meta:
                    S.op("pool", lambda e: e.memset(hT[:, :, 0:112], 0.0), writes=[thT])

            def inproj_fm(ct_off, m, nt, sink):
                i = next_ps()
                for kc in range(16):
                    S.op("pe", lambda e, kc=kc, i=i: e.matmul(ps[i][0:m, 0:nt], lhsT=W[:, kc, ct_off:ct_off + m], rhs=hT[:, kc, 0:nt],
                                                            start=(kc == 0), stop=(kc == 15)), reads=[tW, thT], writes=[tps[i]])
                sink(ps[i][0:m, 0:nt], tps[i])

            tiles = [(-1, 128)] + [(i, 256) for i in range(16)]
            for (ti, nt) in tiles:
                nblk = nt // 128
                blk0 = 0 if ti < 0 else 1 + 2 * ti
                pos0 = blk0 * 128
                for j in range(nblk):
                    if ti < 0:
                        layer_norm_block(None, True, 0)
                    else:
                        r0 = ti * 256 + j * 128
                        layer_norm_block(x_d[r0:r0 + 128, :], False, j)
                if ti >= 0:
                    for h in range(2):
                        inproj_fm(h * 128, 128, nt, lambda p, tp, h=h: evac_copy(QT[:, h, 0:nt], p, [tp], [tQT]))
                for h in range(2):
                    inproj_fm(256 + h * 128, 128, nt, lambda p, tp, h=h: evac_copy(KT[:, h, pos0:pos0 + nt], p, [tp], [tKT]))
                if ti >= 0:
                    for h in range(2):
                        inproj_fm(512 + h * 128, 128, nt, lambda p, tp, h=h: act(gate_a[:, h, 0:nt], p, AF.Silu, [tp], [tga]))
                    for p_ in range(2):
                        inproj_fm(1536 + p_ * 128, 128, nt, lambda p, tp, p_=p_: act(gate_r[:, p_, 0:nt], p, AF.Silu, [tp], [tgr]))
                for zi in range(6):
                    inproj_fm(768 + zi * 128, 128, nt, lambda p, tp, zi=zi: evac_copy(zb[zi][:, 1:1 + nt], p, [tp], [tzb[zi]]))
                for zi in range(2):
                    inproj_fm(1792 + zi * 96, 96, nt, lambda p, tp, zi=zi: evac_copy(zb[6 + zi][0:96, 1:1 + nt], p, [tp], [tzb[6 + zi]]))
                for j in range(nblk):
                    i = next_ps()
                    for kc in range(16):
                        S.op("pe", lambda e, kc=kc, i=i, j=j: e.matmul(ps[i][:, 0:256], lhsT=hT[:, kc, j * 128:(j + 1) * 128],
                                                                     rhs=W[:, kc, 1984:2240], start=(kc == 0), stop=(kc == 15)),
                             reads=[tW, thT], writes=[tps[i]])
                    evac_copy(Vt[:, blk0 + j, :], ps[i][:, 0:256], [tps[i]], [tV])

                chk('inproj%d' % ti)
                r32, k32, v32, t1, t2, sg, aic, kkn, k2, bvec, cs, t3 = wk
                tr32, tk32, tv32, tt1, tt2, tsg, taic, tkkn, tk2, tbvec, tcs, tt3 = twk
                n = nt

                def shift(zi, mucol, out, tout, rows=128):
                    z = zb[zi]
                    tz = tzb[zi]
                    tt("pool", t3[0:rows, 0:n], z[0:rows, 0:n], z[0:rows, 1:1 + n], ALU.subtract, [tz], [tt3])
                    stt(out[0:rows, 0:n], t3[0:rows, 0:n], pvec[0:rows, mucol:mucol + 1], z[0:rows, 1:1 + n], ALU.mult, ALU.add,
                        [tt3, tz, tC], [tout])
                    S.op("pool", lambda e: e.tensor_copy(out=z[0:rows, 0:1], in_=z[0:rows, n:n + 1]), reads=[tz], writes=[tz])

                shift(6, PV_MUWD, t1, tt1, rows=96)
                act(twb[0:96, 0:n], t1[0:96, 0:n], AF.Tanh, [tt1], [ttwb])
                shift(7, PV_MUAD, t2, tt2, rows=96)
                S.op("dve", lambda e: e.tensor_scalar(out=adb[0:96, 0:n], in0=t2[0:96, 0:n], scalar1=1.0, scalar2=None, op0=ALU.mult), reads=[tt2], writes=[tadb])

                import os as _os
                pe_last = ('E', 'pe', len(S.ops['pe']) - 1)
                for t_ in twk2:
                    t_.r[('E', 'pe')] = pe_last

                def shift2(zi, mucol, out, tout, tmp, ttmp):
                    z = zb[zi]
                    tz = tzb[zi]
                    tt("pool", tmp[:, 0:n], z[:, 0:n], z[:, 1:1 + n], ALU.subtract, [tz], [ttmp])
                    stt(out[:, 0:n], tmp[:, 0:n], pvec[:, mucol:mucol + 1], z[:, 1:1 + n], ALU.mult, ALU.add, [ttmp, tz, tC], [tout])
                    S.op("pool", lambda e: e.tensor_copy(out=z[:, 0:1], in_=z[:, n:n + 1]), reads=[tz], writes=[tz])

                def pre_pair(p, wks, twks, f3, tf3, pb):
                    r32, k32, v32, t1, t2, sg, aic, kkn, k2, bvec, cs, t3 = wks
                    tr32, tk32, tv32, tt1, tt2, tsg, taic, tkkn, tk2, tbvec, tcs, tt3 = twks
                    b0, b1, b2 = pb
                    tnb = tnb2[p]
                    shift2(0 + p, PV_MU + 0 + p, r32, tr32, t3, tt3)
                    yield
                    shift2(2 + p, PV_MU + 2 + p, k32, tk32, t3, tt3)
                    yield
                    shift2(4 + p, PV_MU + 4 + p, v32, tv32, t3, tt3)
                    yield
                    S.op("pe", lambda e: e.matmul(ps[b0][:, 0:n], lhsT=lora[0:96, 0, p * 128:(p + 1) * 128], rhs=twb[0:96, 0:n],
                                                  start=True, stop=True), reads=[tC, ttwb], writes=[tps[b0]])
                    act(sg[:, 0:n], ps[b0][:, 0:n], AF.Sigmoid, [tps[b0], tC], [tsg], bias=pvec[:, PV_W0 + p:PV_W0 + p + 1])
                    yield
                    S.op("pe", lambda e: e.matmul(ps[b1][:, 0:n], lhsT=lora[0:96, 1, p * 128:(p + 1) * 128], rhs=adb[0:96, 0:n],
                                                  start=True, stop=True), reads=[tC, tadb], writes=[tps[b1]])
                    act(aic[:, 0:n], ps[b1][:, 0:n], AF.Sigmoid, [tps[b1], tC], [taic], bias=pvec[:, PV_A0 + p:PV_A0 + p + 1])
                    yield
                    ts("dve", t1[:, 0:n], k32[:, 0:n], pvec[:, PV_KK + p:PV_KK + p + 1], None, ALU.mult, None, [tk32, tC], [tt1])
                    tt("pool", t2[:, 0:n], t1[:, 0:n], t1[:, 0:n], ALU.mult, [tt1], [tt2])
                    yield
                    S.op("pe", lambda e: e.matmul(ps[b2][:, 0:n], lhsT=blockones[:], rhs=t2[:, 0:n], start=True, stop=True),
                         reads=[tC, tt2], writes=[tps[b2]])
                    ts("dve", t2[:, 0:n], ps[b2][:, 0:n], 1e-24, None, ALU.max, None, [tps[b2]], [tt2])
                    yield
                    act(t2[:, 0:n], t2[:, 0:n], AF.Ln, [tt2], [tt2])
                    yield
                    act(t2[:, 0:n], t2[:, 0:n], AF.Exp, [tt2], [tt2], scale=-0.5)
                    yield
                    tt("dve", kkn[:, 0:n], t1[:, 0:n], t2[:, 0:n], ALU.mult, [tt1, tt2], [tkkn])
                    yield
                    ts("dve", t1[:, 0:n], aic[:, 0:n], pvec[:, PV_KA + p:PV_KA + p + 1], pder[:, p:p + 1], ALU.mult, ALU.add,
                       [taic, tC], [tt1])
                    yield
                    tt("pool", k2[:, 0:n], k32[:, 0:n], t1[:, 0:n], ALU.mult, [tk32, tt1], [tk2])
                    tt("pool", bvec[:, 0:n], kkn[:, 0:n], aic[:, 0:n], ALU.mult, [tkkn, taic], [tbvec])
                    yield
                    stt(t1[:, 0:n], r32[:, 0:n], pvec[:, PV_RK + p:PV_RK + p + 1], k2[:, 0:n], ALU.mult, ALU.mult, [tr32, tk2, tC], [tt1])
                    yield
                    S.op("pe", lambda e: e.matmul(ps[b0][:, 0:n], lhsT=blockones[:], rhs=t1[:, 0:n], start=True, stop=True),
                         reads=[tC, tt1], writes=[tps[b0]])
                    tt("dve", bonus[p][:, 0:n], ps[b0][:, 0:n], v32[:, 0:n], ALU.mult, [tps[b0], tv32], [tbonus[p]])
                    yield
                    for c in range(nblk):
                        S.op("dve", lambda e: e.tensor_tensor_scan(out=cs[:, c * 128:(c + 1) * 128], data0=ones_f[:, 0:128],
                                                                   data1=sg[:, c * 128:(c + 1) * 128], initial=0.0,
                                                                   op0=ALU.mult, op1=ALU.add), reads=[tsg, tC], writes=[tcs])
                    yield
                    act(t1[:, 0:n], cs[:, 0:n], AF.Exp, [tcs], [tt1], scale=-C0)
                    yield
                    tt("dve", arT[p][:, 1, 0:n], r32[:, 0:n], t1[:, 0:n], ALU.mult, [tr32, tt1], [tarT[p]])
                    for c in range(nblk):
                        S.op("pool", lambda e: e.tensor_copy(out=WC[:, p * 2 + c:p * 2 + c + 1], in_=t1[:, c * 128 + 127:c * 128 + 128]),
                             reads=[tt1], writes=[tWC])
                        ts("dve", nbias[:, p * 2 + c:p * 2 + c + 1], cs[:, c * 128 + 127:c * 128 + 128], -C0, None, ALU.mult, None, [tcs], [tnb])
                    yield
                    tt("pool", t2[:, 0:n], cs[:, 0:n], sg[:, 0:n], ALU.subtract, [tcs, tsg], [tt2])
                    yield
                    act(t2[:, 0:n], t2[:, 0:n], AF.Exp, [tt2], [tt2], scale=-C0)
                    yield
                    stt(arT[p][:, 0, 0:n], kkn[:, 0:n], -1.0, t2[:, 0:n], ALU.mult, ALU.mult, [tkkn, tt2], [tarT[p]])
                    act(t1[:, 0:n], cs[:, 0:n], AF.Exp, [tcs], [tt1], scale=C0)
                    yield
                    tt("dve", ktl[p][:, 0:n], k2[:, 0:n], t1[:, 0:n], ALU.mult, [tk2, tt1], [tktl[p]])
                    tt("pool", btl[p][:, 0:n], bvec[:, 0:n], t1[:, 0:n], ALU.mult, [tbvec, tt1], [tbtl[p]])
                    yield
                    for hh in range(2):
                        hm = blockones[:, hh * 64:hh * 64 + 1]
                        ts("dve" if hh == 0 else "pool", btlm[p][hh][:, 0:n], btl[p][:, 0:n], hm, None, ALU.mult, None, [tbtl[p], tC], [tbtlm[p][hh]])
                        ts("pool" if hh == 0 else "dve", ktlm[p][hh][:, 0:n], ktl[p][:, 0:n], hm, None, ALU.mult, None, [tktl[p], tC], [tktlm[p][hh]])
                    yield
                    for c in range(nblk):
                        S.op("act", lambda e: e.activation(out=t2[:, c * 128:(c + 1) * 128], in_=cs[:, c * 128:(c + 1) * 128],
                                                           func=AF.Exp, bias=nbias[:, p * 2 + c:p * 2 + c + 1], scale=C0),
                             reads=[tcs, tnb], writes=[tt2])
                    yield
                    tt("dve", f3[0][:, 0:n], k2[:, 0:n], t2[:, 0:n], ALU.mult, [tk2, tt2], [tf3[0]])
                    tt("pool", f3[1][:, 0:n], bvec[:, 0:n], t2[:, 0:n], ALU.mult, [tbvec, tt2], [tf3[1]])
                    S.op("act", lambda e: e.activation(out=f3[2][:, 0:n], in_=v32[:, 0:n], func=AF.Copy), reads=[tv32], writes=[tf3[2]])
                    yield
                    for c in range(nblk):
                        for q in range(3):
                            S.op("pe", lambda e: e.matmul(ps[pb[q]][:, 0:128], lhsT=f3[q][:, c * 128:(c + 1) * 128], rhs=ident[:],
                                                          start=True, stop=True), reads=[tf3[q], tC], writes=[tps[pb[q]]])
                        yield
                        for q in range(3):
                            evac_copy(tok3[q][:, c, p * 128:(p + 1) * 128], ps[pb[q]][:, 0:128], [tps[pb[q]]], [ttok3[q]])
                        for hh in range(2):
                            S.op('act', lambda e: e.activation(out=Vz[:, c, 2 * p + hh, hh * 64:(hh + 1) * 64], in_=ps[b2][:, hh * 64:(hh + 1) * 64], func=AF.Copy),
                                 reads=[tps[b2]], writes=[tVz])
                        yield

                gens = [pre_pair(0, wk, twk, fm3, tfm3, (4, 5, 6)), pre_pair(1, wk2, twk2, fm3b, tfm3b, (1, 2, 3))]
                while gens:
                    for g_ in list(gens):
                        try:
                            next(g_)
                        except StopIteration:
                            gens.remove(g_)
                chk('rwpre%d' % ti)
                Khat, Bhat, Vtok = tok3
                tKhat, tBhat, tVtok = ttok3
                ATs = [AT, [wk2[h_].bitcast(BF16).rearrange("p (a n) -> p a n", a=4) for h_ in range(4)]]
                tATs = [tAT, [twk2[h_] for h_ in range(4)]]
                Mcs = [Mc, wk2[4].bitcast(BF16).rearrange("p (h n) -> p h n", h=4)]
                Mtcs = [Mtc, wk2[5].bitcast(BF16).rearrange("p (h n) -> p h n", h=4)]
                Ptcs = [Ptc, wk2[6].bitcast(BF16).rearrange("p (h n) -> p h n", h=4)]
                tMcs, tMtcs, tPtcs = [tMc, twk2[4]], [tMtc, twk2[5]], [tPtc, twk2[6]]
                lvb = [(3, 4, 5), (0, 1, 6)]
                for c in range(nblk):
                    cs_ = slice(c * 128, (c + 1) * 128)
                    ATc, tATc = ATs[c], tATs[c]
                    for h in range(4):
                        p, hh = h // 2, h % 2
                        i = h % 2
                        S.op("pe", lambda e: e.matmul(ps[i][:, 0:256], lhsT=btlm[p][hh][:, cs_], rhs=arT[p][:, :, cs_], start=True, stop=True),
                             reads=[tbtlm[p][hh], tarT[p]], writes=[tps[i]])
                        S.op("pe", lambda e: e.matmul(ps[i][:, 256:512], lhsT=ktlm[p][hh][:, cs_], rhs=arT[p][:, :, cs_], start=True, stop=True),
                             reads=[tktlm[p][hh], tarT[p]], writes=[tps[i]])
                        tt("dve", ATc[h].rearrange("p a n -> p (a n)"), ps[i][:, :], maskT[:], ALU.mult, [tps[i], tC], [tATc[h]])
                        S.op("pe", lambda e: e.matmul(ps[2][:, h * 128:(h + 1) * 128], lhsT=arT[p][:, 0, cs_], rhs=btlm[p][hh][:, cs_], start=True, stop=True),
                             reads=[tbtlm[p][hh], tarT[p]], writes=[tps[2]])
                    for h in range(4):
                        tt("dve", Mcs[c][:, h, :], ps[2][:, h * 128:(h + 1) * 128], masklow[:], ALU.mult, [tps[2], tC], [tMcs[c]])
                        S.op("pool", lambda e: e.tensor_copy(out=Mtcs[c][:, h, :], in_=ATc[h][:, 0, :]), reads=[tATc[h]], writes=[tMtcs[c]])
                        tt("pool", Ptcs[c][:, h, :], ATc[h][:, 0, :], ident[:], ALU.add, [tATc[h], tC], [tPtcs[c]])
                for lvl in range(7):
                    for c in range(nblk):
                        bP, bM, bMt = lvb[c]
                        Mc_, Mtc_, Ptc_ = Mcs[c], Mtcs[c], Ptcs[c]
                        tMc_, tMtc_, tPtc_ = tMcs[c], tMtcs[c], tPtcs[c]
                        if lvl >= 1:
                            for h in range(4):
                                S.op("pe", lambda e: e.matmul(ps[bP][:, h * 128:(h + 1) * 128], lhsT=Mc_[:, h, :], rhs=Ptc_[:, h, :], start=True, stop=True),
                                     reads=[tMc_, tPtc_], writes=[tps[bP]])
                        if lvl < 6:
                            for h in range(4):
                                S.op("pe", lambda e: e.matmul(ps[bM][:, h * 128:(h + 1) * 128], lhsT=Mtc_[:, h, :], rhs=Mc_[:, h, :], start=True, stop=True),
                                     reads=[tMc_, tMtc_], writes=[tps[bM]])
                            if lvl < 5:
                                for h in range(4):
                                    S.op("pe", lambda e: e.matmul(ps[bMt][:, h * 128:(h + 1) * 128], lhsT=Mc_[:, h, :], rhs=Mtc_[:, h, :], start=True, stop=True),
                                         reads=[tMc_, tMtc_], writes=[tps[bMt]])
                    for c in range(nblk):
                        bP, bM, bMt = lvb[c]
                        Mc_, Mtc_, Ptc_ = Mcs[c], Mtcs[c], Ptcs[c]
                        tMc_, tMtc_, tPtc_ = tMcs[c], tMtcs[c], tPtcs[c]
                        if lvl >= 1:
                            tt("dve", Ptc_.rearrange("p h n -> p (h n)"), ps[bP][:, :], Ptc_.rearrange("p h n -> p (h n)"), ALU.add, [tps[bP], tPtc_], [tPtc_])
                        if lvl < 6:
                            S.op("act", lambda e: e.activation(out=Mc_.rearrange("p h n -> p (h n)"), in_=ps[bM][:, :], func=AF.Copy), reads=[tps[bM]], writes=[tMc_])
                            if lvl < 5:
                                S.op("act", lambda e: e.activation(out=Mtc_.rearrange("p h n -> p (h n)"), in_=ps[bMt][:, :], func=AF.Copy), reads=[tps[bMt]], writes=[tMtc_])
                for c in range(nblk):
                    cs_ = slice(c * 128, (c + 1) * 128)
                    AT_, tAT_ = ATs[c], tATs[c]
                    Ptc_, tPtc_ = Ptcs[c], tPtcs[c]
                    for p in range(2):
                        S.op("pe", lambda e: e.matmul(ps[0][:, p * 128:(p + 1) * 128], lhsT=arT[p][:, 0, cs_], rhs=Sbf[:, p, :], start=True, stop=False),
                             reads=[tarT[p], tSbf], writes=[tps[0]])
                        for hh in range(2):
                            h = 2 * p + hh
                            S.op("pe", lambda e: e.matmul(ps[0][:, h * 64:(h + 1) * 64], lhsT=AT_[h][:, 2, :], rhs=Vtok[:, c, h * 64:(h + 1) * 64],
                                                          start=False, stop=(hh == 1)), reads=[tAT_[h], tVtok], writes=[tps[0]])
                    S.op("act", lambda e: e.activation(out=Xsb.rearrange("p h n -> p (h n)"), in_=ps[0][:, 0:256], func=AF.Copy), reads=[tps[0]], writes=[tXsb])
                    for h in range(4):
                        S.op("pe", lambda e: e.matmul(ps[1][:, h * 64:(h + 1) * 64], lhsT=Ptc_[:, h, :], rhs=Xsb[:, h, :], start=True, stop=True),
                             reads=[tPtc_, tXsb], writes=[tps[1]])
                    S.op("act", lambda e: e.activation(out=Usb.rearrange("p h n -> p (h n)"), in_=ps[1][:, 0:256], func=AF.Copy), reads=[tps[1]], writes=[tUsb])
                    for hh in range(2):
                        src = ps[1][:, 0:256].rearrange("p (a b n) -> p a b n", a=2, b=2)[:, :, hh, :]
                        dst = Uz.rearrange("p (a b) n -> p a b n", b=2)[:, :, hh, hh * 64:(hh + 1) * 64]
                        S.op("act", lambda e: e.activation(out=dst, in_=src, func=AF.Copy), reads=[tps[1]], writes=[tUz])
                    for p in range(2):
                        S.op("pe", lambda e: e.matmul(ps[2 + p][:, 0:128], lhsT=Sbf[:, p, :], rhs=arT[p][:, 1, cs_], start=True, stop=False),
                             reads=[tSbf, tarT[p]], writes=[tps[2 + p]])
                        for hh in range(2):
                            h = 2 * p + hh
                            S.op("pe", lambda e: e.matmul(ps[2 + p][:, 0:128], lhsT=Uz[:, h, :], rhs=AT_[h][:, 1, :], start=False, stop=False),
                                 reads=[tUz, tAT_[h]], writes=[tps[2 + p]])
                            S.op("pe", lambda e: e.matmul(ps[2 + p][:, 0:128], lhsT=Vz[:, c, h, :], rhs=AT_[h][:, 3, :], start=False, stop=(hh == 1)),
                                 reads=[tVz, tAT_[h]], writes=[tps[2 + p]])
                        if ti >= 0:
                            evac_copy(yT[p][:, cs_], ps[2 + p][:, 0:128], [tps[2 + p]], [tyT[p]])
                    for p in range(2):
                        S.op("pe", lambda e: e.matmul(ps[4 + p][:, 0:128], lhsT=Bhat[:, c, p * 128:(p + 1) * 128],
                                                      rhs=Usb.rearrange("p h n -> p (h n)")[:, p * 128:(p + 1) * 128], start=True, stop=False),
                             reads=[tBhat, tUsb], writes=[tps[4 + p]])
                        S.op("pe", lambda e: e.matmul(ps[4 + p][:, 0:128], lhsT=Khat[:, c, p * 128:(p + 1) * 128], rhs=Vtok[:, c, p * 128:(p + 1) * 128],
                                                      start=False, stop=True), reads=[tKhat, tVtok], writes=[tps[4 + p]])
                        tt("dve", t3[:, 0:128], ps[4 + p][:, 0:128], blockones[:], ALU.mult, [tps[4 + p], tC], [tt3])
                        stt(S32[:, p, :], S32[:, p, :], WC[:, p * 2 + c:p * 2 + c + 1], t3[:, 0:128], ALU.mult, ALU.add, [tS32, tWC, tt3], [tS32])
                    S.op("act", lambda e: e.activation(out=Sbf.rearrange("p a n -> p (a n)"), in_=S32.rearrange("p a n -> p (a n)"), func=AF.Copy),
                         reads=[tS32], writes=[tSbf])

                chk('chain%d' % ti)
                if ti < 0:
                    continue

                ob = rr["oT"] % 2
                rr["oT"] += 1
                oTt = oT[ob]
                toTt = toT[ob]

                for p in range(2):
                    y = yT[p]
                    S.op("pe", lambda e, y=y: e.matmul(ps[0][:, 0:n], lhsT=blockones[:], rhs=y[:, 0:n], start=True, stop=True),
                         reads=[tC, tyT[p]], writes=[tps[0]])
                    stt(t1[:, 0:n], ps[0][:, 0:n], -1.0 / 64.0, y[:, 0:n], ALU.mult, ALU.add, [tps[0], tyT[p]], [tt1])
                    tt("pool", t2[:, 0:n], t1[:, 0:n], t1[:, 0:n], ALU.mult, [tt1], [tt2])
                    S.op("pe", lambda e: e.matmul(ps[1][:, 0:n], lhsT=blockones[:], rhs=t2[:, 0:n], start=True, stop=True),
                         reads=[tC, tt2], writes=[tps[1]])
                    act(t2[:, 0:n], ps[1][:, 0:n], AF.Ln, [tps[1], tC], [tt2], bias=eps_ap(64e-5), scale=1.0 / 64.0)
                    act(t2[:, 0:n], t2[:, 0:n], AF.Exp, [tt2], [tt2], scale=-0.5)
                    tt("dve", t1[:, 0:n], t1[:, 0:n], t2[:, 0:n], ALU.mult, [tt1, tt2], [tt1])
                    ts("dve", t1[:, 0:n], t1[:, 0:n], pvec[:, PV_GNG + p:PV_GNG + p + 1], pvec[:, PV_GNB + p:PV_GNB + p + 1],
                       ALU.mult, ALU.add, [tt1, tC], [tt1])
                    tt("pool", t1[:, 0:n], t1[:, 0:n], bonus[p][:, 0:n], ALU.add, [tt1, tbonus[p]], [tt1])
                    tt("dve", oTt[:, 2 + p, 0:n], t1[:, 0:n], gate_r[:, p, 0:n], ALU.mult, [tt1, tgr], [toTt])

                chk('rwpost%d' % ti)
                qb0 = blk0
                OO = ps[4][:, :].rearrange("p (m n) -> p m n", m=2)
                SS = ps[5][:, :].rearrange("p (m n) -> p m n", m=2)
                for h in range(2):
                    kbs = list(range(0, qb0 + 2))

                    def geom(kb):
                        delta = kb - qb0
                        q_lo = 128 if delta == 1 else 0
                        return delta, q_lo, n - q_lo, kb % 2

                    def stage_qk(kb):
                        delta, q_lo, nq, par = geom(kb)
                        for m in range(2):
                            rs = slice(m * 64, m * 64 + 64)
                            pi = 2 * par + m
                            S.op("pe", lambda e: e.matmul(ps[pi][:, 0:nq], lhsT=KT[rs, h, kb * 128:(kb + 1) * 128], rhs=QT[rs, h, q_lo:n],
                                                          start=True, stop=True), reads=[tKT, tQT], writes=[tps[pi]])

                    def stage_rest(kb):
                        delta, q_lo, nq, par = geom(kb)
                        near = delta >= -1
                        E = Eb[par]
                        tE = tEb[par]
                        for m in range(2):
                            pi = 2 * par + m
                            if near:
                                boff = 0 if delta >= 0 else 128
                                stt(Ssb[m][:, 0:nq], ps[pi][:, 0:nq], 0.125, bext[h][:, boff:boff + nq], ALU.mult, ALU.add,
                                    [tps[pi], tC], [tSsb[m]])
                                if kb == 0:
                                    S.op("pool", lambda e: e.memset(Ssb[m][0:112, 0:nq], NEG), reads=[], writes=[tSsb[m]])
                                act(E[:, m, 0:nq], Ssb[m][:, 0:nq], AF.Exp, [tSsb[m]], [tE])
                            else:
                                bc = bcol[:, 2 + h:3 + h] if kb == 0 else bcol[:, h:h + 1]
                                act(E[:, m, 0:nq], ps[pi][:, 0:nq], AF.Exp, [tps[pi], tC], [tE], bias=bc, scale=0.125)
                        first = (kb == kbs[0])
                        last = (kb == kbs[-1])
                        if q_lo == 0:
                            Ef = E.rearrange("p m n -> p (m n)")
                            S.op("pe", lambda e: e.matmul(ps[4][:, :], lhsT=Vt[:, kb, h * 128:(h + 1) * 128], rhs=Ef, start=first, stop=last),
                                 reads=[tV, tE], writes=[tps[4]])
                            S.op("pe", lambda e: e.matmul(ps[5][:, :], lhsT=ones_bf[:], rhs=Ef, start=first, stop=last),
                                 reads=[tC, tE], writes=[tps[5]])
                        else:
                            for m in range(2):
                                S.op("pe", lambda e: e.matmul(OO[:, m, q_lo:n], lhsT=Vt[:, kb, h * 128:(h + 1) * 128], rhs=E[:, m, 0:nq],
                                                              start=first, stop=(last and m == 1)), reads=[tV, tE], writes=[tps[4]])
                                S.op("pe", lambda e: e.matmul(SS[:, m, q_lo:n], lhsT=ones_bf[:], rhs=E[:, m, 0:nq],
                                                              start=first, stop=(last and m == 1)), reads=[tC, tE], writes=[tps[5]])

                    def post_a():
                        S.op("dve", lambda e: e.reciprocal(out=t1[:, 0:n], in_=SS[:, 0, 0:n]), reads=[tps[5]], writes=[tt1])
                        S.op("dve", lambda e: e.reciprocal(out=t2[:, 0:n], in_=SS[:, 1, 0:n]), reads=[tps[5]], writes=[tt2])
                        tt("dve", t1[:, 0:n], OO[:, 0, 0:n], t1[:, 0:n], ALU.mult, [tps[4], tt1], [tt1])
                        tt("dve", t2[:, 0:n], OO[:, 1, 0:n], t2[:, 0:n], ALU.mult, [tps[4], tt2], [tt2])

                    def post_b(hp):
                        stt(t1[:, 0:n], t2[:, 0:n], pder[:, 3:4], t1[:, 0:n], ALU.mult, ALU.add, [tt1, tt2, tC], [tt1])
                        tt("pool", t2[:, 0:n], t1[:, 0:n], t1[:, 0:n], ALU.mult, [tt1], [tt2])
                        S.op("pe", lambda e: e.matmul(ps[6][:, 0:n], lhsT=ones_f[:], rhs=t2[:, 0:n], start=True, stop=True),
                             reads=[tC, tt2], writes=[tps[6]])
                        act(t2[:, 0:n], ps[6][:, 0:n], AF.Ln, [tps[6], tC], [tt2], bias=eps_ap(1e-5), scale=1.0 / 128.0)
                        act(t2[:, 0:n], t2[:, 0:n], AF.Exp, [tt2], [tt2], scale=-0.5)
                        tt("dve", t1[:, 0:n], t1[:, 0:n], t2[:, 0:n], ALU.mult, [tt1, tt2], [tt1])
                        stt(oTt[:, hp, 0:n], t1[:, 0:n], pder[:, 2:3], gate_a[:, hp, 0:n], ALU.mult, ALU.mult, [tt1, tga, tC], [toTt])

                    stage_qk(kbs[0])
                    for i_, kb in enumerate(kbs):
                        if i_ + 1 < len(kbs):
                            stage_qk(kbs[i_ + 1])
                        stage_rest(kb)
                        if h == 1 and i_ == min(1, len(kbs) - 1):
                            post_b(0)
                    post_a()
                    if h == 1:
                        post_b(1)

                chk('attn%d' % ti)
                for cc in range(4):
                    S.dma(lambda e: e.dma_start(out=agin[cc].ap()[:, ti * 256:ti * 256 + 256], in_=oTt[:, cc, 0:256]), reads=[toTt], writes=[tagin[cc]],
                          key="oT%d_%d" % (ob, cc))
                if DEBUG:
                    dd = dbg_d.rearrange("(c p) n -> p c n", p=128)[:, :, ti * 256:ti * 256 + 256]
                    S.dma(lambda e, dd=dd, oTt=oTt: e.dma_start(out=dd, in_=oTt[:, :, 0:256]), reads=[toTt], key="oT%d" % ob)

            chk('main')
            for cc in range(4):
                S.dma(lambda e: e.collective_compute("AllGather", ALU.bypass, replica_groups=[[0, 1, 2, 3], [4, 5, 6, 7]],
                                                     ins=[agin[cc].ap()], outs=[agout[cc].ap()]),
                      reads=[tagin[cc]], writes=[tagout[cc]], key="cc%d" % cc, queue="pool", inc=1)
            chk('ag')
            S.barrier()
            apos[0] = 0
            oq = carve(16 * 1024 * 2, BF16, "p (c n) -> p c n", c=16)
            toq = S.tile("oq")
            gbt = carve(4 * D * 4, F32, "p (a n) -> p a n", a=4)
            tgb = S.tile("gbt")
            rsd = [carve(D * 4, F32) for _ in range(2)]
            trsd = [S.tile("rsd%d" % i) for i in range(2)]
            fstE = [carve(4 * 6 * 4, F32, "p (a n) -> p a n", a=4) for _ in range(2)]
            fmvE = [carve(8 * 4, F32) for _ in range(2)]
            tfE = [S.tile("fstE%d" % i) for i in range(2)]
            fstP = [carve(4 * 6 * 4, F32, "p (a n) -> p a n", a=4) for _ in range(2)]
            fmvP = [carve(8 * 4, F32) for _ in range(2)]
            tfP = [S.tile("fstP%d" % i) for i in range(2)]
            assert apos[0] <= arena_w

            for kc in range(16):
                b = kc % 2
                S.dma(lambda e, kc=kc, b=b: e.dma_start(out=xb[b][:, 0:D], in_=wout_d[kc * 128:(kc + 1) * 128, :]),
                      writes=[txb[b]], key="xb%d" % b)
                eng = ("dve", "pool", "act")[kc % 3]
                if eng == "act":
                    S.op("act", lambda e, kc=kc, b=b: e.activation(out=W[:, kc, 0:D], in_=xb[b][:, 0:D], func=AF.Copy),
                         reads=[txb[b]], writes=[tW])
                else:
                    S.op(eng, lambda e, kc=kc, b=b: e.tensor_copy(out=W[:, kc, 0:D], in_=xb[b][:, 0:D]), reads=[txb[b]], writes=[tW])
            gsrc = bass.AP(tensor=gb_d.tensor, offset=0, ap=[[0, 128], [D, 4], [1, D]])
            S.dma(lambda e: e.dma_start(out=gbt, in_=gsrc), writes=[tgb], key="gbt")

            idxt = sbt("idxt", [128, 4], mybir.dt.int32)
            tidx = S.tile("idxt")
            S.dma(lambda e: e.dma_start(out=idxt[:], in_=idx_d), writes=[tidx], key="idxt")
            for kc in range(16):
                r_, c_ = kc // 4, kc % 4
                agv = agout[c_].ap().rearrange("e (q n) -> (e q) n", q=4)
                S.dma(lambda e: e.indirect_dma_start(out=oq[:, kc, :], out_offset=None, in_=agv,
                                                     in_offset=bass.IndirectOffsetOnAxis(ap=idxt[:, r_:r_ + 1], axis=0)),
                      reads=[tagout[c_], tidx], writes=[toq], key="oq", queue="pool")
            alpha = 2.0 ** 0.25
            def ln_emb(blk):
                b = blk % 2
                xt, r, tr = xb[b], rsd[b], trsd[b]
                fst, fmv, tf = fstE[b], fmvE[b], tfE[b]
                S.dma(lambda e: e.dma_start(out=xt[:, 0:D], in_=xq_d[blk * 128:(blk + 1) * 128, :]), writes=[txb[b]], key="xb%d" % b)
                for q in range(4):
                    S.op("dve", lambda e: e.bn_stats(out=fst[:, q, :], in_=xt[:, q * 512:(q + 1) * 512]), reads=[txb[b]], writes=[tf])
                S.op("dve", lambda e: e.bn_aggr(out=fmv[:, 0:2], in_=fst.rearrange("p a n -> p (a n)")), reads=[tf], writes=[tf])
                act(fmv[:, 2:3], fmv[:, 1:2], AF.Ln, [tf], [tf], bias=eps_ap(1e-5))
                act(fmv[:, 2:3], fmv[:, 2:3], AF.Exp, [tf], [tf], scale=-0.5)
                stt(fmv[:, 3:4], fmv[:, 0:1], -1.0, fmv[:, 2:3], ALU.mult, ALU.mult, [tf], [tf])
                S.op("act", lambda e: e.activation(out=r, in_=xt[:, 0:D], func=AF.Identity, bias=fmv[:, 3:4], scale=fmv[:, 2:3]),
                     reads=[txb[b], tf], writes=[tr])
                tt("pool", r, r, gbt[:, 0, :], ALU.mult, [tr, tgb], [tr])
                tt("pool", r, r, gbt[:, 1, :], ALU.add, [tr, tgb], [tr])

            def proj(blk):
                b = blk % 2
                r, tr = rsd[b], trsd[b]
                for ct in range(4):
                    for kc in range(16):
                        S.op("pe", lambda e: e.matmul(ps[ct][:, :], lhsT=oq[:, kc, blk * 128:(blk + 1) * 128],
                                                      rhs=W[:, kc, ct * 512:(ct + 1) * 512], start=(kc == 0), stop=(kc == 15)),
                             reads=[toq, tW], writes=[tps[ct]])
                    stt(r[:, ct * 512:(ct + 1) * 512], r[:, ct * 512:(ct + 1) * 512], alpha, ps[ct][:, :], ALU.mult, ALU.add,
                        [tr, tps[ct]], [tr])

            def post_ln(blk):
                b = blk % 2
                r, tr = rsd[b], trsd[b]
                fst, fmv, tf = fstP[b], fmvP[b], tfP[b]
                for q in range(4):
                    S.op("dve", lambda e: e.bn_stats(out=fst[:, q, :], in_=r[:, q * 512:(q + 1) * 512]), reads=[tr], writes=[tf])
                S.op("dve", lambda e: e.bn_aggr(out=fmv[:, 4:6], in_=fst.rearrange("p a n -> p (a n)")), reads=[tf], writes=[tf])
                act(fmv[:, 6:7], fmv[:, 5:6], AF.Ln, [tf], [tf], bias=eps_ap(1e-5))
                act(fmv[:, 6:7], fmv[:, 6:7], AF.Exp, [tf], [tf], scale=-0.5)
                stt(fmv[:, 7:8], fmv[:, 4:5], -1.0, fmv[:, 6:7], ALU.mult, ALU.mult, [tf], [tf])
                S.op("act", lambda e: e.activation(out=r, in_=r, func=AF.Identity, bias=fmv[:, 7:8], scale=fmv[:, 6:7]),
                     reads=[tr, tf], writes=[tr])
                tt("dve", r, r, gbt[:, 2, :], ALU.mult, [tr, tgb], [tr])
                tt("pool", r, r, gbt[:, 3, :], ALU.add, [tr, tgb], [tr])
                S.dma(lambda e: e.dma_start(out=out_d[blk * 128:(blk + 1) * 128, :], in_=r), reads=[tr], key="rsd%d" % b)

            ln_emb(0)
            for blk in range(8):
                if blk + 1 < 8:
                    ln_emb(blk + 1)
                proj(blk)
                post_ln(blk)

        except _Stop:
            pass
        S.emit()
    return nc


_NC_CACHE = {}


def kernel(**inp):
    maps = _prep_inputs(inp)
    x = np.asarray(inp["x"], np.float32)
    for c in range(8):
        b, r = c // 4, c % 4
        maps[c]["xq"] = np.ascontiguousarray(x[b, r * 1024:(r + 1) * 1024])
        p = np.arange(128)[:, None]
        kc = np.arange(4)[None, :]
        maps[c]["idx"] = ((kc * 128 + p) * 4 + r).astype(np.int32)
    if "nc" not in _NC_CACHE:
        _NC_CACHE["nc"] = build_nc()
    nc = _NC_CACHE["nc"]
    res = run_bass_kernel_spmd(nc, maps, core_ids=list(range(8)))
    out = np.zeros((2, SEQ, D), np.float32)
    for c in range(8):
        b, r = c // 4, c % 4
        out[b, r * 1024:(r + 1) * 1024] = res.results[c]["out"]
    if DEBUG:
        kernel.dbg = [res.results[c]["dbg"] for c in range(8)]
    return out
```

```python
import math
from contextlib import ExitStack

import numpy as np
import ml_dtypes

import concourse.bass as bass
import concourse.mybir as mybir
from concourse.bass_utils import run_bass_kernel_spmd

F32 = mybir.dt.float32
BF16 = mybir.dt.bfloat16
AF = mybir.ActivationFunctionType
ALU = mybir.AluOpType

D = 2048
SEQ = 4096
NB = 33
P_TOK = NB * 128
NCOL = 2240
C0 = math.exp(-0.5)
NEG = -1.0e4
TABW = 511
DEBUG = False
STOP = None


class _Stop(Exception):
    pass


_HITS = {}


def chk(tag):
    import os
    if STOP == tag:
        _HITS[tag] = _HITS.get(tag, 0) + 1
        if _HITS[tag] >= int(os.environ.get("NTH", "1")):
            raise _Stop()


class T:
    __slots__ = ("name", "w", "r", "parent")

    def __init__(self, name, init_r=None, parent=None):
        self.name = name
        self.w = None
        self.r = dict(init_r) if init_r else {}
        self.parent = parent


class _Rec:
    def __init__(self):
        self.call = None

    def __getattr__(self, name):
        def f(*a, **k):
            assert self.call is None
            self.call = (name, a, k)
            return self
        return f


def _freeze(fn):
    r = _Rec()
    fn(r)
    name, a, k = r.call
    return lambda e: getattr(e, name)(*a, **k)


class Sched:
    ENG = ("pe", "act", "dve", "pool", "sp")

    def __init__(self, nc, es):
        self.nc = nc
        self.es = es
        self.ops = {e: [] for e in self.ENG}
        self.seen = {e: {} for e in self.ENG}
        self.dsems = {}
        self.esem = {}
        self.bar = {}
        self.groups = set()

    def tile(self, name):
        return T(name, self.bar)

    def barrier(self):
        b = {}
        for e in self.ENG:
            if e != "sp" and self.ops[e]:
                for i in range(len(self.ops[e]) - 1, -1, -1):
                    if self.ops[e][i]["dma"] is None:
                        b[('E', e)] = ('E', e, i)
                        break
        for k, v in self.dsems.items():
            if v[1] > 0:
                b[('D', k)] = ('D', k, v[1])
        self.bar = b

    @staticmethod
    def _dep(waits, ev):
        key = ev[:2]
        if waits.get(key, -1) < ev[2]:
            waits[key] = ev[2]

    def _collect(self, eng, reads, writes):
        waits = {}
        for t in reads:
            if t.w is not None:
                self._dep(waits, t.w)
        for t in writes:
            if t.w is not None:
                self._dep(waits, t.w)
            for ev in t.r.values():
                self._dep(waits, ev)
        wl = []
        for key, val in waits.items():
            if key[0] == 'E' and key[1] == eng and eng == 'pe':
                continue
            if self.seen[eng].get(key, -1) >= val:
                continue
            self.seen[eng][key] = val
            wl.append((key, val))
        return wl

    def op(self, eng, fn, reads=(), writes=()):
        fn = _freeze(fn)
        extra = [t.parent for t in list(reads) + list(writes) if t.parent is not None]
        if extra:
            reads = list(reads) + extra
        wl = self._collect(eng, reads, writes)
        idx = len(self.ops[eng])
        ev = ('E', eng, idx)
        self.ops[eng].append(dict(waits=wl, fn=fn, sig=False, dma=None))
        for t in reads:
            t.r[('E', eng)] = ev
        for t in writes:
            t.w = ev
            t.r = {}
        return ev

    def dsem(self, key):
        if key not in self.dsems:
            h = self.es.enter_context(self.nc.semaphore("d_" + key))
            self.dsems[key] = [h, 0]
        return self.dsems[key]

    def dma(self, fn, reads=(), writes=(), key=None, queue="sp", inc=16):
        fn = _freeze(fn)
        wl = self._collect(queue, reads, writes)
        if key in self.groups:
            wl = [w for w in wl if w[0] != ('D', key)]
        ds = self.dsem(key)
        ds[1] += inc
        ev = ('D', key, ds[1])
        self.ops[queue].append(dict(waits=wl, fn=fn, sig=False, dma=(key, inc)))
        for t in reads:
            t.r[('D', key)] = ev
        for t in writes:
            t.w = ev
            t.r = {}
        return ev

    def emit(self):
        nc = self.nc
        for e in self.ENG:
            for o in self.ops[e]:
                for key, val in o["waits"]:
                    if key[0] == 'E':
                        self.ops[key[1]][val]["sig"] = True
        cnt = {}
        for e in self.ENG:
            c = 0
            lst = []
            for o in self.ops[e]:
                if o["sig"]:
                    c += 1
                lst.append(c)
            cnt[e] = lst
            self.esem[e] = self.es.enter_context(nc.semaphore("e_" + e))
        finals = [(('D', k), v[1]) for k, v in self.dsems.items() if v[1] > 0]

        def resolve(key, val):
            if key[0] == 'E':
                return self.esem[key[1]], cnt[key[1]][val]
            if key[1] in self.groups:
                val = self.dsems[key[1]][1]
            return self.dsems[key[1]][0], val

        def body(e):
            def run(eng):
                for o in self.ops[e]:
                    for key, val in o["waits"]:
                        s, v = resolve(key, val)
                        eng.wait_ge(s, v)
                    ins = o["fn"](eng)
                    if o["dma"] is not None:
                        ins.then_inc(self.dsems[o["dma"][0]][0], o["dma"][1])
                    elif o["sig"]:
                        ins.then_inc(self.esem[e], 1)
                if e == "sp":
                    for key, val in finals:
                        s, v = resolve(key, val)
                        eng.wait_ge(s, v)
            return run

        with nc.Block() as block:
            block.tensor(body("pe"))
            block.scalar(body("act"))
            block.vector(body("dve"))
            block.gpsimd(body("pool"))
            block.sync(body("sp"))


def _bucket(n):
    n = np.maximum(n, 0)
    nf = np.maximum(n, 1).astype(np.float32)
    large = 16 + (np.log(nf / np.float32(16)) / np.float32(math.log(128 / 16)) * np.float32(16)).astype(np.int32)
    large = np.minimum(large, 31)
    return np.where(n < 16, n, large)


PV_MU = 0
PV_MUWD = 6
PV_MUAD = 7
PV_W0 = 8
PV_A0 = 10
PV_KK = 12
PV_KA = 14
PV_RK = 16
PV_GNG = 18
PV_GNB = 20
PV_SUBLN = 22
PV_LNG = 23
PV_LNB = 39
PV_N = 55


def _consts():
    c = {}
    c["ident"] = np.eye(128, dtype=np.float32).astype(ml_dtypes.bfloat16)
    c["ones_bf"] = np.ones((128, 128), dtype=ml_dtypes.bfloat16)
    c["ones_f"] = np.ones((128, 128), dtype=np.float32)
    bo = np.zeros((128, 128), np.float32)
    bo[:64, :64] = 1.0
    bo[64:, 64:] = 1.0
    c["blockones"] = bo
    s = np.arange(128)[:, None]
    t = np.arange(128)[None, :]
    strict = (s < t).astype(np.float32)
    incl = (s <= t).astype(np.float32)
    m = np.zeros((128, 2, 2, 128), np.float32)
    m[:, :, 0, :] = strict[:, None, :]
    m[:, :, 1, :] = incl[:, None, :]
    c["maskT"] = m.reshape(128, 512)
    c["masklow"] = (s > t).astype(np.float32)
    d = np.arange(TABW) - 127
    oh = np.zeros((33, TABW), np.float32)
    b = _bucket(d)
    for i in range(TABW):
        if d[i] >= 0:
            oh[b[i], i] = 1.0
        else:
            oh[32, i] = NEG
    c["oh"] = oh
    return c


def _prep_inputs(inp):
    f = np.float32
    w_in = np.asarray(inp["w_in"][0], f)
    w_out = np.asarray(inp["w_out"][0], f)
    x = np.asarray(inp["x"], f)
    consts = _consts()
    lam4 = np.stack([inp["lambda_q1"][0], inp["lambda_k1"][0], inp["lambda_q2"][0], inp["lambda_k2"][0]]).astype(f)
    rows = []
    for r in range(4):
        rows += list(range((2 * r) * 128, (2 * r + 2) * 128))
        rows += list(range(1024 + (4 * r) * 64, 1024 + (4 * r + 4) * 64))
    w_out_p = np.ascontiguousarray(w_out[rows])
    gb = np.stack([inp["ln_emb_g"], inp["ln_emb_b"], inp["ln_post_g"][0], inp["ln_post_b"][0]]).astype(f)
    maps = []
    for c in range(8):
        b, hg = c // 4, c % 4
        h0 = 2 * hg
        cols = []
        cols += list(range(h0 * 128, (h0 + 2) * 128))
        cols += list(range(1024 + h0 * 128, 1024 + (h0 + 2) * 128))
        cols += list(range(3072 + h0 * 128, 3072 + (h0 + 2) * 128))
        rb = 4096 + hg * 256
        cols += list(range(rb, rb + 256))
        cols += list(range(rb + 1024, rb + 1024 + 256))
        cols += list(range(rb + 2048, rb + 2048 + 256))
        cols += list(range(7360 + hg * 256, 7360 + hg * 256 + 256))
        cols += list(range(4096 + 3072, 4096 + 3072 + 192))
        cols += list(range(2048 + h0 * 128, 2048 + (h0 + 2) * 128))
        assert len(cols) == NCOL
        wi = np.ascontiguousarray(w_in[:, cols])
        pv = np.zeros((128, PV_N), f)
        mu = inp["rw_mu"][0]
        rs = slice(hg * 256, hg * 256 + 256)
        for p in range(2):
            ps_ = slice(hg * 256 + p * 128, hg * 256 + (p + 1) * 128)
            pv[:, PV_MU + 0 + p] = mu[0:1024][ps_]
            pv[:, PV_MU + 2 + p] = mu[1024:2048][ps_]
            pv[:, PV_MU + 4 + p] = mu[2048:3072][ps_]
            pv[:, PV_W0 + p] = inp["rw_w0"][0][ps_]
            pv[:, PV_A0 + p] = inp["rw_a0"][0][ps_]
            pv[:, PV_KK + p] = inp["rw_k_k"][0][ps_]
            pv[:, PV_KA + p] = inp["rw_k_a"][0][ps_]
            pv[:, PV_RK + p] = inp["rw_r_k"][0].reshape(-1)[ps_]
            pv[:, PV_GNG + p] = inp["rw_gn_g"][0][ps_]
            pv[:, PV_GNB + p] = inp["rw_gn_b"][0][ps_]
        pv[:96, PV_MUWD] = mu[3072:3168]
        pv[:96, PV_MUAD] = mu[3168:3264]
        pv[:, PV_SUBLN] = inp["subln_g"][0]
        pv[:, PV_LNG:PV_LNG + 16] = np.asarray(inp["ln_emb_g"], f).reshape(16, 128).T
        pv[:, PV_LNB:PV_LNB + 16] = np.asarray(inp["ln_emb_b"], f).reshape(16, 128).T
        relb = np.ones((33, 2, 128), f)
        for j in range(2):
            relb[:32, j, :] = np.asarray(inp["rel_bias"], f)[:, h0 + j][:, None]
        lora = np.zeros((96, 2, 256), f)
        lora[:, 0, :] = inp["rw_w_up"][0][:, rs]
        lora[:, 1, :] = inp["rw_a_up"][0][:, rs]
        m = {
            "x": np.ascontiguousarray(x[b]),
            "meta": np.asarray(inp["meta_tokens"], f),
            "w_in": wi,
            "w_out": w_out_p,
            "pvec": pv,
            "relb": relb,
            "lora": lora,
            "lam4": lam4,
            "gb": gb,
        }
        m.update(consts)
        maps.append(m)
    return maps


def build_nc():
    nc = bass.Bass("TRN2", target_bir_lowering=False)

    def din(name, shape, dt=F32):
        return nc.dram_tensor(name, list(shape), dt, kind="ExternalInput").ap()

    x_d = din("x", [SEQ, D])
    meta_d = din("meta", [16, D])
    win_d = din("w_in", [D, NCOL])
    wout_d = din("w_out", [D, D])
    pvec_d = din("pvec", [128, PV_N])
    relb_d = din("relb", [33, 2, 128])
    lora_d = din("lora", [96, 2, 256])
    lam4_d = din("lam4", [4, 64])
    gb_d = din("gb", [4, D])
    ident_d = din("ident", [128, 128], BF16)
    onesbf_d = din("ones_bf", [128, 128], BF16)
    onesf_d = din("ones_f", [128, 128])
    bo_d = din("blockones", [128, 128])
    maskT_d = din("maskT", [128, 512])
    masklow_d = din("masklow", [128, 128])
    oh_d = din("oh", [33, TABW])
    xq_d = din("xq", [1024, D])
    idx_d = din("idx", [128, 4], mybir.dt.int32)
    out_d = nc.dram_tensor("out", [1024, D], F32, kind="ExternalOutput").ap()
    if DEBUG:
        dbg_d = nc.dram_tensor("dbg", [512, SEQ], BF16, kind="ExternalOutput").ap()
    agin = [nc.dram_tensor("agin%d" % c, [128, SEQ], BF16) for c in range(4)]
    agout = [nc.dram_tensor("agout%d" % c, [512, SEQ], BF16) for c in range(4)]
    tab_d = [nc.dram_tensor("tab%d" % j, [128, TABW], F32) for j in range(2)]

    with ExitStack() as es:
        S = Sched(nc, es)

        def sbt(name, shape, dt=F32):
            return es.enter_context(nc.sbuf_tensor("s_" + name, list(shape), dt))

        W = sbt("W", [128, 16, NCOL], BF16)
        tW = S.tile("W")
        xb = [sbt("xb%d" % i, [128, NCOL]) for i in range(2)]
        txb = [S.tile("xb%d" % i) for i in range(2)]
        pvec = sbt("pvec", [128, PV_N])
        pder = sbt("pder", [128, 8])
        ident = sbt("ident", [128, 128], BF16)
        ones_bf = sbt("ones_bf", [128, 128], BF16)
        ones_f = sbt("ones_f", [128, 128])
        blockones = sbt("blockones", [128, 128])
        maskT = sbt("maskT", [128, 512])
        masklow = sbt("masklow", [128, 128])
        lora32 = sbt("lora32", [96, 2, 256])
        lora = sbt("lora", [96, 2, 256], BF16)
        bext = [sbt("bext%d" % j, [128, 384]) for j in range(2)]
        bcol = sbt("bcol", [128, 4])
        tC = S.tile("consts")
        arena_w = 28100
        arena = sbt("arena", [128, arena_w])
        apos = [0]

        def carve(nbytes, dt, shape_str=None, **kw):
            n32 = (nbytes + 3) // 4
            a = apos[0]
            apos[0] += n32
            assert apos[0] <= arena_w, apos[0]
            ap = arena[:, a:a + n32]
            if dt != F32:
                ap = ap.bitcast(dt)
            if shape_str:
                ap = ap.rearrange(shape_str, **kw)
            return ap

        ps = [es.enter_context(nc.psum_tensor("ps%d" % i, [128, 512], F32)) for i in range(7)]
        tps = [S.tile("ps%d" % i) for i in range(7)]
        psb = es.enter_context(nc.psum_tensor("psb", [128, 1024], BF16))
        tpsb = S.tile("psb")

        KT = carve(2 * P_TOK * 2, BF16, "p (h n) -> p h n", h=2)
        tKT = S.tile("KT")
        Vt = carve(NB * 256 * 2, BF16, "p (b n) -> p b n", b=NB)
        tV = S.tile("V")
        xn_off = apos[0]
        xn = carve(D * 2, BF16)
        txn = S.tile("xn")
        hT_off = apos[0]
        hT = carve(16 * 256 * 2, BF16, "p (c n) -> p c n", c=16)
        thT = S.tile("hT")
        wk2 = [arena[:, hT_off + i * 256:hT_off + (i + 1) * 256] for i in range(8)] + \
              [arena[:, xn_off + i * 256:xn_off + (i + 1) * 256] for i in range(4)]
        twk2 = [T("wk2_%d" % i, parent=thT) for i in range(8)] + [T("wk2_%d" % (8 + i), parent=txn) for i in range(4)]
        QT = carve(2 * 256 * 2, BF16, "p (h n) -> p h n", h=2)
        tQT = S.tile("QT")
        gate_a = carve(2 * 256 * 2, BF16, "p (h n) -> p h n", h=2)
        tga = S.tile("gate_a")
        gate_r = carve(2 * 256 * 2, BF16, "p (h n) -> p h n", h=2)
        tgr = S.tile("gate_r")
        zb = [carve(257 * 4, F32) for _ in range(8)]
        tzb = [S.tile("z%d" % i) for i in range(8)]
        NWK = 12
        wk = [carve(256 * 4, F32) for _ in range(NWK)]
        twk = [S.tile("wk%d" % i) for i in range(NWK)]
        bonus = [carve(256 * 4, F32) for _ in range(2)]
        tbonus = [S.tile("bonus%d" % i) for i in range(2)]
        yT = [wk[1], wk[5]]
        tyT = [twk[1], twk[5]]
        twb = carve(256 * 2, BF16)
        adb = carve(256 * 2, BF16)
        ttwb = S.tile("twb")
        tadb = S.tile("adb")
        arT = [carve(2 * 256 * 2, BF16, "p (a n) -> p a n", a=2) for _ in range(2)]
        tarT = [S.tile("arT%d" % i) for i in range(2)]
        ktl = [carve(256 * 2, BF16) for _ in range(2)]
        btl = [carve(256 * 2, BF16) for _ in range(2)]
        tktl = [S.tile("ktl%d" % i) for i in range(2)]
        tbtl = [S.tile("btl%d" % i) for i in range(2)]
        fm3 = [carve(256 * 2, BF16) for _ in range(3)]
        tfm3 = [S.tile("fm3_%d" % i) for i in range(3)]
        fm3b = [carve(256 * 2, BF16) for _ in range(3)]
        tfm3b = [S.tile("fm3b_%d" % i) for i in range(3)]
        tnb2 = [S.tile("nbias%d" % i) for i in range(2)]
        tok3 = [carve(2 * 256 * 2, BF16, "p (c n) -> p c n", c=2) for _ in range(3)]
        ttok3 = [S.tile("tok3_%d" % i) for i in range(3)]
        WC = carve(4 * 4, F32)
        tWC = S.tile("WC")
        nbias = carve(4 * 4, F32)
        tnbias = S.tile("nbias")
        AT = [carve(4 * 128 * 2, BF16, "p (a n) -> p a n", a=4) for _ in range(4)]
        tAT = [S.tile("AT%d" % i) for i in range(4)]
        Mc = carve(4 * 128 * 2, BF16, "p (h n) -> p h n", h=4)
        Mtc = carve(4 * 128 * 2, BF16, "p (h n) -> p h n", h=4)
        Ptc = carve(4 * 128 * 2, BF16, "p (h n) -> p h n", h=4)
        tMc = S.tile("Mc")
        tMtc = S.tile("Mtc")
        tPtc = S.tile("Ptc")
        btlm = [[carve(256 * 2, BF16) for _ in range(2)] for _ in range(2)]
        ktlm = [[carve(256 * 2, BF16) for _ in range(2)] for _ in range(2)]
        tbtlm = [[S.tile("btlm%d%d" % (i, j)) for j in range(2)] for i in range(2)]
        tktlm = [[S.tile("ktlm%d%d" % (i, j)) for j in range(2)] for i in range(2)]
        Vz = carve(2 * 4 * 128 * 2, BF16, "p (c h n) -> p c h n", c=2, h=4)
        tVz = S.tile("Vz")
        Uz = carve(4 * 128 * 2, BF16, "p (h n) -> p h n", h=4)
        tUz = S.tile("Uz")
        Xsb = carve(256 * 2, BF16, "p (h n) -> p h n", h=4)
        Usb = carve(256 * 2, BF16, "p (h n) -> p h n", h=4)
        tXsb = S.tile("Xsb")
        tUsb = S.tile("Usb")
        S32 = carve(2 * 128 * 4, F32, "p (a n) -> p a n", a=2)
        Sbf = carve(2 * 128 * 2, BF16, "p (a n) -> p a n", a=2)
        tS32 = S.tile("S32")
        tSbf = S.tile("Sbf")
        Ssb = [wk[0], wk[2]]
        tSsb = [twk[0], twk[2]]
        Eb = [carve(2 * 256 * 2, BF16, "p (m n) -> p m n", m=2) for _ in range(2)]
        tEb = [S.tile("Eb%d" % i) for i in range(2)]
        oT = [carve(4 * 256 * 2, BF16, "p (c n) -> p c n", c=4) for _ in range(2)]
        toT = [S.tile("oT%d" % i) for i in range(2)]
        stats = carve(4 * 6 * 4, F32, "p (a n) -> p a n", a=4)
        mv = carve(8 * 4, F32)
        tst = S.tile("stats")
        lamw = carve(4 * 64 * 4, F32, "p (a n) -> p a n", a=4)
        tabsb = carve(TABW * 4, F32)
        ttab = S.tile("tabsb")
        main_end = apos[0]
        print("arena main words", main_end, "of", arena_w)

        tagin = [S.tile("agin%d" % c) for c in range(4)]
        tagout = [S.tile("agout%d" % c) for c in range(4)]
        ttabd = [S.tile("tabd%d" % j) for j in range(2)]

        try:
            S.groups.add("consts")

            def cload(dst, src):
                S.dma(lambda e: e.dma_start(out=dst, in_=src), writes=[tC], key="consts")

            cload(pvec[:], pvec_d)
            cload(ident[:], ident_d)
            cload(ones_bf[:], onesbf_d)
            cload(ones_f[:], onesf_d)
            cload(blockones[:], bo_d)
            cload(maskT[:], maskT_d)
            cload(masklow[:], masklow_d)
            cload(lora32[:], lora_d)
            relb = carve(2 * 128 * 4, F32, "p (a n) -> p a n", a=2)
            ohsb = carve(TABW * 4, F32)
            cload(relb[0:33], relb_d)
            cload(ohsb[0:33], oh_d)
            lam_src = bass.AP(tensor=lam4_d.tensor, offset=0, ap=[[0, 128], [64, 4], [1, 64]])
            cload(lamw, lam_src)

            S.op("dve", lambda e: e.tensor_copy(out=lora[:], in_=lora32[:]), reads=[tC], writes=[tC])
            S.op("dve", lambda e: e.tensor_scalar(out=pder[:, 0:2], in0=pvec[:, PV_KA:PV_KA + 2], scalar1=-1.0, scalar2=1.0,
                                                  op0=ALU.mult, op1=ALU.add), reads=[tC], writes=[tC])
            S.op("dve", lambda e: e.tensor_scalar(out=pder[:, 2:3], in0=pvec[:, PV_SUBLN:PV_SUBLN + 1], scalar1=0.8, scalar2=None,
                                                  op0=ALU.mult), reads=[tC], writes=[tC])
            S.op("dve", lambda e: e.tensor_tensor(out=lamw[:, 0, :], in0=lamw[:, 0, :], in1=lamw[:, 1, :], op=ALU.mult),
                 reads=[tC], writes=[tC])
            S.op("dve", lambda e: e.tensor_tensor(out=lamw[:, 2, :], in0=lamw[:, 2, :], in1=lamw[:, 3, :], op=ALU.mult),
                 reads=[tC], writes=[tC])
            S.op("dve", lambda e: e.reduce_sum(out=pder[:, 4:5], in_=lamw[:, 0, :], axis=mybir.AxisListType.X),
                 reads=[tC], writes=[tC])
            S.op("dve", lambda e: e.reduce_sum(out=pder[:, 5:6], in_=lamw[:, 2, :], axis=mybir.AxisListType.X),
                 reads=[tC], writes=[tC])
            S.op("act", lambda e: e.activation(out=pder[:, 4:6], in_=pder[:, 4:6], func=AF.Exp), reads=[tC], writes=[tC])
            S.op("dve", lambda e: e.tensor_tensor(out=pder[:, 3:4], in0=pder[:, 5:6], in1=pder[:, 4:5], op=ALU.subtract),
                 reads=[tC], writes=[tC])
            S.op("dve", lambda e: e.tensor_scalar(out=pder[:, 3:4], in0=pder[:, 3:4], scalar1=-0.2, scalar2=None, op0=ALU.add),
                 reads=[tC], writes=[tC])
            S.op("pool", lambda e: e.memset(S32.rearrange("p a n -> p (a n)"), 0.0), writes=[tS32])
            S.op("pool", lambda e: e.memset(Sbf.rearrange("p a n -> p (a n)"), 0.0), writes=[tSbf])
            S.op("pool", lambda e: e.memset(Vz.rearrange("p c h n -> p (c h n)"), 0.0), writes=[tVz])
            S.op("pool", lambda e: e.memset(Uz.rearrange("p h n -> p (h n)"), 0.0), writes=[tUz])
            for i in range(8):
                S.op("pool", lambda e, i=i: e.memset(zb[i], 0.0), writes=[tzb[i]])

            for j in range(2):
                S.op("pe", lambda e, j=j: e.matmul(ps[0][:, 0:256], lhsT=relb[0:33, j, :], rhs=ohsb[0:33, 0:256], start=True, stop=True),
                     reads=[tC], writes=[tps[0]])
                S.op("pe", lambda e, j=j: e.matmul(ps[1][:, 0:255], lhsT=relb[0:33, j, :], rhs=ohsb[0:33, 256:511], start=True, stop=True),
                     reads=[tC], writes=[tps[1]])
                S.op("dve", lambda e: e.tensor_copy(out=tabsb[:, 0:256], in_=ps[0][:, 0:256]), reads=[tps[0]], writes=[ttab])
                S.op("dve", lambda e: e.tensor_copy(out=tabsb[:, 256:511], in_=ps[1][:, 0:255]), reads=[tps[1]], writes=[ttab])
                S.op("dve", lambda e, j=j: e.tensor_copy(out=bcol[:, j:j + 1], in_=tabsb[:, 510:511]), reads=[ttab], writes=[tC])
                S.op("dve", lambda e, j=j: e.tensor_copy(out=bcol[:, 2 + j:3 + j], in_=tabsb[:, 510:511]), reads=[ttab], writes=[tC])
                S.op("dve", lambda e, j=j: e.memset(bcol[0:112, 2 + j:3 + j], NEG), reads=[], writes=[tC])
                S.dma(lambda e, j=j: e.dma_start(out=tab_d[j].ap(), in_=tabsb), reads=[ttab], writes=[ttabd[j]], key="tabst")
                src = bass.AP(tensor=tab_d[j].ap().tensor, offset=127, ap=[[TABW - 1, 128], [1, 384]])
                S.dma(lambda e, j=j, src=src: e.dma_start(out=bext[j][:], in_=src), reads=[ttabd[j]], writes=[tC], key="bext%d" % j)

            for kc in range(16):
                b = kc % 2
                S.dma(lambda e, kc=kc, b=b: e.dma_start(out=xb[b][:], in_=win_d[kc * 128:(kc + 1) * 128, :]),
                      writes=[txb[b]], key="xb%d" % b)
                eng = ("dve", "pool", "act")[kc % 3]
                if eng == "act":
                    S.op("act", lambda e, kc=kc, b=b: e.activation(out=W[:, kc, :], in_=xb[b][:], func=AF.Copy),
                         reads=[txb[b]], writes=[tW])
                else:
                    S.op(eng, lambda e, kc=kc, b=b: e.tensor_copy(out=W[:, kc, :], in_=xb[b][:]), reads=[txb[b]], writes=[tW])

            chk('setup')
            rr = {"ps": 0, "ev": 0, "xb": 0, "oT": 0}

            def next_ps(n=4):
                i = rr["ps"] % n
                rr["ps"] += 1
                return i

            def evac_copy(out, in_, reads, writes, bf=False):
                k = 1
                if k == 0:
                    S.op("dve", lambda e: e.tensor_scalar(out=out, in0=in_, scalar1=1.0, scalar2=None, op0=ALU.mult), reads=reads, writes=writes)
                else:
                    S.op("act", lambda e: e.activation(out=out, in_=in_, func=AF.Copy), reads=reads, writes=writes)

            def tt(eng, out, a, b, op, reads, writes):
                S.op(eng, lambda e: e.tensor_tensor(out=out, in0=a, in1=b, op=op), reads=reads, writes=writes)

            def ts(eng, out, a, s1, s2, op0, op1, reads, writes):
                if s2 is None:
                    S.op(eng, lambda e: e.tensor_scalar(out=out, in0=a, scalar1=s1, scalar2=None, op0=op0), reads=reads, writes=writes)
                else:
                    S.op(eng, lambda e: e.tensor_scalar(out=out, in0=a, scalar1=s1, scalar2=s2, op0=op0, op1=op1),
                         reads=reads, writes=writes)

            def stt(out, a, sc, b, op0, op1, reads, writes):
                S.op("dve", lambda e: e.scalar_tensor_tensor(out=out, in0=a, scalar=sc, in1=b, op0=op0, op1=op1),
                     reads=reads, writes=writes)

            def act(out, in_, func, reads, writes, bias=0.0, scale=1.0):
                S.op("act", lambda e: e.activation(out=out, in_=in_, func=func, bias=bias, scale=scale), reads=reads, writes=writes)

            def rsqrt_inplace(t, tt_, n, scale, eps):
                act(t[:, 0:n], t[:, 0:n], AF.Ln, [tt_], [tt_], bias=eps_ap(eps), scale=scale)
                act(t[:, 0:n], t[:, 0:n], AF.Exp, [tt_], [tt_], scale=-0.5)

            epsc = {}

            def eps_ap(v):
                if v == 0.0:
                    return 0.0
                if v not in epsc:
                    col = 6 + len(epsc)
                    S.op("pool", lambda e, col=col, v=v: e.memset(pder[:, col:col + 1], float(v)), writes=[tC])
                    epsc[v] = pder[:, col:col + 1]
                return epsc[v]

            eps_ap(1e-5)
            eps_ap(64e-5)

            def layer_norm_block(src_ap, is_meta, blk_slot):
                b = rr["xb"] % 2
                rr["xb"] += 1
                xt = xb[b]
                if is_meta:
                    S.op("pool", lambda e: e.memset(xt[:, 0:D], 0.0), writes=[txb[b]])
                    S.dma(lambda e: e.dma_start(out=xt[112:128, 0:D], in_=meta_d), writes=[txb[b]], key="xb%d" % b)
                else:
                    S.dma(lambda e: e.dma_start(out=xt[:, 0:D], in_=src_ap), writes=[txb[b]], key="xb%d" % b)
                for q in range(4):
                    S.op("dve", lambda e, q=q: e.bn_stats(out=stats[:, q, :], in_=xt[:, q * 512:(q + 1) * 512]),
                         reads=[txb[b]], writes=[tst])
                S.op("dve", lambda e: e.bn_aggr(out=mv[:, 0:2], in_=stats.rearrange("p a n -> p (a n)")), reads=[tst], writes=[tst])
                act(mv[:, 2:3], mv[:, 1:2], AF.Ln, [tst], [tst], bias=eps_ap(1e-5))
                act(mv[:, 2:3], mv[:, 2:3], AF.Exp, [tst], [tst], scale=-0.5)
                stt(mv[:, 3:4], mv[:, 0:1], -1.0, mv[:, 2:3], ALU.mult, ALU.mult, [tst], [tst])
                S.op("act", lambda e: e.activation(out=xn, in_=xt[:, 0:D], func=AF.Identity, bias=mv[:, 3:4], scale=mv[:, 2:3]),
                     reads=[txb[b], tst], writes=[txn])
                for half in range(2):
                    for dc in range(8):
                        c = half * 8 + dc
                        S.op("pe", lambda e, c=c, dc=dc: e.transpose(psb[:, dc * 128:(dc + 1) * 128], xn[:, c * 128:(c + 1) * 128], ident[:]),
                             reads=[txn, tC], writes=[tpsb])
                    for dc in range(8):
                        c = half * 8 + dc
                        dst = hT[:, c, blk_slot * 128:(blk_slot + 1) * 128]
                        src = psb[:, dc * 128:(dc + 1) * 128]
                        if dc % 2 == 0:
                            ts("dve", dst, src, pvec[:, PV_LNG + c:PV_LNG + c + 1], pvec[:, PV_LNB + c:PV_LNB + c + 1],
                               ALU.mult, ALU.add, [tpsb, tC], [thT])
                        else:
                            S.op("act", lambda e, dst=dst, src=src, c=c: e.activation(
                                out=dst, in_=src, func=AF.Identity, bias=pvec[:, PV_LNB + c:PV_LNB + c + 1],
                                scale=pvec[:, PV_LNG + c:PV_LNG + c + 1]), reads=[tpsb, tC], writes=[thT])
                if is_meta:
                    S.op("pool", lambda e: e.memset(hT[:, :, 0:112], 0.0), writes=[thT])

            def inproj_fm(ct_off, m, nt, sink):
                i = next_ps()
                for kc in range(16):
                    S.op("pe", lambda e, kc=kc, i=i: e.matmul(ps[i][0:m, 0:nt], lhsT=W[:, kc, ct_off:ct_off + m], rhs=hT[:, kc, 0:nt],
                                                            start=(kc == 0), stop=(kc == 15)), reads=[tW, thT], writes=[tps[i]])
                sink(ps[i][0:m, 0:nt], tps[i])

            tiles = [(-1, 128)] + [(i, 256) for i in range(16)]
            ln_done = set()
            for (ti, nt) in tiles:
                nblk = nt // 128
                blk0 = 0 if ti < 0 else 1 + 2 * ti
                pos0 = blk0 * 128
                for j in (range(nblk) if ti not in ln_done else []):
                    if ti < 0:
                        layer_norm_block(None, True, 0)
                    else:
                        r0 = ti * 256 + j * 128
                        layer_norm_block(x_d[r0:r0 + 128, :], False, j)
                if ti >= 0:
                    for h in range(2):
                        inproj_fm(h * 128, 128, nt, lambda p, tp, h=h: evac_copy(QT[:, h, 0:nt], p, [tp], [tQT]))
                for h in range(2):
                    inproj_fm(256 + h * 128, 128, nt, lambda p, tp, h=h: evac_copy(KT[:, h, pos0:pos0 + nt], p, [tp], [tKT]))
                if ti >= 0:
                    for h in range(2):
                        inproj_fm(512 + h * 128, 128, nt, lambda p, tp, h=h: act(gate_a[:, h, 0:nt], p, AF.Silu, [tp], [tga]))
                    for p_ in range(2):
                        inproj_fm(1536 + p_ * 128, 128, nt, lambda p, tp, p_=p_: act(gate_r[:, p_, 0:nt], p, AF.Silu, [tp], [tgr]))
                for zi in range(6):
                    inproj_fm(768 + zi * 128, 128, nt, lambda p, tp, zi=zi: evac_copy(zb[zi][:, 1:1 + nt], p, [tp], [tzb[zi]]))
                for zi in range(2):
                    inproj_fm(1792 + zi * 96, 96, nt, lambda p, tp, zi=zi: evac_copy(zb[6 + zi][0:96, 1:1 + nt], p, [tp], [tzb[6 + zi]]))
                for j in range(nblk):
                    i = next_ps()
                    for kc in range(16):
                        S.op("pe", lambda e, kc=kc, i=i, j=j: e.matmul(ps[i][:, 0:256], lhsT=hT[:, kc, j * 128:(j + 1) * 128],
                                                                     rhs=W[:, kc, 1984:2240], start=(kc == 0), stop=(kc == 15)),
                             reads=[tW, thT], writes=[tps[i]])
                    evac_copy(Vt[:, blk0 + j, :], ps[i][:, 0:256], [tps[i]], [tV])

                chk('inproj%d' % ti)
                r32, k32, v32, t1, t2, sg, aic, kkn, k2, bvec, cs, t3 = wk
                tr32, tk32, tv32, tt1, tt2, tsg, taic, tkkn, tk2, tbvec, tcs, tt3 = twk
                n = nt

                def shift(zi, mucol, out, tout, rows=128):
                    z = zb[zi]
                    tz = tzb[zi]
                    tt("pool", t3[0:rows, 0:n], z[0:rows, 0:n], z[0:rows, 1:1 + n], ALU.subtract, [tz], [tt3])
                    stt(out[0:rows, 0:n], t3[0:rows, 0:n], pvec[0:rows, mucol:mucol + 1], z[0:rows, 1:1 + n], ALU.mult, ALU.add,
                        [tt3, tz, tC], [tout])
                    S.op("pool", lambda e: e.tensor_copy(out=z[0:rows, 0:1], in_=z[0:rows, n:n + 1]), reads=[tz], writes=[tz])

                shift(6, PV_MUWD, t1, tt1, rows=96)
                act(twb[0:96, 0:n], t1[0:96, 0:n], AF.Tanh, [tt1], [ttwb])
                shift(7, PV_MUAD, t2, tt2, rows=96)
                S.op("dve", lambda e: e.tensor_scalar(out=adb[0:96, 0:n], in0=t2[0:96, 0:n], scalar1=1.0, scalar2=None, op0=ALU.mult), reads=[tt2], writes=[tadb])

                import os as _os
                pe_last = ('E', 'pe', len(S.ops['pe']) - 1)
                for t_ in twk2:
                    t_.r[('E', 'pe')] = pe_last

                def shift2(zi, mucol, out, tout, tmp, ttmp):
                    z = zb[zi]
                    tz = tzb[zi]
                    tt("pool", tmp[:, 0:n], z[:, 0:n], z[:, 1:1 + n], ALU.subtract, [tz], [ttmp])
                    stt(out[:, 0:n], tmp[:, 0:n], pvec[:, mucol:mucol + 1], z[:, 1:1 + n], ALU.mult, ALU.add, [ttmp, tz, tC], [tout])
                    S.op("pool", lambda e: e.tensor_copy(out=z[:, 0:1], in_=z[:, n:n + 1]), reads=[tz], writes=[tz])

                def pre_pair(p, wks, twks, f3, tf3, pb):
                    r32, k32, v32, t1, t2, sg, aic, kkn, k2, bvec, cs, t3 = wks
                    tr32, tk32, tv32, tt1, tt2, tsg, taic, tkkn, tk2, tbvec, tcs, tt3 = twks
                    b0, b1, b2 = pb
                    tnb = tnb2[p]
                    shift2(0 + p, PV_MU + 0 + p, r32, tr32, t3, tt3)
                    yield
                    shift2(2 + p, PV_MU + 2 + p, k32, tk32, t3, tt3)
                    yield
                    shift2(4 + p, PV_MU + 4 + p, v32, tv32, t3, tt3)
                    yield
                    S.op("pe", lambda e: e.matmul(ps[b0][:, 0:n], lhsT=lora[0:96, 0, p * 128:(p + 1) * 128], rhs=twb[0:96, 0:n],
                                                  start=True, stop=True), reads=[tC, ttwb], writes=[tps[b0]])
                    act(sg[:, 0:n], ps[b0][:, 0:n], AF.Sigmoid, [tps[b0], tC], [tsg], bias=pvec[:, PV_W0 + p:PV_W0 + p + 1])
                    yield
                    S.op("pe", lambda e: e.matmul(ps[b1][:, 0:n], lhsT=lora[0:96, 1, p * 128:(p + 1) * 128], rhs=adb[0:96, 0:n],
                                                  start=True, stop=True), reads=[tC, tadb], writes=[tps[b1]])
                    act(aic[:, 0:n], ps[b1][:, 0:n], AF.Sigmoid, [tps[b1], tC], [taic], bias=pvec[:, PV_A0 + p:PV_A0 + p + 1])
                    yield
                    ts("dve", t1[:, 0:n], k32[:, 0:n], pvec[:, PV_KK + p:PV_KK + p + 1], None, ALU.mult, None, [tk32, tC], [tt1])
                    tt("pool", t2[:, 0:n], t1[:, 0:n], t1[:, 0:n], ALU.mult, [tt1], [tt2])
                    yield
                    S.op("pe", lambda e: e.matmul(ps[b2][:, 0:n], lhsT=blockones[:], rhs=t2[:, 0:n], start=True, stop=True),
                         reads=[tC, tt2], writes=[tps[b2]])
                    ts("dve", t2[:, 0:n], ps[b2][:, 0:n], 1e-24, None, ALU.max, None, [tps[b2]], [tt2])
                    yield
                    act(t2[:, 0:n], t2[:, 0:n], AF.Ln, [tt2], [tt2])
                    yield
                    act(t2[:, 0:n], t2[:, 0:n], AF.Exp, [tt2], [tt2], scale=-0.5)
                    yield
                    tt("dve", kkn[:, 0:n], t1[:, 0:n], t2[:, 0:n], ALU.mult, [tt1, tt2], [tkkn])
                    yield
                    ts("dve", t1[:, 0:n], aic[:, 0:n], pvec[:, PV_KA + p:PV_KA + p + 1], pder[:, p:p + 1], ALU.mult, ALU.add,
                       [taic, tC], [tt1])
                    yield
                    tt("pool", k2[:, 0:n], k32[:, 0:n], t1[:, 0:n], ALU.mult, [tk32, tt1], [tk2])
                    tt("pool", bvec[:, 0:n], kkn[:, 0:n], aic[:, 0:n], ALU.mult, [tkkn, taic], [tbvec])
                    yield
                    stt(t1[:, 0:n], r32[:, 0:n], pvec[:, PV_RK + p:PV_RK + p + 1], k2[:, 0:n], ALU.mult, ALU.mult, [tr32, tk2, tC], [tt1])
                    yield
                    S.op("pe", lambda e: e.matmul(ps[b0][:, 0:n], lhsT=blockones[:], rhs=t1[:, 0:n], start=True, stop=True),
                         reads=[tC, tt1], writes=[tps[b0]])
                    tt("dve", bonus[p][:, 0:n], ps[b0][:, 0:n], v32[:, 0:n], ALU.mult, [tps[b0], tv32], [tbonus[p]])
                    yield
                    for c in range(nblk):
                        S.op("dve", lambda e: e.tensor_tensor_scan(out=cs[:, c * 128:(c + 1) * 128], data0=ones_f[:, 0:128],
                                                                   data1=sg[:, c * 128:(c + 1) * 128], initial=0.0,
                                                                   op0=ALU.mult, op1=ALU.add), reads=[tsg, tC], writes=[tcs])
                    yield
                    act(t1[:, 0:n], cs[:, 0:n], AF.Exp, [tcs], [tt1], scale=-C0)
                    yield
                    tt("dve", arT[p][:, 1, 0:n], r32[:, 0:n], t1[:, 0:n], ALU.mult, [tr32, tt1], [tarT[p]])
                    for c in range(nblk):
                        S.op("pool", lambda e: e.tensor_copy(out=WC[:, p * 2 + c:p * 2 + c + 1], in_=t1[:, c * 128 + 127:c * 128 + 128]),
                             reads=[tt1], writes=[tWC])
                        ts("dve", nbias[:, p * 2 + c:p * 2 + c + 1], cs[:, c * 128 + 127:c * 128 + 128], -C0, None, ALU.mult, None, [tcs], [tnb])
                    yield
                    tt("pool", t2[:, 0:n], cs[:, 0:n], sg[:, 0:n], ALU.subtract, [tcs, tsg], [tt2])
                    yield
                    act(t2[:, 0:n], t2[:, 0:n], AF.Exp, [tt2], [tt2], scale=-C0)
                    yield
                    stt(arT[p][:, 0, 0:n], kkn[:, 0:n], -1.0, t2[:, 0:n], ALU.mult, ALU.mult, [tkkn, tt2], [tarT[p]])
                    act(t1[:, 0:n], cs[:, 0:n], AF.Exp, [tcs], [tt1], scale=C0)
                    yield
                    tt("dve", ktl[p][:, 0:n], k2[:, 0:n], t1[:, 0:n], ALU.mult, [tk2, tt1], [tktl[p]])
                    tt("pool", btl[p][:, 0:n], bvec[:, 0:n], t1[:, 0:n], ALU.mult, [tbvec, tt1], [tbtl[p]])
                    yield
                    for hh in range(2):
                        hm = blockones[:, hh * 64:hh * 64 + 1]
                        ts("dve" if hh == 0 else "pool", btlm[p][hh][:, 0:n], btl[p][:, 0:n], hm, None, ALU.mult, None, [tbtl[p], tC], [tbtlm[p][hh]])
                        ts("pool" if hh == 0 else "dve", ktlm[p][hh][:, 0:n], ktl[p][:, 0:n], hm, None, ALU.mult, None, [tktl[p], tC], [tktlm[p][hh]])
                    yield
                    for c in range(nblk):
                        S.op("act", lambda e: e.activation(out=t2[:, c * 128:(c + 1) * 128], in_=cs[:, c * 128:(c + 1) * 128],
                                                           func=AF.Exp, bias=nbias[:, p * 2 + c:p * 2 + c + 1], scale=C0),
                             reads=[tcs, tnb], writes=[tt2])
                    yield
                    tt("dve", f3[0][:, 0:n], k2[:, 0:n], t2[:, 0:n], ALU.mult, [tk2, tt2], [tf3[0]])
                    tt("pool", f3[1][:, 0:n], bvec[:, 0:n], t2[:, 0:n], ALU.mult, [tbvec, tt2], [tf3[1]])
                    S.op("act", lambda e: e.activation(out=f3[2][:, 0:n], in_=v32[:, 0:n], func=AF.Copy), reads=[tv32], writes=[tf3[2]])
                    yield
                    for c in range(nblk):
                        for q in range(3):
                            S.op("pe", lambda e: e.matmul(ps[pb[q]][:, 0:128], lhsT=f3[q][:, c * 128:(c + 1) * 128], rhs=ident[:],
                                                          start=True, stop=True), reads=[tf3[q], tC], writes=[tps[pb[q]]])
                        yield
                        for q in range(3):
                            evac_copy(tok3[q][:, c, p * 128:(p + 1) * 128], ps[pb[q]][:, 0:128], [tps[pb[q]]], [ttok3[q]])
                        for hh in range(2):
                            S.op('act', lambda e: e.activation(out=Vz[:, c, 2 * p + hh, hh * 64:(hh + 1) * 64], in_=ps[b2][:, hh * 64:(hh + 1) * 64], func=AF.Copy),
                                 reads=[tps[b2]], writes=[tVz])
                        yield

                gens = [pre_pair(0, wk, twk, fm3, tfm3, (4, 5, 6)), pre_pair(1, wk2, twk2, fm3b, tfm3b, (1, 2, 3))]
                while gens:
                    for g_ in list(gens):
                        try:
                            next(g_)
                        except StopIteration:
                            gens.remove(g_)
                chk('rwpre%d' % ti)
                Khat, Bhat, Vtok = tok3
                tKhat, tBhat, tVtok = ttok3
                ATs = [AT, [wk2[h_].bitcast(BF16).rearrange("p (a n) -> p a n", a=4) for h_ in range(4)]]
                tATs = [tAT, [twk2[h_] for h_ in range(4)]]
                Mcs = [Mc, wk2[4].bitcast(BF16).rearrange("p (h n) -> p h n", h=4)]
                Mtcs = [Mtc, wk2[5].bitcast(BF16).rearrange("p (h n) -> p h n", h=4)]
                Ptcs = [Ptc, wk2[6].bitcast(BF16).rearrange("p (h n) -> p h n", h=4)]
                tMcs, tMtcs, tPtcs = [tMc, twk2[4]], [tMtc, twk2[5]], [tPtc, twk2[6]]
                lvb = [(3, 4, 5), (0, 1, 6)]
                for c in range(nblk):
                    cs_ = slice(c * 128, (c + 1) * 128)
                    ATc, tATc = ATs[c], tATs[c]
                    for h in range(4):
                        p, hh = h // 2, h % 2
                        i = h % 2
                        S.op("pe", lambda e: e.matmul(ps[i][:, 0:256], lhsT=btlm[p][hh][:, cs_], rhs=arT[p][:, :, cs_], start=True, stop=True),
                             reads=[tbtlm[p][hh], tarT[p]], writes=[tps[i]])
                        S.op("pe", lambda e: e.matmul(ps[i][:, 256:512], lhsT=ktlm[p][hh][:, cs_], rhs=arT[p][:, :, cs_], start=True, stop=True),
                             reads=[tktlm[p][hh], tarT[p]], writes=[tps[i]])
                        tt("dve", ATc[h].rearrange("p a n -> p (a n)"), ps[i][:, :], maskT[:], ALU.mult, [tps[i], tC], [tATc[h]])
                        S.op("pe", lambda e: e.matmul(ps[2][:, h * 128:(h + 1) * 128], lhsT=arT[p][:, 0, cs_], rhs=btlm[p][hh][:, cs_], start=True, stop=True),
                             reads=[tbtlm[p][hh], tarT[p]], writes=[tps[2]])
                    for h in range(4):
                        tt("dve", Mcs[c][:, h, :], ps[2][:, h * 128:(h + 1) * 128], masklow[:], ALU.mult, [tps[2], tC], [tMcs[c]])
                        S.op("pool", lambda e: e.tensor_copy(out=Mtcs[c][:, h, :], in_=ATc[h][:, 0, :]), reads=[tATc[h]], writes=[tMtcs[c]])
                        tt("pool", Ptcs[c][:, h, :], ATc[h][:, 0, :], ident[:], ALU.add, [tATc[h], tC], [tPtcs[c]])
                for lvl in range(7):
                    for c in range(nblk):
                        bP, bM, bMt = lvb[c]
                        Mc_, Mtc_, Ptc_ = Mcs[c], Mtcs[c], Ptcs[c]
                        tMc_, tMtc_, tPtc_ = tMcs[c], tMtcs[c], tPtcs[c]
                        if lvl >= 1:
                            for h in range(4):
                                S.op("pe", lambda e: e.matmul(ps[bP][:, h * 128:(h + 1) * 128], lhsT=Mc_[:, h, :], rhs=Ptc_[:, h, :], start=True, stop=True),
                                     reads=[tMc_, tPtc_], writes=[tps[bP]])
                        if lvl < 6:
                            for h in range(4):
                                S.op("pe", lambda e: e.matmul(ps[bM][:, h * 128:(h + 1) * 128], lhsT=Mtc_[:, h, :], rhs=Mc_[:, h, :], start=True, stop=True),
                                     reads=[tMc_, tMtc_], writes=[tps[bM]])
                            if lvl < 5:
                                for h in range(4):
                                    S.op("pe", lambda e: e.matmul(ps[bMt][:, h * 128:(h + 1) * 128], lhsT=Mc_[:, h, :], rhs=Mtc_[:, h, :], start=True, stop=True),
                                         reads=[tMc_, tMtc_], writes=[tps[bMt]])
                    for c in range(nblk):
                        bP, bM, bMt = lvb[c]
                        Mc_, Mtc_, Ptc_ = Mcs[c], Mtcs[c], Ptcs[c]
                        tMc_, tMtc_, tPtc_ = tMcs[c], tMtcs[c], tPtcs[c]
                        if lvl >= 1:
                            tt("dve", Ptc_.rearrange("p h n -> p (h n)"), ps[bP][:, :], Ptc_.rearrange("p h n -> p (h n)"), ALU.add, [tps[bP], tPtc_], [tPtc_])
                        if lvl < 6:
                            S.op("act", lambda e: e.activation(out=Mc_.rearrange("p h n -> p (h n)"), in_=ps[bM][:, :], func=AF.Copy), reads=[tps[bM]], writes=[tMc_])
                            if lvl < 5:
                                S.op("act", lambda e: e.activation(out=Mtc_.rearrange("p h n -> p (h n)"), in_=ps[bMt][:, :], func=AF.Copy), reads=[tps[bMt]], writes=[tMtc_])
                for c in range(nblk):
                    cs_ = slice(c * 128, (c + 1) * 128)
                    AT_, tAT_ = ATs[c], tATs[c]
                    Ptc_, tPtc_ = Ptcs[c], tPtcs[c]
                    for p in range(2):
                        S.op("pe", lambda e: e.matmul(ps[0][:, p * 128:(p + 1) * 128], lhsT=arT[p][:, 0, cs_], rhs=Sbf[:, p, :], start=True, stop=False),
                             reads=[tarT[p], tSbf], writes=[tps[0]])
                        for hh in range(2):
                            h = 2 * p + hh
                            S.op("pe", lambda e: e.matmul(ps[0][:, h * 64:(h + 1) * 64], lhsT=AT_[h][:, 2, :], rhs=Vtok[:, c, h * 64:(h + 1) * 64],
                                                          start=False, stop=(hh == 1)), reads=[tAT_[h], tVtok], writes=[tps[0]])
                    S.op("act", lambda e: e.activation(out=Xsb.rearrange("p h n -> p (h n)"), in_=ps[0][:, 0:256], func=AF.Copy), reads=[tps[0]], writes=[tXsb])
                    for h in range(4):
                        S.op("pe", lambda e: e.matmul(ps[1][:, h * 64:(h + 1) * 64], lhsT=Ptc_[:, h, :], rhs=Xsb[:, h, :], start=True, stop=True),
                             reads=[tPtc_, tXsb], writes=[tps[1]])
                    S.op("act", lambda e: e.activation(out=Usb.rearrange("p h n -> p (h n)"), in_=ps[1][:, 0:256], func=AF.Copy), reads=[tps[1]], writes=[tUsb])
                    for hh in range(2):
                        src = ps[1][:, 0:256].rearrange("p (a b n) -> p a b n", a=2, b=2)[:, :, hh, :]
                        dst = Uz.rearrange("p (a b) n -> p a b n", b=2)[:, :, hh, hh * 64:(hh + 1) * 64]
                        S.op("act", lambda e: e.activation(out=dst, in_=src, func=AF.Copy), reads=[tps[1]], writes=[tUz])
                    for p in range(2):
                        S.op("pe", lambda e: e.matmul(ps[2 + p][:, 0:128], lhsT=Sbf[:, p, :], rhs=arT[p][:, 1, cs_], start=True, stop=False),
                             reads=[tSbf, tarT[p]], writes=[tps[2 + p]])
                        for hh in range(2):
                            h = 2 * p + hh
                            S.op("pe", lambda e: e.matmul(ps[2 + p][:, 0:128], lhsT=Uz[:, h, :], rhs=AT_[h][:, 1, :], start=False, stop=False),
                                 reads=[tUz, tAT_[h]], writes=[tps[2 + p]])
                            S.op("pe", lambda e: e.matmul(ps[2 + p][:, 0:128], lhsT=Vz[:, c, h, :], rhs=AT_[h][:, 3, :], start=False, stop=(hh == 1)),
                                 reads=[tVz, tAT_[h]], writes=[tps[2 + p]])
                        if ti >= 0:
                            evac_copy(yT[p][:, cs_], ps[2 + p][:, 0:128], [tps[2 + p]], [tyT[p]])
                    for p in range(2):
                        S.op("pe", lambda e: e.matmul(ps[4 + p][:, 0:128], lhsT=Bhat[:, c, p * 128:(p + 1) * 128],
                                                      rhs=Usb.rearrange("p h n -> p (h n)")[:, p * 128:(p + 1) * 128], start=True, stop=False),
                             reads=[tBhat, tUsb], writes=[tps[4 + p]])
                        S.op("pe", lambda e: e.matmul(ps[4 + p][:, 0:128], lhsT=Khat[:, c, p * 128:(p + 1) * 128], rhs=Vtok[:, c, p * 128:(p + 1) * 128],
                                                      start=False, stop=True), reads=[tKhat, tVtok], writes=[tps[4 + p]])
                        tt("dve", t3[:, 0:128], ps[4 + p][:, 0:128], blockones[:], ALU.mult, [tps[4 + p], tC], [tt3])
                        stt(S32[:, p, :], S32[:, p, :], WC[:, p * 2 + c:p * 2 + c + 1], t3[:, 0:128], ALU.mult, ALU.add, [tS32, tWC, tt3], [tS32])
                    S.op("act", lambda e: e.activation(out=Sbf.rearrange("p a n -> p (a n)"), in_=S32.rearrange("p a n -> p (a n)"), func=AF.Copy),
                         reads=[tS32], writes=[tSbf])

                chk('chain%d' % ti)
                if ti < 0:
                    continue

                ob = rr["oT"] % 2
                rr["oT"] += 1
                oTt = oT[ob]
                toTt = toT[ob]

                for p in range(2):
                    y = yT[p]
                    S.op("pe", lambda e, y=y: e.matmul(ps[0][:, 0:n], lhsT=blockones[:], rhs=y[:, 0:n], start=True, stop=True),
                         reads=[tC, tyT[p]], writes=[tps[0]])
                    stt(t1[:, 0:n], ps[0][:, 0:n], -1.0 / 64.0, y[:, 0:n], ALU.mult, ALU.add, [tps[0], tyT[p]], [tt1])
                    tt("pool", t2[:, 0:n], t1[:, 0:n], t1[:, 0:n], ALU.mult, [tt1], [tt2])
                    S.op("pe", lambda e: e.matmul(ps[1][:, 0:n], lhsT=blockones[:], rhs=t2[:, 0:n], start=True, stop=True),
                         reads=[tC, tt2], writes=[tps[1]])
                    act(t2[:, 0:n], ps[1][:, 0:n], AF.Ln, [tps[1], tC], [tt2], bias=eps_ap(64e-5), scale=1.0 / 64.0)
                    act(t2[:, 0:n], t2[:, 0:n], AF.Exp, [tt2], [tt2], scale=-0.5)
                    tt("dve", t1[:, 0:n], t1[:, 0:n], t2[:, 0:n], ALU.mult, [tt1, tt2], [tt1])
                    ts("dve", t1[:, 0:n], t1[:, 0:n], pvec[:, PV_GNG + p:PV_GNG + p + 1], pvec[:, PV_GNB + p:PV_GNB + p + 1],
                       ALU.mult, ALU.add, [tt1, tC], [tt1])
                    tt("pool", t1[:, 0:n], t1[:, 0:n], bonus[p][:, 0:n], ALU.add, [tt1, tbonus[p]], [tt1])
                    tt("dve", oTt[:, 2 + p, 0:n], t1[:, 0:n], gate_r[:, p, 0:n], ALU.mult, [tt1, tgr], [toTt])

                chk('rwpost%d' % ti)
                qb0 = blk0
                OO = ps[4][:, :].rearrange("p (m n) -> p m n", m=2)
                SS = ps[5][:, :].rearrange("p (m n) -> p m n", m=2)
                for h in range(2):
                    kbs = list(range(0, qb0 + 2))

                    def geom(kb):
                        delta = kb - qb0
                        q_lo = 128 if delta == 1 else 0
                        return delta, q_lo, n - q_lo, kb % 2

                    def stage_qk(kb):
                        delta, q_lo, nq, par = geom(kb)
                        for m in range(2):
                            rs = slice(m * 64, m * 64 + 64)
                            pi = 2 * par + m
                            S.op("pe", lambda e: e.matmul(ps[pi][:, 0:nq], lhsT=KT[rs, h, kb * 128:(kb + 1) * 128], rhs=QT[rs, h, q_lo:n],
                                                          start=True, stop=True), reads=[tKT, tQT], writes=[tps[pi]])

                    def stage_rest(kb):
                        delta, q_lo, nq, par = geom(kb)
                        near = delta >= -1
                        E = Eb[par]
                        tE = tEb[par]
                        for m in range(2):
                            pi = 2 * par + m
                            if near:
                                boff = 0 if delta >= 0 else 128
                                stt(Ssb[m][:, 0:nq], ps[pi][:, 0:nq], 0.125, bext[h][:, boff:boff + nq], ALU.mult, ALU.add,
                                    [tps[pi], tC], [tSsb[m]])
                                if kb == 0:
                                    S.op("pool", lambda e: e.memset(Ssb[m][0:112, 0:nq], NEG), reads=[], writes=[tSsb[m]])
                                act(E[:, m, 0:nq], Ssb[m][:, 0:nq], AF.Exp, [tSsb[m]], [tE])
                            else:
                                bc = bcol[:, 2 + h:3 + h] if kb == 0 else bcol[:, h:h + 1]
                                act(E[:, m, 0:nq], ps[pi][:, 0:nq], AF.Exp, [tps[pi], tC], [tE], bias=bc, scale=0.125)
                        first = (kb == kbs[0])
                        last = (kb == kbs[-1])
                        if q_lo == 0:
                            Ef = E.rearrange("p m n -> p (m n)")
                            S.op("pe", lambda e: e.matmul(ps[4][:, :], lhsT=Vt[:, kb, h * 128:(h + 1) * 128], rhs=Ef, start=first, stop=last),
                                 reads=[tV, tE], writes=[tps[4]])
                            S.op("pe", lambda e: e.matmul(ps[5][:, :], lhsT=ones_bf[:], rhs=Ef, start=first, stop=last),
                                 reads=[tC, tE], writes=[tps[5]])
                        else:
                            for m in range(2):
                                S.op("pe", lambda e: e.matmul(OO[:, m, q_lo:n], lhsT=Vt[:, kb, h * 128:(h + 1) * 128], rhs=E[:, m, 0:nq],
                                                              start=first, stop=(last and m == 1)), reads=[tV, tE], writes=[tps[4]])
                                S.op("pe", lambda e: e.matmul(SS[:, m, q_lo:n], lhsT=ones_bf[:], rhs=E[:, m, 0:nq],
                                                              start=first, stop=(last and m == 1)), reads=[tC, tE], writes=[tps[5]])

                    def post_a():
                        S.op("dve", lambda e: e.reciprocal(out=t1[:, 0:n], in_=SS[:, 0, 0:n]), reads=[tps[5]], writes=[tt1])
                        S.op("dve", lambda e: e.reciprocal(out=t2[:, 0:n], in_=SS[:, 1, 0:n]), reads=[tps[5]], writes=[tt2])
                        tt("dve", t1[:, 0:n], OO[:, 0, 0:n], t1[:, 0:n], ALU.mult, [tps[4], tt1], [tt1])
                        tt("dve", t2[:, 0:n], OO[:, 1, 0:n], t2[:, 0:n], ALU.mult, [tps[4], tt2], [tt2])

                    def post_b(hp):
                        stt(t1[:, 0:n], t2[:, 0:n], pder[:, 3:4], t1[:, 0:n], ALU.mult, ALU.add, [tt1, tt2, tC], [tt1])
                        tt("pool", t2[:, 0:n], t1[:, 0:n], t1[:, 0:n], ALU.mult, [tt1], [tt2])
                        S.op("pe", lambda e: e.matmul(ps[6][:, 0:n], lhsT=ones_f[:], rhs=t2[:, 0:n], start=True, stop=True),
                             reads=[tC, tt2], writes=[tps[6]])
                        act(t2[:, 0:n], ps[6][:, 0:n], AF.Ln, [tps[6], tC], [tt2], bias=eps_ap(1e-5), scale=1.0 / 128.0)
                        act(t2[:, 0:n], t2[:, 0:n], AF.Exp, [tt2], [tt2], scale=-0.5)
                        tt("dve", t1[:, 0:n], t1[:, 0:n], t2[:, 0:n], ALU.mult, [tt1, tt2], [tt1])
                        stt(oTt[:, hp, 0:n], t1[:, 0:n], pder[:, 2:3], gate_a[:, hp, 0:n], ALU.mult, ALU.mult, [tt1, tga, tC], [toTt])

                    stage_qk(kbs[0])
                    for i_, kb in enumerate(kbs):
                        if i_ + 1 < len(kbs):
                            stage_qk(kbs[i_ + 1])
                        stage_rest(kb)
                        if i_ == 0 and ti + 1 <= 15:
                            r0n = (ti + 1) * 256 + h * 128
                            layer_norm_block(x_d[r0n:r0n + 128, :], False, h)
                            if h == 1:
                                ln_done.add(ti + 1)
                        if h == 1 and i_ == min(1, len(kbs) - 1):
                            post_b(0)
                    post_a()
                    if h == 1:
                        post_b(1)

                chk('attn%d' % ti)
                for cc in range(4):
                    S.dma(lambda e: e.dma_start(out=agin[cc].ap()[:, ti * 256:ti * 256 + 256], in_=oTt[:, cc, 0:256]), reads=[toTt], writes=[tagin[cc]],
                          key="oT%d_%d" % (ob, cc))
                if DEBUG:
                    dd = dbg_d.rearrange("(c p) n -> p c n", p=128)[:, :, ti * 256:ti * 256 + 256]
                    S.dma(lambda e, dd=dd, oTt=oTt: e.dma_start(out=dd, in_=oTt[:, :, 0:256]), reads=[toTt], key="oT%d" % ob)

            chk('main')
            for cc in range(4):
                S.dma(lambda e: e.collective_compute("AllGather", ALU.bypass, replica_groups=[[0, 1, 2, 3], [4, 5, 6, 7]],
                                                     ins=[agin[cc].ap()], outs=[agout[cc].ap()]),
                      reads=[tagin[cc]], writes=[tagout[cc]], key="cc%d" % cc, queue="pool", inc=1)
            chk('ag')
            S.barrier()
            apos[0] = 0
            oq = carve(16 * 1024 * 2, BF16, "p (c n) -> p c n", c=16)
            toq = S.tile("oq")
            gbt = carve(4 * D * 4, F32, "p (a n) -> p a n", a=4)
            tgb = S.tile("gbt")
            rsd = [carve(D * 4, F32) for _ in range(2)]
            trsd = [S.tile("rsd%d" % i) for i in range(2)]
            fstE = [carve(4 * 6 * 4, F32, "p (a n) -> p a n", a=4) for _ in range(2)]
            fmvE = [carve(8 * 4, F32) for _ in range(2)]
            tfE = [S.tile("fstE%d" % i) for i in range(2)]
            fstP = [carve(4 * 6 * 4, F32, "p (a n) -> p a n", a=4) for _ in range(2)]
            fmvP = [carve(8 * 4, F32) for _ in range(2)]
            tfP = [S.tile("fstP%d" % i) for i in range(2)]
            assert apos[0] <= arena_w

            for kc in range(16):
                b = kc % 2
                S.dma(lambda e, kc=kc, b=b: e.dma_start(out=xb[b][:, 0:D], in_=wout_d[kc * 128:(kc + 1) * 128, :]),
                      writes=[txb[b]], key="xb%d" % b)
                eng = ("dve", "pool", "act")[kc % 3]
                if eng == "act":
                    S.op("act", lambda e, kc=kc, b=b: e.activation(out=W[:, kc, 0:D], in_=xb[b][:, 0:D], func=AF.Copy),
                         reads=[txb[b]], writes=[tW])
                else:
                    S.op(eng, lambda e, kc=kc, b=b: e.tensor_copy(out=W[:, kc, 0:D], in_=xb[b][:, 0:D]), reads=[txb[b]], writes=[tW])
            gsrc = bass.AP(tensor=gb_d.tensor, offset=0, ap=[[0, 128], [D, 4], [1, D]])
            S.dma(lambda e: e.dma_start(out=gbt, in_=gsrc), writes=[tgb], key="gbt")

            idxt = sbt("idxt", [128, 4], mybir.dt.int32)
            tidx = S.tile("idxt")
            S.dma(lambda e: e.dma_start(out=idxt[:], in_=idx_d), writes=[tidx], key="idxt")
            for kc in range(16):
                r_, c_ = kc // 4, kc % 4
                agv = agout[c_].ap().rearrange("e (q n) -> (e q) n", q=4)
                S.dma(lambda e: e.indirect_dma_start(out=oq[:, kc, :], out_offset=None, in_=agv,
                                                     in_offset=bass.IndirectOffsetOnAxis(ap=idxt[:, r_:r_ + 1], axis=0)),
                      reads=[tagout[c_], tidx], writes=[toq], key="oq", queue="pool")
            alpha = 2.0 ** 0.25
            def ln_emb(blk):
                b = blk % 2
                xt, r, tr = xb[b], rsd[b], trsd[b]
                fst, fmv, tf = fstE[b], fmvE[b], tfE[b]
                S.dma(lambda e: e.dma_start(out=xt[:, 0:D], in_=xq_d[blk * 128:(blk + 1) * 128, :]), writes=[txb[b]], key="xb%d" % b)
                for q in range(4):
                    S.op("dve", lambda e: e.bn_stats(out=fst[:, q, :], in_=xt[:, q * 512:(q + 1) * 512]), reads=[txb[b]], writes=[tf])
                S.op("dve", lambda e: e.bn_aggr(out=fmv[:, 0:2], in_=fst.rearrange("p a n -> p (a n)")), reads=[tf], writes=[tf])
                act(fmv[:, 2:3], fmv[:, 1:2], AF.Ln, [tf], [tf], bias=eps_ap(1e-5))
                act(fmv[:, 2:3], fmv[:, 2:3], AF.Exp, [tf], [tf], scale=-0.5)
                stt(fmv[:, 3:4], fmv[:, 0:1], -1.0, fmv[:, 2:3], ALU.mult, ALU.mult, [tf], [tf])
                S.op("act", lambda e: e.activation(out=r, in_=xt[:, 0:D], func=AF.Identity, bias=fmv[:, 3:4], scale=fmv[:, 2:3]),
                     reads=[txb[b], tf], writes=[tr])
                tt("pool", r, r, gbt[:, 0, :], ALU.mult, [tr, tgb], [tr])
                tt("pool", r, r, gbt[:, 1, :], ALU.add, [tr, tgb], [tr])

            def proj(blk):
                b = blk % 2
                r, tr = rsd[b], trsd[b]
                for ct in range(4):
                    for kc in range(16):
                        S.op("pe", lambda e: e.matmul(ps[ct][:, :], lhsT=oq[:, kc, blk * 128:(blk + 1) * 128],
                                                      rhs=W[:, kc, ct * 512:(ct + 1) * 512], start=(kc == 0), stop=(kc == 15)),
                             reads=[toq, tW], writes=[tps[ct]])
                    stt(r[:, ct * 512:(ct + 1) * 512], r[:, ct * 512:(ct + 1) * 512], alpha, ps[ct][:, :], ALU.mult, ALU.add,
                        [tr, tps[ct]], [tr])

            def post_ln(blk):
                b = blk % 2
                r, tr = rsd[b], trsd[b]
                fst, fmv, tf = fstP[b], fmvP[b], tfP[b]
                for q in range(4):
                    S.op("dve", lambda e: e.bn_stats(out=fst[:, q, :], in_=r[:, q * 512:(q + 1) * 512]), reads=[tr], writes=[tf])
                S.op("dve", lambda e: e.bn_aggr(out=fmv[:, 4:6], in_=fst.rearrange("p a n -> p (a n)")), reads=[tf], writes=[tf])
                act(fmv[:, 6:7], fmv[:, 5:6], AF.Ln, [tf], [tf], bias=eps_ap(1e-5))
                act(fmv[:, 6:7], fmv[:, 6:7], AF.Exp, [tf], [tf], scale=-0.5)
                stt(fmv[:, 7:8], fmv[:, 4:5], -1.0, fmv[:, 6:7], ALU.mult, ALU.mult, [tf], [tf])
                S.op("act", lambda e: e.activation(out=r, in_=r, func=AF.Identity, bias=fmv[:, 7:8], scale=fmv[:, 6:7]),
                     reads=[tr, tf], writes=[tr])
                tt("dve", r, r, gbt[:, 2, :], ALU.mult, [tr, tgb], [tr])
                tt("pool", r, r, gbt[:, 3, :], ALU.add, [tr, tgb], [tr])
                S.dma(lambda e: e.dma_start(out=out_d[blk * 128:(blk + 1) * 128, :], in_=r), reads=[tr], key="rsd%d" % b)

            ln_emb(0)
            for blk in range(8):
                if blk + 1 < 8:
                    ln_emb(blk + 1)
                proj(blk)
                post_ln(blk)

        except _Stop:
            pass
        S.emit()
    return nc


_NC_CACHE = {}


def kernel(**inp):
    maps = _prep_inputs(inp)
    x = np.asarray(inp["x"], np.float32)
    for c in range(8):
        b, r = c // 4, c % 4
        maps[c]["xq"] = np.ascontiguousarray(x[b, r * 1024:(r + 1) * 1024])
        p = np.arange(128)[:, None]
        kc = np.arange(4)[None, :]
        maps[c]["idx"] = ((kc * 128 + p) * 4 + r).astype(np.int32)
    if "nc" not in _NC_CACHE:
        _NC_CACHE["nc"] = build_nc()
    nc = _NC_CACHE["nc"]
    res = run_bass_kernel_spmd(nc, maps, core_ids=list(range(8)))
    out = np.zeros((2, SEQ, D), np.float32)
    for c in range(8):
        b, r = c // 4, c % 4
        out[b, r * 1024:(r + 1) * 1024] = res.results[c]["out"]
    if DEBUG:
        kernel.dbg = [res.results[c]["dbg"] for c in range(8)]
    return out
```
